# Optimizing a Trainium2 kernel written in Bass

```python
import math
import jax, jax.numpy as jnp
from jax import lax
import numpy as np

D_MODEL = 1024
BATCH = 2
SEQ = 16384
DEPTH = 2

CHUNK = 64
N_MIXERS = 4
HEAD_DIM = 64
GROUP_WIDTH = D_MODEL // N_MIXERS
HEADS_PER_GROUP = GROUP_WIDTH // HEAD_DIM
LEFT_CHUNKS = 8
BAND = (LEFT_CHUNKS + 1) * CHUNK
REL_CLIP = 128
CONV_WIDTH = 4
DIFF_QK_DIM = HEAD_DIM // 2
Q_BLOCK = 128
D_FF = 2816
EPS = 1e-6
NEG = -1e30

IN_PROJ_SIZES = [GROUP_WIDTH, GROUP_WIDTH, GROUP_WIDTH,
                 2 * GROUP_WIDTH, GROUP_WIDTH, GROUP_WIDTH,
                 HEADS_PER_GROUP, HEADS_PER_GROUP,
                 GROUP_WIDTH, GROUP_WIDTH, GROUP_WIDTH,
                 GROUP_WIDTH, GROUP_WIDTH, GROUP_WIDTH]
IN_COLS = sum(IN_PROJ_SIZES)
IN_PROJ_OFFSETS = [int(o) for o in np.cumsum(IN_PROJ_SIZES)[:-1]]

kernel_name = "hybrid_chunk_causal_encoder"


def rms_norm(x, g):
    xf = x.astype(jnp.float32)
    y = xf * lax.rsqrt(jnp.mean(xf * xf, axis=-1, keepdims=True) + EPS)
    return (y * g.astype(jnp.float32)).astype(x.dtype)


def swiglu(x, wg, wu, wd):
    return (jax.nn.silu(x @ wg) * (x @ wu)) @ wd


def split_heads(t, n_heads):
    b, s, w = t.shape
    return t.reshape(b, s, n_heads, w // n_heads).transpose(0, 2, 1, 3)


def merge_heads(t):
    b, h, s, d = t.shape
    return t.transpose(0, 2, 1, 3).reshape(b, s, h * d)


def causal_depthwise_conv(x, w, bias):
    width = w.shape[0]
    s = x.shape[1]
    xp = jnp.pad(x, ((0, 0), (width - 1, 0), (0, 0)))
    acc = bias
    for j in range(width):
        acc = acc + xp[:, j:j + s] * w[j]
    return acc


def chunk_relpos_attention(q, k, v, gq, gk, rel_bias):
    b, h, s, d = q.shape
    nc = s // CHUNK
    pad = LEFT_CHUNKS * CHUNK
    q = rms_norm(q, gq)
    k = rms_norm(k, gk)
    kp = jnp.pad(k, ((0, 0), (0, 0), (pad, 0), (0, 0))).reshape(b, h, nc + LEFT_CHUNKS, CHUNK, d)
    vp = jnp.pad(v, ((0, 0), (0, 0), (pad, 0), (0, 0))).reshape(b, h, nc + LEFT_CHUNKS, CHUNK, d)
    kb = jnp.concatenate([kp[:, :, i:i + nc] for i in range(LEFT_CHUNKS + 1)], axis=3)
    vb = jnp.concatenate([vp[:, :, i:i + nc] for i in range(LEFT_CHUNKS + 1)], axis=3)
    qc = q.reshape(b, h, nc, CHUNK, d)
    scores = jnp.einsum("bhcid,bhcjd->bhcij", qc, kb).astype(jnp.float32) * (d ** -0.5)
    rel = pad + jnp.arange(CHUNK)[:, None] - jnp.arange(BAND)[None, :]
    idx = jnp.clip(rel, -REL_CLIP, REL_CLIP) + REL_CLIP
    bias = rel_bias[:, idx].astype(jnp.float32)
    key_pos = jnp.arange(nc)[:, None] * CHUNK - pad + jnp.arange(BAND)[None, :]
    valid = key_pos >= 0
    scores = jnp.where(valid[None, None, :, None, :], scores + bias[None, :, None], NEG)
    probs = jax.nn.softmax(scores, axis=-1).astype(v.dtype)
    out = jnp.einsum("bhcij,bhcjd->bhcid", probs, vb)
    return out.reshape(b, h, s, d)


def mlstm_chunkwise(q, k, v, i_pre, f_pre):
    b, h, s, d = q.shape
    nc = s // CHUNK
    f32 = jnp.float32
    q = q.astype(f32)
    k = k.astype(f32) * (d ** -0.5)
    v = v.astype(f32)
    i_pre = i_pre.astype(f32)
    log_f = jax.nn.log_sigmoid(f_pre.astype(f32))

    def chunks(t):
        return jnp.moveaxis(t.reshape(b, h, nc, CHUNK, *t.shape[3:]), 2, 0)

    causal = jnp.tril(jnp.ones((CHUNK, CHUNK), dtype=bool))

    def step(carry, xs):
        c_mat, n_vec, m = carry
        qc, kc, vc, ic, lfc = xs
        bcum = jnp.cumsum(lfc, axis=-1)
        dmat = jnp.where(causal, bcum[..., :, None] - bcum[..., None, :] + ic[..., None, :], -jnp.inf)
        inter = bcum + m[..., None]
        m_t = jnp.maximum(inter, jnp.max(dmat, axis=-1))
        w_intra = jnp.exp(dmat - m_t[..., None])
        s_inter = jnp.exp(inter - m_t)
        qk = jnp.einsum("bhtd,bhsd->bhts", qc, kc) * w_intra
        num = s_inter[..., None] * jnp.einsum("bhtd,bhde->bhte", qc, c_mat) + jnp.einsum("bhts,bhse->bhte", qk, vc)
        den = s_inter * jnp.einsum("bhtd,bhd->bht", qc, n_vec) + jnp.sum(qk, axis=-1)
        h_out = num / jnp.maximum(jnp.abs(den), jnp.exp(-m_t))[..., None]
        b_tot = bcum[..., -1]
        g = b_tot[..., None] - bcum + ic
        m_new = jnp.maximum(b_tot + m, jnp.max(g, axis=-1))
        a = jnp.exp(b_tot + m - m_new)
        wg = jnp.exp(g - m_new[..., None])
        c_new = a[..., None, None] * c_mat + jnp.einsum("bhs,bhsd,bhse->bhde", wg, kc, vc)
        n_new = a[..., None] * n_vec + jnp.einsum("bhs,bhsd->bhd", wg, kc)
        return (c_new, n_new, m_new), h_out

    init = (jnp.zeros((b, h, d, d), f32), jnp.zeros((b, h, d), f32), jnp.zeros((b, h), f32))
    _, hs = lax.scan(step, init, (chunks(q), chunks(k), chunks(v), chunks(i_pre), chunks(log_f)))
    return jnp.moveaxis(hs, 0, 2).reshape(b, h, s, d)


def diff_attention(q, k, v, lam):
    b, h, _, s, dq = q.shape
    nb = s // Q_BLOCK
    qb = jnp.moveaxis(q.reshape(b, h, 2, nb, Q_BLOCK, dq), 3, 0)
    k_chunk = jnp.arange(s) // CHUNK

    def block(args):
        qblk, bi = args
        sc = jnp.einsum("bhmqd,bhmkd->bhmqk", qblk, k).astype(jnp.float32) * (dq ** -0.5)
        q_chunk = (bi * Q_BLOCK + jnp.arange(Q_BLOCK)) // CHUNK
        mask = k_chunk[None, :] <= q_chunk[:, None]
        p = jax.nn.softmax(jnp.where(mask, sc, NEG), axis=-1)
        w = p[:, :, 0] - lam * p[:, :, 1]
        return jnp.einsum("bhqk,bhke->bhqe", w.astype(v.dtype), v)

    out = lax.map(block, (qb, jnp.arange(nb)))
    return jnp.moveaxis(out, 0, 2).reshape(b, h, s, v.shape[-1])


def stick_breaking_attention(q, k, v):
    b, h, s, d = q.shape
    nb = s // Q_BLOCK
    qb = jnp.moveaxis(q.reshape(b, h, nb, Q_BLOCK, d), 2, 0)
    k_pos = jnp.arange(s)

    def block(args):
        qblk, bi = args
        z = jnp.einsum("bhqd,bhkd->bhqk", qblk, k).astype(jnp.float32) * (d ** -0.5)
        q_pos = bi * Q_BLOCK + jnp.arange(Q_BLOCK)
        before = k_pos[None, :] < q_pos[:, None]
        log_keep = jnp.where(before, jax.nn.log_sigmoid(-z), 0.0)
        between = lax.cumsum(log_keep, axis=3, reverse=True) - log_keep
        a = jnp.where(before, jnp.exp(jax.nn.log_sigmoid(z) + between), 0.0)
        return jnp.einsum("bhqk,bhke->bhqe", a.astype(v.dtype), v)

    out = lax.map(block, (qb, jnp.arange(nb)))
    return jnp.moveaxis(out, 0, 2).reshape(b, h, s, d)


def hybrid_mix(h, w_in, a_q_norm, a_k_norm, a_rel_bias, b_conv_w, b_conv_b, b_gate_bias,
               b_out_norm, c_q_norm, c_k_norm, c_lambda, c_out_norm, lam_init):
    nh = HEADS_PER_GROUP
    p = h @ w_in
    (aq, ak, av, bqk, bv, bo, bi, bf, cq, ck, cv, dq, dk, dv) = jnp.split(p, IN_PROJ_OFFSETS, axis=-1)
    b, s, _ = h.shape

    ya = merge_heads(chunk_relpos_attention(split_heads(aq, nh), split_heads(ak, nh), split_heads(av, nh),
                                            a_q_norm, a_k_norm, a_rel_bias))

    bqk = jax.nn.silu(causal_depthwise_conv(bqk, b_conv_w, b_conv_b))
    bq, bk = jnp.split(bqk, 2, axis=-1)
    i_pre = (bi + b_gate_bias[0]).transpose(0, 2, 1)
    f_pre = (bf + b_gate_bias[1]).transpose(0, 2, 1)
    hb = mlstm_chunkwise(split_heads(bq, nh), split_heads(bk, nh), split_heads(bv, nh), i_pre, f_pre)
    hb = rms_norm(hb, b_out_norm[:, None, :])
    yb = merge_heads(hb).astype(h.dtype) * jax.nn.sigmoid(bo)

    cq = cq.reshape(b, s, nh, 2, DIFF_QK_DIM).transpose(0, 2, 3, 1, 4)
    ck = ck.reshape(b, s, nh, 2, DIFF_QK_DIM).transpose(0, 2, 3, 1, 4)
    lv = c_lambda.astype(jnp.float32)
    lam = jnp.exp(jnp.sum(lv[0] * lv[1])) - jnp.exp(jnp.sum(lv[2] * lv[3])) + lam_init
    hc = diff_attention(rms_norm(cq, c_q_norm), rms_norm(ck, c_k_norm), split_heads(cv, nh), lam)
    yc = merge_heads(rms_norm(hc, c_out_norm) * (1.0 - lam_init))

    yd = merge_heads(stick_breaking_attention(split_heads(dq, nh), split_heads(dk, nh), split_heads(dv, nh)))

    return jnp.concatenate([ya, yb, yc, yd], axis=-1)


def setup_inputs(seed: int = 0) -> dict:
    key = jax.random.key(seed)
    ks = jax.random.split(key, 32)
    f32 = jnp.float32

    def nrm(k, shape, scale):
        return jax.random.normal(k, shape, f32) * scale

    def gain(k, shape):
        return 1.0 + 0.05 * jax.random.normal(k, shape, f32)

    i_bias = nrm(ks[13], (DEPTH, HEADS_PER_GROUP), 0.1)
    f_bias = jnp.linspace(3.0, 6.0, HEADS_PER_GROUP, dtype=f32)[None, :] + nrm(ks[14], (DEPTH, HEADS_PER_GROUP), 0.1)
    return {
        "x": jax.random.normal(ks[0], (BATCH, SEQ, D_MODEL), f32),
        "ffn1_norm": gain(ks[1], (DEPTH, D_MODEL)),
        "ffn1_wg": nrm(ks[2], (DEPTH, D_MODEL, D_FF), D_MODEL ** -0.5),
        "ffn1_wu": nrm(ks[3], (DEPTH, D_MODEL, D_FF), D_MODEL ** -0.5),
        "ffn1_wd": nrm(ks[4], (DEPTH, D_FF, D_MODEL), D_FF ** -0.5),
        "mix_norm": gain(ks[5], (DEPTH, D_MODEL)),
        "w_in": nrm(ks[6], (DEPTH, D_MODEL, IN_COLS), D_MODEL ** -0.5),
        "a_q_norm": gain(ks[7], (DEPTH, HEAD_DIM)),
        "a_k_norm": gain(ks[8], (DEPTH, HEAD_DIM)),
        "a_rel_bias": nrm(ks[9], (DEPTH, HEADS_PER_GROUP, 2 * REL_CLIP + 1), 0.2),
        "b_conv_w": nrm(ks[10], (DEPTH, CONV_WIDTH, 2 * GROUP_WIDTH), CONV_WIDTH ** -0.5),
        "b_conv_b": nrm(ks[11], (DEPTH, 2 * GROUP_WIDTH), 0.02),
        "b_gate_bias": jnp.stack([i_bias, f_bias], axis=1),
        "b_out_norm": gain(ks[12], (DEPTH, HEADS_PER_GROUP, HEAD_DIM)),
        "c_q_norm": gain(ks[15], (DEPTH, DIFF_QK_DIM)),
        "c_k_norm": gain(ks[16], (DEPTH, DIFF_QK_DIM)),
        "c_lambda": nrm(ks[17], (DEPTH, 4, DIFF_QK_DIM), 0.1),
        "c_out_norm": gain(ks[18], (DEPTH, HEAD_DIM)),
        "w_out": nrm(ks[19], (DEPTH, D_MODEL, D_MODEL), D_MODEL ** -0.5),
        "ffn2_norm": gain(ks[20], (DEPTH, D_MODEL)),
        "ffn2_wg": nrm(ks[21], (DEPTH, D_MODEL, D_FF), D_MODEL ** -0.5),
        "ffn2_wu": nrm(ks[22], (DEPTH, D_MODEL, D_FF), D_MODEL ** -0.5),
        "ffn2_wd": nrm(ks[23], (DEPTH, D_FF, D_MODEL), D_FF ** -0.5),
    }


def reference(x, ffn1_norm, ffn1_wg, ffn1_wu, ffn1_wd, mix_norm, w_in, a_q_norm, a_k_norm, a_rel_bias,
              b_conv_w, b_conv_b, b_gate_bias, b_out_norm, c_q_norm, c_k_norm, c_lambda, c_out_norm,
              w_out, ffn2_norm, ffn2_wg, ffn2_wu, ffn2_wd):
    for l in range(DEPTH):
        lam_init = 0.8 - 0.6 * math.exp(-0.3 * l)
        x = x + 0.5 * swiglu(rms_norm(x, ffn1_norm[l]), ffn1_wg[l], ffn1_wu[l], ffn1_wd[l])
        h = rms_norm(x, mix_norm[l])
        y = hybrid_mix(h, w_in[l], a_q_norm[l], a_k_norm[l], a_rel_bias[l], b_conv_w[l], b_conv_b[l],
                       b_gate_bias[l], b_out_norm[l], c_q_norm[l], c_k_norm[l], c_lambda[l], c_out_norm[l],
                       lam_init)
        x = x + y @ w_out[l]
        x = x + 0.5 * swiglu(rms_norm(x, ffn2_norm[l]), ffn2_wg[l], ffn2_wu[l], ffn2_wd[l])
    return x
```

```python
import numpy as np
from contextlib import ExitStack
import concourse.bass as bass
import concourse.mybir as mybir

F32 = mybir.dt.float32
BF16 = mybir.dt.bfloat16
AF = mybir.ActivationFunctionType
ALU = mybir.AluOpType
AX = mybir.AxisListType

EPOCH = 4096


class Buf:
    __slots__ = ("w", "r", "name")

    def __init__(self, name=""):
        self.w = None
        self.r = {}
        self.name = name


class KB:
    def __init__(self, nc, stack):
        self.nc = nc
        self.st = stack
        self.E = {"pe": nc.tensor, "act": nc.scalar, "dve": nc.vector, "pool": nc.gpsimd, "sp": nc.sync}
        self.cnt = {e: 0 for e in self.E}
        self.sems = {e: [] for e in self.E}
        self.waited = {e: {} for e in self.E}
        self.ndma = 12
        self.dma_sems = {}
        self.dma_cnt = {}
        self.dma_rr = {}
        self.nsem = 0
        self.uid = 0
        self.mem = stack

    def sem(self, name):
        self.nsem += 1
        return self.st.enter_context(self.nc.semaphore(name))

    def sbuf(self, name, shape, dt):
        self.uid += 1
        return self.mem.enter_context(self.nc.sbuf_tensor(f"sb{self.uid}_" + name, list(shape), dt))

    def psum(self, name, shape, dt):
        self.uid += 1
        return self.mem.enter_context(self.nc.psum_tensor(f"ps{self.uid}_" + name, list(shape), dt))

    def buf(self, name=""):
        return Buf(name)

    def _esem(self, e, n):
        ep = (n - 1) // EPOCH
        while len(self.sems[e]) <= ep:
            self.sems[e].append(self.sem(f"c_{e}_{len(self.sems[e])}"))
        return self.sems[e][ep], (n - 1) % EPOCH + 1

    def _wait(self, e, ev):
        if ev[0] == "e":
            _, src, n = ev
            if src == e and e == "pe":
                return
            key = ("e", src)
            if self.waited[e].get(key, 0) >= n:
                return
            if src == e and n > self.cnt[e]:
                raise RuntimeError("self-wait on future event")
            s, v = self._esem(src, n)
            self.E[e].wait_ge(s, v)
            self.waited[e][key] = n
        else:
            _, q, i, k = ev
            key = ("d", q, i)
            if self.waited[e].get(key, 0) >= k:
                return
            self.E[e].wait_ge(self.dma_sems[q][i], 16 * k)
            self.waited[e][key] = k

    @staticmethod
    def _evkey(ev):
        return (ev[0], ev[1]) if ev[0] == "e" else (ev[0], ev[1], ev[2])

    def _collect(self, reads, writes):
        deps = []
        for b in reads:
            if b.w is not None:
                deps.append(b.w)
        for b in writes:
            if b.w is not None:
                deps.append(b.w)
            deps.extend(b.r.values())
        return deps

    def _record(self, ev, reads, writes):
        k = self._evkey(ev)
        for b in reads:
            b.r[k] = ev
        for b in writes:
            b.w = ev
            b.r = {}

    def op(self, e, fn, reads=(), writes=(), inc=True):
        for ev in self._collect(reads, writes):
            self._wait(e, ev)
        ins = fn()
        if inc:
            self.cnt[e] += 1
            s, v = self._esem(e, self.cnt[e])
            ins.then_inc(s, 1)
            ev = ("e", e, self.cnt[e])
        else:
            ev = ("e", e, self.cnt[e] + 1)
        self._record(ev, reads, writes)
        return ins

    def dma(self, q, out, in_, reads=(), writes=(), **kw):
        for ev in self._collect(reads, writes):
            self._wait(q, ev)
        if q not in self.dma_sems:
            self.dma_sems[q] = [self.sem(f"d_{q}_{i}") for i in range(self.ndma)]
            self.dma_cnt[q] = [0] * self.ndma
            self.dma_rr[q] = 0
        i = self.dma_rr[q]
        self.dma_rr[q] = (i + 1) % self.ndma
        if self.dma_cnt[q][i] > 0:
            self._wait(q, ("d", q, i, self.dma_cnt[q][i]))
        self.dma_cnt[q][i] += 1
        ins = self.E[q].dma_start(out=out, in_=in_, **kw)
        ins.then_inc(self.dma_sems[q][i], 16)
        ev = ("d", q, i, self.dma_cnt[q][i])
        self._record(ev, reads, writes)
        return ins

    def barrier(self, extra_sems=()):
        for e in self.E:
            for q in self.dma_sems:
                for i in range(self.ndma):
                    if self.dma_cnt[q][i] > 0:
                        self._wait(e, ("d", q, i, self.dma_cnt[q][i]))
            for src in ("pe", "act", "dve", "pool"):
                if self.cnt[src] > 0 and not (src == e and e == "pe"):
                    self._wait(e, ("e", src, self.cnt[src]))
            for (sm, v) in extra_sems:
                self.E[e].wait_ge(sm, v)

    def allgather(self, src2d, dst2d, groups, chunk_rows=128):
        self.barrier()
        R_ = src2d.shape[0]
        nk = R_ // chunk_rows
        ng = len(groups[0])
        sms = []
        for k in range(nk):
            sm = self.sem(f"cc{self.nsem}")
            self.nc.gpsimd.collective_compute("AllGather", ALU.bypass, replica_groups=groups,
                                              ins=[src2d[k * chunk_rows:(k + 1) * chunk_rows, :]],
                                              outs=[dst2d[k * ng * chunk_rows:(k + 1) * ng * chunk_rows, :]]).then_inc(sm, 1)
            sms.append(sm)
        for e in self.E:
            for sm in sms:
                self.E[e].wait_ge(sm, 1)

    def finish(self):
        for q in self.dma_sems:
            for i in range(self.ndma):
                if self.dma_cnt[q][i] > 0:
                    self._wait("sp", ("d", q, i, self.dma_cnt[q][i]))
        for e in ("pe", "act", "dve", "pool"):
            if self.cnt[e] > 0:
                self._wait("sp", ("e", e, self.cnt[e]))


D = 1024
DFF = 2816
NFC = DFF // 128
TT = 256
SUB = TT // 128
EPS = 1e-6


class TokRes:
    def __init__(self, kb, with_pre):
        nc = kb.nc
        self.kb = kb
        self.Wg = kb.sbuf("Wg", [128, 8, DFF], BF16); self.Wg_b = kb.buf()
        self.Wu = kb.sbuf("Wu", [128, 8, DFF], BF16); self.Wu_b = kb.buf()
        self.Wd = kb.sbuf("Wd", [128, NFC, D], BF16); self.Wd_b = kb.buf()
        self.stage = [kb.sbuf(f"stage{i}", [128, 1024], F32) for i in range(2)]
        self.stage_b = [kb.buf() for _ in range(2)]
        self.gt = kb.sbuf("gt", [128, 8], F32); self.gt_b = kb.buf()
        self.ident = kb.sbuf("ident", [128, 128], BF16); self.ident_b = kb.buf()
        self.xt = [kb.sbuf(f"xt{i}", [128, SUB, D], F32) for i in range(2)]
        self.xt_b = [kb.buf() for _ in range(2)]
        self.xn = kb.sbuf("xn", [128, D], BF16); self.xn_b = kb.buf()
        self.st = kb.sbuf("stat", [128, 8], F32); self.st_b = kb.buf()
        self.xnT = [kb.sbuf(f"xnT{i}", [128, 8, TT], BF16) for i in range(2)]
        self.xnT_b = [kb.buf() for _ in range(2)]
        self.hid = kb.sbuf("hid", [128, NFC, TT], BF16)
        self.hid_b = [kb.buf() for _ in range(NFC)]
        self.sg = [kb.sbuf(f"sg{i}", [128, TT], F32) for i in range(2)]
        self.sg_b = [kb.buf() for _ in range(2)]
        self.hTo = kb.sbuf("hTo", [128, 8, TT], BF16); self.hTo_b = kb.buf()
        self.with_pre = with_pre
        if with_pre:
            self.Wo = kb.sbuf("Wo", [128, 8, D], BF16); self.Wo_b = kb.buf()
            self.yt = [kb.sbuf(f"yt{i}", [128, 8, TT], BF16) for i in range(2)]
            self.yt_b = [kb.buf() for _ in range(2)]
        self.psg = [kb.psum(f"psg{i}", [128, 512], F32) for i in range(2)]
        self.psg_b = [kb.buf() for _ in range(2)]
        self.psd = [kb.psum(f"psd{i}", [128, 512], F32) for i in range(2)]
        self.psd_b = [kb.buf() for _ in range(2)]
        self.tp = [kb.psum(f"tp{i}", [128, 8, 128], BF16) for i in range(2)]
        self.tp_b = [kb.buf() for _ in range(2)]
        self.ntp = 0
        self.npsd = 0
        self.ncast = 0


def load_consts(kb, R, ident_d):
    kb.dma("sp", R.ident[:], ident_d, writes=[R.ident_b])


def load_ffn_weights(kb, R, g_lay, wg, wu, wd, w_out=None):
    nc = kb.nc
    kb.dma("sp", R.gt[:], g_lay, writes=[R.gt_b])

    def cast(dst_ap, dst_b, src_ap, src_b, scal):
        e = "dve" if R.ncast % 2 == 0 else "pool"
        R.ncast += 1
        E = kb.E[e]
        if scal is None:
            kb.op(e, lambda: E.tensor_copy(out=dst_ap, in_=src_ap), reads=[src_b], writes=[dst_b])
        else:
            kb.op(e, lambda: E.tensor_scalar(out=dst_ap, in0=src_ap, scalar1=scal, scalar2=None, op0=ALU.mult),
                  reads=[src_b, R.gt_b], writes=[dst_b])

    k = 0
    for (W, Wb, src) in ((R.Wg, R.Wg_b, wg), (R.Wu, R.Wu_b, wu)):
        for kc in range(8):
            for (c0, c1) in ((0, 1024), (1024, 2048), (2048, DFF)):
                sb = k % 2; k += 1
                kb.dma("sp", R.stage[sb][:, 0:c1 - c0], src[kc * 128:(kc + 1) * 128, c0:c1],
                       writes=[R.stage_b[sb]])
                cast(W[:, kc, c0:c1], Wb, R.stage[sb][:, 0:c1 - c0], R.stage_b[sb], R.gt[:, kc:kc + 1])
    for fc in range(NFC):
        sb = k % 2; k += 1
        kb.dma("sp", R.stage[sb][:, 0:D], wd[fc * 128:(fc + 1) * 128, :], writes=[R.stage_b[sb]])
        cast(R.Wd[:, fc, :], R.Wd_b, R.stage[sb][:, 0:D], R.stage_b[sb], None)
    if w_out is not None:
        for kc in range(8):
            sb = k % 2; k += 1
            kb.dma("sp", R.stage[sb][:, 0:D], w_out[kc * 128:(kc + 1) * 128, :], writes=[R.stage_b[sb]])
            cast(R.Wo[:, kc, :], R.Wo_b, R.stage[sb][:, 0:D], R.stage_b[sb], None)


def norm_transpose(kb, R, x_ap, x_b, dstT, dstT_b, s):
    nc = kb.nc
    ss = R.st[:, 0:1]; rs = R.st[:, 1:2]; rstd = R.st[:, 2:3]
    kb.op("act", lambda: nc.scalar.activation(out=R.xn[:], in_=x_ap, func=AF.Square, accum_out=ss),
          reads=[x_b], writes=[R.xn_b, R.st_b])
    kb.op("act", lambda: nc.scalar.activation(out=rs, in_=ss, func=AF.Sqrt, bias=EPS, scale=1.0 / D),
          reads=[R.st_b], writes=[R.st_b])
    kb.op("dve", lambda: nc.vector.reciprocal(out=rstd, in_=rs), reads=[R.st_b], writes=[R.st_b])
    kb.op("dve", lambda: nc.vector.tensor_scalar(out=R.xn[:], in0=x_ap, scalar1=rstd, scalar2=None, op0=ALU.mult),
          reads=[x_b, R.st_b], writes=[R.xn_b])
    ti = R.ntp % 2; R.ntp += 1
    tp = R.tp[ti]; tpb = R.tp_b[ti]
    for kc in range(8):
        kb.op("pe", lambda kc=kc: nc.tensor.transpose(out=tp[:, kc, :], in_=R.xn[:, kc * 128:(kc + 1) * 128],
                                                      identity=R.ident[:]),
              reads=[R.xn_b, R.ident_b], writes=[tpb], inc=(kc == 7))
    kb.op("act", lambda: nc.scalar.copy(out=dstT[:, :, s * 128:(s + 1) * 128], in_=tp[:, :, :]),
          reads=[tpb], writes=[dstT_b])


def token_pass(kb, R, T, x_in, x_out, pre=None, post=None, in_bufs=None, out_bufs=None, pre_bufs=None, post_bufs=None):
    nc = kb.nc
    NT = T // TT

    def stage_load(i):
        bi = i % 2
        kb.dma("sp", R.xt[bi][:, :, :], x_in[i * TT:(i + 1) * TT, :].rearrange("(s p) d -> p s d", p=128),
               reads=([in_bufs[i]] if in_bufs else []), writes=[R.xt_b[bi]])
        if pre is not None:
            kb.dma("sp", R.yt[bi][:, :, :], pre[:, i * TT:(i + 1) * TT].rearrange("(c p) t -> p c t", p=128),
                   reads=([pre_bufs[i]] if pre_bufs else []), writes=[R.yt_b[bi]])

    def stage_pre(i):
        bi = i % 2
        if pre is None:
            return
        for s in range(SUB):
            for h in range(2):
                pi = R.npsd % 2; R.npsd += 1
                for kc in range(8):
                    kb.op("pe", lambda kc=kc: nc.tensor.matmul(R.psd[pi][:, :], lhsT=R.yt[bi][:, kc, s * 128:(s + 1) * 128],
                                                               rhs=R.Wo[:, kc, h * 512:(h + 1) * 512],
                                                               start=(kc == 0), stop=(kc == 7)),
                          reads=[R.yt_b[bi], R.Wo_b], writes=[R.psd_b[pi]], inc=(kc == 7))
                xs = R.xt[bi][:, s, h * 512:(h + 1) * 512]
                kb.op("dve", lambda: nc.vector.tensor_tensor(out=xs, in0=R.psd[pi][:, :], in1=xs, op=ALU.add),
                      reads=[R.psd_b[pi], R.xt_b[bi]], writes=[R.xt_b[bi]])

    def stage_a(i):
        bi = i % 2
        for s in range(SUB):
            norm_transpose(kb, R, R.xt[bi][:, s, :], R.xt_b[bi], R.xnT[bi], R.xnT_b[bi], s)

    def stage_b(i):
        bi = i % 2
        for fc in range(NFC):
            gi = fc % 2
            for (W, Wb, off) in ((R.Wg, R.Wg_b, 0), (R.Wu, R.Wu_b, 256)):
                for kc in range(8):
                    kb.op("pe", lambda kc=kc, W=W, off=off: nc.tensor.matmul(
                        R.psg[gi][:, off:off + TT], lhsT=W[:, kc, fc * 128:(fc + 1) * 128], rhs=R.xnT[bi][:, kc, :],
                        start=(kc == 0), stop=(kc == 7)),
                          reads=[Wb, R.xnT_b[bi]], writes=[R.psg_b[gi]], inc=(kc == 7))
            kb.op("act", lambda: nc.scalar.activation(out=R.sg[gi][:, :], in_=R.psg[gi][:, 0:TT], func=AF.Silu),
                  reads=[R.psg_b[gi]], writes=[R.sg_b[gi]])
            kb.op("dve", lambda: nc.vector.tensor_tensor(out=R.hid[:, fc, :], in0=R.sg[gi][:, :],
                                                         in1=R.psg[gi][:, 256:256 + TT], op=ALU.mult),
                  reads=[R.sg_b[gi], R.psg_b[gi]], writes=[R.hid_b[fc]])

    def stage_c(i):
        bi = i % 2
        for s in range(SUB):
            for h in range(2):
                pi = R.npsd % 2; R.npsd += 1
                for fc in range(NFC):
                    kb.op("pe", lambda fc=fc: nc.tensor.matmul(R.psd[pi][:, :], lhsT=R.hid[:, fc, s * 128:(s + 1) * 128],
                                                               rhs=R.Wd[:, fc, h * 512:(h + 1) * 512],
                                                               start=(fc == 0), stop=(fc == NFC - 1)),
                          reads=[R.hid_b[fc], R.Wd_b], writes=[R.psd_b[pi]], inc=(fc == NFC - 1))
                xs = R.xt[bi][:, s, h * 512:(h + 1) * 512]
                kb.op("dve", lambda: nc.vector.scalar_tensor_tensor(out=xs, in0=R.psd[pi][:, :], scalar=0.5, in1=xs,
                                                                    op0=ALU.mult, op1=ALU.add),
                      reads=[R.psd_b[pi], R.xt_b[bi]], writes=[R.xt_b[bi]])
            if post is not None:
                norm_transpose(kb, R, R.xt[bi][:, s, :], R.xt_b[bi], R.hTo, R.hTo_b, s)
        kb.dma("pool", x_out[i * TT:(i + 1) * TT, :].rearrange("(s p) d -> p s d", p=128), R.xt[bi][:, :, :],
               reads=[R.xt_b[bi]], writes=([out_bufs[i]] if out_bufs else []))
        if post is not None:
            kb.dma("pool", post[:, i * TT:(i + 1) * TT].rearrange("(c p) t -> p c t", p=128), R.hTo[:, :, :],
                   reads=[R.hTo_b], writes=([post_bufs[i]] if post_bufs else []))

    stage_load(0)
    stage_pre(0)
    stage_a(0)
    for i in range(NT):
        if i + 1 < NT:
            stage_load(i + 1)
        stage_b(i)
        if i + 1 < NT:
            stage_pre(i + 1)
            stage_a(i + 1)
        stage_c(i)


QT_ = 512
NW = 834
A_Q, A_K, A_V = 0, 64, 128
B_Q, B_K, B_V, B_O, B_I, B_F = 192, 256, 320, 384, 448, 449
C_Q, C_K, C_V = 450, 514, 578
D_Q, D_K, D_V = 642, 706, 770


class MixRes:
    def __init__(self, kb, S):
        self.S = S
        self.NB = S // 128
        self.NQ = S // QT_
        NB = self.NB
        self.Wm = kb.sbuf("Wm", [128, 8, NW], BF16); self.Wm_b = kb.buf()
        self.gm = kb.sbuf("gm", [128, 8], F32); self.gm_b = kb.buf()
        self.ident = kb.sbuf("identm", [128, 128], BF16); self.ident_b = kb.buf()
        self.ht = [kb.sbuf(f"ht{i}", [128, 8, QT_], BF16) for i in range(2)]; self.ht_b = [kb.buf() for _ in range(2)]
        self.QT = kb.sbuf("QT", [64, S], BF16); self.QT_b = kb.buf()
        self.KT = kb.sbuf("KT", [64, S], BF16); self.KT_b = kb.buf()
        self.Va = kb.sbuf("Va", [128, NB, 128], BF16); self.Va_b = kb.buf()
        self.par = kb.sbuf("par", [128, 32], F32); self.par_b = kb.buf()
        self.cst = kb.sbuf("cst", [128, 256], BF16); self.cst_b = kb.buf()
        self.sq = [kb.sbuf(f"sq{i}", [64, QT_], BF16) for i in range(2)]; self.sq_b = [kb.buf() for _ in range(2)]
        self.rr = [kb.sbuf(f"rr{i}", [64, QT_], F32) for i in range(2)]; self.rr_b = [kb.buf() for _ in range(2)]
        self.e32 = [kb.sbuf(f"e32_{i}", [128, 2, QT_], F32) for i in range(2)]; self.e32_b = [kb.buf() for _ in range(2)]
        self.wst = [self.e32[i][:, :, :].rearrange("p a b -> p (a b)")[:, 0:NW] for i in range(2)]; self.wst_b = self.e32_b
        self.e16 = [kb.sbuf(f"e16_{i}", [128, 2, QT_], BF16) for i in range(3)]; self.e16_b = [kb.buf() for _ in range(3)]
        self.sp16 = [kb.sbuf(f"sp16_{i}", [128, QT_], BF16) for i in range(2)]; self.sp16_b = [kb.buf() for _ in range(2)]
        self.fin = [kb.sbuf(f"fin{i}", [64, QT_], F32) for i in range(3)]; self.fin_b = [kb.buf() for _ in range(3)]
        self.yo = [kb.sbuf(f"yo{i}", [64, QT_], BF16) for i in range(2)]; self.yo_b = [kb.buf() for _ in range(2)]
        self.mask = kb.sbuf("mask", [128, 4, QT_], BF16); self.mask_b = kb.buf()
        self.EB = kb.sbuf("EB", [128, 8, QT_], F32); self.EB_b = kb.buf()
        self.tri = kb.sbuf("tri", [128, 2, 128], BF16); self.tri_b = kb.buf()
        self.ps = [kb.psum(f"pm{i}", [128, 512], F32) for i in range(7)]
        self.ps_b = [kb.buf() for _ in range(7)]
        self.tpbank = kb.psum("tpbank", [128, 8, 128], BF16)
        self.tpb = [self.tpbank[:, i, :] for i in range(8)]
        self.tpb_b = [kb.buf() for _ in range(8)]
        self.nyo = 0
        self.ne16 = 0


def mix_load_common(kb, R, wsel, gmix_lay, ident_d, cst_d):
    nc = kb.nc
    kb.dma("sp", R.gm[:], gmix_lay, writes=[R.gm_b])
    kb.dma("sp", R.ident[:], ident_d, writes=[R.ident_b])
    kb.dma("sp", R.cst[:], cst_d, writes=[R.cst_b])
    for kc in range(8):
        sb = kc % 2
        kb.dma("sp", R.wst[sb][:, :], wsel[kc * 128:(kc + 1) * 128, :], writes=[R.wst_b[sb]])
        kb.op("dve", lambda: nc.vector.tensor_scalar(out=R.Wm[:, kc, :], in0=R.wst[sb][:, :], scalar1=R.gm[:, kc:kc + 1],
                                                     scalar2=None, op0=ALU.mult),
              reads=[R.wst_b[sb], R.gm_b], writes=[R.Wm_b])


def load_ht(kb, R, hT, qt):
    bi = qt % 2
    if callable(hT):
        src = hT(qt).rearrange("c p t -> p c t")
    else:
        src = hT[:, qt * QT_:(qt + 1) * QT_].rearrange("(c p) t -> p c t", p=128)
    kb.dma("sp", R.ht[bi][:, :, :], src, writes=[R.ht_b[bi]])
    return R.ht[bi], R.ht_b[bi]


def proj_fm(kb, R, ht, ht_b, c0, ncol, pb):
    nc = kb.nc
    for kc in range(8):
        kb.op("pe", lambda kc=kc: nc.tensor.matmul(R.ps[pb][0:ncol, :], lhsT=R.Wm[:, kc, c0:c0 + ncol], rhs=ht[:, kc, :],
                                                   start=(kc == 0), stop=(kc == 7)),
              reads=[R.Wm_b, ht_b], writes=[R.ps_b[pb]], inc=(kc == 7))


def proj_tm(kb, R, ht, ht_b, c0, ncol, pb, s):
    nc = kb.nc
    for kc in range(8):
        kb.op("pe", lambda kc=kc: nc.tensor.matmul(R.ps[pb][:, s * 128:s * 128 + ncol], lhsT=ht[:, kc, s * 128:(s + 1) * 128],
                                                   rhs=R.Wm[:, kc, c0:c0 + ncol], start=(kc == 0), stop=(kc == 7)),
              reads=[R.Wm_b, ht_b], writes=[R.ps_b[pb]], inc=(kc == 7))


def qk_norm_store(kb, R, pb, pb2, dst, dst_b, qt, gcol, cmat, inv_n, i2):
    nc = kb.nc
    sq, sqb = R.sq[i2], R.sq_b[i2]
    rr, rrb = R.rr[i2], R.rr_b[i2]
    kb.op("act", lambda: nc.scalar.activation(out=sq[:, :], in_=R.ps[pb][0:64, :], func=AF.Square),
          reads=[R.ps_b[pb]], writes=[sqb])
    kb.op("pe", lambda: nc.tensor.matmul(R.ps[pb2][0:64, :], lhsT=cmat, rhs=sq[:, :], start=True, stop=True),
          reads=[sqb, R.cst_b], writes=[R.ps_b[pb2]])
    kb.op("act", lambda: nc.scalar.activation(out=rr[:, :], in_=R.ps[pb2][0:64, :], func=AF.Sqrt, bias=EPS, scale=inv_n),
          reads=[R.ps_b[pb2]], writes=[rrb])
    kb.op("dve", lambda: nc.vector.reciprocal(out=rr[:, :], in_=rr[:, :]), reads=[rrb], writes=[rrb])
    kb.op("dve", lambda: nc.vector.scalar_tensor_tensor(out=dst[:, qt * QT_:(qt + 1) * QT_], in0=R.ps[pb][0:64, :],
                                                        scalar=R.par[0:64, gcol:gcol + 1], in1=rr[:, :],
                                                        op0=ALU.mult, op1=ALU.mult),
          reads=[R.ps_b[pb], rrb, R.par_b], writes=[dst_b])


def v_store(kb, R, ht, ht_b, c0, qt, pb):
    nc = kb.nc
    for s in range(4):
        proj_tm(kb, R, ht, ht_b, c0, 64, pb, s)
    src = R.ps[pb][:, :].rearrange("p (s c) -> p s c", c=128)[:, :, 0:64]
    kb.op("act", lambda: nc.scalar.copy(out=R.Va[:, qt * 4:(qt + 1) * 4, 0:64], in_=src),
          reads=[R.ps_b[pb]], writes=[R.Va_b])


def ydst(yT_d, row0, qt):
    if callable(yT_d):
        return yT_d(row0, qt)
    return yT_d[row0:row0 + 64, qt * QT_:(qt + 1) * QT_]


def out_store(kb, R, yT_d, row0, qt, src_fn, reads):
    i = R.nyo % 2; R.nyo += 1
    src_fn(R.yo[i], R.yo_b[i])
    kb.dma("pool", ydst(yT_d, row0, qt), R.yo[i][:, :], reads=[R.yo_b[i]])


def mixer_c(kb, R, hT, yT_d, row0, masks_c):
    nc = kb.nc
    S, NB, NQ = R.S, R.NB, R.NQ
    kb.dma("sp", R.mask[:, :, :], masks_c[:, :, 0, :], writes=[R.mask_b])
    kb.op("pool", lambda: nc.gpsimd.memset(R.Va[:, :, 64:128], 1.0), writes=[R.Va_b])
    bd32 = R.cst[0:64, 64:128]
    ones64 = R.cst[0:64, 0:64]
    for qt in range(NQ):
        ht, htb = load_ht(kb, R, hT, qt)
        proj_fm(kb, R, ht, htb, C_Q, 64, 0)
        proj_fm(kb, R, ht, htb, C_K, 64, 2)
        qk_norm_store(kb, R, 0, 1, R.QT, R.QT_b, qt, 0, bd32, 1.0 / 32, 0)
        qk_norm_store(kb, R, 2, 3, R.KT, R.KT_b, qt, 1, bd32, 1.0 / 32, 1)
        v_store(kb, R, ht, htb, C_V, qt, 4 + (qt % 2))
    for qt in range(NQ):
        nkb = 4 * qt + 4
        O0, O1 = 4, 5
        for kbk in range(nkb):
            sb = 2 * (kbk % 2)
            for m in range(2):
                kb.op("pe", lambda m=m: nc.tensor.matmul(R.ps[sb + m][:, :],
                                                         lhsT=R.KT[m * 32:(m + 1) * 32, kbk * 128:(kbk + 1) * 128],
                                                         rhs=R.QT[m * 32:(m + 1) * 32, qt * QT_:(qt + 1) * QT_],
                                                         start=True, stop=True),
                      reads=[R.KT_b, R.QT_b], writes=[R.ps_b[sb + m]])
            ei = R.ne16 % 3; R.ne16 += 1
            e, eb = R.e16[ei], R.e16_b[ei]
            for m in range(2):
                kb.op("act", lambda m=m: nc.scalar.activation(out=e[:, m, :], in_=R.ps[sb + m][:, :], func=AF.Exp),
                      reads=[R.ps_b[sb + m]], writes=[eb])
            r = kbk - 4 * qt
            if r >= 0:
                for m in range(2):
                    kb.op("dve", lambda m=m: nc.vector.tensor_tensor(out=e[:, m, :], in0=e[:, m, :], in1=R.mask[:, r, :], op=ALU.mult),
                          reads=[eb, R.mask_b], writes=[eb])
            for m in range(2):
                kb.op("pe", lambda m=m: nc.tensor.matmul(R.ps[O0 + m][:, :], lhsT=R.Va[:, kbk, :], rhs=e[:, m, :],
                                                         start=(kbk == 0), stop=(kbk == nkb - 1)),
                      reads=[R.Va_b, eb], writes=[R.ps_b[O0 + m]])
        f0, f1, f2 = R.fin
        b0, b1, b2 = R.fin_b
        kb.op("dve", lambda: nc.vector.reciprocal(out=f0[:, :], in_=R.ps[O0][64:128, :]), reads=[R.ps_b[O0]], writes=[b0])
        kb.op("dve", lambda: nc.vector.tensor_tensor(out=f0[:, :], in0=R.ps[O0][0:64, :], in1=f0[:, :], op=ALU.mult),
              reads=[R.ps_b[O0], b0], writes=[b0])
        kb.op("dve", lambda: nc.vector.reciprocal(out=f1[:, :], in_=R.ps[O1][64:128, :]), reads=[R.ps_b[O1]], writes=[b1])
        kb.op("dve", lambda: nc.vector.tensor_tensor(out=f1[:, :], in0=R.ps[O1][0:64, :], in1=f1[:, :], op=ALU.mult),
              reads=[R.ps_b[O1], b1], writes=[b1])
        kb.op("dve", lambda: nc.vector.scalar_tensor_tensor(out=f2[:, :], in0=f1[:, :], scalar=R.par[0:64, 2:3], in1=f0[:, :],
                                                            op0=ALU.mult, op1=ALU.add),
              reads=[b0, b1, R.par_b], writes=[b2])
        kb.op("act", lambda: nc.scalar.activation(out=R.sq[0][:, :], in_=f2[:, :], func=AF.Square), reads=[b2], writes=[R.sq_b[0]])
        kb.op("pe", lambda: nc.tensor.matmul(R.ps[6][0:64, :], lhsT=ones64, rhs=R.sq[0][:, :], start=True, stop=True),
              reads=[R.sq_b[0], R.cst_b], writes=[R.ps_b[6]])
        kb.op("act", lambda: nc.scalar.activation(out=R.rr[0][:, :], in_=R.ps[6][0:64, :], func=AF.Sqrt, bias=EPS, scale=1.0 / 64),
              reads=[R.ps_b[6]], writes=[R.rr_b[0]])
        kb.op("dve", lambda: nc.vector.reciprocal(out=R.rr[0][:, :], in_=R.rr[0][:, :]), reads=[R.rr_b[0]], writes=[R.rr_b[0]])

        def fn(yo, yob):
            kb.op("dve", lambda: nc.vector.scalar_tensor_tensor(out=yo[:, :], in0=f2[:, :], scalar=R.par[0:64, 3:4], in1=R.rr[0][:, :],
                                                                op0=ALU.mult, op1=ALU.mult),
                  reads=[b2, R.rr_b[0], R.par_b], writes=[yob])
        out_store(kb, R, yT_d, row0, qt, fn, None)


def mixer_d(kb, R, hT, yT_d, row0, masks_d, tri_d):
    nc = kb.nc
    S, NB, NQ = R.S, R.NB, R.NQ
    kb.dma("sp", R.mask[:, :, :], masks_d, writes=[R.mask_b])
    kb.dma("sp", R.tri[:, :, :], tri_d, writes=[R.tri_b])
    for qt in range(NQ):
        ht, htb = load_ht(kb, R, hT, qt)
        proj_fm(kb, R, ht, htb, D_Q, 64, 0)
        proj_fm(kb, R, ht, htb, D_K, 64, 1)
        kb.op("act", lambda: nc.scalar.activation(out=R.QT[:, qt * QT_:(qt + 1) * QT_], in_=R.ps[0][0:64, :], func=AF.Copy, scale=0.125),
              reads=[R.ps_b[0]], writes=[R.QT_b])
        kb.op("dve", lambda: nc.vector.tensor_copy(out=R.KT[:, qt * QT_:(qt + 1) * QT_], in_=R.ps[1][0:64, :]),
              reads=[R.ps_b[1]], writes=[R.KT_b])
        v_store(kb, R, ht, htb, D_V, qt, 4 + (qt % 2))
    RB, OB = 4, 5
    for qt in range(NQ):
        kbs = list(range(4 * qt + 3, -1, -1))
        n = len(kbs)

        def z_mm(i):
            kbk = kbs[i]
            zb = i % 2
            kb.op("pe", lambda: nc.tensor.matmul(R.ps[zb][:, :], lhsT=R.KT[:, kbk * 128:(kbk + 1) * 128],
                                                 rhs=R.QT[:, qt * QT_:(qt + 1) * QT_], start=True, stop=True),
                  reads=[R.KT_b, R.QT_b], writes=[R.ps_b[zb]])

        def esp(i):
            kbk = kbs[i]
            zb = i % 2
            e, eb = R.e32[i % 2], R.e32_b[i % 2]
            sp, spb = R.sp16[i % 2], R.sp16_b[i % 2]
            kb.op("act", lambda: nc.scalar.activation(out=e[:, 0, :], in_=R.ps[zb][:, :], func=AF.Exp),
                  reads=[R.ps_b[zb]], writes=[eb])
            kb.op("act", lambda: nc.scalar.activation(out=sp[:, :], in_=e[:, 0, :], func=AF.Ln, bias=1.0),
                  reads=[eb], writes=[spb])
            r = kbk - 4 * qt
            if r >= 0:
                kb.op("dve", lambda: nc.vector.tensor_tensor(out=sp[:, :], in0=sp[:, :], in1=R.mask[:, r, :], op=ALU.mult),
                      reads=[spb, R.mask_b], writes=[spb])

        def rest(i):
            kbk = kbs[i]
            e, eb = R.e32[i % 2], R.e32_b[i % 2]
            sp, spb = R.sp16[i % 2], R.sp16_b[i % 2]
            kb.op("pe", lambda: nc.tensor.matmul(R.ps[RB][:, :], lhsT=R.tri[:, 0, :], rhs=sp[:, :], start=(i == 0), stop=False),
                  reads=[R.tri_b, spb], writes=[R.ps_b[RB]])
            tt, ttb = R.e16[i % 2], R.e16_b[i % 2]
            kb.op("act", lambda: nc.scalar.activation(out=tt[:, 0, :], in_=R.ps[RB][:, :], func=AF.Exp, scale=-1.0),
                  reads=[R.ps_b[RB]], writes=[ttb])
            kb.op("pe", lambda: nc.tensor.matmul(R.ps[RB][:, :], lhsT=R.tri[:, 1, :], rhs=sp[:, :], start=False, stop=(i == n - 1)),
                  reads=[R.tri_b, spb], writes=[R.ps_b[RB]])
            kb.op("dve", lambda: nc.vector.tensor_tensor(out=tt[:, 1, :], in0=e[:, 0, :], in1=tt[:, 0, :], op=ALU.mult),
                  reads=[eb, ttb], writes=[ttb])
            r = kbk - 4 * qt
            if r >= 0:
                kb.op("dve", lambda: nc.vector.tensor_tensor(out=tt[:, 1, :], in0=tt[:, 1, :], in1=R.mask[:, r, :], op=ALU.mult),
                      reads=[ttb, R.mask_b], writes=[ttb])
            kb.op("pe", lambda: nc.tensor.matmul(R.ps[OB][0:64, :], lhsT=R.Va[:, kbk, 0:64], rhs=tt[:, 1, :],
                                                 start=(i == 0), stop=(i == n - 1)),
                  reads=[R.Va_b, ttb], writes=[R.ps_b[OB]])

        z_mm(0)
        esp(0)
        for i in range(n):
            if i + 1 < n:
                z_mm(i + 1)
                esp(i + 1)
            rest(i)

        def fn(yo, yob):
            kb.op("dve", lambda: nc.vector.tensor_copy(out=yo[:, :], in_=R.ps[OB][0:64, :]), reads=[R.ps_b[OB]], writes=[yob])
        out_store(kb, R, yT_d, row0, qt, fn, None)


def mixer_a(kb, R, hT, yT_d, row0, biasT_d):
    nc = kb.nc
    S, NB, NQ = R.S, R.NB, R.NQ
    kb.dma("sp", R.EB[:, :, :], biasT_d, writes=[R.EB_b])
    for r in range(8):
        kb.op("act", lambda r=r: nc.scalar.activation(out=R.EB[:, r, :], in_=R.EB[:, r, :], func=AF.Exp),
              reads=[R.EB_b], writes=[R.EB_b])
    kb.op("pool", lambda: nc.gpsimd.memset(R.Va[:, :, 64:128], 1.0), writes=[R.Va_b])
    ones64 = R.cst[0:64, 0:64]
    for qt in range(NQ):
        ht, htb = load_ht(kb, R, hT, qt)
        proj_fm(kb, R, ht, htb, A_Q, 64, 0)
        proj_fm(kb, R, ht, htb, A_K, 64, 2)
        qk_norm_store(kb, R, 0, 1, R.QT, R.QT_b, qt, 4, ones64, 1.0 / 64, 0)
        qk_norm_store(kb, R, 2, 3, R.KT, R.KT_b, qt, 5, ones64, 1.0 / 64, 1)
        v_store(kb, R, ht, htb, A_V, qt, 4 + (qt % 2))
    OB = 4
    for qt in range(NQ):
        rs = [r for r in range(8) if 4 * qt - 4 + r >= 0]
        for j, r in enumerate(rs):
            kbk = 4 * qt - 4 + r
            sb = j % 2
            kb.op("pe", lambda: nc.tensor.matmul(R.ps[sb][:, :], lhsT=R.KT[:, kbk * 128:(kbk + 1) * 128],
                                                 rhs=R.QT[:, qt * QT_:(qt + 1) * QT_], start=True, stop=True),
                  reads=[R.KT_b, R.QT_b], writes=[R.ps_b[sb]])
            e, eb = R.e32[j % 2], R.e32_b[j % 2]
            p, pbuf = R.e16[j % 2], R.e16_b[j % 2]
            kb.op("act", lambda: nc.scalar.activation(out=e[:, 0, :], in_=R.ps[sb][:, :], func=AF.Exp),
                  reads=[R.ps_b[sb]], writes=[eb])
            kb.op("dve", lambda: nc.vector.tensor_tensor(out=p[:, 0, :], in0=e[:, 0, :], in1=R.EB[:, r, :], op=ALU.mult),
                  reads=[eb, R.EB_b], writes=[pbuf])
            kb.op("pe", lambda: nc.tensor.matmul(R.ps[OB][:, :], lhsT=R.Va[:, kbk, :], rhs=p[:, 0, :],
                                                 start=(j == 0), stop=(j == len(rs) - 1)),
                  reads=[R.Va_b, pbuf], writes=[R.ps_b[OB]])
        f0, b0 = R.fin[0], R.fin_b[0]
        kb.op("dve", lambda: nc.vector.reciprocal(out=f0[:, :], in_=R.ps[OB][64:128, :]), reads=[R.ps_b[OB]], writes=[b0])

        def fn(yo, yob):
            kb.op("dve", lambda: nc.vector.tensor_tensor(out=yo[:, :], in0=R.ps[OB][0:64, :], in1=f0[:, :], op=ALU.mult),
                  reads=[R.ps_b[OB], b0], writes=[yob])
        out_store(kb, R, yT_d, row0, qt, fn, None)


class MixResB:
    def __init__(self, kb, R):
        NB = R.NB
        self.Osig = R.EB[:, :, :].bitcast(BF16).rearrange("p a (b c) -> p (a b) c", c=64)[:, 0:NB, :]; self.Osig_b = R.EB_b
        self.G = kb.sbuf("Gates", [128, 8, NB], F32); self.G_b = kb.buf()
        self.trif = kb.sbuf("trif", [128, 2, 128], F32); self.trif_b = kb.buf()
        self.cw = kb.sbuf("convw", [128, 8], F32); self.cw_b = kb.buf()
        self.gob = kb.sbuf("gob", [128, 64], F32); self.gob_b = kb.buf()
        self.St = [kb.sbuf(f"St{i}", [64, 65], F32) for i in range(2)]; self.St_b = [kb.buf() for _ in range(2)]
        self.Sb = [kb.sbuf(f"Sb{i}", [64, 65], BF16) for i in range(2)]; self.Sb_b = [kb.buf() for _ in range(2)]
        self.tok = [kb.sbuf(f"tok{i}", [128, 3, 64], BF16) for i in range(2)]; self.tok_b = [kb.buf() for _ in range(2)]
        self.qkT = [kb.sbuf(f"qkT{i}", [64, 2, 128], BF16) for i in range(2)]; self.qkT_b = [kb.buf() for _ in range(2)]
        self.qkm = [kb.sbuf(f"qkm{i}", [128, 128], BF16) for i in range(2)]; self.qkm_b = [kb.buf() for _ in range(2)]
        self.cm = kb.sbuf("cmask", [128, 128], F32); self.cm_b = kb.buf()
        self.hn = [kb.sbuf(f"hn{i}", [128, 64], F32) for i in range(2)]; self.hn_b = [kb.buf() for _ in range(2)]
        self.hs = [kb.sbuf(f"hs{i}", [128, 8], F32) for i in range(2)]; self.hs_b = [kb.buf() for _ in range(2)]
        self.yb = [kb.sbuf(f"yb{i}", [128, 64], BF16) for i in range(2)]; self.yb_b = [kb.buf() for _ in range(2)]
        self.jk = kb.sbuf("jk", [128, 64], BF16); self.jk_b = kb.buf()


def mixer_b(kb, R, RB_, hT, yT_d, row0, bpar_d, trif_d, cmask_d, gob_d):
    nc = kb.nc
    S, NB, NQ = R.S, R.NB, R.NQ
    B = RB_
    kb.dma("sp", B.cw[:, :], bpar_d, writes=[B.cw_b])
    kb.dma("sp", B.trif[:, :, :], trif_d, writes=[B.trif_b])
    kb.dma("sp", B.cm[:, :], cmask_d, writes=[B.cm_b])
    kb.dma("sp", B.gob[:, :], gob_d, writes=[B.gob_b])
    kb.op("pool", lambda: nc.gpsimd.memset(R.Va[:, :, 64:65], 1.0), writes=[R.Va_b])
    kb.op("dve", lambda: nc.vector.tensor_scalar(out=B.cw[:, 7:8], in0=B.cw[:, 6:7], scalar1=-1.0, scalar2=None, op0=ALU.mult),
          reads=[B.cw_b], writes=[B.cw_b])
    cv = [R.e32[i][:, :, :].rearrange("p a b -> p (a b)") for i in range(2)]
    cvb = R.e32_b
    accA = R.e16[0][:, :, :].rearrange("p a b -> p (a b)").bitcast(F32); accB_ = R.e16[1][:, :, :].rearrange("p a b -> p (a b)").bitcast(F32)
    accA_b = R.e16_b[0]; accB_b = R.e16_b[1]
    for qt in range(NQ):
        ht, htb = load_ht(kb, R, hT, qt)
        ci = qt % 2
        proj_fm(kb, R, ht, htb, B_Q, 128, 0)
        if qt == 0:
            kb.op("dve", lambda: nc.vector.memset(cv[ci][:, 0:3], 0.0), writes=[cvb[ci]])
        else:
            kb.op("dve", lambda: nc.vector.tensor_copy(out=cv[ci][:, 0:3], in_=cv[1 - ci][:, 512:515]),
                  reads=[cvb[1 - ci]], writes=[cvb[ci]])
        kb.op("act", lambda: nc.scalar.copy(out=cv[ci][:, 3:515], in_=R.ps[0][:, :]), reads=[R.ps_b[0]], writes=[cvb[ci]])
        kb.op("dve", lambda: nc.vector.tensor_scalar(out=accA, in0=cv[ci][:, 3:515], scalar1=B.cw[:, 3:4], scalar2=B.cw[:, 4:5],
                                                     op0=ALU.mult, op1=ALU.add),
              reads=[cvb[ci], B.cw_b], writes=[accA_b])
        for j in (2, 1, 0):
            kb.op("dve", lambda j=j: nc.vector.scalar_tensor_tensor(out=accA, in0=cv[ci][:, j:j + 512], scalar=B.cw[:, j:j + 1],
                                                                    in1=accA, op0=ALU.mult, op1=ALU.add),
                  reads=[cvb[ci], B.cw_b, accA_b], writes=[accA_b])
        kb.op("act", lambda: nc.scalar.activation(out=accB_, in_=accA, func=AF.Sigmoid), reads=[accA_b], writes=[accB_b])
        kb.op("dve", lambda: nc.vector.tensor_tensor(out=accB_, in0=accA, in1=accB_, op=ALU.mult), reads=[accA_b, accB_b], writes=[accB_b])
        kb.op("act", lambda: nc.scalar.copy(out=R.QT[:, qt * QT_:(qt + 1) * QT_], in_=accB_[0:64, :]), reads=[accB_b], writes=[R.QT_b])
        kb.op("act", lambda: nc.scalar.copy(out=R.KT[:, qt * QT_:(qt + 1) * QT_], in_=accB_[64:128, :]), reads=[accB_b], writes=[R.KT_b])
        pb = 4 + (qt % 2)
        for s in range(4):
            nonlocal_pb = 3 + ((qt * 4 + s) % 4)
            for kc in range(8):
                kb.op("pe", lambda kc=kc: nc.tensor.matmul(R.ps[nonlocal_pb][:, 0:130], lhsT=ht[:, kc, s * 128:(s + 1) * 128],
                                                           rhs=R.Wm[:, kc, B_V:B_V + 130], start=(kc == 0), stop=(kc == 7)),
                      reads=[R.Wm_b, htb], writes=[R.ps_b[nonlocal_pb]], inc=(kc == 7))
            blk = qt * 4 + s
            kb.op("dve", lambda: nc.vector.tensor_copy(out=R.Va[:, blk, 0:64], in_=R.ps[nonlocal_pb][:, 0:64]),
                  reads=[R.ps_b[nonlocal_pb]], writes=[R.Va_b])
            kb.op("act", lambda: nc.scalar.activation(out=B.Osig[:, blk, :], in_=R.ps[nonlocal_pb][:, 64:128], func=AF.Sigmoid),
                  reads=[R.ps_b[nonlocal_pb]], writes=[B.Osig_b])
            kb.op("dve", lambda: nc.vector.tensor_copy(out=B.G[:, 0:2, blk], in_=R.ps[nonlocal_pb][:, 128:130]),
                  reads=[R.ps_b[nonlocal_pb]], writes=[B.G_b])
    G = B.G
    kb.op("act", lambda: nc.scalar.activation(out=G[:, 2, :], in_=G[:, 1, :], func=AF.Exp, scale=-1.0, bias=B.cw[:, 7:8]),
          reads=[B.G_b, B.cw_b], writes=[B.G_b])
    kb.op("act", lambda: nc.scalar.activation(out=G[:, 2, :], in_=G[:, 2, :], func=AF.Ln, bias=1.0), reads=[B.G_b], writes=[B.G_b])
    kb.op("dve", lambda: nc.vector.tensor_scalar(out=G[:, 2, :], in0=G[:, 2, :], scalar1=-1.0, scalar2=None, op0=ALU.mult),
          reads=[B.G_b], writes=[B.G_b])
    kb.op("pe", lambda: nc.tensor.matmul(R.ps[0][:, 0:NB], lhsT=B.trif[:, 0, :], rhs=G[:, 2, :], start=True, stop=True),
          reads=[B.trif_b, B.G_b], writes=[R.ps_b[0]])
    kb.op("pe", lambda: nc.tensor.matmul(R.ps[1][:, 0:NB], lhsT=B.trif[:, 1, :], rhs=G[:, 2, :], start=True, stop=True),
          reads=[B.trif_b, B.G_b], writes=[R.ps_b[1]])
    kb.op("dve", lambda: nc.vector.tensor_copy(out=G[:, 3, :], in_=R.ps[0][:, 0:NB]), reads=[R.ps_b[0]], writes=[B.G_b])
    kb.op("act", lambda: nc.scalar.activation(out=G[:, 4, :], in_=G[:, 3, :], func=AF.Exp), reads=[B.G_b], writes=[B.G_b])
    kb.op("dve", lambda: nc.vector.tensor_tensor(out=G[:, 5, :], in0=G[:, 0, :], in1=G[:, 3, :], op=ALU.subtract),
          reads=[B.G_b], writes=[B.G_b])
    kb.op("dve", lambda: nc.vector.tensor_tensor(out=G[:, 6, :], in0=G[:, 5, :], in1=R.ps[1][:, 0:NB], op=ALU.add),
          reads=[B.G_b, R.ps_b[1]], writes=[B.G_b])
    kb.op("act", lambda: nc.scalar.activation(out=G[:, 5, :], in_=G[:, 5, :], func=AF.Exp, bias=B.cw[:, 5:6]),
          reads=[B.G_b, B.cw_b], writes=[B.G_b])
    kb.op("act", lambda: nc.scalar.activation(out=G[:, 6, :], in_=G[:, 6, :], func=AF.Exp, bias=B.cw[:, 5:6]),
          reads=[B.G_b, B.cw_b], writes=[B.G_b])
    kb.op("dve", lambda: nc.vector.tensor_scalar(out=G[:, 5:7, :], in0=G[:, 5:7, :], scalar1=0.125, scalar2=None, op0=ALU.mult),
          reads=[B.G_b], writes=[B.G_b])
    kb.op("act", lambda: nc.scalar.activation(out=G[:, 7, :], in_=R.ps[1][:, 0:NB], func=AF.Exp), reads=[R.ps_b[1]], writes=[B.G_b])
    kb.op("dve", lambda: nc.vector.memset(B.St[0][:, :], 0.0), writes=[B.St_b[0]])
    kb.op("dve", lambda: nc.vector.memset(B.Sb[0][:, :], 0.0), writes=[B.Sb_b[0]])
    PT, PT2, PS_, PO, PU = 0, 1, 2, 3, 6
    for b in range(NB):
        i2 = b % 2
        tok, tokb = B.tok[i2], B.tok_b[i2]
        qkT, qkTb = B.qkT[i2], B.qkT_b[i2]
        tpA = R.ps[0][:, :].bitcast(BF16); tpq = tpA[:, 0:128]; tpk = tpA[:, 128:256]
        kb.op("pe", lambda: nc.tensor.transpose(out=tpq[:, 0:64], in_=R.QT[:, b * 128:(b + 1) * 128], identity=R.ident[0:64, 0:64]),
              reads=[R.QT_b, R.ident_b], writes=[R.ps_b[0]])
        kb.op("pe", lambda: nc.tensor.transpose(out=tpk[:, 0:64], in_=R.KT[:, b * 128:(b + 1) * 128], identity=R.ident[0:64, 0:64]),
              reads=[R.KT_b, R.ident_b], writes=[R.ps_b[0]])
        kb.op("dve", lambda: nc.vector.tensor_scalar(out=tok[:, 0, :], in0=tpq[:, 0:64], scalar1=G[:, 4, b:b + 1], scalar2=None, op0=ALU.mult),
              reads=[R.ps_b[0], B.G_b], writes=[tokb])
        kb.op("dve", lambda: nc.vector.tensor_scalar(out=tok[:, 1, :], in0=tpk[:, 0:64], scalar1=G[:, 5, b:b + 1], scalar2=None, op0=ALU.mult),
              reads=[R.ps_b[0], B.G_b], writes=[tokb])
        kb.op("dve", lambda: nc.vector.tensor_scalar(out=tok[:, 2, :], in0=tpk[:, 0:64], scalar1=G[:, 6, b:b + 1], scalar2=None, op0=ALU.mult),
              reads=[R.ps_b[0], B.G_b], writes=[tokb])
        tpB = R.ps[1][:, :].bitcast(BF16); tq2 = tpB[:, 0:128]; tk2 = tpB[:, 128:256]
        kb.op("pe", lambda: nc.tensor.transpose(out=tq2[0:64, :], in_=tok[:, 0, :], identity=R.ident[:, :]),
              reads=[tokb, R.ident_b], writes=[R.ps_b[1]])
        kb.op("pe", lambda: nc.tensor.transpose(out=tk2[0:64, :], in_=tok[:, 1, :], identity=R.ident[:, :]),
              reads=[tokb, R.ident_b], writes=[R.ps_b[1]])
        kb.op("act", lambda: nc.scalar.copy(out=qkT[:, 0, :], in_=tq2[0:64, :]), reads=[R.ps_b[1]], writes=[qkTb])
        kb.op("act", lambda: nc.scalar.copy(out=qkT[:, 1, :], in_=tk2[0:64, :]), reads=[R.ps_b[1]], writes=[qkTb])
        kb.op("pe", lambda: nc.tensor.matmul(R.ps[PS_][:, 0:128], lhsT=qkT[:, 1, :], rhs=qkT[:, 0, :], start=True, stop=True),
              reads=[qkTb], writes=[R.ps_b[PS_]])
        qkm, qkmb = B.qkm[i2], B.qkm_b[i2]
        kb.op("dve", lambda: nc.vector.tensor_tensor(out=qkm[:, :], in0=R.ps[PS_][:, 0:128], in1=B.cm[:, :], op=ALU.mult),
              reads=[R.ps_b[PS_], B.cm_b], writes=[qkmb])
        Sp, Spb = B.Sb[i2], B.Sb_b[i2]
        po = PO + (b % 2)
        kb.op("pe", lambda: nc.tensor.matmul(R.ps[po][:, 0:65], lhsT=qkm[:, :], rhs=R.Va[:, b, 0:65], start=True, stop=False),
              reads=[qkmb, R.Va_b], writes=[R.ps_b[po]], inc=False)
        kb.op("pe", lambda: nc.tensor.matmul(R.ps[po][:, 0:65], lhsT=qkT[:, 0, :], rhs=Sp[:, :], start=False, stop=True),
              reads=[qkTb, Spb], writes=[R.ps_b[po]])
        kb.op("pe", lambda: nc.tensor.matmul(R.ps[PU][0:64, 0:65], lhsT=tok[:, 2, :], rhs=R.Va[:, b, 0:65], start=True, stop=True),
              reads=[tokb, R.Va_b], writes=[R.ps_b[PU]])
        Sn, Snb = B.St[1 - i2], B.St_b[1 - i2]
        So, Sob = B.St[i2], B.St_b[i2]
        kb.op("dve", lambda: nc.vector.scalar_tensor_tensor(out=Sn[:, :], in0=So[:, :], scalar=G[0:64, 7, b:b + 1], in1=R.ps[PU][0:64, 0:65],
                                                            op0=ALU.mult, op1=ALU.add),
              reads=[Sob, B.G_b, R.ps_b[PU]], writes=[Snb])
        kb.op("act", lambda: nc.scalar.copy(out=B.Sb[1 - i2][:, :], in_=Sn[:, :]), reads=[Snb], writes=[B.Sb_b[1 - i2]])
        hs, hsb = B.hs[i2], B.hs_b[i2]
        hn, hnb = B.hn[i2], B.hn_b[i2]
        kb.op("act", lambda: nc.scalar.activation(out=hs[:, 5:6], in_=R.ps[po][:, 64:65], func=AF.Abs),
              reads=[R.ps_b[po]], writes=[hsb])
        kb.op("dve", lambda: nc.vector.tensor_scalar(out=hs[:, 0:1], in0=hs[:, 5:6], scalar1=1.0, scalar2=None, op0=ALU.max),
              reads=[hsb], writes=[hsb])
        kb.op("dve", lambda: nc.vector.reciprocal(out=hs[:, 1:2], in_=hs[:, 0:1]), reads=[hsb], writes=[hsb])
        kb.op("dve", lambda: nc.vector.tensor_scalar(out=hn[:, :], in0=R.ps[po][:, 0:64], scalar1=hs[:, 1:2], scalar2=None, op0=ALU.mult),
              reads=[R.ps_b[po], hsb], writes=[hnb])
        kb.op("act", lambda: nc.scalar.activation(out=B.jk[:, :], in_=hn[:, :], func=AF.Square, accum_out=hs[:, 2:3]),
              reads=[hnb], writes=[B.jk_b, hsb])
        kb.op("act", lambda: nc.scalar.activation(out=hs[:, 3:4], in_=hs[:, 2:3], func=AF.Sqrt, bias=EPS, scale=1.0 / 64),
              reads=[hsb], writes=[hsb])
        kb.op("dve", lambda: nc.vector.reciprocal(out=hs[:, 4:5], in_=hs[:, 3:4]), reads=[hsb], writes=[hsb])
        kb.op("dve", lambda: nc.vector.scalar_tensor_tensor(out=hn[:, :], in0=hn[:, :], scalar=hs[:, 4:5], in1=B.gob[:, :],
                                                            op0=ALU.mult, op1=ALU.mult),
              reads=[hnb, hsb, B.gob_b], writes=[hnb])
        yb, ybb = B.yb[i2], B.yb_b[i2]
        kb.op("dve", lambda: nc.vector.tensor_tensor(out=yb[:, :], in0=hn[:, :], in1=B.Osig[:, b, :], op=ALU.mult),
              reads=[hnb, B.Osig_b], writes=[ybb])
        ty = R.ps[5][:, :].bitcast(BF16)[:, 0:128]
        kb.op("pe", lambda: nc.tensor.transpose(out=ty[0:64, :], in_=yb[:, :], identity=R.ident[:, :]),
              reads=[ybb, R.ident_b], writes=[R.ps_b[5]])
        qt = b // 4
        if b % 4 == 0:
            R.cur_yo = R.nyo % 2; R.nyo += 1
        yo, yob = R.yo[R.cur_yo], R.yo_b[R.cur_yo]
        kb.op("act", lambda: nc.scalar.copy(out=yo[:, (b % 4) * 128:(b % 4 + 1) * 128], in_=ty[0:64, :]),
              reads=[R.ps_b[5]], writes=[yob])
        if b % 4 == 3:
            kb.dma("pool", ydst(yT_d, row0, qt), yo[:, :], reads=[yob])


def mix_params(kb, R, praw_d, clam_d):
    nc = kb.nc
    pr = R.par
    kb.dma("sp", pr[0:64, 16:24], praw_d, writes=[R.par_b])
    cl = R.rr[0][:, 0:128].rearrange("p (a b) -> p a b", a=4)
    kb.dma("sp", cl, clam_d, writes=[R.rr_b[0]])
    V = nc.vector
    kb.op("dve", lambda: V.tensor_scalar(out=pr[0:64, 0:1], in0=pr[0:64, 16:17], scalar1=32 ** -0.5, scalar2=None, op0=ALU.mult), reads=[R.par_b], writes=[R.par_b])
    kb.op("dve", lambda: V.tensor_copy(out=pr[0:64, 1:2], in_=pr[0:64, 17:18]), reads=[R.par_b], writes=[R.par_b])
    kb.op("dve", lambda: V.tensor_tensor(out=pr[0:64, 3:4], in0=pr[0:64, 18:19], in1=pr[0:64, 22:23], op=ALU.mult), reads=[R.par_b], writes=[R.par_b])
    kb.op("dve", lambda: V.tensor_scalar(out=pr[0:64, 4:5], in0=pr[0:64, 19:20], scalar1=0.125, scalar2=None, op0=ALU.mult), reads=[R.par_b], writes=[R.par_b])
    kb.op("dve", lambda: V.tensor_copy(out=pr[0:64, 5:6], in_=pr[0:64, 20:21]), reads=[R.par_b], writes=[R.par_b])
    pp = R.rr[1][:, 0:64].rearrange("p (a b) -> p a b", a=2)
    kb.op("dve", lambda: V.tensor_tensor(out=pp[:, 0, :], in0=cl[:, 0, :], in1=cl[:, 1, :], op=ALU.mult), reads=[R.rr_b[0]], writes=[R.rr_b[1]])
    kb.op("dve", lambda: V.tensor_tensor(out=pp[:, 1, :], in0=cl[:, 2, :], in1=cl[:, 3, :], op=ALU.mult), reads=[R.rr_b[0]], writes=[R.rr_b[1]])
    kb.op("dve", lambda: V.reduce_sum(out=pr[0:64, 8:10], in_=pp, axis=AX.X), reads=[R.rr_b[1]], writes=[R.par_b])
    kb.op("act", lambda: nc.scalar.activation(out=pr[0:64, 10:12], in_=pr[0:64, 8:10], func=AF.Exp), reads=[R.par_b], writes=[R.par_b])
    kb.op("dve", lambda: V.tensor_tensor(out=pr[0:64, 12:13], in0=pr[0:64, 11:12], in1=pr[0:64, 10:11], op=ALU.subtract), reads=[R.par_b], writes=[R.par_b])
    kb.op("dve", lambda: V.tensor_tensor(out=pr[0:64, 2:3], in0=pr[0:64, 12:13], in1=pr[0:64, 21:22], op=ALU.subtract), reads=[R.par_b], writes=[R.par_b])

import ml_dtypes
bf16 = ml_dtypes.bfloat16
GW = 256
OFF = dict(aq=0, ak=256, av=512, bqk=768, bv=1280, bo=1536, bi=1792, bf=1796, cq=1800, ck=2056, cv=2312, dq=2568, dk=2824, dv=3080)

def sel_cols(j):
    c = []
    r = lambda o: list(range(o + j * 64, o + j * 64 + 64))
    c += r(OFF['aq']) + r(OFF['ak']) + r(OFF['av'])
    c += r(OFF['bqk']) + r(OFF['bqk'] + 256) + r(OFF['bv']) + r(OFF['bo']) + [OFF['bi'] + j, OFF['bf'] + j]
    c += r(OFF['cq']) + r(OFF['ck']) + r(OFF['cv'])
    c += r(OFF['dq']) + r(OFF['dk']) + r(OFF['dv'])
    return np.array(c)

def const_inputs():
    d = {}
    d['ident'] = np.eye(128, dtype=np.float32).astype(bf16)
    cst = np.zeros((128, 256), np.float32)
    cst[0:64, 0:64] = 1.0
    cst[0:32, 64:96] = 1.0; cst[32:64, 96:128] = 1.0
    d['cst'] = cst.astype(bf16)
    s = np.arange(128)[:, None, None, None]; r = np.arange(4)[None, :, None, None]; t = np.arange(512)[None, None, None, :]
    mc = ((2 * r + (s >= 64)) <= (t // 64)).astype(np.float32)
    d['masks_c'] = np.broadcast_to(mc, (128, 4, 2, 512)).astype(bf16).copy()
    s = np.arange(128)[:, None, None]; r = np.arange(4)[None, :, None]; t = np.arange(512)[None, None, :]
    d['masks_d'] = ((128 * r + s) < t).astype(np.float32).astype(bf16)
    j = np.arange(128)[:, None]; s2 = np.arange(128)[None, :]
    tri = np.zeros((128, 2, 128), np.float32)
    tri[:, 0, :] = (j >= s2); tri[:, 1, :] = (j < s2)
    d['tri'] = tri.astype(bf16)
    trif = np.zeros((128, 2, 128), np.float32)
    trif[:, 0, :] = (j <= s2); trif[:, 1, :] = 1.0
    d['trif'] = trif
    d['cmask'] = (j <= s2).astype(np.float32)
    return d

def bias_index():
    s = np.arange(128)[:, None, None]; r = np.arange(8)[None, :, None]; t = np.arange(512)[None, None, :]
    rel = t - s + 512 - 128 * r
    idx = np.clip(rel, -128, 128) + 128
    dd = t // 64 + 8 - 2 * r - s // 64
    vis = (dd >= 0) & (dd <= 8)
    return idx, vis

_IDX, _VIS = bias_index()

def layer_core_inputs(P, l, j, lam_init=None):
    d = {}
    d['wsel'] = np.ascontiguousarray(P['w_in'][l][:, sel_cols(j)])
    d['gmix'] = np.ascontiguousarray(P['mix_norm'][l].reshape(8, 128).T)
    praw = np.zeros((64, 8), np.float32)
    praw[:, 0] = np.tile(P['c_q_norm'][l], 2); praw[:, 1] = np.tile(P['c_k_norm'][l], 2)
    praw[:, 2] = P['c_out_norm'][l]; praw[:, 3] = P['a_q_norm'][l]; praw[:, 4] = P['a_k_norm'][l]
    if lam_init is None:
        lam_init = 0.8 - 0.6 * np.exp(-0.3 * l)
    praw[:, 5] = lam_init; praw[:, 6] = 1.0 - lam_init
    d['praw'] = praw
    d['clam'] = np.ascontiguousarray(np.broadcast_to(P['c_lambda'][l][None], (64, 4, 32))).astype(np.float32)
    rb = P['a_rel_bias'][l][j]
    d['biasT'] = np.where(_VIS, rb[_IDX], np.float32(-1e30)).astype(np.float32)
    bpar = np.zeros((128, 8), np.float32)
    ch = np.concatenate([np.arange(j * 64, j * 64 + 64), 256 + np.arange(j * 64, j * 64 + 64)])
    bpar[:, 0:4] = P['b_conv_w'][l][:, ch].T
    bpar[:, 4] = P['b_conv_b'][l][ch]
    bpar[:, 5] = P['b_gate_bias'][l][0, j]
    bpar[:, 6] = P['b_gate_bias'][l][1, j]
    d['bpar'] = bpar
    d['gob'] = np.ascontiguousarray(np.broadcast_to(P['b_out_norm'][l][j][None], (128, 64))).astype(np.float32)
    return d


from concourse.bass_utils import run_bass_kernel_spmd

SEQ = 16384
NCORE = 8
TPC = 4096
DEPTH = 2
GROUPS = [[0, 1, 2, 3], [4, 5, 6, 7]]


def _din(nc, name, shape, dt):
    return nc.dram_tensor(name, list(shape), dt, kind="ExternalInput").ap()


def _dout(nc, name, shape, dt):
    return nc.dram_tensor(name, list(shape), dt, kind="ExternalOutput").ap()


def _dint(nc, name, shape, dt):
    return nc.dram_tensor(name, list(shape), dt, kind="Internal").ap()


MIX_IN = dict(wsel=([D, NW], F32), gmix=([128, 8], F32), praw=([64, 8], F32), clam=([64, 4, 32], F32),
              biasT=([128, 8, 512], F32), bpar=([128, 8], F32), gob=([128, 64], F32))
CONST_IN = dict(ident=([128, 128], BF16), cst=([128, 256], BF16), masks_c=([128, 4, 2, 512], BF16),
                masks_d=([128, 4, 512], BF16), tri=([128, 2, 128], BF16), trif=([128, 2, 128], F32), cmask=([128, 128], F32))


def build_fused(S=SEQ, T=TPC):
    nc = bass.Bass("TRN2", target_bir_lowering=False)
    NQr = T // QT_
    x_in = _din(nc, "x_in", [T, D], F32)
    x_out = _dout(nc, "x_out", [T, D], F32)
    Cn = {k: _din(nc, k, sh, dt) for k, (sh, dt) in CONST_IN.items()}
    ffn = {}
    for l in range(DEPTH):
        for f in ("ffn1", "ffn2"):
            ffn[(f, l)] = dict(g=_din(nc, f"{f}_g{l}", [128, 8], F32), wg=_din(nc, f"{f}_wg{l}", [D, DFF], F32),
                               wu=_din(nc, f"{f}_wu{l}", [D, DFF], F32), wd=_din(nc, f"{f}_wd{l}", [DFF, D], F32))
    wo = [_din(nc, f"wo{l}", [D, D], F32) for l in range(DEPTH)]
    mx = [{k: _din(nc, f"{k}{l}", sh, dt) for k, (sh, dt) in MIX_IN.items()} for l in range(DEPTH)]
    xa = _dint(nc, "xa", [T, D], F32); xb = _dint(nc, "xb", [T, D], F32); xc = _dint(nc, "xc", [T, D], F32)
    hT_loc = _dint(nc, "hT_loc", [D, T], BF16)
    hT_all = _dint(nc, "hT_all", [4 * D, T], BF16)
    yT_loc = _dint(nc, "yT_loc", [D, T], BF16)
    yT_all = _dint(nc, "yT_all", [4 * D, T], BF16)
    yT_mine = _dint(nc, "yT_mine", [D, T], BF16)

    hv = hT_all.rearrange("(k r p) t -> k r p t", k=8, r=4)

    def hT_src(qt):
        r, o = qt // NQr, (qt % NQr) * QT_
        return hv[:, r, :, o:o + QT_]

    def y_dst(row0, qt):
        q, o = qt // NQr, (qt % NQr) * QT_
        return yT_loc[q * 256 + row0:q * 256 + row0 + 64, o:o + QT_]

    with ExitStack() as st:
        kb = KB(nc, st)
        pid = nc.sync.partition_id()
        qv = pid % 4

        ymine_b = kb.buf()

        def fetch_mine():
            yv2 = yT_all.rearrange("(q h j p) t -> q h j p t", q=4, h=2, j=4)
            for j in range(4):
                for h in range(2):
                    kb.dma("sp", yT_mine[j * 256 + h * 128:j * 256 + (h + 1) * 128, :],
                           yv2[bass.ds(qv, 1), h, j, :, :].rearrange("o p t -> (o p) t"), writes=[ymine_b])

        def tok_phase(passes):
            with ExitStack() as mem:
                kb.mem = mem
                R = TokRes(kb, any(p.get("wo") is not None for p in passes))
                load_consts(kb, R, Cn["ident"])
                prev_bufs = None
                for k, p in enumerate(passes):
                    w = p["ffn"]
                    load_ffn_weights(kb, R, w["g"], w["wg"], w["wu"], w["wd"], p.get("wo"))
                    ob = [kb.buf() for _ in range(T // TT)] if k + 1 < len(passes) else None
                    has_pre = p.get("wo") is not None
                    token_pass(kb, R, T, p["xi"], p["xo"], pre=(yT_mine if has_pre else None),
                               post=p.get("post"), in_bufs=prev_bufs, out_bufs=ob,
                               pre_bufs=([ymine_b] * (T // TT) if has_pre else None))
                    prev_bufs = ob
                kb.barrier()
            kb.mem = st

        def mix_phase(l):
            with ExitStack() as mem:
                kb.mem = mem
                R = MixRes(kb, S)
                RB = MixResB(kb, R)
                m = mx[l]
                mix_load_common(kb, R, m["wsel"], m["gmix"], Cn["ident"], Cn["cst"])
                mix_params(kb, R, m["praw"], m["clam"])
                mixer_a(kb, R, hT_src, y_dst, 0, m["biasT"])
                mixer_b(kb, R, RB, hT_src, y_dst, 64, m["bpar"], Cn["trif"], Cn["cmask"], m["gob"])
                mixer_c(kb, R, hT_src, y_dst, 128, Cn["masks_c"])
                mixer_d(kb, R, hT_src, y_dst, 192, Cn["masks_d"], Cn["tri"])
                kb.barrier()
            kb.mem = st

        tok_phase([dict(ffn=ffn[("ffn1", 0)], xi=x_in, xo=xa, post=hT_loc)])
        kb.allgather(hT_loc, hT_all, GROUPS)
        mix_phase(0)
        kb.allgather(yT_loc, yT_all, GROUPS)
        fetch_mine()
        tok_phase([dict(ffn=ffn[("ffn2", 0)], wo=wo[0], xi=xa, xo=xb),
                   dict(ffn=ffn[("ffn1", 1)], xi=xb, xo=xc, post=hT_loc)])
        kb.allgather(hT_loc, hT_all, GROUPS)
        mix_phase(1)
        kb.allgather(yT_loc, yT_all, GROUPS)
        fetch_mine()
        tok_phase([dict(ffn=ffn[("ffn2", 1)], wo=wo[1], xi=xc, xo=x_out)])
        kb.finish()
    return nc


def _lay(g):
    return np.ascontiguousarray(np.asarray(g, np.float32).reshape(8, 128).T)


def _wo_perm(w_out):
    idx = np.arange(1024).reshape(4, 4, 64)
    perm = idx.transpose(1, 0, 2).reshape(-1)
    return np.ascontiguousarray(w_out[perm, :])


def make_in_maps(P, TPC=TPC):
    x = np.ascontiguousarray(P["x"], dtype=np.float32).reshape(-1, D)
    C = const_inputs()
    shared = dict(C)
    for l in range(DEPTH):
        for f in ("ffn1", "ffn2"):
            shared[f"{f}_g{l}"] = _lay(P[f + "_norm"][l])
            shared[f"{f}_wg{l}"] = np.ascontiguousarray(P[f + "_wg"][l], dtype=np.float32)
            shared[f"{f}_wu{l}"] = np.ascontiguousarray(P[f + "_wu"][l], dtype=np.float32)
            shared[f"{f}_wd{l}"] = np.ascontiguousarray(P[f + "_wd"][l], dtype=np.float32)
        shared[f"wo{l}"] = _wo_perm(np.asarray(P["w_out"][l], np.float32))
    ims = []
    for c in range(NCORE):
        d = dict(shared)
        d["x_in"] = x[c * TPC:(c + 1) * TPC]
        j = c % 4
        for l in range(DEPTH):
            for k, v in layer_core_inputs(P, l, j).items():
                d[f"{k}{l}"] = v
        ims.append(d)
    return ims


def kernel(**inputs):
    P = {k: np.asarray(v) for k, v in inputs.items()}
    nc = build_fused()
    ims = make_in_maps(P)
    res = run_bass_kernel_spmd(nc, ims, core_ids=list(range(NCORE)))
    out = np.concatenate([r["x_out"] for r in res.results], axis=0).reshape(2, SEQ, D).astype(np.float32)
    return out
```

```python
import numpy as np
from contextlib import ExitStack
import concourse.bass as bass
import concourse.mybir as mybir

F32 = mybir.dt.float32
BF16 = mybir.dt.bfloat16
AF = mybir.ActivationFunctionType
ALU = mybir.AluOpType
AX = mybir.AxisListType

EPOCH = 4096


class Buf:
    __slots__ = ("w", "r", "name")

    def __init__(self, name=""):
        self.w = None
        self.r = {}
        self.name = name


class KB:
    def __init__(self, nc, stack):
        self.nc = nc
        self.st = stack
        self.E = {"pe": nc.tensor, "act": nc.scalar, "dve": nc.vector, "pool": nc.gpsimd, "sp": nc.sync}
        self.cnt = {e: 0 for e in self.E}
        self.sems = {e: [] for e in self.E}
        self.waited = {e: {} for e in self.E}
        self.ndma = 12
        self.dma_sems = {}
        self.dma_cnt = {}
        self.dma_rr = {}
        self.nsem = 0
        self.uid = 0
        self.mem = stack

    def sem(self, name):
        self.nsem += 1
        return self.st.enter_context(self.nc.semaphore(name))

    def sbuf(self, name, shape, dt):
        self.uid += 1
        return self.mem.enter_context(self.nc.sbuf_tensor(f"sb{self.uid}_" + name, list(shape), dt))

    def psum(self, name, shape, dt):
        self.uid += 1
        return self.mem.enter_context(self.nc.psum_tensor(f"ps{self.uid}_" + name, list(shape), dt))

    def buf(self, name=""):
        return Buf(name)

    def _esem(self, e, n):
        ep = (n - 1) // EPOCH
        while len(self.sems[e]) <= ep:
            self.sems[e].append(self.sem(f"c_{e}_{len(self.sems[e])}"))
        return self.sems[e][ep], (n - 1) % EPOCH + 1

    def _wait(self, e, ev):
        if ev[0] == "e":
            _, src, n = ev
            if src == e and e == "pe":
                return
            key = ("e", src)
            if self.waited[e].get(key, 0) >= n:
                return
            if src == e and n > self.cnt[e]:
                raise RuntimeError("self-wait on future event")
            s, v = self._esem(src, n)
            self.E[e].wait_ge(s, v)
            self.waited[e][key] = n
        else:
            _, q, i, k = ev
            key = ("d", q, i)
            if self.waited[e].get(key, 0) >= k:
                return
            self.E[e].wait_ge(self.dma_sems[q][i], 16 * k)
            self.waited[e][key] = k

    @staticmethod
    def _evkey(ev):
        return (ev[0], ev[1]) if ev[0] == "e" else (ev[0], ev[1], ev[2])

    def _collect(self, reads, writes):
        deps = []
        for b in reads:
            if b.w is not None:
                deps.append(b.w)
        for b in writes:
            if b.w is not None:
                deps.append(b.w)
            deps.extend(b.r.values())
        return deps

    def _record(self, ev, reads, writes):
        k = self._evkey(ev)
        for b in reads:
            b.r[k] = ev
        for b in writes:
            b.w = ev
            b.r = {}

    def op(self, e, fn, reads=(), writes=(), inc=True):
        for ev in self._collect(reads, writes):
            self._wait(e, ev)
        ins = fn()
        if inc:
            self.cnt[e] += 1
            s, v = self._esem(e, self.cnt[e])
            ins.then_inc(s, 1)
            ev = ("e", e, self.cnt[e])
        else:
            ev = ("e", e, self.cnt[e] + 1)
        self._record(ev, reads, writes)
        return ins

    def dma(self, q, out, in_, reads=(), writes=(), **kw):
        for ev in self._collect(reads, writes):
            self._wait(q, ev)
        if q not in self.dma_sems:
            self.dma_sems[q] = [self.sem(f"d_{q}_{i}") for i in range(self.ndma)]
            self.dma_cnt[q] = [0] * self.ndma
            self.dma_rr[q] = 0
        i = self.dma_rr[q]
        self.dma_rr[q] = (i + 1) % self.ndma
        if self.dma_cnt[q][i] > 0:
            self._wait(q, ("d", q, i, self.dma_cnt[q][i]))
        self.dma_cnt[q][i] += 1
        ins = self.E[q].dma_start(out=out, in_=in_, **kw)
        ins.then_inc(self.dma_sems[q][i], 16)
        ev = ("d", q, i, self.dma_cnt[q][i])
        self._record(ev, reads, writes)
        return ins

    def barrier(self, extra_sems=()):
        for e in self.E:
            for q in self.dma_sems:
                for i in range(self.ndma):
                    if self.dma_cnt[q][i] > 0:
                        self._wait(e, ("d", q, i, self.dma_cnt[q][i]))
            for src in ("pe", "act", "dve", "pool"):
                if self.cnt[src] > 0 and not (src == e and e == "pe"):
                    self._wait(e, ("e", src, self.cnt[src]))
            for (sm, v) in extra_sems:
                self.E[e].wait_ge(sm, v)

    def allgather(self, src2d, dst2d, groups, chunk_rows=128):
        self.barrier()
        R_ = src2d.shape[0]
        nk = R_ // chunk_rows
        ng = len(groups[0])
        sms = []
        for k in range(nk):
            sm = self.sem(f"cc{self.nsem}")
            self.nc.gpsimd.collective_compute("AllGather", ALU.bypass, replica_groups=groups,
                                              ins=[src2d[k * chunk_rows:(k + 1) * chunk_rows, :]],
                                              outs=[dst2d[k * ng * chunk_rows:(k + 1) * ng * chunk_rows, :]]).then_inc(sm, 1)
            sms.append(sm)
        for e in self.E:
            for sm in sms:
                self.E[e].wait_ge(sm, 1)

    def finish(self):
        for q in self.dma_sems:
            for i in range(self.ndma):
                if self.dma_cnt[q][i] > 0:
                    self._wait("sp", ("d", q, i, self.dma_cnt[q][i]))
        for e in ("pe", "act", "dve", "pool"):
            if self.cnt[e] > 0:
                self._wait("sp", ("e", e, self.cnt[e]))


D = 1024
DFF = 2816
NFC = DFF // 128
TT = 256
SUB = TT // 128
EPS = 1e-6


class TokRes:
    def __init__(self, kb, with_pre):
        nc = kb.nc
        self.kb = kb
        self.Wg = kb.sbuf("Wg", [128, 8, DFF], BF16); self.Wg_b = kb.buf()
        self.Wu = kb.sbuf("Wu", [128, 8, DFF], BF16); self.Wu_b = kb.buf()
        self.Wd = kb.sbuf("Wd", [128, NFC, D], BF16); self.Wd_b = kb.buf()
        self.stage = [kb.sbuf(f"stage{i}", [128, 1024], F32) for i in range(2)]
        self.stage_b = [kb.buf() for _ in range(2)]
        self.gt = kb.sbuf("gt", [128, 8], F32); self.gt_b = kb.buf()
        self.ident = kb.sbuf("ident", [128, 128], BF16); self.ident_b = kb.buf()
        self.xt = [kb.sbuf(f"xt{i}", [128, SUB, D], F32) for i in range(2)]
        self.xt_b = [kb.buf() for _ in range(2)]
        self.xn = kb.sbuf("xn", [128, D], BF16); self.xn_b = kb.buf()
        self.st = kb.sbuf("stat", [128, 8], F32); self.st_b = kb.buf()
        self.xnT = [kb.sbuf(f"xnT{i}", [128, 8, TT], BF16) for i in range(2)]
        self.xnT_b = [kb.buf() for _ in range(2)]
        self.hid = kb.sbuf("hid", [128, NFC, TT], BF16)
        self.hid_b = [kb.buf() for _ in range(NFC)]
        self.sg = [kb.sbuf(f"sg{i}", [128, TT], F32) for i in range(2)]
        self.sg_b = [kb.buf() for _ in range(2)]
        self.hTo = kb.sbuf("hTo", [128, 8, TT], BF16); self.hTo_b = kb.buf()
        self.with_pre = with_pre
        if with_pre:
            self.Wo = kb.sbuf("Wo", [128, 8, D], BF16); self.Wo_b = kb.buf()
            self.yt = [kb.sbuf(f"yt{i}", [128, 8, TT], BF16) for i in range(2)]
            self.yt_b = [kb.buf() for _ in range(2)]
        self.psg = [kb.psum(f"psg{i}", [128, 512], F32) for i in range(2)]
        self.psg_b = [kb.buf() for _ in range(2)]
        self.psd = [kb.psum(f"psd{i}", [128, 512], F32) for i in range(2)]
        self.psd_b = [kb.buf() for _ in range(2)]
        self.tp = [kb.psum(f"tp{i}", [128, 8, 128], BF16) for i in range(2)]
        self.tp_b = [kb.buf() for _ in range(2)]
        self.ntp = 0
        self.npsd = 0
        self.ncast = 0


def load_consts(kb, R, ident_d):
    kb.dma("sp", R.ident[:], ident_d, writes=[R.ident_b])


def load_ffn_weights(kb, R, g_lay, wg, wu, wd, w_out=None):
    nc = kb.nc
    kb.dma("sp", R.gt[:], g_lay, writes=[R.gt_b])

    def cast(dst_ap, dst_b, src_ap, src_b, scal):
        e = "dve" if R.ncast % 2 == 0 else "pool"
        R.ncast += 1
        E = kb.E[e]
        if scal is None:
            kb.op(e, lambda: E.tensor_copy(out=dst_ap, in_=src_ap), reads=[src_b], writes=[dst_b])
        else:
            kb.op(e, lambda: E.tensor_scalar(out=dst_ap, in0=src_ap, scalar1=scal, scalar2=None, op0=ALU.mult),
                  reads=[src_b, R.gt_b], writes=[dst_b])

    k = 0
    for (W, Wb, src) in ((R.Wg, R.Wg_b, wg), (R.Wu, R.Wu_b, wu)):
        for kc in range(8):
            for (c0, c1) in ((0, 1024), (1024, 2048), (2048, DFF)):
                sb = k % 2; k += 1
                kb.dma("sp", R.stage[sb][:, 0:c1 - c0], src[kc * 128:(kc + 1) * 128, c0:c1],
                       writes=[R.stage_b[sb]])
                cast(W[:, kc, c0:c1], Wb, R.stage[sb][:, 0:c1 - c0], R.stage_b[sb], R.gt[:, kc:kc + 1])
    for fc in range(NFC):
        sb = k % 2; k += 1
        kb.dma("sp", R.stage[sb][:, 0:D], wd[fc * 128:(fc + 1) * 128, :], writes=[R.stage_b[sb]])
        cast(R.Wd[:, fc, :], R.Wd_b, R.stage[sb][:, 0:D], R.stage_b[sb], None)
    if w_out is not None:
        for kc in range(8):
            sb = k % 2; k += 1
            kb.dma("sp", R.stage[sb][:, 0:D], w_out[kc * 128:(kc + 1) * 128, :], writes=[R.stage_b[sb]])
            cast(R.Wo[:, kc, :], R.Wo_b, R.stage[sb][:, 0:D], R.stage_b[sb], None)


def norm_transpose(kb, R, x_ap, x_b, dstT, dstT_b, s):
    nc = kb.nc
    ss = R.st[:, 0:1]; rs = R.st[:, 1:2]; rstd = R.st[:, 2:3]
    kb.op("act", lambda: nc.scalar.activation(out=R.xn[:], in_=x_ap, func=AF.Square, accum_out=ss),
          reads=[x_b], writes=[R.xn_b, R.st_b])
    kb.op("act", lambda: nc.scalar.activation(out=rs, in_=ss, func=AF.Sqrt, bias=EPS, scale=1.0 / D),
          reads=[R.st_b], writes=[R.st_b])
    kb.op("dve", lambda: nc.vector.reciprocal(out=rstd, in_=rs), reads=[R.st_b], writes=[R.st_b])
    kb.op("dve", lambda: nc.vector.tensor_scalar(out=R.xn[:], in0=x_ap, scalar1=rstd, scalar2=None, op0=ALU.mult),
          reads=[x_b, R.st_b], writes=[R.xn_b])
    ti = R.ntp % 2; R.ntp += 1
    tp = R.tp[ti]; tpb = R.tp_b[ti]
    for kc in range(8):
        kb.op("pe", lambda kc=kc: nc.tensor.transpose(out=tp[:, kc, :], in_=R.xn[:, kc * 128:(kc + 1) * 128],
                                                      identity=R.ident[:]),
              reads=[R.xn_b, R.ident_b], writes=[tpb], inc=(kc == 7))
    kb.op("act", lambda: nc.scalar.copy(out=dstT[:, :, s * 128:(s + 1) * 128], in_=tp[:, :, :]),
          reads=[tpb], writes=[dstT_b])


def token_pass(kb, R, T, x_in, x_out, pre=None, post=None, in_bufs=None, out_bufs=None, pre_bufs=None, post_bufs=None):
    nc = kb.nc
    NT = T // TT

    def stage_load(i):
        bi = i % 2
        kb.dma("sp", R.xt[bi][:, :, :], x_in[i * TT:(i + 1) * TT, :].rearrange("(s p) d -> p s d", p=128),
               reads=([in_bufs[i]] if in_bufs else []), writes=[R.xt_b[bi]])
        if pre is not None:
            kb.dma("sp", R.yt[bi][:, :, :], pre[:, i * TT:(i + 1) * TT].rearrange("(c p) t -> p c t", p=128),
                   reads=([pre_bufs[i]] if pre_bufs else []), writes=[R.yt_b[bi]])

    def stage_pre(i):
        bi = i % 2
        if pre is None:
            return
        for s in range(SUB):
            for h in range(2):
                pi = R.npsd % 2; R.npsd += 1
                for kc in range(8):
                    kb.op("pe", lambda kc=kc: nc.tensor.matmul(R.psd[pi][:, :], lhsT=R.yt[bi][:, kc, s * 128:(s + 1) * 128],
                                                               rhs=R.Wo[:, kc, h * 512:(h + 1) * 512],
                                                               start=(kc == 0), stop=(kc == 7)),
                          reads=[R.yt_b[bi], R.Wo_b], writes=[R.psd_b[pi]], inc=(kc == 7))
                xs = R.xt[bi][:, s, h * 512:(h + 1) * 512]
                kb.op("dve", lambda: nc.vector.tensor_tensor(out=xs, in0=R.psd[pi][:, :], in1=xs, op=ALU.add),
                      reads=[R.psd_b[pi], R.xt_b[bi]], writes=[R.xt_b[bi]])

    def stage_a(i):
        bi = i % 2
        for s in range(SUB):
            norm_transpose(kb, R, R.xt[bi][:, s, :], R.xt_b[bi], R.xnT[bi], R.xnT_b[bi], s)

    def stage_b(i):
        bi = i % 2
        for fc in range(NFC):
            gi = fc % 2
            for (W, Wb, off) in ((R.Wg, R.Wg_b, 0), (R.Wu, R.Wu_b, 256)):
                for kc in range(8):
                    kb.op("pe", lambda kc=kc, W=W, off=off: nc.tensor.matmul(
                        R.psg[gi][:, off:off + TT], lhsT=W[:, kc, fc * 128:(fc + 1) * 128], rhs=R.xnT[bi][:, kc, :],
                        start=(kc == 0), stop=(kc == 7)),
                          reads=[Wb, R.xnT_b[bi]], writes=[R.psg_b[gi]], inc=(kc == 7))
            kb.op("act", lambda: nc.scalar.activation(out=R.sg[gi][:, :], in_=R.psg[gi][:, 0:TT], func=AF.Silu),
                  reads=[R.psg_b[gi]], writes=[R.sg_b[gi]])
            kb.op("dve", lambda: nc.vector.tensor_tensor(out=R.hid[:, fc, :], in0=R.sg[gi][:, :],
                                                         in1=R.psg[gi][:, 256:256 + TT], op=ALU.mult),
                  reads=[R.sg_b[gi], R.psg_b[gi]], writes=[R.hid_b[fc]])

    def stage_c(i):
        bi = i % 2
        for s in range(SUB):
            for h in range(2):
                pi = R.npsd % 2; R.npsd += 1
                for fc in range(NFC):
                    kb.op("pe", lambda fc=fc: nc.tensor.matmul(R.psd[pi][:, :], lhsT=R.hid[:, fc, s * 128:(s + 1) * 128],
                                                               rhs=R.Wd[:, fc, h * 512:(h + 1) * 512],
                                                               start=(fc == 0), stop=(fc == NFC - 1)),
                          reads=[R.hid_b[fc], R.Wd_b], writes=[R.psd_b[pi]], inc=(fc == NFC - 1))
                xs = R.xt[bi][:, s, h * 512:(h + 1) * 512]
                kb.op("dve", lambda: nc.vector.scalar_tensor_tensor(out=xs, in0=R.psd[pi][:, :], scalar=0.5, in1=xs,
                                                                    op0=ALU.mult, op1=ALU.add),
                      reads=[R.psd_b[pi], R.xt_b[bi]], writes=[R.xt_b[bi]])
            if post is not None:
                norm_transpose(kb, R, R.xt[bi][:, s, :], R.xt_b[bi], R.hTo, R.hTo_b, s)
        kb.dma("pool", x_out[i * TT:(i + 1) * TT, :].rearrange("(s p) d -> p s d", p=128), R.xt[bi][:, :, :],
               reads=[R.xt_b[bi]], writes=([out_bufs[i]] if out_bufs else []))
        if post is not None:
            kb.dma("pool", post[:, i * TT:(i + 1) * TT].rearrange("(c p) t -> p c t", p=128), R.hTo[:, :, :],
                   reads=[R.hTo_b], writes=([post_bufs[i]] if post_bufs else []))

    stage_load(0)
    stage_pre(0)
    stage_a(0)
    for i in range(NT):
        if i + 1 < NT:
            stage_load(i + 1)
        stage_b(i)
        if i + 1 < NT:
            stage_pre(i + 1)
            stage_a(i + 1)
        stage_c(i)


QT_ = 512
NW = 834
A_Q, A_K, A_V = 0, 64, 128
B_Q, B_K, B_V, B_O, B_I, B_F = 192, 256, 320, 384, 448, 449
C_Q, C_K, C_V = 450, 514, 578
D_Q, D_K, D_V = 642, 706, 770


class MixRes:
    def __init__(self, kb, S):
        self.S = S
        self.NB = S // 128
        self.NQ = S // QT_
        NB = self.NB
        self.Wm = kb.sbuf("Wm", [128, 8, NW], BF16); self.Wm_b = kb.buf()
        self.gm = kb.sbuf("gm", [128, 8], F32); self.gm_b = kb.buf()
        self.ident = kb.sbuf("identm", [128, 128], BF16); self.ident_b = kb.buf()
        self.ht = [kb.sbuf(f"ht{i}", [128, 8, QT_], BF16) for i in range(2)]; self.ht_b = [kb.buf() for _ in range(2)]
        self.QT = kb.sbuf("QT", [64, S], BF16); self.QT_b = kb.buf()
        self.KT = kb.sbuf("KT", [64, S], BF16); self.KT_b = kb.buf()
        self.Va = kb.sbuf("Va", [128, NB, 128], BF16); self.Va_b = kb.buf()
        self.par = kb.sbuf("par", [128, 32], F32); self.par_b = kb.buf()
        self.cst = kb.sbuf("cst", [128, 256], BF16); self.cst_b = kb.buf()
        self.sq = [kb.sbuf(f"sq{i}", [64, QT_], BF16) for i in range(2)]; self.sq_b = [kb.buf() for _ in range(2)]
        self.rr = [kb.sbuf(f"rr{i}", [64, QT_], F32) for i in range(2)]; self.rr_b = [kb.buf() for _ in range(2)]
        self.e32 = [kb.sbuf(f"e32_{i}", [128, 2, QT_], F32) for i in range(2)]; self.e32_b = [kb.buf() for _ in range(2)]
        self.wst = [self.e32[i][:, :, :].rearrange("p a b -> p (a b)")[:, 0:NW] for i in range(2)]; self.wst_b = self.e32_b
        self.e16 = [kb.sbuf(f"e16_{i}", [128, 2, QT_], BF16) for i in range(3)]; self.e16_b = [kb.buf() for _ in range(3)]
        self.sp16 = [kb.sbuf(f"sp16_{i}", [128, QT_], BF16) for i in range(4)]; self.sp16_b = [kb.buf() for _ in range(4)]
        self.spx = self.sp16; self.spx_b = self.sp16_b
        self.fin = [kb.sbuf(f"fin{i}", [64, QT_], F32) for i in range(3)]; self.fin_b = [kb.buf() for _ in range(3)]
        self.yo = [kb.sbuf(f"yo{i}", [64, QT_], BF16) for i in range(2)]; self.yo_b = [kb.buf() for _ in range(2)]
        self.mask = kb.sbuf("mask", [128, 4, QT_], BF16); self.mask_b = kb.buf()
        self.EB = kb.sbuf("EB", [128, 8, QT_], F32); self.EB_b = kb.buf()
        self.tri = kb.sbuf("tri", [128, 2, 128], BF16); self.tri_b = kb.buf()
        self.ps = [kb.psum(f"pm{i}", [128, 512], F32) for i in range(7)]
        self.ps_b = [kb.buf() for _ in range(7)]
        self.tpbank = kb.psum("tpbank", [128, 8, 128], BF16)
        self.tpb = [self.tpbank[:, i, :] for i in range(8)]
        self.tpb_b = [kb.buf() for _ in range(8)]
        self.nyo = 0
        self.ne16 = 0


def mix_load_common(kb, R, wsel, gmix_lay, ident_d, cst_d):
    nc = kb.nc
    kb.dma("sp", R.gm[:], gmix_lay, writes=[R.gm_b])
    kb.dma("sp", R.ident[:], ident_d, writes=[R.ident_b])
    kb.dma("sp", R.cst[:], cst_d, writes=[R.cst_b])
    for kc in range(8):
        sb = kc % 2
        kb.dma("sp", R.wst[sb][:, :], wsel[kc * 128:(kc + 1) * 128, :], writes=[R.wst_b[sb]])
        kb.op("dve", lambda: nc.vector.tensor_scalar(out=R.Wm[:, kc, :], in0=R.wst[sb][:, :], scalar1=R.gm[:, kc:kc + 1],
                                                     scalar2=None, op0=ALU.mult),
              reads=[R.wst_b[sb], R.gm_b], writes=[R.Wm_b])


def load_ht(kb, R, hT, qt):
    bi = qt % 2
    if callable(hT):
        src = hT(qt).rearrange("c p t -> p c t")
    else:
        src = hT[:, qt * QT_:(qt + 1) * QT_].rearrange("(c p) t -> p c t", p=128)
    kb.dma("sp", R.ht[bi][:, :, :], src, writes=[R.ht_b[bi]])
    return R.ht[bi], R.ht_b[bi]


def proj_fm(kb, R, ht, ht_b, c0, ncol, pb):
    nc = kb.nc
    for kc in range(8):
        kb.op("pe", lambda kc=kc: nc.tensor.matmul(R.ps[pb][0:ncol, :], lhsT=R.Wm[:, kc, c0:c0 + ncol], rhs=ht[:, kc, :],
                                                   start=(kc == 0), stop=(kc == 7)),
              reads=[R.Wm_b, ht_b], writes=[R.ps_b[pb]], inc=(kc == 7))


def proj_tm(kb, R, ht, ht_b, c0, ncol, pb, s):
    nc = kb.nc
    for kc in range(8):
        kb.op("pe", lambda kc=kc: nc.tensor.matmul(R.ps[pb][:, s * 128:s * 128 + ncol], lhsT=ht[:, kc, s * 128:(s + 1) * 128],
                                                   rhs=R.Wm[:, kc, c0:c0 + ncol], start=(kc == 0), stop=(kc == 7)),
              reads=[R.Wm_b, ht_b], writes=[R.ps_b[pb]], inc=(kc == 7))


def qk_norm_store(kb, R, pb, pb2, dst, dst_b, qt, gcol, cmat, inv_n, i2):
    nc = kb.nc
    sq, sqb = R.sq[i2], R.sq_b[i2]
    rr, rrb = R.rr[i2], R.rr_b[i2]
    kb.op("act", lambda: nc.scalar.activation(out=sq[:, :], in_=R.ps[pb][0:64, :], func=AF.Square),
          reads=[R.ps_b[pb]], writes=[sqb])
    kb.op("pe", lambda: nc.tensor.matmul(R.ps[pb2][0:64, :], lhsT=cmat, rhs=sq[:, :], start=True, stop=True),
          reads=[sqb, R.cst_b], writes=[R.ps_b[pb2]])
    kb.op("act", lambda: nc.scalar.activation(out=rr[:, :], in_=R.ps[pb2][0:64, :], func=AF.Sqrt, bias=EPS, scale=inv_n),
          reads=[R.ps_b[pb2]], writes=[rrb])
    kb.op("dve", lambda: nc.vector.reciprocal(out=rr[:, :], in_=rr[:, :]), reads=[rrb], writes=[rrb])
    kb.op("dve", lambda: nc.vector.scalar_tensor_tensor(out=dst[:, qt * QT_:(qt + 1) * QT_], in0=R.ps[pb][0:64, :],
                                                        scalar=R.par[0:64, gcol:gcol + 1], in1=rr[:, :],
                                                        op0=ALU.mult, op1=ALU.mult),
          reads=[R.ps_b[pb], rrb, R.par_b], writes=[dst_b])


def v_store(kb, R, ht, ht_b, c0, qt, pb):
    nc = kb.nc
    for s in range(4):
        proj_tm(kb, R, ht, ht_b, c0, 64, pb, s)
    src = R.ps[pb][:, :].rearrange("p (s c) -> p s c", c=128)[:, :, 0:64]
    kb.op("act", lambda: nc.scalar.copy(out=R.Va[:, qt * 4:(qt + 1) * 4, 0:64], in_=src),
          reads=[R.ps_b[pb]], writes=[R.Va_b])


def ydst(yT_d, row0, qt):
    if callable(yT_d):
        return yT_d(row0, qt)
    return yT_d[row0:row0 + 64, qt * QT_:(qt + 1) * QT_]


def out_store(kb, R, yT_d, row0, qt, src_fn, reads):
    i = R.nyo % 2; R.nyo += 1
    src_fn(R.yo[i], R.yo_b[i])
    kb.dma("pool", ydst(yT_d, row0, qt), R.yo[i][:, :], reads=[R.yo_b[i]])


def mixer_c(kb, R, hT, yT_d, row0, masks_c):
    nc = kb.nc
    S, NB, NQ = R.S, R.NB, R.NQ
    kb.dma("sp", R.mask[:, :, :], masks_c[:, :, 0, :], writes=[R.mask_b])
    kb.op("pool", lambda: nc.gpsimd.memset(R.Va[:, :, 64:128], 1.0), writes=[R.Va_b])
    bd32 = R.cst[0:64, 64:128]
    ones64 = R.cst[0:64, 0:64]
    for qt in range(NQ):
        ht, htb = load_ht(kb, R, hT, qt)
        proj_fm(kb, R, ht, htb, C_Q, 64, 0)
        proj_fm(kb, R, ht, htb, C_K, 64, 2)
        qk_norm_store(kb, R, 0, 1, R.QT, R.QT_b, qt, 0, bd32, 1.0 / 32, 0)
        qk_norm_store(kb, R, 2, 3, R.KT, R.KT_b, qt, 1, bd32, 1.0 / 32, 1)
        v_store(kb, R, ht, htb, C_V, qt, 4 + (qt % 2))
    for qt in range(NQ):
        nkb = 4 * qt + 4
        O0, O1 = 4, 5
        estate = {}

        def s_step(kbk):
            sb = 2 * (kbk % 2)
            for m in range(2):
                kb.op("pe", lambda m=m: nc.tensor.matmul(R.ps[sb + m][:, :],
                                                         lhsT=R.KT[m * 32:(m + 1) * 32, kbk * 128:(kbk + 1) * 128],
                                                         rhs=R.QT[m * 32:(m + 1) * 32, qt * QT_:(qt + 1) * QT_],
                                                         start=True, stop=True),
                      reads=[R.KT_b, R.QT_b], writes=[R.ps_b[sb + m]])
            ei = R.ne16 % 3; R.ne16 += 1
            e, eb = R.e16[ei], R.e16_b[ei]
            estate[kbk] = (e, eb)
            for m in range(2):
                kb.op("act", lambda m=m: nc.scalar.activation(out=e[:, m, :], in_=R.ps[sb + m][:, :], func=AF.Exp),
                      reads=[R.ps_b[sb + m]], writes=[eb])
            r = kbk - 4 * qt
            if r >= 0:
                for m in range(2):
                    kb.op("dve", lambda m=m: nc.vector.tensor_tensor(out=e[:, m, :], in0=e[:, m, :], in1=R.mask[:, r, :], op=ALU.mult),
                          reads=[eb, R.mask_b], writes=[eb])

        def pv_step(kbk):
            e, eb = estate.pop(kbk)
            for m in range(2):
                kb.op("pe", lambda m=m: nc.tensor.matmul(R.ps[O0 + m][:, :], lhsT=R.Va[:, kbk, :], rhs=e[:, m, :],
                                                         start=(kbk == 0), stop=(kbk == nkb - 1)),
                      reads=[R.Va_b, eb], writes=[R.ps_b[O0 + m]])

        s_step(0)
        for kbk in range(nkb):
            if kbk + 1 < nkb:
                s_step(kbk + 1)
            pv_step(kbk)
        f0, f1, f2 = R.fin
        b0, b1, b2 = R.fin_b
        kb.op("dve", lambda: nc.vector.reciprocal(out=f0[:, :], in_=R.ps[O0][64:128, :]), reads=[R.ps_b[O0]], writes=[b0])
        kb.op("dve", lambda: nc.vector.tensor_tensor(out=f0[:, :], in0=R.ps[O0][0:64, :], in1=f0[:, :], op=ALU.mult),
              reads=[R.ps_b[O0], b0], writes=[b0])
        kb.op("dve", lambda: nc.vector.reciprocal(out=f1[:, :], in_=R.ps[O1][64:128, :]), reads=[R.ps_b[O1]], writes=[b1])
        kb.op("dve", lambda: nc.vector.tensor_tensor(out=f1[:, :], in0=R.ps[O1][0:64, :], in1=f1[:, :], op=ALU.mult),
              reads=[R.ps_b[O1], b1], writes=[b1])
        kb.op("dve", lambda: nc.vector.scalar_tensor_tensor(out=f2[:, :], in0=f1[:, :], scalar=R.par[0:64, 2:3], in1=f0[:, :],
                                                            op0=ALU.mult, op1=ALU.add),
              reads=[b0, b1, R.par_b], writes=[b2])
        kb.op("act", lambda: nc.scalar.activation(out=R.sq[0][:, :], in_=f2[:, :], func=AF.Square), reads=[b2], writes=[R.sq_b[0]])
        kb.op("pe", lambda: nc.tensor.matmul(R.ps[6][0:64, :], lhsT=ones64, rhs=R.sq[0][:, :], start=True, stop=True),
              reads=[R.sq_b[0], R.cst_b], writes=[R.ps_b[6]])
        kb.op("act", lambda: nc.scalar.activation(out=R.rr[0][:, :], in_=R.ps[6][0:64, :], func=AF.Sqrt, bias=EPS, scale=1.0 / 64),
              reads=[R.ps_b[6]], writes=[R.rr_b[0]])
        kb.op("dve", lambda: nc.vector.reciprocal(out=R.rr[0][:, :], in_=R.rr[0][:, :]), reads=[R.rr_b[0]], writes=[R.rr_b[0]])

        def fn(yo, yob):
            kb.op("dve", lambda: nc.vector.scalar_tensor_tensor(out=yo[:, :], in0=f2[:, :], scalar=R.par[0:64, 3:4], in1=R.rr[0][:, :],
                                                                op0=ALU.mult, op1=ALU.mult),
                  reads=[b2, R.rr_b[0], R.par_b], writes=[yob])
        out_store(kb, R, yT_d, row0, qt, fn, None)


def mixer_d(kb, R, hT, yT_d, row0, masks_d, tri_d):
    nc = kb.nc
    S, NB, NQ = R.S, R.NB, R.NQ
    kb.dma("sp", R.mask[:, :, :], masks_d, writes=[R.mask_b])
    kb.dma("sp", R.tri[:, :, :], tri_d, writes=[R.tri_b])
    for qt in range(NQ):
        ht, htb = load_ht(kb, R, hT, qt)
        proj_fm(kb, R, ht, htb, D_Q, 64, 0)
        proj_fm(kb, R, ht, htb, D_K, 64, 1)
        kb.op("act", lambda: nc.scalar.activation(out=R.QT[:, qt * QT_:(qt + 1) * QT_], in_=R.ps[0][0:64, :], func=AF.Copy, scale=0.125),
              reads=[R.ps_b[0]], writes=[R.QT_b])
        kb.op("dve", lambda: nc.vector.tensor_copy(out=R.KT[:, qt * QT_:(qt + 1) * QT_], in_=R.ps[1][0:64, :]),
              reads=[R.ps_b[1]], writes=[R.KT_b])
        v_store(kb, R, ht, htb, D_V, qt, 4 + (qt % 2))
    RB, OB = 4, 5
    LA = 3
    ed_b = [kb.buf() for _ in range(4)]
    for qt in range(NQ):
        kbs = list(range(4 * qt + 3, -1, -1))
        n = len(kbs)

        def ebuf(i):
            return R.e32[(i // 2) % 2][:, i % 2, :], ed_b[i % 4]

        def spbuf(i):
            return R.spx[i % 4], R.spx_b[i % 4]

        def z_esp(i):
            kbk = kbs[i]
            zb = i % 4
            kb.op("pe", lambda: nc.tensor.matmul(R.ps[zb][:, :], lhsT=R.KT[:, kbk * 128:(kbk + 1) * 128],
                                                 rhs=R.QT[:, qt * QT_:(qt + 1) * QT_], start=True, stop=True),
                  reads=[R.KT_b, R.QT_b], writes=[R.ps_b[zb]])
            e, eb = ebuf(i)
            sp, spb = spbuf(i)
            kb.op("act", lambda: nc.scalar.activation(out=e, in_=R.ps[zb][:, :], func=AF.Exp),
                  reads=[R.ps_b[zb]], writes=[eb])
            kb.op("act", lambda: nc.scalar.activation(out=sp[:, :], in_=e, func=AF.Ln, bias=1.0),
                  reads=[eb], writes=[spb])
            r = kbk - 4 * qt
            if r >= 0:
                kb.op("dve", lambda: nc.vector.tensor_tensor(out=sp[:, :], in0=sp[:, :], in1=R.mask[:, r, :], op=ALU.mult),
                      reads=[spb, R.mask_b], writes=[spb])

        def chain(i):
            kbk = kbs[i]
            e, eb = ebuf(i)
            sp, spb = spbuf(i)
            kb.op("pe", lambda: nc.tensor.matmul(R.ps[RB][:, :], lhsT=R.tri[:, 0, :], rhs=sp[:, :], start=(i == 0), stop=False),
                  reads=[R.tri_b, spb], writes=[R.ps_b[RB]])
            tt, ttb = R.e16[i % 3], R.e16_b[i % 3]
            kb.op("act", lambda: nc.scalar.activation(out=tt[:, 0, :], in_=R.ps[RB][:, :], func=AF.Exp, scale=-1.0),
                  reads=[R.ps_b[RB]], writes=[ttb])
            kb.op("pe", lambda: nc.tensor.matmul(R.ps[RB][:, :], lhsT=R.tri[:, 1, :], rhs=sp[:, :], start=False, stop=(i == n - 1)),
                  reads=[R.tri_b, spb], writes=[R.ps_b[RB]])
            kb.op("dve", lambda: nc.vector.tensor_tensor(out=tt[:, 1, :], in0=e, in1=tt[:, 0, :], op=ALU.mult),
                  reads=[eb, ttb], writes=[ttb])
            r = kbk - 4 * qt
            if r >= 0:
                kb.op("dve", lambda: nc.vector.tensor_tensor(out=tt[:, 1, :], in0=tt[:, 1, :], in1=R.mask[:, r, :], op=ALU.mult),
                      reads=[ttb, R.mask_b], writes=[ttb])

        def pv(i):
            kbk = kbs[i]
            tt, ttb = R.e16[i % 3], R.e16_b[i % 3]
            kb.op("pe", lambda: nc.tensor.matmul(R.ps[OB][0:64, :], lhsT=R.Va[:, kbk, 0:64], rhs=tt[:, 1, :],
                                                 start=(i == 0), stop=(i == n - 1)),
                  reads=[R.Va_b, ttb], writes=[R.ps_b[OB]])

        for i in range(min(LA, n)):
            z_esp(i)
        for i in range(n):
            if i + LA < n:
                z_esp(i + LA)
            chain(i)
            if i >= 1:
                pv(i - 1)
        pv(n - 1)

        def fn(yo, yob):
            kb.op("dve", lambda: nc.vector.tensor_copy(out=yo[:, :], in_=R.ps[OB][0:64, :]), reads=[R.ps_b[OB]], writes=[yob])
        out_store(kb, R, yT_d, row0, qt, fn, None)


def mixer_a(kb, R, hT, yT_d, row0, biasT_d):
    nc = kb.nc
    S, NB, NQ = R.S, R.NB, R.NQ
    kb.dma("sp", R.EB[:, :, :], biasT_d, writes=[R.EB_b])
    for r in range(8):
        kb.op("act", lambda r=r: nc.scalar.activation(out=R.EB[:, r, :], in_=R.EB[:, r, :], func=AF.Exp),
              reads=[R.EB_b], writes=[R.EB_b])
    kb.op("pool", lambda: nc.gpsimd.memset(R.Va[:, :, 64:128], 1.0), writes=[R.Va_b])
    ones64 = R.cst[0:64, 0:64]
    for qt in range(NQ):
        ht, htb = load_ht(kb, R, hT, qt)
        proj_fm(kb, R, ht, htb, A_Q, 64, 0)
        proj_fm(kb, R, ht, htb, A_K, 64, 2)
        qk_norm_store(kb, R, 0, 1, R.QT, R.QT_b, qt, 4, ones64, 1.0 / 64, 0)
        qk_norm_store(kb, R, 2, 3, R.KT, R.KT_b, qt, 5, ones64, 1.0 / 64, 1)
        v_store(kb, R, ht, htb, A_V, qt, 4 + (qt % 2))
    OB = 4
    for qt in range(NQ):
        rs = [r for r in range(8) if 4 * qt - 4 + r >= 0]
        for j, r in enumerate(rs):
            kbk = 4 * qt - 4 + r
            sb = j % 2
            kb.op("pe", lambda: nc.tensor.matmul(R.ps[sb][:, :], lhsT=R.KT[:, kbk * 128:(kbk + 1) * 128],
                                                 rhs=R.QT[:, qt * QT_:(qt + 1) * QT_], start=True, stop=True),
                  reads=[R.KT_b, R.QT_b], writes=[R.ps_b[sb]])
            e, eb = R.e32[j % 2], R.e32_b[j % 2]
            p, pbuf = R.e16[j % 2], R.e16_b[j % 2]
            kb.op("act", lambda: nc.scalar.activation(out=e[:, 0, :], in_=R.ps[sb][:, :], func=AF.Exp),
                  reads=[R.ps_b[sb]], writes=[eb])
            kb.op("dve", lambda: nc.vector.tensor_tensor(out=p[:, 0, :], in0=e[:, 0, :], in1=R.EB[:, r, :], op=ALU.mult),
                  reads=[eb, R.EB_b], writes=[pbuf])
            kb.op("pe", lambda: nc.tensor.matmul(R.ps[OB][:, :], lhsT=R.Va[:, kbk, :], rhs=p[:, 0, :],
                                                 start=(j == 0), stop=(j == len(rs) - 1)),
                  reads=[R.Va_b, pbuf], writes=[R.ps_b[OB]])
        f0, b0 = R.fin[0], R.fin_b[0]
        kb.op("dve", lambda: nc.vector.reciprocal(out=f0[:, :], in_=R.ps[OB][64:128, :]), reads=[R.ps_b[OB]], writes=[b0])

        def fn(yo, yob):
            kb.op("dve", lambda: nc.vector.tensor_tensor(out=yo[:, :], in0=R.ps[OB][0:64, :], in1=f0[:, :], op=ALU.mult),
                  reads=[R.ps_b[OB], b0], writes=[yob])
        out_store(kb, R, yT_d, row0, qt, fn, None)


class MixResB:
    def __init__(self, kb, R):
        NB = R.NB
        self.Osig = R.EB[:, :, :].bitcast(BF16).rearrange("p a (b c) -> p (a b) c", c=64)[:, 0:NB, :]; self.Osig_b = R.EB_b
        self.G = kb.sbuf("Gates", [128, 8, NB], F32); self.G_b = kb.buf()
        self.trif = kb.sbuf("trif", [128, 2, 128], F32); self.trif_b = kb.buf()
        self.cw = kb.sbuf("convw", [128, 8], F32); self.cw_b = kb.buf()
        self.gob = kb.sbuf("gob", [128, 64], F32); self.gob_b = kb.buf()
        self.St = [kb.sbuf(f"St{i}", [64, 65], F32) for i in range(2)]; self.St_b = [kb.buf() for _ in range(2)]
        self.Sb = [kb.sbuf(f"Sb{i}", [64, 65], BF16) for i in range(2)]; self.Sb_b = [kb.buf() for _ in range(2)]
        self.tok = [kb.sbuf(f"tok{i}", [128, 3, 64], BF16) for i in range(2)]; self.tok_b = [kb.buf() for _ in range(2)]
        self.qkT = [kb.sbuf(f"qkT{i}", [64, 2, 128], BF16) for i in range(2)]; self.qkT_b = [kb.buf() for _ in range(2)]
        self.qkm = [kb.sbuf(f"qkm{i}", [128, 128], BF16) for i in range(2)]; self.qkm_b = [kb.buf() for _ in range(2)]
        self.cm = kb.sbuf("cmask", [128, 128], F32); self.cm_b = kb.buf()
        self.hn = [kb.sbuf(f"hn{i}", [128, 64], F32) for i in range(2)]; self.hn_b = [kb.buf() for _ in range(2)]
        self.hs = [kb.sbuf(f"hs{i}", [128, 8], F32) for i in range(2)]; self.hs_b = [kb.buf() for _ in range(2)]
        self.yb = [kb.sbuf(f"yb{i}", [128, 64], BF16) for i in range(2)]; self.yb_b = [kb.buf() for _ in range(2)]
        self.jk = kb.sbuf("jk", [128, 64], BF16); self.jk_b = kb.buf()


def mixer_b(kb, R, RB_, hT, yT_d, row0, bpar_d, trif_d, cmask_d, gob_d):
    nc = kb.nc
    S, NB, NQ = R.S, R.NB, R.NQ
    B = RB_
    kb.dma("sp", B.cw[:, :], bpar_d, writes=[B.cw_b])
    kb.dma("sp", B.trif[:, :, :], trif_d, writes=[B.trif_b])
    kb.dma("sp", B.cm[:, :], cmask_d, writes=[B.cm_b])
    kb.dma("sp", B.gob[:, :], gob_d, writes=[B.gob_b])
    kb.op("pool", lambda: nc.gpsimd.memset(R.Va[:, :, 64:65], 1.0), writes=[R.Va_b])
    kb.op("dve", lambda: nc.vector.tensor_scalar(out=B.cw[:, 7:8], in0=B.cw[:, 6:7], scalar1=-1.0, scalar2=None, op0=ALU.mult),
          reads=[B.cw_b], writes=[B.cw_b])
    cv = [R.e32[i][:, :, :].rearrange("p a b -> p (a b)") for i in range(2)]
    cvb = R.e32_b
    accA = R.e16[0][:, :, :].rearrange("p a b -> p (a b)").bitcast(F32); accB_ = R.e16[1][:, :, :].rearrange("p a b -> p (a b)").bitcast(F32)
    accA_b = R.e16_b[0]; accB_b = R.e16_b[1]
    for qt in range(NQ):
        ht, htb = load_ht(kb, R, hT, qt)
        ci = qt % 2
        proj_fm(kb, R, ht, htb, B_Q, 128, 0)
        if qt == 0:
            kb.op("dve", lambda: nc.vector.memset(cv[ci][:, 0:3], 0.0), writes=[cvb[ci]])
        else:
            kb.op("dve", lambda: nc.vector.tensor_copy(out=cv[ci][:, 0:3], in_=cv[1 - ci][:, 512:515]),
                  reads=[cvb[1 - ci]], writes=[cvb[ci]])
        kb.op("act", lambda: nc.scalar.copy(out=cv[ci][:, 3:515], in_=R.ps[0][:, :]), reads=[R.ps_b[0]], writes=[cvb[ci]])
        kb.op("dve", lambda: nc.vector.tensor_scalar(out=accA, in0=cv[ci][:, 3:515], scalar1=B.cw[:, 3:4], scalar2=B.cw[:, 4:5],
                                                     op0=ALU.mult, op1=ALU.add),
              reads=[cvb[ci], B.cw_b], writes=[accA_b])
        for j in (2, 1, 0):
            kb.op("dve", lambda j=j: nc.vector.scalar_tensor_tensor(out=accA, in0=cv[ci][:, j:j + 512], scalar=B.cw[:, j:j + 1],
                                                                    in1=accA, op0=ALU.mult, op1=ALU.add),
                  reads=[cvb[ci], B.cw_b, accA_b], writes=[accA_b])
        kb.op("act", lambda: nc.scalar.activation(out=accB_, in_=accA, func=AF.Sigmoid), reads=[accA_b], writes=[accB_b])
        kb.op("dve", lambda: nc.vector.tensor_tensor(out=accB_, in0=accA, in1=accB_, op=ALU.mult), reads=[accA_b, accB_b], writes=[accB_b])
        kb.op("act", lambda: nc.scalar.copy(out=R.QT[:, qt * QT_:(qt + 1) * QT_], in_=accB_[0:64, :]), reads=[accB_b], writes=[R.QT_b])
        kb.op("act", lambda: nc.scalar.copy(out=R.KT[:, qt * QT_:(qt + 1) * QT_], in_=accB_[64:128, :]), reads=[accB_b], writes=[R.KT_b])
        pb = 4 + (qt % 2)
        for s in range(4):
            nonlocal_pb = 3 + ((qt * 4 + s) % 4)
            for kc in range(8):
                kb.op("pe", lambda kc=kc: nc.tensor.matmul(R.ps[nonlocal_pb][:, 0:130], lhsT=ht[:, kc, s * 128:(s + 1) * 128],
                                                           rhs=R.Wm[:, kc, B_V:B_V + 130], start=(kc == 0), stop=(kc == 7)),
                      reads=[R.Wm_b, htb], writes=[R.ps_b[nonlocal_pb]], inc=(kc == 7))
            blk = qt * 4 + s
            kb.op("dve", lambda: nc.vector.tensor_copy(out=R.Va[:, blk, 0:64], in_=R.ps[nonlocal_pb][:, 0:64]),
                  reads=[R.ps_b[nonlocal_pb]], writes=[R.Va_b])
            kb.op("act", lambda: nc.scalar.activation(out=B.Osig[:, blk, :], in_=R.ps[nonlocal_pb][:, 64:128], func=AF.Sigmoid),
                  reads=[R.ps_b[nonlocal_pb]], writes=[B.Osig_b])
            kb.op("dve", lambda: nc.vector.tensor_copy(out=B.G[:, 0:2, blk], in_=R.ps[nonlocal_pb][:, 128:130]),
                  reads=[R.ps_b[nonlocal_pb]], writes=[B.G_b])
    G = B.G
    kb.op("act", lambda: nc.scalar.activation(out=G[:, 2, :], in_=G[:, 1, :], func=AF.Exp, scale=-1.0, bias=B.cw[:, 7:8]),
          reads=[B.G_b, B.cw_b], writes=[B.G_b])
    kb.op("act", lambda: nc.scalar.activation(out=G[:, 2, :], in_=G[:, 2, :], func=AF.Ln, bias=1.0), reads=[B.G_b], writes=[B.G_b])
    kb.op("dve", lambda: nc.vector.tensor_scalar(out=G[:, 2, :], in0=G[:, 2, :], scalar1=-1.0, scalar2=None, op0=ALU.mult),
          reads=[B.G_b], writes=[B.G_b])
    kb.op("pe", lambda: nc.tensor.matmul(R.ps[0][:, 0:NB], lhsT=B.trif[:, 0, :], rhs=G[:, 2, :], start=True, stop=True),
          reads=[B.trif_b, B.G_b], writes=[R.ps_b[0]])
    kb.op("pe", lambda: nc.tensor.matmul(R.ps[1][:, 0:NB], lhsT=B.trif[:, 1, :], rhs=G[:, 2, :], start=True, stop=True),
          reads=[B.trif_b, B.G_b], writes=[R.ps_b[1]])
    kb.op("dve", lambda: nc.vector.tensor_copy(out=G[:, 3, :], in_=R.ps[0][:, 0:NB]), reads=[R.ps_b[0]], writes=[B.G_b])
    kb.op("act", lambda: nc.scalar.activation(out=G[:, 4, :], in_=G[:, 3, :], func=AF.Exp), reads=[B.G_b], writes=[B.G_b])
    kb.op("dve", lambda: nc.vector.tensor_tensor(out=G[:, 5, :], in0=G[:, 0, :], in1=G[:, 3, :], op=ALU.subtract),
          reads=[B.G_b], writes=[B.G_b])
    kb.op("dve", lambda: nc.vector.tensor_tensor(out=G[:, 6, :], in0=G[:, 5, :], in1=R.ps[1][:, 0:NB], op=ALU.add),
          reads=[B.G_b, R.ps_b[1]], writes=[B.G_b])
    kb.op("act", lambda: nc.scalar.activation(out=G[:, 5, :], in_=G[:, 5, :], func=AF.Exp, bias=B.cw[:, 5:6]),
          reads=[B.G_b, B.cw_b], writes=[B.G_b])
    kb.op("act", lambda: nc.scalar.activation(out=G[:, 6, :], in_=G[:, 6, :], func=AF.Exp, bias=B.cw[:, 5:6]),
          reads=[B.G_b, B.cw_b], writes=[B.G_b])
    kb.op("dve", lambda: nc.vector.tensor_scalar(out=G[:, 5:7, :], in0=G[:, 5:7, :], scalar1=0.125, scalar2=None, op0=ALU.mult),
          reads=[B.G_b], writes=[B.G_b])
    kb.op("act", lambda: nc.scalar.activation(out=G[:, 7, :], in_=R.ps[1][:, 0:NB], func=AF.Exp), reads=[R.ps_b[1]], writes=[B.G_b])
    kb.op("dve", lambda: nc.vector.memset(B.St[0][:, :], 0.0), writes=[B.St_b[0]])
    kb.op("dve", lambda: nc.vector.memset(B.Sb[0][:, :], 0.0), writes=[B.Sb_b[0]])
    PT, PT2, PS_, PO, PU = 0, 1, 2, 3, 6
    for b in range(NB):
        i2 = b % 2
        tok, tokb = B.tok[i2], B.tok_b[i2]
        qkT, qkTb = B.qkT[i2], B.qkT_b[i2]
        tpA = R.ps[0][:, :].bitcast(BF16); tpq = tpA[:, 0:128]; tpk = tpA[:, 128:256]
        kb.op("pe", lambda: nc.tensor.transpose(out=tpq[:, 0:64], in_=R.QT[:, b * 128:(b + 1) * 128], identity=R.ident[0:64, 0:64]),
              reads=[R.QT_b, R.ident_b], writes=[R.ps_b[0]])
        kb.op("pe", lambda: nc.tensor.transpose(out=tpk[:, 0:64], in_=R.KT[:, b * 128:(b + 1) * 128], identity=R.ident[0:64, 0:64]),
              reads=[R.KT_b, R.ident_b], writes=[R.ps_b[0]])
        kb.op("dve", lambda: nc.vector.tensor_scalar(out=tok[:, 0, :], in0=tpq[:, 0:64], scalar1=G[:, 4, b:b + 1], scalar2=None, op0=ALU.mult),
              reads=[R.ps_b[0], B.G_b], writes=[tokb])
        kb.op("dve", lambda: nc.vector.tensor_scalar(out=tok[:, 1, :], in0=tpk[:, 0:64], scalar1=G[:, 5, b:b + 1], scalar2=None, op0=ALU.mult),
              reads=[R.ps_b[0], B.G_b], writes=[tokb])
        kb.op("dve", lambda: nc.vector.tensor_scalar(out=tok[:, 2, :], in0=tpk[:, 0:64], scalar1=G[:, 6, b:b + 1], scalar2=None, op0=ALU.mult),
              reads=[R.ps_b[0], B.G_b], writes=[tokb])
        tpB = R.ps[1][:, :].bitcast(BF16); tq2 = tpB[:, 0:128]; tk2 = tpB[:, 128:256]
        kb.op("pe", lambda: nc.tensor.transpose(out=tq2[0:64, :], in_=tok[:, 0, :], identity=R.ident[:, :]),
              reads=[tokb, R.ident_b], writes=[R.ps_b[1]])
        kb.op("pe", lambda: nc.tensor.transpose(out=tk2[0:64, :], in_=tok[:, 1, :], identity=R.ident[:, :]),
              reads=[tokb, R.ident_b], writes=[R.ps_b[1]])
        kb.op("act", lambda: nc.scalar.copy(out=qkT[:, 0, :], in_=tq2[0:64, :]), reads=[R.ps_b[1]], writes=[qkTb])
        kb.op("act", lambda: nc.scalar.copy(out=qkT[:, 1, :], in_=tk2[0:64, :]), reads=[R.ps_b[1]], writes=[qkTb])
        kb.op("pe", lambda: nc.tensor.matmul(R.ps[PS_][:, 0:128], lhsT=qkT[:, 1, :], rhs=qkT[:, 0, :], start=True, stop=True),
              reads=[qkTb], writes=[R.ps_b[PS_]])
        qkm, qkmb = B.qkm[i2], B.qkm_b[i2]
        kb.op("dve", lambda: nc.vector.tensor_tensor(out=qkm[:, :], in0=R.ps[PS_][:, 0:128], in1=B.cm[:, :], op=ALU.mult),
              reads=[R.ps_b[PS_], B.cm_b], writes=[qkmb])
        Sp, Spb = B.Sb[i2], B.Sb_b[i2]
        po = PO + (b % 2)
        kb.op("pe", lambda: nc.tensor.matmul(R.ps[po][:, 0:65], lhsT=qkm[:, :], rhs=R.Va[:, b, 0:65], start=True, stop=False),
              reads=[qkmb, R.Va_b], writes=[R.ps_b[po]], inc=False)
        kb.op("pe", lambda: nc.tensor.matmul(R.ps[po][:, 0:65], lhsT=qkT[:, 0, :], rhs=Sp[:, :], start=False, stop=True),
              reads=[qkTb, Spb], writes=[R.ps_b[po]])
        kb.op("pe", lambda: nc.tensor.matmul(R.ps[PU][0:64, 0:65], lhsT=tok[:, 2, :], rhs=R.Va[:, b, 0:65], start=True, stop=True),
              reads=[tokb, R.Va_b], writes=[R.ps_b[PU]])
        Sn, Snb = B.St[1 - i2], B.St_b[1 - i2]
        So, Sob = B.St[i2], B.St_b[i2]
        kb.op("dve", lambda: nc.vector.scalar_tensor_tensor(out=Sn[:, :], in0=So[:, :], scalar=G[0:64, 7, b:b + 1], in1=R.ps[PU][0:64, 0:65],
                                                            op0=ALU.mult, op1=ALU.add),
              reads=[Sob, B.G_b, R.ps_b[PU]], writes=[Snb])
        kb.op("act", lambda: nc.scalar.copy(out=B.Sb[1 - i2][:, :], in_=Sn[:, :]), reads=[Snb], writes=[B.Sb_b[1 - i2]])
        hs, hsb = B.hs[i2], B.hs_b[i2]
        hn, hnb = B.hn[i2], B.hn_b[i2]
        kb.op("act", lambda: nc.scalar.activation(out=hs[:, 5:6], in_=R.ps[po][:, 64:65], func=AF.Abs),
              reads=[R.ps_b[po]], writes=[hsb])
        kb.op("dve", lambda: nc.vector.tensor_scalar(out=hs[:, 0:1], in0=hs[:, 5:6], scalar1=1.0, scalar2=None, op0=ALU.max),
              reads=[hsb], writes=[hsb])
        kb.op("dve", lambda: nc.vector.reciprocal(out=hs[:, 1:2], in_=hs[:, 0:1]), reads=[hsb], writes=[hsb])
        kb.op("dve", lambda: nc.vector.tensor_scalar(out=hn[:, :], in0=R.ps[po][:, 0:64], scalar1=hs[:, 1:2], scalar2=None, op0=ALU.mult),
              reads=[R.ps_b[po], hsb], writes=[hnb])
        kb.op("act", lambda: nc.scalar.activation(out=B.jk[:, :], in_=hn[:, :], func=AF.Square, accum_out=hs[:, 2:3]),
              reads=[hnb], writes=[B.jk_b, hsb])
        kb.op("act", lambda: nc.scalar.activation(out=hs[:, 3:4], in_=hs[:, 2:3], func=AF.Sqrt, bias=EPS, scale=1.0 / 64),
              reads=[hsb], writes=[hsb])
        kb.op("dve", lambda: nc.vector.reciprocal(out=hs[:, 4:5], in_=hs[:, 3:4]), reads=[hsb], writes=[hsb])
        kb.op("dve", lambda: nc.vector.scalar_tensor_tensor(out=hn[:, :], in0=hn[:, :], scalar=hs[:, 4:5], in1=B.gob[:, :],
                                                            op0=ALU.mult, op1=ALU.mult),
              reads=[hnb, hsb, B.gob_b], writes=[hnb])
        yb, ybb = B.yb[i2], B.yb_b[i2]
        kb.op("dve", lambda: nc.vector.tensor_tensor(out=yb[:, :], in0=hn[:, :], in1=B.Osig[:, b, :], op=ALU.mult),
              reads=[hnb, B.Osig_b], writes=[ybb])
        ty = R.ps[5][:, :].bitcast(BF16)[:, 0:128]
        kb.op("pe", lambda: nc.tensor.transpose(out=ty[0:64, :], in_=yb[:, :], identity=R.ident[:, :]),
              reads=[ybb, R.ident_b], writes=[R.ps_b[5]])
        qt = b // 4
        if b % 4 == 0:
            R.cur_yo = R.nyo % 2; R.nyo += 1
        yo, yob = R.yo[R.cur_yo], R.yo_b[R.cur_yo]
        kb.op("act", lambda: nc.scalar.copy(out=yo[:, (b % 4) * 128:(b % 4 + 1) * 128], in_=ty[0:64, :]),
              reads=[R.ps_b[5]], writes=[yob])
        if b % 4 == 3:
            kb.dma("pool", ydst(yT_d, row0, qt), yo[:, :], reads=[yob])


def mix_params(kb, R, praw_d, clam_d):
    nc = kb.nc
    pr = R.par
    kb.dma("sp", pr[0:64, 16:24], praw_d, writes=[R.par_b])
    cl = R.rr[0][:, 0:128].rearrange("p (a b) -> p a b", a=4)
    kb.dma("sp", cl, clam_d, writes=[R.rr_b[0]])
    V = nc.vector
    kb.op("dve", lambda: V.tensor_scalar(out=pr[0:64, 0:1], in0=pr[0:64, 16:17], scalar1=32 ** -0.5, scalar2=None, op0=ALU.mult), reads=[R.par_b], writes=[R.par_b])
    kb.op("dve", lambda: V.tensor_copy(out=pr[0:64, 1:2], in_=pr[0:64, 17:18]), reads=[R.par_b], writes=[R.par_b])
    kb.op("dve", lambda: V.tensor_tensor(out=pr[0:64, 3:4], in0=pr[0:64, 18:19], in1=pr[0:64, 22:23], op=ALU.mult), reads=[R.par_b], writes=[R.par_b])
    kb.op("dve", lambda: V.tensor_scalar(out=pr[0:64, 4:5], in0=pr[0:64, 19:20], scalar1=0.125, scalar2=None, op0=ALU.mult), reads=[R.par_b], writes=[R.par_b])
    kb.op("dve", lambda: V.tensor_copy(out=pr[0:64, 5:6], in_=pr[0:64, 20:21]), reads=[R.par_b], writes=[R.par_b])
    pp = R.rr[1][:, 0:64].rearrange("p (a b) -> p a b", a=2)
    kb.op("dve", lambda: V.tensor_tensor(out=pp[:, 0, :], in0=cl[:, 0, :], in1=cl[:, 1, :], op=ALU.mult), reads=[R.rr_b[0]], writes=[R.rr_b[1]])
    kb.op("dve", lambda: V.tensor_tensor(out=pp[:, 1, :], in0=cl[:, 2, :], in1=cl[:, 3, :], op=ALU.mult), reads=[R.rr_b[0]], writes=[R.rr_b[1]])
    kb.op("dve", lambda: V.reduce_sum(out=pr[0:64, 8:10], in_=pp, axis=AX.X), reads=[R.rr_b[1]], writes=[R.par_b])
    kb.op("act", lambda: nc.scalar.activation(out=pr[0:64, 10:12], in_=pr[0:64, 8:10], func=AF.Exp), reads=[R.par_b], writes=[R.par_b])
    kb.op("dve", lambda: V.tensor_tensor(out=pr[0:64, 12:13], in0=pr[0:64, 11:12], in1=pr[0:64, 10:11], op=ALU.subtract), reads=[R.par_b], writes=[R.par_b])
    kb.op("dve", lambda: V.tensor_tensor(out=pr[0:64, 2:3], in0=pr[0:64, 12:13], in1=pr[0:64, 21:22], op=ALU.subtract), reads=[R.par_b], writes=[R.par_b])

import ml_dtypes
bf16 = ml_dtypes.bfloat16
GW = 256
OFF = dict(aq=0, ak=256, av=512, bqk=768, bv=1280, bo=1536, bi=1792, bf=1796, cq=1800, ck=2056, cv=2312, dq=2568, dk=2824, dv=3080)

def sel_cols(j):
    c = []
    r = lambda o: list(range(o + j * 64, o + j * 64 + 64))
    c += r(OFF['aq']) + r(OFF['ak']) + r(OFF['av'])
    c += r(OFF['bqk']) + r(OFF['bqk'] + 256) + r(OFF['bv']) + r(OFF['bo']) + [OFF['bi'] + j, OFF['bf'] + j]
    c += r(OFF['cq']) + r(OFF['ck']) + r(OFF['cv'])
    c += r(OFF['dq']) + r(OFF['dk']) + r(OFF['dv'])
    return np.array(c)

def const_inputs():
    d = {}
    d['ident'] = np.eye(128, dtype=np.float32).astype(bf16)
    cst = np.zeros((128, 256), np.float32)
    cst[0:64, 0:64] = 1.0
    cst[0:32, 64:96] = 1.0; cst[32:64, 96:128] = 1.0
    d['cst'] = cst.astype(bf16)
    s = np.arange(128)[:, None, None, None]; r = np.arange(4)[None, :, None, None]; t = np.arange(512)[None, None, None, :]
    mc = ((2 * r + (s >= 64)) <= (t // 64)).astype(np.float32)
    d['masks_c'] = np.broadcast_to(mc, (128, 4, 2, 512)).astype(bf16).copy()
    s = np.arange(128)[:, None, None]; r = np.arange(4)[None, :, None]; t = np.arange(512)[None, None, :]
    d['masks_d'] = ((128 * r + s) < t).astype(np.float32).astype(bf16)
    j = np.arange(128)[:, None]; s2 = np.arange(128)[None, :]
    tri = np.zeros((128, 2, 128), np.float32)
    tri[:, 0, :] = (j >= s2); tri[:, 1, :] = (j < s2)
    d['tri'] = tri.astype(bf16)
    trif = np.zeros((128, 2, 128), np.float32)
    trif[:, 0, :] = (j <= s2); trif[:, 1, :] = 1.0
    d['trif'] = trif
    d['cmask'] = (j <= s2).astype(np.float32)
    return d

def bias_index():
    s = np.arange(128)[:, None, None]; r = np.arange(8)[None, :, None]; t = np.arange(512)[None, None, :]
    rel = t - s + 512 - 128 * r
    idx = np.clip(rel, -128, 128) + 128
    dd = t // 64 + 8 - 2 * r - s // 64
    vis = (dd >= 0) & (dd <= 8)
    return idx, vis

_IDX, _VIS = bias_index()

def layer_core_inputs(P, l, j, lam_init=None):
    d = {}
    d['wsel'] = np.ascontiguousarray(P['w_in'][l][:, sel_cols(j)])
    d['gmix'] = np.ascontiguousarray(P['mix_norm'][l].reshape(8, 128).T)
    praw = np.zeros((64, 8), np.float32)
    praw[:, 0] = np.tile(P['c_q_norm'][l], 2); praw[:, 1] = np.tile(P['c_k_norm'][l], 2)
    praw[:, 2] = P['c_out_norm'][l]; praw[:, 3] = P['a_q_norm'][l]; praw[:, 4] = P['a_k_norm'][l]
    if lam_init is None:
        lam_init = 0.8 - 0.6 * np.exp(-0.3 * l)
    praw[:, 5] = lam_init; praw[:, 6] = 1.0 - lam_init
    d['praw'] = praw
    d['clam'] = np.ascontiguousarray(np.broadcast_to(P['c_lambda'][l][None], (64, 4, 32))).astype(np.float32)
    rb = P['a_rel_bias'][l][j]
    d['biasT'] = np.where(_VIS, rb[_IDX], np.float32(-1e30)).astype(np.float32)
    bpar = np.zeros((128, 8), np.float32)
    ch = np.concatenate([np.arange(j * 64, j * 64 + 64), 256 + np.arange(j * 64, j * 64 + 64)])
    bpar[:, 0:4] = P['b_conv_w'][l][:, ch].T
    bpar[:, 4] = P['b_conv_b'][l][ch]
    bpar[:, 5] = P['b_gate_bias'][l][0, j]
    bpar[:, 6] = P['b_gate_bias'][l][1, j]
    d['bpar'] = bpar
    d['gob'] = np.ascontiguousarray(np.broadcast_to(P['b_out_norm'][l][j][None], (128, 64))).astype(np.float32)
    return d


from concourse.bass_utils import run_bass_kernel_spmd

SEQ = 16384
NCORE = 8
TPC = 4096
DEPTH = 2
GROUPS = [[0, 1, 2, 3], [4, 5, 6, 7]]


def _din(nc, name, shape, dt):
    return nc.dram_tensor(name, list(shape), dt, kind="ExternalInput").ap()


def _dout(nc, name, shape, dt):
    return nc.dram_tensor(name, list(shape), dt, kind="ExternalOutput").ap()


def _dint(nc, name, shape, dt):
    return nc.dram_tensor(name, list(shape), dt, kind="Internal").ap()


MIX_IN = dict(wsel=([D, NW], F32), gmix=([128, 8], F32), praw=([64, 8], F32), clam=([64, 4, 32], F32),
              biasT=([128, 8, 512], F32), bpar=([128, 8], F32), gob=([128, 64], F32))
CONST_IN = dict(ident=([128, 128], BF16), cst=([128, 256], BF16), masks_c=([128, 4, 2, 512], BF16),
                masks_d=([128, 4, 512], BF16), tri=([128, 2, 128], BF16), trif=([128, 2, 128], F32), cmask=([128, 128], F32))


def build_fused(S=SEQ, T=TPC):
    nc = bass.Bass("TRN2", target_bir_lowering=False)
    NQr = T // QT_
    x_in = _din(nc, "x_in", [T, D], F32)
    x_out = _dout(nc, "x_out", [T, D], F32)
    Cn = {k: _din(nc, k, sh, dt) for k, (sh, dt) in CONST_IN.items()}
    ffn = {}
    for l in range(DEPTH):
        for f in ("ffn1", "ffn2"):
            ffn[(f, l)] = dict(g=_din(nc, f"{f}_g{l}", [128, 8], F32), wg=_din(nc, f"{f}_wg{l}", [D, DFF], F32),
                               wu=_din(nc, f"{f}_wu{l}", [D, DFF], F32), wd=_din(nc, f"{f}_wd{l}", [DFF, D], F32))
    wo = [_din(nc, f"wo{l}", [D, D], F32) for l in range(DEPTH)]
    mx = [{k: _din(nc, f"{k}{l}", sh, dt) for k, (sh, dt) in MIX_IN.items()} for l in range(DEPTH)]
    xa = _dint(nc, "xa", [T, D], F32); xb = _dint(nc, "xb", [T, D], F32); xc = _dint(nc, "xc", [T, D], F32)
    hT_loc = _dint(nc, "hT_loc", [D, T], BF16)
    hT_all = _dint(nc, "hT_all", [4 * D, T], BF16)
    yT_loc = _dint(nc, "yT_loc", [D, T], BF16)
    yT_all = _dint(nc, "yT_all", [4 * D, T], BF16)
    yT_mine = _dint(nc, "yT_mine", [D, T], BF16)

    hv = hT_all.rearrange("(k r p) t -> k r p t", k=8, r=4)

    def hT_src(qt):
        r, o = qt // NQr, (qt % NQr) * QT_
        return hv[:, r, :, o:o + QT_]

    def y_dst(row0, qt):
        q, o = qt // NQr, (qt % NQr) * QT_
        return yT_loc[q * 256 + row0:q * 256 + row0 + 64, o:o + QT_]

    with ExitStack() as st:
        kb = KB(nc, st)
        pid = nc.sync.partition_id()
        qv = pid % 4

        ymine_b = kb.buf()

        def fetch_mine():
            yv2 = yT_all.rearrange("(q h j p) t -> q h j p t", q=4, h=2, j=4)
            for j in range(4):
                for h in range(2):
                    kb.dma("sp", yT_mine[j * 256 + h * 128:j * 256 + (h + 1) * 128, :],
                           yv2[bass.ds(qv, 1), h, j, :, :].rearrange("o p t -> (o p) t"), writes=[ymine_b])

        def tok_phase(passes):
            with ExitStack() as mem:
                kb.mem = mem
                R = TokRes(kb, any(p.get("wo") is not None for p in passes))
                load_consts(kb, R, Cn["ident"])
                prev_bufs = None
                for k, p in enumerate(passes):
                    w = p["ffn"]
                    load_ffn_weights(kb, R, w["g"], w["wg"], w["wu"], w["wd"], p.get("wo"))
                    ob = [kb.buf() for _ in range(T // TT)] if k + 1 < len(passes) else None
                    has_pre = p.get("wo") is not None
                    token_pass(kb, R, T, p["xi"], p["xo"], pre=(yT_mine if has_pre else None),
                               post=p.get("post"), in_bufs=prev_bufs, out_bufs=ob,
                               pre_bufs=([ymine_b] * (T // TT) if has_pre else None))
                    prev_bufs = ob
                kb.barrier()
            kb.mem = st

        def mix_phase(l):
            with ExitStack() as mem:
                kb.mem = mem
                R = MixRes(kb, S)
                RB = MixResB(kb, R)
                m = mx[l]
                mix_load_common(kb, R, m["wsel"], m["gmix"], Cn["ident"], Cn["cst"])
                mix_params(kb, R, m["praw"], m["clam"])
                mixer_a(kb, R, hT_src, y_dst, 0, m["biasT"])
                kb.barrier()
                mixer_b(kb, R, RB, hT_src, y_dst, 64, m["bpar"], Cn["trif"], Cn["cmask"], m["gob"])
                kb.barrier()
                mixer_c(kb, R, hT_src, y_dst, 128, Cn["masks_c"])
                kb.barrier()
                mixer_d(kb, R, hT_src, y_dst, 192, Cn["masks_d"], Cn["tri"])
                kb.barrier()
            kb.mem = st

        tok_phase([dict(ffn=ffn[("ffn1", 0)], xi=x_in, xo=xa, post=hT_loc)])
        kb.allgather(hT_loc, hT_all, GROUPS)
        mix_phase(0)
        kb.allgather(yT_loc, yT_all, GROUPS)
        fetch_mine()
        tok_phase([dict(ffn=ffn[("ffn2", 0)], wo=wo[0], xi=xa, xo=xb),
                   dict(ffn=ffn[("ffn1", 1)], xi=xb, xo=xc, post=hT_loc)])
        kb.allgather(hT_loc, hT_all, GROUPS)
        mix_phase(1)
        kb.allgather(yT_loc, yT_all, GROUPS)
        fetch_mine()
        tok_phase([dict(ffn=ffn[("ffn2", 1)], wo=wo[1], xi=xc, xo=x_out)])
        kb.finish()
    return nc


def build_mixer_prog(S=SEQ):
    nc = bass.Bass("TRN2", target_bir_lowering=False)
    hT = _din(nc, "hT", [D, S], BF16)
    Cn = {k: _din(nc, k, sh, dt) for k, (sh, dt) in CONST_IN.items()}
    m = {k: _din(nc, k, sh, dt) for k, (sh, dt) in MIX_IN.items()}
    yT = _dout(nc, "yT", [256, S], BF16)
    with ExitStack() as st:
        kb = KB(nc, st)
        R = MixRes(kb, S)
        RB = MixResB(kb, R)
        mix_load_common(kb, R, m["wsel"], m["gmix"], Cn["ident"], Cn["cst"])
        mix_params(kb, R, m["praw"], m["clam"])
        mixer_a(kb, R, hT, yT, 0, m["biasT"])
        kb.barrier()
        mixer_b(kb, R, RB, hT, yT, 64, m["bpar"], Cn["trif"], Cn["cmask"], m["gob"])
        kb.barrier()
        mixer_c(kb, R, hT, yT, 128, Cn["masks_c"])
        kb.barrier()
        mixer_d(kb, R, hT, yT, 192, Cn["masks_d"], Cn["tri"])
        kb.finish()
    return nc


def _lay(g):
    return np.ascontiguousarray(np.asarray(g, np.float32).reshape(8, 128).T)


def _wo_perm(w_out):
    idx = np.arange(1024).reshape(4, 4, 64)
    perm = idx.transpose(1, 0, 2).reshape(-1)
    return np.ascontiguousarray(w_out[perm, :])


def make_in_maps(P, TPC=TPC):
    x = np.ascontiguousarray(P["x"], dtype=np.float32).reshape(-1, D)
    C = const_inputs()
    shared = dict(C)
    for l in range(DEPTH):
        for f in ("ffn1", "ffn2"):
            shared[f"{f}_g{l}"] = _lay(P[f + "_norm"][l])
            shared[f"{f}_wg{l}"] = np.ascontiguousarray(P[f + "_wg"][l], dtype=np.float32)
            shared[f"{f}_wu{l}"] = np.ascontiguousarray(P[f + "_wu"][l], dtype=np.float32)
            shared[f"{f}_wd{l}"] = np.ascontiguousarray(P[f + "_wd"][l], dtype=np.float32)
        shared[f"wo{l}"] = _wo_perm(np.asarray(P["w_out"][l], np.float32))
    ims = []
    for c in range(NCORE):
        d = dict(shared)
        d["x_in"] = x[c * TPC:(c + 1) * TPC]
        j = c % 4
        for l in range(DEPTH):
            for k, v in layer_core_inputs(P, l, j).items():
                d[f"{k}{l}"] = v
        ims.append(d)
    return ims


def kernel(**inputs):
    P = {k: np.asarray(v) for k, v in inputs.items()}
    nc = build_fused()
    ims = make_in_maps(P)
    res = run_bass_kernel_spmd(nc, ims, core_ids=list(range(NCORE)))
    out = np.concatenate([r["x_out"] for r in res.results], axis=0).reshape(2, SEQ, D).astype(np.float32)
    return out
```

```python
import numpy as np
from contextlib import ExitStack
import concourse.bass as bass
import concourse.mybir as mybir

F32 = mybir.dt.float32
BF16 = mybir.dt.bfloat16
AF = mybir.ActivationFunctionType
ALU = mybir.AluOpType
AX = mybir.AxisListType

EPOCH = 4096


class Buf:
    __slots__ = ("w", "r", "name")

    def __init__(self, name=""):
        self.w = None
        self.r = {}
        self.name = name


class KB:
    def __init__(self, nc, stack):
        self.nc = nc
        self.st = stack
        self.E = {"pe": nc.tensor, "act": nc.scalar, "dve": nc.vector, "pool": nc.gpsimd, "sp": nc.sync}
        self.cnt = {e: 0 for e in self.E}
        self.sems = {e: [] for e in self.E}
        self.waited = {e: {} for e in self.E}
        self.ndma = 12
        self.dma_sems = {}
        self.dma_cnt = {}
        self.dma_rr = {}
        self.nsem = 0
        self.uid = 0
        self.mem = stack

    def sem(self, name):
        self.nsem += 1
        return self.st.enter_context(self.nc.semaphore(name))

    def sbuf(self, name, shape, dt):
        self.uid += 1
        return self.mem.enter_context(self.nc.sbuf_tensor(f"sb{self.uid}_" + name, list(shape), dt))

    def psum(self, name, shape, dt):
        self.uid += 1
        return self.mem.enter_context(self.nc.psum_tensor(f"ps{self.uid}_" + name, list(shape), dt))

    def buf(self, name=""):
        return Buf(name)

    def _esem(self, e, n):
        ep = (n - 1) // EPOCH
        while len(self.sems[e]) <= ep:
            self.sems[e].append(self.sem(f"c_{e}_{len(self.sems[e])}"))
        return self.sems[e][ep], (n - 1) % EPOCH + 1

    def _wait(self, e, ev):
        if ev[0] == "e":
            _, src, n = ev
            if src == e and e == "pe":
                return
            key = ("e", src)
            if self.waited[e].get(key, 0) >= n:
                return
            if src == e and n > self.cnt[e]:
                raise RuntimeError("self-wait on future event")
            s, v = self._esem(src, n)
            self.E[e].wait_ge(s, v)
            self.waited[e][key] = n
        else:
            _, q, i, k = ev
            key = ("d", q, i)
            if self.waited[e].get(key, 0) >= k:
                return
            self.E[e].wait_ge(self.dma_sems[q][i], 16 * k)
            self.waited[e][key] = k

    @staticmethod
    def _evkey(ev):
        return (ev[0], ev[1]) if ev[0] == "e" else (ev[0], ev[1], ev[2])

    def _collect(self, reads, writes):
        deps = []
        for b in reads:
            if b.w is not None:
                deps.append(b.w)
        for b in writes:
            if b.w is not None:
                deps.append(b.w)
            deps.extend(b.r.values())
        return deps

    def _record(self, ev, reads, writes):
        k = self._evkey(ev)
        for b in reads:
            b.r[k] = ev
        for b in writes:
            b.w = ev
            b.r = {}

    def op(self, e, fn, reads=(), writes=(), inc=True):
        for ev in self._collect(reads, writes):
            self._wait(e, ev)
        ins = fn()
        if inc:
            self.cnt[e] += 1
            s, v = self._esem(e, self.cnt[e])
            ins.then_inc(s, 1)
            ev = ("e", e, self.cnt[e])
        else:
            ev = ("e", e, self.cnt[e] + 1)
        self._record(ev, reads, writes)
        return ins

    def dma(self, q, out, in_, reads=(), writes=(), **kw):
        for ev in self._collect(reads, writes):
            self._wait(q, ev)
        if q not in self.dma_sems:
            self.dma_sems[q] = [self.sem(f"d_{q}_{i}") for i in range(self.ndma)]
            self.dma_cnt[q] = [0] * self.ndma
            self.dma_rr[q] = 0
        i = self.dma_rr[q]
        self.dma_rr[q] = (i + 1) % self.ndma
        if self.dma_cnt[q][i] > 0:
            self._wait(q, ("d", q, i, self.dma_cnt[q][i]))
        self.dma_cnt[q][i] += 1
        ins = self.E[q].dma_start(out=out, in_=in_, **kw)
        ins.then_inc(self.dma_sems[q][i], 16)
        ev = ("d", q, i, self.dma_cnt[q][i])
        self._record(ev, reads, writes)
        return ins

    def barrier(self, extra_sems=()):
        for e in self.E:
            for q in self.dma_sems:
                for i in range(self.ndma):
                    if self.dma_cnt[q][i] > 0:
                        self._wait(e, ("d", q, i, self.dma_cnt[q][i]))
            for src in ("pe", "act", "dve", "pool"):
                if self.cnt[src] > 0 and not (src == e and e == "pe"):
                    self._wait(e, ("e", src, self.cnt[src]))
            for (sm, v) in extra_sems:
                self.E[e].wait_ge(sm, v)

    def allgather(self, src2d, dst2d, groups, chunk_rows=128):
        self.barrier()
        R_ = src2d.shape[0]
        nk = R_ // chunk_rows
        ng = len(groups[0])
        sms = []
        for k in range(nk):
            sm = self.sem(f"cc{self.nsem}")
            self.nc.gpsimd.collective_compute("AllGather", ALU.bypass, replica_groups=groups,
                                              ins=[src2d[k * chunk_rows:(k + 1) * chunk_rows, :]],
                                              outs=[dst2d[k * ng * chunk_rows:(k + 1) * ng * chunk_rows, :]]).then_inc(sm, 1)
            sms.append(sm)
        for e in self.E:
            for sm in sms:
                self.E[e].wait_ge(sm, 1)

    def finish(self):
        for q in self.dma_sems:
            for i in range(self.ndma):
                if self.dma_cnt[q][i] > 0:
                    self._wait("sp", ("d", q, i, self.dma_cnt[q][i]))
        for e in ("pe", "act", "dve", "pool"):
            if self.cnt[e] > 0:
                self._wait("sp", ("e", e, self.cnt[e]))


D = 1024
DFF = 2816
NFC = DFF // 128
TT = 256
SUB = TT // 128
EPS = 1e-6


class TokRes:
    def __init__(self, kb, with_pre):
        nc = kb.nc
        self.kb = kb
        self.Wg = kb.sbuf("Wg", [128, 8, DFF], BF16); self.Wg_b = kb.buf()
        self.Wu = kb.sbuf("Wu", [128, 8, DFF], BF16); self.Wu_b = kb.buf()
        self.Wd = kb.sbuf("Wd", [128, NFC, D], BF16); self.Wd_b = kb.buf()
        self.stage = [kb.sbuf(f"stage{i}", [128, 1024], F32) for i in range(2)]
        self.stage_b = [kb.buf() for _ in range(2)]
        self.gt = kb.sbuf("gt", [128, 8], F32); self.gt_b = kb.buf()
        self.ident = kb.sbuf("ident", [128, 128], BF16); self.ident_b = kb.buf()
        self.xt = [kb.sbuf(f"xt{i}", [128, SUB, D], F32) for i in range(2)]
        self.xt_b = [kb.buf() for _ in range(2)]
        self.xn = kb.sbuf("xn", [128, D], BF16); self.xn_b = kb.buf()
        self.st = kb.sbuf("stat", [128, 8], F32); self.st_b = kb.buf()
        self.xnT = [kb.sbuf(f"xnT{i}", [128, 8, TT], BF16) for i in range(2)]
        self.xnT_b = [kb.buf() for _ in range(2)]
        self.hid = kb.sbuf("hid", [128, NFC, TT], BF16)
        self.hid_b = [kb.buf() for _ in range(NFC)]
        self.sg = [kb.sbuf(f"sg{i}", [128, TT], F32) for i in range(2)]
        self.sg_b = [kb.buf() for _ in range(2)]
        self.hTo = kb.sbuf("hTo", [128, 8, TT], BF16); self.hTo_b = kb.buf()
        self.with_pre = with_pre
        if with_pre:
            self.Wo = kb.sbuf("Wo", [128, 8, D], BF16); self.Wo_b = kb.buf()
            self.yt = [kb.sbuf(f"yt{i}", [128, 8, TT], BF16) for i in range(2)]
            self.yt_b = [kb.buf() for _ in range(2)]
        self.psg = [kb.psum(f"psg{i}", [128, 512], F32) for i in range(2)]
        self.psg_b = [kb.buf() for _ in range(2)]
        self.psd = [kb.psum(f"psd{i}", [128, 512], F32) for i in range(2)]
        self.psd_b = [kb.buf() for _ in range(2)]
        self.tp = [kb.psum(f"tp{i}", [128, 8, 128], BF16) for i in range(2)]
        self.tp_b = [kb.buf() for _ in range(2)]
        self.ntp = 0
        self.npsd = 0
        self.ncast = 0


def load_consts(kb, R, ident_d):
    kb.dma("sp", R.ident[:], ident_d, writes=[R.ident_b])


def load_ffn_weights(kb, R, g_lay, wg, wu, wd, w_out=None):
    nc = kb.nc
    kb.dma("sp", R.gt[:], g_lay, writes=[R.gt_b])

    def cast(dst_ap, dst_b, src_ap, src_b, scal):
        e = "dve" if R.ncast % 2 == 0 else "pool"
        R.ncast += 1
        E = kb.E[e]
        if scal is None:
            kb.op(e, lambda: E.tensor_copy(out=dst_ap, in_=src_ap), reads=[src_b], writes=[dst_b])
        else:
            kb.op(e, lambda: E.tensor_scalar(out=dst_ap, in0=src_ap, scalar1=scal, scalar2=None, op0=ALU.mult),
                  reads=[src_b, R.gt_b], writes=[dst_b])

    k = 0
    for (W, Wb, src) in ((R.Wg, R.Wg_b, wg), (R.Wu, R.Wu_b, wu)):
        for kc in range(8):
            for (c0, c1) in ((0, 1024), (1024, 2048), (2048, DFF)):
                sb = k % 2; k += 1
                kb.dma("sp", R.stage[sb][:, 0:c1 - c0], src[kc * 128:(kc + 1) * 128, c0:c1],
                       writes=[R.stage_b[sb]])
                cast(W[:, kc, c0:c1], Wb, R.stage[sb][:, 0:c1 - c0], R.stage_b[sb], R.gt[:, kc:kc + 1])
    for fc in range(NFC):
        sb = k % 2; k += 1
        kb.dma("sp", R.stage[sb][:, 0:D], wd[fc * 128:(fc + 1) * 128, :], writes=[R.stage_b[sb]])
        cast(R.Wd[:, fc, :], R.Wd_b, R.stage[sb][:, 0:D], R.stage_b[sb], None)
    if w_out is not None:
        for kc in range(8):
            sb = k % 2; k += 1
            kb.dma("sp", R.stage[sb][:, 0:D], w_out[kc * 128:(kc + 1) * 128, :], writes=[R.stage_b[sb]])
            cast(R.Wo[:, kc, :], R.Wo_b, R.stage[sb][:, 0:D], R.stage_b[sb], None)


def norm_transpose(kb, R, x_ap, x_b, dstT, dstT_b, s):
    nc = kb.nc
    ss = R.st[:, 0:1]; rs = R.st[:, 1:2]; rstd = R.st[:, 2:3]
    kb.op("act", lambda: nc.scalar.activation(out=R.xn[:], in_=x_ap, func=AF.Square, accum_out=ss),
          reads=[x_b], writes=[R.xn_b, R.st_b])
    kb.op("act", lambda: nc.scalar.activation(out=rs, in_=ss, func=AF.Sqrt, bias=EPS, scale=1.0 / D),
          reads=[R.st_b], writes=[R.st_b])
    kb.op("dve", lambda: nc.vector.reciprocal(out=rstd, in_=rs), reads=[R.st_b], writes=[R.st_b])
    kb.op("dve", lambda: nc.vector.tensor_scalar(out=R.xn[:], in0=x_ap, scalar1=rstd, scalar2=None, op0=ALU.mult),
          reads=[x_b, R.st_b], writes=[R.xn_b])
    ti = R.ntp % 2; R.ntp += 1
    tp = R.tp[ti]; tpb = R.tp_b[ti]
    for kc in range(8):
        kb.op("pe", lambda kc=kc: nc.tensor.transpose(out=tp[:, kc, :], in_=R.xn[:, kc * 128:(kc + 1) * 128],
                                                      identity=R.ident[:]),
              reads=[R.xn_b, R.ident_b], writes=[tpb], inc=(kc == 7))
    kb.op("act", lambda: nc.scalar.copy(out=dstT[:, :, s * 128:(s + 1) * 128], in_=tp[:, :, :]),
          reads=[tpb], writes=[dstT_b])


def token_pass(kb, R, T, x_in, x_out, pre=None, post=None, in_bufs=None, out_bufs=None, pre_bufs=None, post_bufs=None):
    nc = kb.nc
    NT = T // TT

    def stage_load(i):
        bi = i % 2
        kb.dma("sp", R.xt[bi][:, :, :], x_in[i * TT:(i + 1) * TT, :].rearrange("(s p) d -> p s d", p=128),
               reads=([in_bufs[i]] if in_bufs else []), writes=[R.xt_b[bi]])
        if pre is not None:
            kb.dma("sp", R.yt[bi][:, :, :], pre[:, i * TT:(i + 1) * TT].rearrange("(c p) t -> p c t", p=128),
                   reads=([pre_bufs[i]] if pre_bufs else []), writes=[R.yt_b[bi]])

    def stage_pre(i):
        bi = i % 2
        if pre is None:
            return
        for s in range(SUB):
            for h in range(2):
                pi = R.npsd % 2; R.npsd += 1
                for kc in range(8):
                    kb.op("pe", lambda kc=kc: nc.tensor.matmul(R.psd[pi][:, :], lhsT=R.yt[bi][:, kc, s * 128:(s + 1) * 128],
                                                               rhs=R.Wo[:, kc, h * 512:(h + 1) * 512],
                                                               start=(kc == 0), stop=(kc == 7)),
                          reads=[R.yt_b[bi], R.Wo_b], writes=[R.psd_b[pi]], inc=(kc == 7))
                xs = R.xt[bi][:, s, h * 512:(h + 1) * 512]
                kb.op("dve", lambda: nc.vector.tensor_tensor(out=xs, in0=R.psd[pi][:, :], in1=xs, op=ALU.add),
                      reads=[R.psd_b[pi], R.xt_b[bi]], writes=[R.xt_b[bi]])

    def stage_a(i):
        bi = i % 2
        for s in range(SUB):
            norm_transpose(kb, R, R.xt[bi][:, s, :], R.xt_b[bi], R.xnT[bi], R.xnT_b[bi], s)

    def stage_b(i):
        bi = i % 2
        for fc in range(NFC):
            gi = fc % 2
            for (W, Wb, off) in ((R.Wg, R.Wg_b, 0), (R.Wu, R.Wu_b, 256)):
                for kc in range(8):
                    kb.op("pe", lambda kc=kc, W=W, off=off: nc.tensor.matmul(
                        R.psg[gi][:, off:off + TT], lhsT=W[:, kc, fc * 128:(fc + 1) * 128], rhs=R.xnT[bi][:, kc, :],
                        start=(kc == 0), stop=(kc == 7)),
                          reads=[Wb, R.xnT_b[bi]], writes=[R.psg_b[gi]], inc=(kc == 7))
            kb.op("act", lambda: nc.scalar.activation(out=R.sg[gi][:, :], in_=R.psg[gi][:, 0:TT], func=AF.Silu),
                  reads=[R.psg_b[gi]], writes=[R.sg_b[gi]])
            kb.op("dve", lambda: nc.vector.tensor_tensor(out=R.hid[:, fc, :], in0=R.sg[gi][:, :],
                                                         in1=R.psg[gi][:, 256:256 + TT], op=ALU.mult),
                  reads=[R.sg_b[gi], R.psg_b[gi]], writes=[R.hid_b[fc]])

    def stage_c(i):
        bi = i % 2
        for s in range(SUB):
            for h in range(2):
                pi = R.npsd % 2; R.npsd += 1
                for fc in range(NFC):
                    kb.op("pe", lambda fc=fc: nc.tensor.matmul(R.psd[pi][:, :], lhsT=R.hid[:, fc, s * 128:(s + 1) * 128],
                                                               rhs=R.Wd[:, fc, h * 512:(h + 1) * 512],
                                                               start=(fc == 0), stop=(fc == NFC - 1)),
                          reads=[R.hid_b[fc], R.Wd_b], writes=[R.psd_b[pi]], inc=(fc == NFC - 1))
                xs = R.xt[bi][:, s, h * 512:(h + 1) * 512]
                kb.op("dve", lambda: nc.vector.scalar_tensor_tensor(out=xs, in0=R.psd[pi][:, :], scalar=0.5, in1=xs,
                                                                    op0=ALU.mult, op1=ALU.add),
                      reads=[R.psd_b[pi], R.xt_b[bi]], writes=[R.xt_b[bi]])
            if post is not None:
                norm_transpose(kb, R, R.xt[bi][:, s, :], R.xt_b[bi], R.hTo, R.hTo_b, s)
        kb.dma("pool", x_out[i * TT:(i + 1) * TT, :].rearrange("(s p) d -> p s d", p=128), R.xt[bi][:, :, :],
               reads=[R.xt_b[bi]], writes=([out_bufs[i]] if out_bufs else []))
        if post is not None:
            kb.dma("pool", post[:, i * TT:(i + 1) * TT].rearrange("(c p) t -> p c t", p=128), R.hTo[:, :, :],
                   reads=[R.hTo_b], writes=([post_bufs[i]] if post_bufs else []))

    stage_load(0)
    stage_pre(0)
    stage_a(0)
    for i in range(NT):
        if i + 1 < NT:
            stage_load(i + 1)
        stage_b(i)
        if i + 1 < NT:
            stage_pre(i + 1)
            stage_a(i + 1)
        stage_c(i)


QT_ = 512
NW = 834
A_Q, A_K, A_V = 0, 64, 128
B_Q, B_K, B_V, B_O, B_I, B_F = 192, 256, 320, 384, 448, 449
C_Q, C_K, C_V = 450, 514, 578
D_Q, D_K, D_V = 642, 706, 770


class MixRes:
    def __init__(self, kb, S):
        self.S = S
        self.NB = S // 128
        self.NQ = S // QT_
        NB = self.NB
        self.Wm = kb.sbuf("Wm", [128, 8, NW], BF16); self.Wm_b = kb.buf()
        self.gm = kb.sbuf("gm", [128, 8], F32); self.gm_b = kb.buf()
        self.ident = kb.sbuf("identm", [128, 128], BF16); self.ident_b = kb.buf()
        self.ht = [kb.sbuf(f"ht{i}", [128, 8, QT_], BF16) for i in range(2)]; self.ht_b = [kb.buf() for _ in range(2)]
        self.QT = kb.sbuf("QT", [64, S], BF16); self.QT_b = kb.buf()
        self.KT = kb.sbuf("KT", [64, S], BF16); self.KT_b = kb.buf()
        self.Va = kb.sbuf("Va", [128, NB, 128], BF16); self.Va_b = kb.buf()
        self.par = kb.sbuf("par", [128, 32], F32); self.par_b = kb.buf()
        self.cst = kb.sbuf("cst", [128, 256], BF16); self.cst_b = kb.buf()
        self.sq = [kb.sbuf(f"sq{i}", [64, QT_], BF16) for i in range(2)]; self.sq_b = [kb.buf() for _ in range(2)]
        self.rr = [kb.sbuf(f"rr{i}", [64, QT_], F32) for i in range(2)]; self.rr_b = [kb.buf() for _ in range(2)]
        self.e32 = [kb.sbuf(f"e32_{i}", [128, 2, QT_], F32) for i in range(2)]; self.e32_b = [kb.buf() for _ in range(2)]
        self.wst = [self.e32[i][:, :, :].rearrange("p a b -> p (a b)")[:, 0:NW] for i in range(2)]; self.wst_b = self.e32_b
        self.e16 = [kb.sbuf(f"e16_{i}", [128, 2, QT_], BF16) for i in range(3)]; self.e16_b = [kb.buf() for _ in range(3)]
        self.sp16 = [kb.sbuf(f"sp16_{i}", [128, QT_], BF16) for i in range(4)]; self.sp16_b = [kb.buf() for _ in range(4)]
        self.spx = self.sp16; self.spx_b = self.sp16_b
        self.fin = [kb.sbuf(f"fin{i}", [64, QT_], F32) for i in range(3)]; self.fin_b = [kb.buf() for _ in range(3)]
        self.yo = [kb.sbuf(f"yo{i}", [64, QT_], BF16) for i in range(2)]; self.yo_b = [kb.buf() for _ in range(2)]
        self.mask = kb.sbuf("mask", [128, 4, QT_], BF16); self.mask_b = kb.buf()
        self.EB = kb.sbuf("EB", [128, 8, QT_], F32); self.EB_b = kb.buf()
        self.tri = kb.sbuf("tri", [128, 2, 128], BF16); self.tri_b = kb.buf()
        self.pp = [kb.psum(f"pp{i}", [128, 2, 512], F32) for i in range(4)]
        self.ps = [self.pp[i // 2][:, i % 2, :] for i in range(8)]
        self.ps_b = [kb.buf() for _ in range(8)]
        self.nyo = 0
        self.ne16 = 0


def mix_load_common(kb, R, wsel, gmix_lay, ident_d, cst_d):
    nc = kb.nc
    kb.dma("sp", R.gm[:], gmix_lay, writes=[R.gm_b])
    kb.dma("sp", R.ident[:], ident_d, writes=[R.ident_b])
    kb.dma("sp", R.cst[:], cst_d, writes=[R.cst_b])
    for kc in range(8):
        sb = kc % 2
        kb.dma("sp", R.wst[sb][:, :], wsel[kc * 128:(kc + 1) * 128, :], writes=[R.wst_b[sb]])
        kb.op("dve", lambda: nc.vector.tensor_scalar(out=R.Wm[:, kc, :], in0=R.wst[sb][:, :], scalar1=R.gm[:, kc:kc + 1],
                                                     scalar2=None, op0=ALU.mult),
              reads=[R.wst_b[sb], R.gm_b], writes=[R.Wm_b])


def load_ht(kb, R, hT, qt):
    bi = qt % 2
    if callable(hT):
        src = hT(qt).rearrange("c p t -> p c t")
    else:
        src = hT[:, qt * QT_:(qt + 1) * QT_].rearrange("(c p) t -> p c t", p=128)
    kb.dma("sp", R.ht[bi][:, :, :], src, writes=[R.ht_b[bi]])
    return R.ht[bi], R.ht_b[bi]


def proj_fm(kb, R, ht, ht_b, c0, ncol, pb):
    nc = kb.nc
    for kc in range(8):
        kb.op("pe", lambda kc=kc: nc.tensor.matmul(R.ps[pb][0:ncol, :], lhsT=R.Wm[:, kc, c0:c0 + ncol], rhs=ht[:, kc, :],
                                                   start=(kc == 0), stop=(kc == 7)),
              reads=[R.Wm_b, ht_b], writes=[R.ps_b[pb]], inc=(kc == 7))


def proj_tm(kb, R, ht, ht_b, c0, ncol, pb, s):
    nc = kb.nc
    for kc in range(8):
        kb.op("pe", lambda kc=kc: nc.tensor.matmul(R.ps[pb][:, s * 128:s * 128 + ncol], lhsT=ht[:, kc, s * 128:(s + 1) * 128],
                                                   rhs=R.Wm[:, kc, c0:c0 + ncol], start=(kc == 0), stop=(kc == 7)),
              reads=[R.Wm_b, ht_b], writes=[R.ps_b[pb]], inc=(kc == 7))


def qk_norm_store(kb, R, pb, pb2, dst, dst_b, qt, gcol, cmat, inv_n, i2):
    nc = kb.nc
    sq, sqb = R.sq[i2], R.sq_b[i2]
    rr, rrb = R.rr[i2], R.rr_b[i2]
    kb.op("act", lambda: nc.scalar.activation(out=sq[:, :], in_=R.ps[pb][0:64, :], func=AF.Square),
          reads=[R.ps_b[pb]], writes=[sqb])
    kb.op("pe", lambda: nc.tensor.matmul(R.ps[pb2][0:64, :], lhsT=cmat, rhs=sq[:, :], start=True, stop=True),
          reads=[sqb, R.cst_b], writes=[R.ps_b[pb2]])
    kb.op("act", lambda: nc.scalar.activation(out=rr[:, :], in_=R.ps[pb2][0:64, :], func=AF.Sqrt, bias=EPS, scale=inv_n),
          reads=[R.ps_b[pb2]], writes=[rrb])
    kb.op("dve", lambda: nc.vector.reciprocal(out=rr[:, :], in_=rr[:, :]), reads=[rrb], writes=[rrb])
    kb.op("dve", lambda: nc.vector.scalar_tensor_tensor(out=dst[:, qt * QT_:(qt + 1) * QT_], in0=R.ps[pb][0:64, :],
                                                        scalar=R.par[0:64, gcol:gcol + 1], in1=rr[:, :],
                                                        op0=ALU.mult, op1=ALU.mult),
          reads=[R.ps_b[pb], rrb, R.par_b], writes=[dst_b])


def v_store(kb, R, ht, ht_b, c0, qt, pb):
    nc = kb.nc
    for s in range(4):
        proj_tm(kb, R, ht, ht_b, c0, 64, pb, s)
    src = R.ps[pb][:, :].rearrange("p (s c) -> p s c", c=128)[:, :, 0:64]
    kb.op("act", lambda: nc.scalar.copy(out=R.Va[:, qt * 4:(qt + 1) * 4, 0:64], in_=src),
          reads=[R.ps_b[pb]], writes=[R.Va_b])


def ydst(yT_d, row0, qt):
    if callable(yT_d):
        return yT_d(row0, qt)
    return yT_d[row0:row0 + 64, qt * QT_:(qt + 1) * QT_]


def out_store(kb, R, yT_d, row0, qt, src_fn, reads):
    i = R.nyo % 2; R.nyo += 1
    src_fn(R.yo[i], R.yo_b[i])
    kb.dma("pool", ydst(yT_d, row0, qt), R.yo[i][:, :], reads=[R.yo_b[i]])


def mixer_c(kb, R, hT, yT_d, row0, masks_c):
    nc = kb.nc
    S, NB, NQ = R.S, R.NB, R.NQ
    kb.dma("sp", R.mask[:, :, :], masks_c[:, :, 0, :], writes=[R.mask_b])
    kb.op("pool", lambda: nc.gpsimd.memset(R.Va[:, :, 64:128], 1.0), writes=[R.Va_b])
    bd32 = R.cst[0:64, 64:128]
    ones64 = R.cst[0:64, 0:64]
    for qt in range(NQ):
        ht, htb = load_ht(kb, R, hT, qt)
        proj_fm(kb, R, ht, htb, C_Q, 64, 0)
        proj_fm(kb, R, ht, htb, C_K, 64, 2)
        qk_norm_store(kb, R, 0, 1, R.QT, R.QT_b, qt, 0, bd32, 1.0 / 32, 0)
        qk_norm_store(kb, R, 2, 3, R.KT, R.KT_b, qt, 1, bd32, 1.0 / 32, 1)
        v_store(kb, R, ht, htb, C_V, qt, 4 + (qt % 2))
    for qt in range(NQ):
        nkb = 4 * qt + 4
        O0, O1 = 6, 7
        estate = {}

        def s_step(kbk):
            pj = kbk % 3
            sb = 2 * pj
            for m in range(2):
                kb.op("pe", lambda m=m: nc.tensor.matmul(R.ps[sb + m],
                                                         lhsT=R.KT[m * 32:(m + 1) * 32, kbk * 128:(kbk + 1) * 128],
                                                         rhs=R.QT[m * 32:(m + 1) * 32, qt * QT_:(qt + 1) * QT_],
                                                         start=True, stop=True),
                      reads=[R.KT_b, R.QT_b], writes=[R.ps_b[sb + m]])
            ei = R.ne16 % 3; R.ne16 += 1
            e, eb = R.e16[ei], R.e16_b[ei]
            estate[kbk] = (e, eb)
            kb.op("act", lambda: nc.scalar.activation(out=e[:, :, :], in_=R.pp[pj][:, :, :], func=AF.Exp),
                  reads=[R.ps_b[sb], R.ps_b[sb + 1]], writes=[eb])
            r = kbk - 4 * qt
            if r >= 0:
                for m in range(2):
                    kb.op("dve", lambda m=m: nc.vector.tensor_tensor(out=e[:, m, :], in0=e[:, m, :], in1=R.mask[:, r, :], op=ALU.mult),
                          reads=[eb, R.mask_b], writes=[eb])

        def pv_step(kbk):
            e, eb = estate.pop(kbk)
            for m in range(2):
                kb.op("pe", lambda m=m: nc.tensor.matmul(R.ps[O0 + m][:, :], lhsT=R.Va[:, kbk, :], rhs=e[:, m, :],
                                                         start=(kbk == 0), stop=(kbk == nkb - 1)),
                      reads=[R.Va_b, eb], writes=[R.ps_b[O0 + m]])

        s_step(0)
        if nkb > 1:
            s_step(1)
        for kbk in range(nkb):
            if kbk + 2 < nkb:
                s_step(kbk + 2)
            pv_step(kbk)
        f0, f1, f2 = R.fin
        b0, b1, b2 = R.fin_b
        kb.op("dve", lambda: nc.vector.reciprocal(out=f0[:, :], in_=R.ps[O0][64:128, :]), reads=[R.ps_b[O0]], writes=[b0])
        kb.op("dve", lambda: nc.vector.tensor_tensor(out=f0[:, :], in0=R.ps[O0][0:64, :], in1=f0[:, :], op=ALU.mult),
              reads=[R.ps_b[O0], b0], writes=[b0])
        kb.op("dve", lambda: nc.vector.reciprocal(out=f1[:, :], in_=R.ps[O1][64:128, :]), reads=[R.ps_b[O1]], writes=[b1])
        kb.op("dve", lambda: nc.vector.tensor_tensor(out=f1[:, :], in0=R.ps[O1][0:64, :], in1=f1[:, :], op=ALU.mult),
              reads=[R.ps_b[O1], b1], writes=[b1])
        kb.op("dve", lambda: nc.vector.scalar_tensor_tensor(out=f2[:, :], in0=f1[:, :], scalar=R.par[0:64, 2:3], in1=f0[:, :],
                                                            op0=ALU.mult, op1=ALU.add),
              reads=[b0, b1, R.par_b], writes=[b2])
        kb.op("act", lambda: nc.scalar.activation(out=R.sq[0][:, :], in_=f2[:, :], func=AF.Square), reads=[b2], writes=[R.sq_b[0]])
        kb.op("pe", lambda: nc.tensor.matmul(R.ps[0][0:64, :], lhsT=ones64, rhs=R.sq[0][:, :], start=True, stop=True),
              reads=[R.sq_b[0], R.cst_b], writes=[R.ps_b[0]])
        kb.op("act", lambda: nc.scalar.activation(out=R.rr[0][:, :], in_=R.ps[0][0:64, :], func=AF.Sqrt, bias=EPS, scale=1.0 / 64),
              reads=[R.ps_b[0]], writes=[R.rr_b[0]])
        kb.op("dve", lambda: nc.vector.reciprocal(out=R.rr[0][:, :], in_=R.rr[0][:, :]), reads=[R.rr_b[0]], writes=[R.rr_b[0]])

        def fn(yo, yob):
            kb.op("dve", lambda: nc.vector.scalar_tensor_tensor(out=yo[:, :], in0=f2[:, :], scalar=R.par[0:64, 3:4], in1=R.rr[0][:, :],
                                                                op0=ALU.mult, op1=ALU.mult),
                  reads=[b2, R.rr_b[0], R.par_b], writes=[yob])
        out_store(kb, R, yT_d, row0, qt, fn, None)


def mixer_d(kb, R, hT, yT_d, row0, masks_d, tri_d):
    nc = kb.nc
    S, NB, NQ = R.S, R.NB, R.NQ
    kb.dma("sp", R.mask[:, :, :], masks_d, writes=[R.mask_b])
    kb.dma("sp", R.tri[:, :, :], tri_d, writes=[R.tri_b])
    for qt in range(NQ):
        ht, htb = load_ht(kb, R, hT, qt)
        proj_fm(kb, R, ht, htb, D_Q, 64, 0)
        proj_fm(kb, R, ht, htb, D_K, 64, 1)
        kb.op("act", lambda: nc.scalar.activation(out=R.QT[:, qt * QT_:(qt + 1) * QT_], in_=R.ps[0][0:64, :], func=AF.Copy, scale=0.125),
              reads=[R.ps_b[0]], writes=[R.QT_b])
        kb.op("dve", lambda: nc.vector.tensor_copy(out=R.KT[:, qt * QT_:(qt + 1) * QT_], in_=R.ps[1][0:64, :]),
              reads=[R.ps_b[1]], writes=[R.KT_b])
        v_store(kb, R, ht, htb, D_V, qt, 4 + (qt % 2))
    RB, OB = 4, 5
    ed_b = [kb.buf() for _ in range(4)]
    for qt in range(NQ):
        kbs = list(range(4 * qt + 3, -1, -1))
        n = len(kbs)

        def ebuf(i):
            return R.e32[(i // 2) % 2][:, i % 2, :], ed_b[i % 4]

        def spbuf(i):
            return R.spx[i % 4], R.spx_b[i % 4]

        def z_mm(i):
            kbk = kbs[i]
            zb = i % 4
            kb.op("pe", lambda: nc.tensor.matmul(R.ps[zb][:, :], lhsT=R.KT[:, kbk * 128:(kbk + 1) * 128],
                                                 rhs=R.QT[:, qt * QT_:(qt + 1) * QT_], start=True, stop=True),
                  reads=[R.KT_b, R.QT_b], writes=[R.ps_b[zb]])

        def esp(i):
            kbk = kbs[i]
            zb = i % 4
            e, eb = ebuf(i)
            sp, spb = spbuf(i)
            kb.op("act", lambda: nc.scalar.activation(out=e, in_=R.ps[zb][:, :], func=AF.Exp),
                  reads=[R.ps_b[zb]], writes=[eb])
            kb.op("act", lambda: nc.scalar.activation(out=sp[:, :], in_=e, func=AF.Ln, bias=1.0),
                  reads=[eb], writes=[spb])
            r = kbk - 4 * qt
            if r >= 0:
                kb.op("dve", lambda: nc.vector.tensor_tensor(out=sp[:, :], in0=sp[:, :], in1=R.mask[:, r, :], op=ALU.mult),
                      reads=[spb, R.mask_b], writes=[spb])

        def chain_a(i):
            sp, spb = spbuf(i)
            kb.op("pe", lambda: nc.tensor.matmul(R.ps[RB][:, :], lhsT=R.tri[:, 0, :], rhs=sp[:, :], start=(i == 0), stop=False),
                  reads=[R.tri_b, spb], writes=[R.ps_b[RB]])

        def chain_b(i):
            kbk = kbs[i]
            e, eb = ebuf(i)
            sp, spb = spbuf(i)
            tt, ttb = R.e16[i % 3], R.e16_b[i % 3]
            kb.op("act", lambda: nc.scalar.activation(out=tt[:, 0, :], in_=R.ps[RB][:, :], func=AF.Exp, scale=-1.0),
                  reads=[R.ps_b[RB]], writes=[ttb])
            kb.op("pe", lambda: nc.tensor.matmul(R.ps[RB][:, :], lhsT=R.tri[:, 1, :], rhs=sp[:, :], start=False, stop=(i == n - 1)),
                  reads=[R.tri_b, spb], writes=[R.ps_b[RB]])
            kb.op("dve", lambda: nc.vector.tensor_tensor(out=tt[:, 1, :], in0=e, in1=tt[:, 0, :], op=ALU.mult),
                  reads=[eb, ttb], writes=[ttb])
            r = kbk - 4 * qt
            if r >= 0:
                kb.op("dve", lambda: nc.vector.tensor_tensor(out=tt[:, 1, :], in0=tt[:, 1, :], in1=R.mask[:, r, :], op=ALU.mult),
                      reads=[ttb, R.mask_b], writes=[ttb])

        def pv(i):
            kbk = kbs[i]
            tt, ttb = R.e16[i % 3], R.e16_b[i % 3]
            kb.op("pe", lambda: nc.tensor.matmul(R.ps[OB][0:64, :], lhsT=R.Va[:, kbk, 0:64], rhs=tt[:, 1, :],
                                                 start=(i == 0), stop=(i == n - 1)),
                  reads=[R.Va_b, ttb], writes=[R.ps_b[OB]])

        for i in range(min(3, n)):
            z_mm(i)
        for i in range(min(2, n)):
            esp(i)
        for i in range(n):
            if i + 2 < n:
                esp(i + 2)
            chain_a(i)
            if i + 3 < n:
                z_mm(i + 3)
            chain_b(i)
            if i >= 1:
                pv(i - 1)
        pv(n - 1)

        def fn(yo, yob):
            kb.op("dve", lambda: nc.vector.tensor_copy(out=yo[:, :], in_=R.ps[OB][0:64, :]), reads=[R.ps_b[OB]], writes=[yob])
        out_store(kb, R, yT_d, row0, qt, fn, None)


def mixer_a(kb, R, hT, yT_d, row0, biasT_d):
    nc = kb.nc
    S, NB, NQ = R.S, R.NB, R.NQ
    kb.dma("sp", R.EB[:, :, :], biasT_d, writes=[R.EB_b])
    for r in range(8):
        kb.op("act", lambda r=r: nc.scalar.activation(out=R.EB[:, r, :], in_=R.EB[:, r, :], func=AF.Exp),
              reads=[R.EB_b], writes=[R.EB_b])
    kb.op("pool", lambda: nc.gpsimd.memset(R.Va[:, :, 64:128], 1.0), writes=[R.Va_b])
    ones64 = R.cst[0:64, 0:64]
    for qt in range(NQ):
        ht, htb = load_ht(kb, R, hT, qt)
        proj_fm(kb, R, ht, htb, A_Q, 64, 0)
        proj_fm(kb, R, ht, htb, A_K, 64, 2)
        qk_norm_store(kb, R, 0, 1, R.QT, R.QT_b, qt, 4, ones64, 1.0 / 64, 0)
        qk_norm_store(kb, R, 2, 3, R.KT, R.KT_b, qt, 5, ones64, 1.0 / 64, 1)
        v_store(kb, R, ht, htb, A_V, qt, 4 + (qt % 2))
    OB = 4
    for qt in range(NQ):
        rs = [r for r in range(8) if 4 * qt - 4 + r >= 0]
        for j, r in enumerate(rs):
            kbk = 4 * qt - 4 + r
            sb = j % 2
            kb.op("pe", lambda: nc.tensor.matmul(R.ps[sb][:, :], lhsT=R.KT[:, kbk * 128:(kbk + 1) * 128],
                                                 rhs=R.QT[:, qt * QT_:(qt + 1) * QT_], start=True, stop=True),
                  reads=[R.KT_b, R.QT_b], writes=[R.ps_b[sb]])
            e, eb = R.e32[j % 2], R.e32_b[j % 2]
            p, pbuf = R.e16[j % 2], R.e16_b[j % 2]
            kb.op("act", lambda: nc.scalar.activation(out=e[:, 0, :], in_=R.ps[sb][:, :], func=AF.Exp),
                  reads=[R.ps_b[sb]], writes=[eb])
            kb.op("dve", lambda: nc.vector.tensor_tensor(out=p[:, 0, :], in0=e[:, 0, :], in1=R.EB[:, r, :], op=ALU.mult),
                  reads=[eb, R.EB_b], writes=[pbuf])
            kb.op("pe", lambda: nc.tensor.matmul(R.ps[OB][:, :], lhsT=R.Va[:, kbk, :], rhs=p[:, 0, :],
                                                 start=(j == 0), stop=(j == len(rs) - 1)),
                  reads=[R.Va_b, pbuf], writes=[R.ps_b[OB]])
        f0, b0 = R.fin[0], R.fin_b[0]
        kb.op("dve", lambda: nc.vector.reciprocal(out=f0[:, :], in_=R.ps[OB][64:128, :]), reads=[R.ps_b[OB]], writes=[b0])

        def fn(yo, yob):
            kb.op("dve", lambda: nc.vector.tensor_tensor(out=yo[:, :], in0=R.ps[OB][0:64, :], in1=f0[:, :], op=ALU.mult),
                  reads=[R.ps_b[OB], b0], writes=[yob])
        out_store(kb, R, yT_d, row0, qt, fn, None)


class MixResB:
    def __init__(self, kb, R):
        NB = R.NB
        self.Osig = R.EB[:, :, :].bitcast(BF16).rearrange("p a (b c) -> p (a b) c", c=64)[:, 0:NB, :]; self.Osig_b = R.EB_b
        self.G = kb.sbuf("Gates", [128, 8, NB], F32); self.G_b = kb.buf()
        self.trif = kb.sbuf("trif", [128, 2, 128], F32); self.trif_b = kb.buf()
        self.cw = kb.sbuf("convw", [128, 8], F32); self.cw_b = kb.buf()
        self.gob = kb.sbuf("gob", [128, 64], F32); self.gob_b = kb.buf()
        self.St = [kb.sbuf(f"St{i}", [64, 65], F32) for i in range(2)]; self.St_b = [kb.buf() for _ in range(2)]
        self.Sb = [kb.sbuf(f"Sb{i}", [64, 65], BF16) for i in range(2)]; self.Sb_b = [kb.buf() for _ in range(2)]
        self.tok = [kb.sbuf(f"tok{i}", [128, 3, 64], BF16) for i in range(2)]; self.tok_b = [kb.buf() for _ in range(2)]
        self.qkT = [kb.sbuf(f"qkT{i}", [64, 2, 128], BF16) for i in range(2)]; self.qkT_b = [kb.buf() for _ in range(2)]
        self.qkm = [kb.sbuf(f"qkm{i}", [128, 128], BF16) for i in range(2)]; self.qkm_b = [kb.buf() for _ in range(2)]
        self.cm = kb.sbuf("cmask", [128, 128], F32); self.cm_b = kb.buf()
        self.hn = [kb.sbuf(f"hn{i}", [128, 64], F32) for i in range(2)]; self.hn_b = [kb.buf() for _ in range(2)]
        self.hs = [kb.sbuf(f"hs{i}", [128, 8], F32) for i in range(2)]; self.hs_b = [kb.buf() for _ in range(2)]
        self.yb = [kb.sbuf(f"yb{i}", [128, 64], BF16) for i in range(2)]; self.yb_b = [kb.buf() for _ in range(2)]
        self.jk = kb.sbuf("jk", [128, 64], BF16); self.jk_b = kb.buf()


def mixer_b(kb, R, RB_, hT, yT_d, row0, bpar_d, trif_d, cmask_d, gob_d):
    nc = kb.nc
    S, NB, NQ = R.S, R.NB, R.NQ
    B = RB_
    kb.dma("sp", B.cw[:, :], bpar_d, writes=[B.cw_b])
    kb.dma("sp", B.trif[:, :, :], trif_d, writes=[B.trif_b])
    kb.dma("sp", B.cm[:, :], cmask_d, writes=[B.cm_b])
    kb.dma("sp", B.gob[:, :], gob_d, writes=[B.gob_b])
    kb.op("pool", lambda: nc.gpsimd.memset(R.Va[:, :, 64:65], 1.0), writes=[R.Va_b])
    kb.op("dve", lambda: nc.vector.tensor_scalar(out=B.cw[:, 7:8], in0=B.cw[:, 6:7], scalar1=-1.0, scalar2=None, op0=ALU.mult),
          reads=[B.cw_b], writes=[B.cw_b])
    cv = [R.e32[i][:, :, :].rearrange("p a b -> p (a b)") for i in range(2)]
    cvb = R.e32_b
    accA = R.e16[0][:, :, :].rearrange("p a b -> p (a b)").bitcast(F32); accB_ = R.e16[1][:, :, :].rearrange("p a b -> p (a b)").bitcast(F32)
    accA_b = R.e16_b[0]; accB_b = R.e16_b[1]
    for qt in range(NQ):
        ht, htb = load_ht(kb, R, hT, qt)
        ci = qt % 2
        proj_fm(kb, R, ht, htb, B_Q, 128, 0)
        if qt == 0:
            kb.op("dve", lambda: nc.vector.memset(cv[ci][:, 0:3], 0.0), writes=[cvb[ci]])
        else:
            kb.op("dve", lambda: nc.vector.tensor_copy(out=cv[ci][:, 0:3], in_=cv[1 - ci][:, 512:515]),
                  reads=[cvb[1 - ci]], writes=[cvb[ci]])
        kb.op("act", lambda: nc.scalar.copy(out=cv[ci][:, 3:515], in_=R.ps[0][:, :]), reads=[R.ps_b[0]], writes=[cvb[ci]])
        kb.op("dve", lambda: nc.vector.tensor_scalar(out=accA, in0=cv[ci][:, 3:515], scalar1=B.cw[:, 3:4], scalar2=B.cw[:, 4:5],
                                                     op0=ALU.mult, op1=ALU.add),
              reads=[cvb[ci], B.cw_b], writes=[accA_b])
        for j in (2, 1, 0):
            kb.op("dve", lambda j=j: nc.vector.scalar_tensor_tensor(out=accA, in0=cv[ci][:, j:j + 512], scalar=B.cw[:, j:j + 1],
                                                                    in1=accA, op0=ALU.mult, op1=ALU.add),
                  reads=[cvb[ci], B.cw_b, accA_b], writes=[accA_b])
        kb.op("act", lambda: nc.scalar.activation(out=accB_, in_=accA, func=AF.Sigmoid), reads=[accA_b], writes=[accB_b])
        kb.op("dve", lambda: nc.vector.tensor_tensor(out=accB_, in0=accA, in1=accB_, op=ALU.mult), reads=[accA_b, accB_b], writes=[accB_b])
        kb.op("act", lambda: nc.scalar.copy(out=R.QT[:, qt * QT_:(qt + 1) * QT_], in_=accB_[0:64, :]), reads=[accB_b], writes=[R.QT_b])
        kb.op("act", lambda: nc.scalar.copy(out=R.KT[:, qt * QT_:(qt + 1) * QT_], in_=accB_[64:128, :]), reads=[accB_b], writes=[R.KT_b])
        pb = 4 + (qt % 2)
        for s in range(4):
            nonlocal_pb = 3 + ((qt * 4 + s) % 4)
            for kc in range(8):
                kb.op("pe", lambda kc=kc: nc.tensor.matmul(R.ps[nonlocal_pb][:, 0:130], lhsT=ht[:, kc, s * 128:(s + 1) * 128],
                                                           rhs=R.Wm[:, kc, B_V:B_V + 130], start=(kc == 0), stop=(kc == 7)),
                      reads=[R.Wm_b, htb], writes=[R.ps_b[nonlocal_pb]], inc=(kc == 7))
            blk = qt * 4 + s
            kb.op("dve", lambda: nc.vector.tensor_copy(out=R.Va[:, blk, 0:64], in_=R.ps[nonlocal_pb][:, 0:64]),
                  reads=[R.ps_b[nonlocal_pb]], writes=[R.Va_b])
            kb.op("act", lambda: nc.scalar.activation(out=B.Osig[:, blk, :], in_=R.ps[nonlocal_pb][:, 64:128], func=AF.Sigmoid),
                  reads=[R.ps_b[nonlocal_pb]], writes=[B.Osig_b])
            kb.op("dve", lambda: nc.vector.tensor_copy(out=B.G[:, 0:2, blk], in_=R.ps[nonlocal_pb][:, 128:130]),
                  reads=[R.ps_b[nonlocal_pb]], writes=[B.G_b])
    G = B.G
    kb.op("act", lambda: nc.scalar.activation(out=G[:, 2, :], in_=G[:, 1, :], func=AF.Exp, scale=-1.0, bias=B.cw[:, 7:8]),
          reads=[B.G_b, B.cw_b], writes=[B.G_b])
    kb.op("act", lambda: nc.scalar.activation(out=G[:, 2, :], in_=G[:, 2, :], func=AF.Ln, bias=1.0), reads=[B.G_b], writes=[B.G_b])
    kb.op("dve", lambda: nc.vector.tensor_scalar(out=G[:, 2, :], in0=G[:, 2, :], scalar1=-1.0, scalar2=None, op0=ALU.mult),
          reads=[B.G_b], writes=[B.G_b])
    kb.op("pe", lambda: nc.tensor.matmul(R.ps[0][:, 0:NB], lhsT=B.trif[:, 0, :], rhs=G[:, 2, :], start=True, stop=True),
          reads=[B.trif_b, B.G_b], writes=[R.ps_b[0]])
    kb.op("pe", lambda: nc.tensor.matmul(R.ps[1][:, 0:NB], lhsT=B.trif[:, 1, :], rhs=G[:, 2, :], start=True, stop=True),
          reads=[B.trif_b, B.G_b], writes=[R.ps_b[1]])
    kb.op("dve", lambda: nc.vector.tensor_copy(out=G[:, 3, :], in_=R.ps[0][:, 0:NB]), reads=[R.ps_b[0]], writes=[B.G_b])
    kb.op("act", lambda: nc.scalar.activation(out=G[:, 4, :], in_=G[:, 3, :], func=AF.Exp), reads=[B.G_b], writes=[B.G_b])
    kb.op("dve", lambda: nc.vector.tensor_tensor(out=G[:, 5, :], in0=G[:, 0, :], in1=G[:, 3, :], op=ALU.subtract),
          reads=[B.G_b], writes=[B.G_b])
    kb.op("dve", lambda: nc.vector.tensor_tensor(out=G[:, 6, :], in0=G[:, 5, :], in1=R.ps[1][:, 0:NB], op=ALU.add),
          reads=[B.G_b, R.ps_b[1]], writes=[B.G_b])
    kb.op("act", lambda: nc.scalar.activation(out=G[:, 5, :], in_=G[:, 5, :], func=AF.Exp, bias=B.cw[:, 5:6]),
          reads=[B.G_b, B.cw_b], writes=[B.G_b])
    kb.op("act", lambda: nc.scalar.activation(out=G[:, 6, :], in_=G[:, 6, :], func=AF.Exp, bias=B.cw[:, 5:6]),
          reads=[B.G_b, B.cw_b], writes=[B.G_b])
    kb.op("dve", lambda: nc.vector.tensor_scalar(out=G[:, 5:7, :], in0=G[:, 5:7, :], scalar1=0.125, scalar2=None, op0=ALU.mult),
          reads=[B.G_b], writes=[B.G_b])
    kb.op("act", lambda: nc.scalar.activation(out=G[:, 7, :], in_=R.ps[1][:, 0:NB], func=AF.Exp), reads=[R.ps_b[1]], writes=[B.G_b])
    kb.op("dve", lambda: nc.vector.memset(B.St[0][:, :], 0.0), writes=[B.St_b[0]])
    kb.op("dve", lambda: nc.vector.memset(B.Sb[0][:, :], 0.0), writes=[B.Sb_b[0]])
    PT, PT2, PS_, PO, PU = 0, 1, 2, 3, 6
    for b in range(NB):
        i2 = b % 2
        tok, tokb = B.tok[i2], B.tok_b[i2]
        qkT, qkTb = B.qkT[i2], B.qkT_b[i2]
        tpA = R.ps[0][:, :].bitcast(BF16); tpq = tpA[:, 0:128]; tpk = tpA[:, 128:256]
        kb.op("pe", lambda: nc.tensor.transpose(out=tpq[:, 0:64], in_=R.QT[:, b * 128:(b + 1) * 128], identity=R.ident[0:64, 0:64]),
              reads=[R.QT_b, R.ident_b], writes=[R.ps_b[0]])
        kb.op("pe", lambda: nc.tensor.transpose(out=tpk[:, 0:64], in_=R.KT[:, b * 128:(b + 1) * 128], identity=R.ident[0:64, 0:64]),
              reads=[R.KT_b, R.ident_b], writes=[R.ps_b[0]])
        kb.op("dve", lambda: nc.vector.tensor_scalar(out=tok[:, 0, :], in0=tpq[:, 0:64], scalar1=G[:, 4, b:b + 1], scalar2=None, op0=ALU.mult),
              reads=[R.ps_b[0], B.G_b], writes=[tokb])
        kb.op("dve", lambda: nc.vector.tensor_scalar(out=tok[:, 1, :], in0=tpk[:, 0:64], scalar1=G[:, 5, b:b + 1], scalar2=None, op0=ALU.mult),
              reads=[R.ps_b[0], B.G_b], writes=[tokb])
        kb.op("dve", lambda: nc.vector.tensor_scalar(out=tok[:, 2, :], in0=tpk[:, 0:64], scalar1=G[:, 6, b:b + 1], scalar2=None, op0=ALU.mult),
              reads=[R.ps_b[0], B.G_b], writes=[tokb])
        tpB = R.ps[1][:, :].bitcast(BF16); tq2 = tpB[:, 0:128]; tk2 = tpB[:, 128:256]
        kb.op("pe", lambda: nc.tensor.transpose(out=tq2[0:64, :], in_=tok[:, 0, :], identity=R.ident[:, :]),
              reads=[tokb, R.ident_b], writes=[R.ps_b[1]])
        kb.op("pe", lambda: nc.tensor.transpose(out=tk2[0:64, :], in_=tok[:, 1, :], identity=R.ident[:, :]),
              reads=[tokb, R.ident_b], writes=[R.ps_b[1]])
        kb.op("act", lambda: nc.scalar.copy(out=qkT[:, 0, :], in_=tq2[0:64, :]), reads=[R.ps_b[1]], writes=[qkTb])
        kb.op("act", lambda: nc.scalar.copy(out=qkT[:, 1, :], in_=tk2[0:64, :]), reads=[R.ps_b[1]], writes=[qkTb])
        kb.op("pe", lambda: nc.tensor.matmul(R.ps[PS_][:, 0:128], lhsT=qkT[:, 1, :], rhs=qkT[:, 0, :], start=True, stop=True),
              reads=[qkTb], writes=[R.ps_b[PS_]])
        qkm, qkmb = B.qkm[i2], B.qkm_b[i2]
        kb.op("dve", lambda: nc.vector.tensor_tensor(out=qkm[:, :], in0=R.ps[PS_][:, 0:128], in1=B.cm[:, :], op=ALU.mult),
              reads=[R.ps_b[PS_], B.cm_b], writes=[qkmb])
        Sp, Spb = B.Sb[i2], B.Sb_b[i2]
        po = PO + (b % 2)
        kb.op("pe", lambda: nc.tensor.matmul(R.ps[po][:, 0:65], lhsT=qkm[:, :], rhs=R.Va[:, b, 0:65], start=True, stop=False),
              reads=[qkmb, R.Va_b], writes=[R.ps_b[po]], inc=False)
        kb.op("pe", lambda: nc.tensor.matmul(R.ps[po][:, 0:65], lhsT=qkT[:, 0, :], rhs=Sp[:, :], start=False, stop=True),
              reads=[qkTb, Spb], writes=[R.ps_b[po]])
        kb.op("pe", lambda: nc.tensor.matmul(R.ps[PU][0:64, 0:65], lhsT=tok[:, 2, :], rhs=R.Va[:, b, 0:65], start=True, stop=True),
              reads=[tokb, R.Va_b], writes=[R.ps_b[PU]])
        Sn, Snb = B.St[1 - i2], B.St_b[1 - i2]
        So, Sob = B.St[i2], B.St_b[i2]
        kb.op("dve", lambda: nc.vector.scalar_tensor_tensor(out=Sn[:, :], in0=So[:, :], scalar=G[0:64, 7, b:b + 1], in1=R.ps[PU][0:64, 0:65],
                                                            op0=ALU.mult, op1=ALU.add),
              reads=[Sob, B.G_b, R.ps_b[PU]], writes=[Snb])
        kb.op("act", lambda: nc.scalar.copy(out=B.Sb[1 - i2][:, :], in_=Sn[:, :]), reads=[Snb], writes=[B.Sb_b[1 - i2]])
        hs, hsb = B.hs[i2], B.hs_b[i2]
        hn, hnb = B.hn[i2], B.hn_b[i2]
        kb.op("act", lambda: nc.scalar.activation(out=hs[:, 5:6], in_=R.ps[po][:, 64:65], func=AF.Abs),
              reads=[R.ps_b[po]], writes=[hsb])
        kb.op("dve", lambda: nc.vector.tensor_scalar(out=hs[:, 0:1], in0=hs[:, 5:6], scalar1=1.0, scalar2=None, op0=ALU.max),
              reads=[hsb], writes=[hsb])
        kb.op("dve", lambda: nc.vector.reciprocal(out=hs[:, 1:2], in_=hs[:, 0:1]), reads=[hsb], writes=[hsb])
        kb.op("dve", lambda: nc.vector.tensor_scalar(out=hn[:, :], in0=R.ps[po][:, 0:64], scalar1=hs[:, 1:2], scalar2=None, op0=ALU.mult),
              reads=[R.ps_b[po], hsb], writes=[hnb])
        kb.op("act", lambda: nc.scalar.activation(out=B.jk[:, :], in_=hn[:, :], func=AF.Square, accum_out=hs[:, 2:3]),
              reads=[hnb], writes=[B.jk_b, hsb])
        kb.op("act", lambda: nc.scalar.activation(out=hs[:, 3:4], in_=hs[:, 2:3], func=AF.Sqrt, bias=EPS, scale=1.0 / 64),
              reads=[hsb], writes=[hsb])
        kb.op("dve", lambda: nc.vector.reciprocal(out=hs[:, 4:5], in_=hs[:, 3:4]), reads=[hsb], writes=[hsb])
        kb.op("dve", lambda: nc.vector.scalar_tensor_tensor(out=hn[:, :], in0=hn[:, :], scalar=hs[:, 4:5], in1=B.gob[:, :],
                                                            op0=ALU.mult, op1=ALU.mult),
              reads=[hnb, hsb, B.gob_b], writes=[hnb])
        yb, ybb = B.yb[i2], B.yb_b[i2]
        kb.op("dve", lambda: nc.vector.tensor_tensor(out=yb[:, :], in0=hn[:, :], in1=B.Osig[:, b, :], op=ALU.mult),
              reads=[hnb, B.Osig_b], writes=[ybb])
        ty = R.ps[5][:, :].bitcast(BF16)[:, 0:128]
        kb.op("pe", lambda: nc.tensor.transpose(out=ty[0:64, :], in_=yb[:, :], identity=R.ident[:, :]),
              reads=[ybb, R.ident_b], writes=[R.ps_b[5]])
        qt = b // 4
        if b % 4 == 0:
            R.cur_yo = R.nyo % 2; R.nyo += 1
        yo, yob = R.yo[R.cur_yo], R.yo_b[R.cur_yo]
        kb.op("act", lambda: nc.scalar.copy(out=yo[:, (b % 4) * 128:(b % 4 + 1) * 128], in_=ty[0:64, :]),
              reads=[R.ps_b[5]], writes=[yob])
        if b % 4 == 3:
            kb.dma("pool", ydst(yT_d, row0, qt), yo[:, :], reads=[yob])


def mix_params(kb, R, praw_d, clam_d):
    nc = kb.nc
    pr = R.par
    kb.dma("sp", pr[0:64, 16:24], praw_d, writes=[R.par_b])
    cl = R.rr[0][:, 0:128].rearrange("p (a b) -> p a b", a=4)
    kb.dma("sp", cl, clam_d, writes=[R.rr_b[0]])
    V = nc.vector
    kb.op("dve", lambda: V.tensor_scalar(out=pr[0:64, 0:1], in0=pr[0:64, 16:17], scalar1=32 ** -0.5, scalar2=None, op0=ALU.mult), reads=[R.par_b], writes=[R.par_b])
    kb.op("dve", lambda: V.tensor_copy(out=pr[0:64, 1:2], in_=pr[0:64, 17:18]), reads=[R.par_b], writes=[R.par_b])
    kb.op("dve", lambda: V.tensor_tensor(out=pr[0:64, 3:4], in0=pr[0:64, 18:19], in1=pr[0:64, 22:23], op=ALU.mult), reads=[R.par_b], writes=[R.par_b])
    kb.op("dve", lambda: V.tensor_scalar(out=pr[0:64, 4:5], in0=pr[0:64, 19:20], scalar1=0.125, scalar2=None, op0=ALU.mult), reads=[R.par_b], writes=[R.par_b])
    kb.op("dve", lambda: V.tensor_copy(out=pr[0:64, 5:6], in_=pr[0:64, 20:21]), reads=[R.par_b], writes=[R.par_b])
    pp = R.rr[1][:, 0:64].rearrange("p (a b) -> p a b", a=2)
    kb.op("dve", lambda: V.tensor_tensor(out=pp[:, 0, :], in0=cl[:, 0, :], in1=cl[:, 1, :], op=ALU.mult), reads=[R.rr_b[0]], writes=[R.rr_b[1]])
    kb.op("dve", lambda: V.tensor_tensor(out=pp[:, 1, :], in0=cl[:, 2, :], in1=cl[:, 3, :], op=ALU.mult), reads=[R.rr_b[0]], writes=[R.rr_b[1]])
    kb.op("dve", lambda: V.reduce_sum(out=pr[0:64, 8:10], in_=pp, axis=AX.X), reads=[R.rr_b[1]], writes=[R.par_b])
    kb.op("act", lambda: nc.scalar.activation(out=pr[0:64, 10:12], in_=pr[0:64, 8:10], func=AF.Exp), reads=[R.par_b], writes=[R.par_b])
    kb.op("dve", lambda: V.tensor_tensor(out=pr[0:64, 12:13], in0=pr[0:64, 11:12], in1=pr[0:64, 10:11], op=ALU.subtract), reads=[R.par_b], writes=[R.par_b])
    kb.op("dve", lambda: V.tensor_tensor(out=pr[0:64, 2:3], in0=pr[0:64, 12:13], in1=pr[0:64, 21:22], op=ALU.subtract), reads=[R.par_b], writes=[R.par_b])

import ml_dtypes
bf16 = ml_dtypes.bfloat16
GW = 256
OFF = dict(aq=0, ak=256, av=512, bqk=768, bv=1280, bo=1536, bi=1792, bf=1796, cq=1800, ck=2056, cv=2312, dq=2568, dk=2824, dv=3080)

def sel_cols(j):
    c = []
    r = lambda o: list(range(o + j * 64, o + j * 64 + 64))
    c += r(OFF['aq']) + r(OFF['ak']) + r(OFF['av'])
    c += r(OFF['bqk']) + r(OFF['bqk'] + 256) + r(OFF['bv']) + r(OFF['bo']) + [OFF['bi'] + j, OFF['bf'] + j]
    c += r(OFF['cq']) + r(OFF['ck']) + r(OFF['cv'])
    c += r(OFF['dq']) + r(OFF['dk']) + r(OFF['dv'])
    return np.array(c)

def const_inputs():
    d = {}
    d['ident'] = np.eye(128, dtype=np.float32).astype(bf16)
    cst = np.zeros((128, 256), np.float32)
    cst[0:64, 0:64] = 1.0
    cst[0:32, 64:96] = 1.0; cst[32:64, 96:128] = 1.0
    d['cst'] = cst.astype(bf16)
    s = np.arange(128)[:, None, None, None]; r = np.arange(4)[None, :, None, None]; t = np.arange(512)[None, None, None, :]
    mc = ((2 * r + (s >= 64)) <= (t // 64)).astype(np.float32)
    d['masks_c'] = np.broadcast_to(mc, (128, 4, 2, 512)).astype(bf16).copy()
    s = np.arange(128)[:, None, None]; r = np.arange(4)[None, :, None]; t = np.arange(512)[None, None, :]
    d['masks_d'] = ((128 * r + s) < t).astype(np.float32).astype(bf16)
    j = np.arange(128)[:, None]; s2 = np.arange(128)[None, :]
    tri = np.zeros((128, 2, 128), np.float32)
    tri[:, 0, :] = (j >= s2); tri[:, 1, :] = (j < s2)
    d['tri'] = tri.astype(bf16)
    trif = np.zeros((128, 2, 128), np.float32)
    trif[:, 0, :] = (j <= s2); trif[:, 1, :] = 1.0
    d['trif'] = trif
    d['cmask'] = (j <= s2).astype(np.float32)
    return d

def bias_index():
    s = np.arange(128)[:, None, None]; r = np.arange(8)[None, :, None]; t = np.arange(512)[None, None, :]
    rel = t - s + 512 - 128 * r
    idx = np.clip(rel, -128, 128) + 128
    dd = t // 64 + 8 - 2 * r - s // 64
    vis = (dd >= 0) & (dd <= 8)
    return idx, vis

_IDX, _VIS = bias_index()

def layer_core_inputs(P, l, j, lam_init=None):
    d = {}
    d['wsel'] = np.ascontiguousarray(P['w_in'][l][:, sel_cols(j)])
    d['gmix'] = np.ascontiguousarray(P['mix_norm'][l].reshape(8, 128).T)
    praw = np.zeros((64, 8), np.float32)
    praw[:, 0] = np.tile(P['c_q_norm'][l], 2); praw[:, 1] = np.tile(P['c_k_norm'][l], 2)
    praw[:, 2] = P['c_out_norm'][l]; praw[:, 3] = P['a_q_norm'][l]; praw[:, 4] = P['a_k_norm'][l]
    if lam_init is None:
        lam_init = 0.8 - 0.6 * np.exp(-0.3 * l)
    praw[:, 5] = lam_init; praw[:, 6] = 1.0 - lam_init
    d['praw'] = praw
    d['clam'] = np.ascontiguousarray(np.broadcast_to(P['c_lambda'][l][None], (64, 4, 32))).astype(np.float32)
    rb = P['a_rel_bias'][l][j]
    d['biasT'] = np.where(_VIS, rb[_IDX], np.float32(-1e30)).astype(np.float32)
    bpar = np.zeros((128, 8), np.float32)
    ch = np.concatenate([np.arange(j * 64, j * 64 + 64), 256 + np.arange(j * 64, j * 64 + 64)])
    bpar[:, 0:4] = P['b_conv_w'][l][:, ch].T
    bpar[:, 4] = P['b_conv_b'][l][ch]
    bpar[:, 5] = P['b_gate_bias'][l][0, j]
    bpar[:, 6] = P['b_gate_bias'][l][1, j]
    d['bpar'] = bpar
    d['gob'] = np.ascontiguousarray(np.broadcast_to(P['b_out_norm'][l][j][None], (128, 64))).astype(np.float32)
    return d


from concourse.bass_utils import run_bass_kernel_spmd

SEQ = 16384
NCORE = 8
TPC = 4096
DEPTH = 2
GROUPS = [[0, 1, 2, 3], [4, 5, 6, 7]]


def _din(nc, name, shape, dt):
    return nc.dram_tensor(name, list(shape), dt, kind="ExternalInput").ap()


def _dout(nc, name, shape, dt):
    return nc.dram_tensor(name, list(shape), dt, kind="ExternalOutput").ap()


def _dint(nc, name, shape, dt):
    return nc.dram_tensor(name, list(shape), dt, kind="Internal").ap()


MIX_IN = dict(wsel=([D, NW], F32), gmix=([128, 8], F32), praw=([64, 8], F32), clam=([64, 4, 32], F32),
              biasT=([128, 8, 512], F32), bpar=([128, 8], F32), gob=([128, 64], F32))
CONST_IN = dict(ident=([128, 128], BF16), cst=([128, 256], BF16), masks_c=([128, 4, 2, 512], BF16),
                masks_d=([128, 4, 512], BF16), tri=([128, 2, 128], BF16), trif=([128, 2, 128], F32), cmask=([128, 128], F32))


def build_fused(S=SEQ, T=TPC):
    nc = bass.Bass("TRN2", target_bir_lowering=False)
    NQr = T // QT_
    x_in = _din(nc, "x_in", [T, D], F32)
    x_out = _dout(nc, "x_out", [T, D], F32)
    Cn = {k: _din(nc, k, sh, dt) for k, (sh, dt) in CONST_IN.items()}
    ffn = {}
    for l in range(DEPTH):
        for f in ("ffn1", "ffn2"):
            ffn[(f, l)] = dict(g=_din(nc, f"{f}_g{l}", [128, 8], F32), wg=_din(nc, f"{f}_wg{l}", [D, DFF], F32),
                               wu=_din(nc, f"{f}_wu{l}", [D, DFF], F32), wd=_din(nc, f"{f}_wd{l}", [DFF, D], F32))
    wo = [_din(nc, f"wo{l}", [D, D], F32) for l in range(DEPTH)]
    mx = [{k: _din(nc, f"{k}{l}", sh, dt) for k, (sh, dt) in MIX_IN.items()} for l in range(DEPTH)]
    xa = _dint(nc, "xa", [T, D], F32); xb = _dint(nc, "xb", [T, D], F32); xc = _dint(nc, "xc", [T, D], F32)
    hT_loc = _dint(nc, "hT_loc", [D, T], BF16)
    hT_all = _dint(nc, "hT_all", [4 * D, T], BF16)
    yT_loc = _dint(nc, "yT_loc", [D, T], BF16)
    yT_all = _dint(nc, "yT_all", [4 * D, T], BF16)
    yT_mine = _dint(nc, "yT_mine", [D, T], BF16)

    hv = hT_all.rearrange("(k r p) t -> k r p t", k=8, r=4)

    def hT_src(qt):
        r, o = qt // NQr, (qt % NQr) * QT_
        return hv[:, r, :, o:o + QT_]

    def y_dst(row0, qt):
        q, o = qt // NQr, (qt % NQr) * QT_
        return yT_loc[q * 256 + row0:q * 256 + row0 + 64, o:o + QT_]

    with ExitStack() as st:
        kb = KB(nc, st)
        pid = nc.sync.partition_id()
        qv = pid % 4

        ymine_b = kb.buf()

        def fetch_mine():
            yv2 = yT_all.rearrange("(q h j p) t -> q h j p t", q=4, h=2, j=4)
            for j in range(4):
                for h in range(2):
                    kb.dma("sp", yT_mine[j * 256 + h * 128:j * 256 + (h + 1) * 128, :],
                           yv2[bass.ds(qv, 1), h, j, :, :].rearrange("o p t -> (o p) t"), writes=[ymine_b])

        def tok_phase(passes):
            with ExitStack() as mem:
                kb.mem = mem
                R = TokRes(kb, any(p.get("wo") is not None for p in passes))
                load_consts(kb, R, Cn["ident"])
                prev_bufs = None
                for k, p in enumerate(passes):
                    w = p["ffn"]
                    load_ffn_weights(kb, R, w["g"], w["wg"], w["wu"], w["wd"], p.get("wo"))
                    ob = [kb.buf() for _ in range(T // TT)] if k + 1 < len(passes) else None
                    has_pre = p.get("wo") is not None
                    token_pass(kb, R, T, p["xi"], p["xo"], pre=(yT_mine if has_pre else None),
                               post=p.get("post"), in_bufs=prev_bufs, out_bufs=ob,
                               pre_bufs=([ymine_b] * (T // TT) if has_pre else None))
                    prev_bufs = ob
                kb.barrier()
            kb.mem = st

        def mix_phase(l):
            with ExitStack() as mem:
                kb.mem = mem
                R = MixRes(kb, S)
                RB = MixResB(kb, R)
                m = mx[l]
                mix_load_common(kb, R, m["wsel"], m["gmix"], Cn["ident"], Cn["cst"])
                mix_params(kb, R, m["praw"], m["clam"])
                mixer_a(kb, R, hT_src, y_dst, 0, m["biasT"])
                kb.barrier()
                mixer_b(kb, R, RB, hT_src, y_dst, 64, m["bpar"], Cn["trif"], Cn["cmask"], m["gob"])
                kb.barrier()
                mixer_c(kb, R, hT_src, y_dst, 128, Cn["masks_c"])
                kb.barrier()
                mixer_d(kb, R, hT_src, y_dst, 192, Cn["masks_d"], Cn["tri"])
                kb.barrier()
            kb.mem = st

        tok_phase([dict(ffn=ffn[("ffn1", 0)], xi=x_in, xo=xa, post=hT_loc)])
        kb.allgather(hT_loc, hT_all, GROUPS)
        mix_phase(0)
        kb.allgather(yT_loc, yT_all, GROUPS)
        fetch_mine()
        tok_phase([dict(ffn=ffn[("ffn2", 0)], wo=wo[0], xi=xa, xo=xb),
                   dict(ffn=ffn[("ffn1", 1)], xi=xb, xo=xc, post=hT_loc)])
        kb.allgather(hT_loc, hT_all, GROUPS)
        mix_phase(1)
        kb.allgather(yT_loc, yT_all, GROUPS)
        fetch_mine()
        tok_phase([dict(ffn=ffn[("ffn2", 1)], wo=wo[1], xi=xc, xo=x_out)])
        kb.finish()
    return nc


def build_mixer_prog(S=SEQ):
    nc = bass.Bass("TRN2", target_bir_lowering=False)
    hT = _din(nc, "hT", [D, S], BF16)
    Cn = {k: _din(nc, k, sh, dt) for k, (sh, dt) in CONST_IN.items()}
    m = {k: _din(nc, k, sh, dt) for k, (sh, dt) in MIX_IN.items()}
    yT = _dout(nc, "yT", [256, S], BF16)
    with ExitStack() as st:
        kb = KB(nc, st)
        R = MixRes(kb, S)
        RB = MixResB(kb, R)
        mix_load_common(kb, R, m["wsel"], m["gmix"], Cn["ident"], Cn["cst"])
        mix_params(kb, R, m["praw"], m["clam"])
        mixer_a(kb, R, hT, yT, 0, m["biasT"])
        kb.barrier()
        mixer_b(kb, R, RB, hT, yT, 64, m["bpar"], Cn["trif"], Cn["cmask"], m["gob"])
        kb.barrier()
        mixer_c(kb, R, hT, yT, 128, Cn["masks_c"])
        kb.barrier()
        mixer_d(kb, R, hT, yT, 192, Cn["masks_d"], Cn["tri"])
        kb.finish()
    return nc


def _lay(g):
    return np.ascontiguousarray(np.asarray(g, np.float32).reshape(8, 128).T)


def _wo_perm(w_out):
    idx = np.arange(1024).reshape(4, 4, 64)
    perm = idx.transpose(1, 0, 2).reshape(-1)
    return np.ascontiguousarray(w_out[perm, :])


def make_in_maps(P, TPC=TPC):
    x = np.ascontiguousarray(P["x"], dtype=np.float32).reshape(-1, D)
    C = const_inputs()
    shared = dict(C)
    for l in range(DEPTH):
        for f in ("ffn1", "ffn2"):
            shared[f"{f}_g{l}"] = _lay(P[f + "_norm"][l])
            shared[f"{f}_wg{l}"] = np.ascontiguousarray(P[f + "_wg"][l], dtype=np.float32)
            shared[f"{f}_wu{l}"] = np.ascontiguousarray(P[f + "_wu"][l], dtype=np.float32)
            shared[f"{f}_wd{l}"] = np.ascontiguousarray(P[f + "_wd"][l], dtype=np.float32)
        shared[f"wo{l}"] = _wo_perm(np.asarray(P["w_out"][l], np.float32))
    ims = []
    for c in range(NCORE):
        d = dict(shared)
        d["x_in"] = x[c * TPC:(c + 1) * TPC]
        j = c % 4
        for l in range(DEPTH):
            for k, v in layer_core_inputs(P, l, j).items():
                d[f"{k}{l}"] = v
        ims.append(d)
    return ims


def kernel(**inputs):
    P = {k: np.asarray(v) for k, v in inputs.items()}
    nc = build_fused()
    ims = make_in_maps(P)
    res = run_bass_kernel_spmd(nc, ims, core_ids=list(range(NCORE)))
    out = np.concatenate([r["x_out"] for r in res.results], axis=0).reshape(2, SEQ, D).astype(np.float32)
    return out
```

```python
import numpy as np
from contextlib import ExitStack
import concourse.bass as bass
import concourse.mybir as mybir

F32 = mybir.dt.float32
BF16 = mybir.dt.bfloat16
AF = mybir.ActivationFunctionType
ALU = mybir.AluOpType
AX = mybir.AxisListType

EPOCH = 4096


class Buf:
    __slots__ = ("w", "r", "name")

    def __init__(self, name=""):
        self.w = None
        self.r = {}
        self.name = name


class KB:
    def __init__(self, nc, stack):
        self.nc = nc
        self.st = stack
        self.E = {"pe": nc.tensor, "act": nc.scalar, "dve": nc.vector, "pool": nc.gpsimd, "sp": nc.sync}
        self.cnt = {e: 0 for e in self.E}
        self.sems = {e: [] for e in self.E}
        self.waited = {e: {} for e in self.E}
        self.ndma = 12
        self.dma_sems = {}
        self.dma_cnt = {}
        self.dma_rr = {}
        self.nsem = 0
        self.uid = 0
        self.mem = stack

    def sem(self, name):
        self.nsem += 1
        return self.st.enter_context(self.nc.semaphore(name))

    def sbuf(self, name, shape, dt):
        self.uid += 1
        return self.mem.enter_context(self.nc.sbuf_tensor(f"sb{self.uid}_" + name, list(shape), dt))

    def psum(self, name, shape, dt):
        self.uid += 1
        return self.mem.enter_context(self.nc.psum_tensor(f"ps{self.uid}_" + name, list(shape), dt))

    def buf(self, name=""):
        return Buf(name)

    def _esem(self, e, n):
        ep = (n - 1) // EPOCH
        while len(self.sems[e]) <= ep:
            self.sems[e].append(self.sem(f"c_{e}_{len(self.sems[e])}"))
        return self.sems[e][ep], (n - 1) % EPOCH + 1

    def _wait(self, e, ev):
        if ev[0] == "e":
            _, src, n = ev
            if src == e and e == "pe":
                return
            key = ("e", src)
            if self.waited[e].get(key, 0) >= n:
                return
            if src == e and n > self.cnt[e]:
                raise RuntimeError("self-wait on future event")
            s, v = self._esem(src, n)
            self.E[e].wait_ge(s, v)
            self.waited[e][key] = n
        else:
            _, q, i, k = ev
            key = ("d", q, i)
            if self.waited[e].get(key, 0) >= k:
                return
            self.E[e].wait_ge(self.dma_sems[q][i], 16 * k)
            self.waited[e][key] = k

    @staticmethod
    def _evkey(ev):
        return (ev[0], ev[1]) if ev[0] == "e" else (ev[0], ev[1], ev[2])

    def _collect(self, reads, writes):
        deps = []
        for b in reads:
            if b.w is not None:
                deps.append(b.w)
        for b in writes:
            if b.w is not None:
                deps.append(b.w)
            deps.extend(b.r.values())
        return deps

    def _record(self, ev, reads, writes):
        k = self._evkey(ev)
        for b in reads:
            b.r[k] = ev
        for b in writes:
            b.w = ev
            b.r = {}

    def op(self, e, fn, reads=(), writes=(), inc=True):
        for ev in self._collect(reads, writes):
            self._wait(e, ev)
        ins = fn()
        if inc:
            self.cnt[e] += 1
            s, v = self._esem(e, self.cnt[e])
            ins.then_inc(s, 1)
            ev = ("e", e, self.cnt[e])
        else:
            ev = ("e", e, self.cnt[e] + 1)
        self._record(ev, reads, writes)
        return ins

    def dma(self, q, out, in_, reads=(), writes=(), **kw):
        for ev in self._collect(reads, writes):
            self._wait(q, ev)
        if q not in self.dma_sems:
            self.dma_sems[q] = [self.sem(f"d_{q}_{i}") for i in range(self.ndma)]
            self.dma_cnt[q] = [0] * self.ndma
            self.dma_rr[q] = 0
        i = self.dma_rr[q]
        self.dma_rr[q] = (i + 1) % self.ndma
        if self.dma_cnt[q][i] > 0:
            self._wait(q, ("d", q, i, self.dma_cnt[q][i]))
        self.dma_cnt[q][i] += 1
        ins = self.E[q].dma_start(out=out, in_=in_, **kw)
        ins.then_inc(self.dma_sems[q][i], 16)
        ev = ("d", q, i, self.dma_cnt[q][i])
        self._record(ev, reads, writes)
        return ins

    def barrier(self, extra_sems=()):
        for e in self.E:
            for q in self.dma_sems:
                for i in range(self.ndma):
                    if self.dma_cnt[q][i] > 0:
                        self._wait(e, ("d", q, i, self.dma_cnt[q][i]))
            for src in ("pe", "act", "dve", "pool"):
                if self.cnt[src] > 0 and not (src == e and e == "pe"):
                    self._wait(e, ("e", src, self.cnt[src]))
            for (sm, v) in extra_sems:
                self.E[e].wait_ge(sm, v)

    def allgather(self, src2d, dst2d, groups, chunk_rows=128):
        self.barrier()
        R_ = src2d.shape[0]
        nk = R_ // chunk_rows
        ng = len(groups[0])
        sms = []
        for k in range(nk):
            sm = self.sem(f"cc{self.nsem}")
            self.nc.gpsimd.collective_compute("AllGather", ALU.bypass, replica_groups=groups,
                                              ins=[src2d[k * chunk_rows:(k + 1) * chunk_rows, :]],
                                              outs=[dst2d[k * ng * chunk_rows:(k + 1) * ng * chunk_rows, :]]).then_inc(sm, 1)
            sms.append(sm)
        for e in self.E:
            for sm in sms:
                self.E[e].wait_ge(sm, 1)

    def finish(self):
        for q in self.dma_sems:
            for i in range(self.ndma):
                if self.dma_cnt[q][i] > 0:
                    self._wait("sp", ("d", q, i, self.dma_cnt[q][i]))
        for e in ("pe", "act", "dve", "pool"):
            if self.cnt[e] > 0:
                self._wait("sp", ("e", e, self.cnt[e]))


D = 1024
DFF = 2816
NFC = DFF // 128
TT = 256
SUB = TT // 128
EPS = 1e-6


class TokRes:
    def __init__(self, kb, with_pre):
        nc = kb.nc
        self.kb = kb
        self.Wg = kb.sbuf("Wg", [128, 8, DFF], BF16); self.Wg_b = kb.buf()
        self.Wu = kb.sbuf("Wu", [128, 8, DFF], BF16); self.Wu_b = kb.buf()
        self.Wd = kb.sbuf("Wd", [128, NFC, D], BF16); self.Wd_b = kb.buf()
        self.stage = [kb.sbuf(f"stage{i}", [128, 1024], F32) for i in range(2)]
        self.stage_b = [kb.buf() for _ in range(2)]
        self.gt = kb.sbuf("gt", [128, 8], F32); self.gt_b = kb.buf()
        self.ident = kb.sbuf("ident", [128, 128], BF16); self.ident_b = kb.buf()
        self.xt = [kb.sbuf(f"xt{i}", [128, SUB, D], F32) for i in range(2)]
        self.xt_b = [kb.buf() for _ in range(2)]
        self.xn = kb.sbuf("xn", [128, D], BF16); self.xn_b = kb.buf()
        self.st = kb.sbuf("stat", [128, 8], F32); self.st_b = kb.buf()
        self.xnT = [kb.sbuf(f"xnT{i}", [128, 8, TT], BF16) for i in range(2)]
        self.xnT_b = [kb.buf() for _ in range(2)]
        self.hid = kb.sbuf("hid", [128, NFC, TT], BF16)
        self.hid_b = [kb.buf() for _ in range(NFC)]
        self.sg = [kb.sbuf(f"sg{i}", [128, TT], F32) for i in range(2)]
        self.sg_b = [kb.buf() for _ in range(2)]
        self.hTo = kb.sbuf("hTo", [128, 8, TT], BF16); self.hTo_b = kb.buf()
        self.with_pre = with_pre
        if with_pre:
            self.Wo = kb.sbuf("Wo", [128, 8, D], BF16); self.Wo_b = kb.buf()
            self.yt = [kb.sbuf(f"yt{i}", [128, 8, TT], BF16) for i in range(2)]
            self.yt_b = [kb.buf() for _ in range(2)]
        self.psg = [kb.psum(f"psg{i}", [128, 512], F32) for i in range(2)]
        self.psg_b = [kb.buf() for _ in range(2)]
        self.psd = [kb.psum(f"psd{i}", [128, 512], F32) for i in range(2)]
        self.psd_b = [kb.buf() for _ in range(2)]
        self.tp = [kb.psum(f"tp{i}", [128, 8, 128], BF16) for i in range(2)]
        self.tp_b = [kb.buf() for _ in range(2)]
        self.ntp = 0
        self.npsd = 0
        self.ncast = 0


def load_consts(kb, R, ident_d):
    kb.dma("sp", R.ident[:], ident_d, writes=[R.ident_b])


def load_ffn_weights(kb, R, g_lay, wg, wu, wd, w_out=None):
    nc = kb.nc
    kb.dma("sp", R.gt[:], g_lay, writes=[R.gt_b])

    def cast(dst_ap, dst_b, src_ap, src_b, scal):
        e = ("dve", "act", "dve", "act", "pool")[R.ncast % 5]
        R.ncast += 1
        E = kb.E[e]
        if e == "act":
            if scal is None:
                kb.op(e, lambda: E.copy(out=dst_ap, in_=src_ap), reads=[src_b], writes=[dst_b])
            else:
                kb.op(e, lambda: E.activation(out=dst_ap, in_=src_ap, func=AF.Copy, scale=scal),
                      reads=[src_b, R.gt_b], writes=[dst_b])
        elif scal is None:
            kb.op(e, lambda: E.tensor_copy(out=dst_ap, in_=src_ap), reads=[src_b], writes=[dst_b])
        else:
            kb.op(e, lambda: E.tensor_scalar(out=dst_ap, in0=src_ap, scalar1=scal, scalar2=None, op0=ALU.mult),
                  reads=[src_b, R.gt_b], writes=[dst_b])

    k = 0
    for (W, Wb, src) in ((R.Wg, R.Wg_b, wg), (R.Wu, R.Wu_b, wu)):
        for kc in range(8):
            for (c0, c1) in ((0, 1024), (1024, 2048), (2048, DFF)):
                sb = k % 2; k += 1
                kb.dma("sp", R.stage[sb][:, 0:c1 - c0], src[kc * 128:(kc + 1) * 128, c0:c1],
                       writes=[R.stage_b[sb]])
                cast(W[:, kc, c0:c1], Wb, R.stage[sb][:, 0:c1 - c0], R.stage_b[sb], R.gt[:, kc:kc + 1])
    for fc in range(NFC):
        sb = k % 2; k += 1
        kb.dma("sp", R.stage[sb][:, 0:D], wd[fc * 128:(fc + 1) * 128, :], writes=[R.stage_b[sb]])
        cast(R.Wd[:, fc, :], R.Wd_b, R.stage[sb][:, 0:D], R.stage_b[sb], None)
    if w_out is not None:
        for kc in range(8):
            sb = k % 2; k += 1
            kb.dma("sp", R.stage[sb][:, 0:D], w_out[kc * 128:(kc + 1) * 128, :], writes=[R.stage_b[sb]])
            cast(R.Wo[:, kc, :], R.Wo_b, R.stage[sb][:, 0:D], R.stage_b[sb], None)


def norm_transpose(kb, R, x_ap, x_b, dstT, dstT_b, s):
    nc = kb.nc
    ss = R.st[:, 0:1]; rs = R.st[:, 1:2]; rstd = R.st[:, 2:3]
    kb.op("act", lambda: nc.scalar.activation(out=R.xn[:], in_=x_ap, func=AF.Square, accum_out=ss),
          reads=[x_b], writes=[R.xn_b, R.st_b])
    kb.op("act", lambda: nc.scalar.activation(out=rs, in_=ss, func=AF.Sqrt, bias=EPS, scale=1.0 / D),
          reads=[R.st_b], writes=[R.st_b])
    kb.op("dve", lambda: nc.vector.reciprocal(out=rstd, in_=rs), reads=[R.st_b], writes=[R.st_b])
    kb.op("dve", lambda: nc.vector.tensor_scalar(out=R.xn[:], in0=x_ap, scalar1=rstd, scalar2=None, op0=ALU.mult),
          reads=[x_b, R.st_b], writes=[R.xn_b])
    ti = R.ntp % 2; R.ntp += 1
    tp = R.tp[ti]; tpb = R.tp_b[ti]
    for kc in range(8):
        kb.op("pe", lambda kc=kc: nc.tensor.transpose(out=tp[:, kc, :], in_=R.xn[:, kc * 128:(kc + 1) * 128],
                                                      identity=R.ident[:]),
              reads=[R.xn_b, R.ident_b], writes=[tpb], inc=(kc == 7))
    kb.op("act", lambda: nc.scalar.copy(out=dstT[:, :, s * 128:(s + 1) * 128], in_=tp[:, :, :]),
          reads=[tpb], writes=[dstT_b])


def token_pass(kb, R, T, x_in, x_out, pre=None, post=None, in_bufs=None, out_bufs=None, pre_bufs=None, post_bufs=None):
    nc = kb.nc
    NT = T // TT

    def stage_load(i):
        bi = i % 2
        kb.dma("sp", R.xt[bi][:, :, :], x_in[i * TT:(i + 1) * TT, :].rearrange("(s p) d -> p s d", p=128),
               reads=([in_bufs[i]] if in_bufs else []), writes=[R.xt_b[bi]])
        if pre is not None:
            kb.dma("sp", R.yt[bi][:, :, :], pre[:, i * TT:(i + 1) * TT].rearrange("(c p) t -> p c t", p=128),
                   reads=([pre_bufs[i]] if pre_bufs else []), writes=[R.yt_b[bi]])

    def stage_pre(i):
        bi = i % 2
        if pre is None:
            return
        for s in range(SUB):
            for h in range(2):
                pi = R.npsd % 2; R.npsd += 1
                for kc in range(8):
                    kb.op("pe", lambda kc=kc: nc.tensor.matmul(R.psd[pi][:, :], lhsT=R.yt[bi][:, kc, s * 128:(s + 1) * 128],
                                                               rhs=R.Wo[:, kc, h * 512:(h + 1) * 512],
                                                               start=(kc == 0), stop=(kc == 7)),
                          reads=[R.yt_b[bi], R.Wo_b], writes=[R.psd_b[pi]], inc=(kc == 7))
                xs = R.xt[bi][:, s, h * 512:(h + 1) * 512]
                kb.op("dve", lambda: nc.vector.tensor_tensor(out=xs, in0=R.psd[pi][:, :], in1=xs, op=ALU.add),
                      reads=[R.psd_b[pi], R.xt_b[bi]], writes=[R.xt_b[bi]])

    def stage_a(i):
        bi = i % 2
        for s in range(SUB):
            norm_transpose(kb, R, R.xt[bi][:, s, :], R.xt_b[bi], R.xnT[bi], R.xnT_b[bi], s)

    def stage_b(i):
        bi = i % 2
        for fc in range(NFC):
            gi = fc % 2
            for (W, Wb, off) in ((R.Wg, R.Wg_b, 0), (R.Wu, R.Wu_b, 256)):
                for kc in range(8):
                    kb.op("pe", lambda kc=kc, W=W, off=off: nc.tensor.matmul(
                        R.psg[gi][:, off:off + TT], lhsT=W[:, kc, fc * 128:(fc + 1) * 128], rhs=R.xnT[bi][:, kc, :],
                        start=(kc == 0), stop=(kc == 7)),
                          reads=[Wb, R.xnT_b[bi]], writes=[R.psg_b[gi]], inc=(kc == 7))
            kb.op("act", lambda: nc.scalar.activation(out=R.sg[gi][:, :], in_=R.psg[gi][:, 0:TT], func=AF.Silu),
                  reads=[R.psg_b[gi]], writes=[R.sg_b[gi]])
            kb.op("dve", lambda: nc.vector.tensor_tensor(out=R.hid[:, fc, :], in0=R.sg[gi][:, :],
                                                         in1=R.psg[gi][:, 256:256 + TT], op=ALU.mult),
                  reads=[R.sg_b[gi], R.psg_b[gi]], writes=[R.hid_b[fc]])

    def stage_c(i):
        bi = i % 2
        for s in range(SUB):
            for h in range(2):
                pi = R.npsd % 2; R.npsd += 1
                for fc in range(NFC):
                    kb.op("pe", lambda fc=fc: nc.tensor.matmul(R.psd[pi][:, :], lhsT=R.hid[:, fc, s * 128:(s + 1) * 128],
                                                               rhs=R.Wd[:, fc, h * 512:(h + 1) * 512],
                                                               start=(fc == 0), stop=(fc == NFC - 1)),
                          reads=[R.hid_b[fc], R.Wd_b], writes=[R.psd_b[pi]], inc=(fc == NFC - 1))
                xs = R.xt[bi][:, s, h * 512:(h + 1) * 512]
                kb.op("dve", lambda: nc.vector.scalar_tensor_tensor(out=xs, in0=R.psd[pi][:, :], scalar=0.5, in1=xs,
                                                                    op0=ALU.mult, op1=ALU.add),
                      reads=[R.psd_b[pi], R.xt_b[bi]], writes=[R.xt_b[bi]])
            if post is not None:
                norm_transpose(kb, R, R.xt[bi][:, s, :], R.xt_b[bi], R.hTo, R.hTo_b, s)
        kb.dma("pool", x_out[i * TT:(i + 1) * TT, :].rearrange("(s p) d -> p s d", p=128), R.xt[bi][:, :, :],
               reads=[R.xt_b[bi]], writes=([out_bufs[i]] if out_bufs else []))
        if post is not None:
            kb.dma("pool", post[:, i * TT:(i + 1) * TT].rearrange("(c p) t -> p c t", p=128), R.hTo[:, :, :],
                   reads=[R.hTo_b], writes=([post_bufs[i]] if post_bufs else []))

    stage_load(0)
    stage_pre(0)
    stage_a(0)
    for i in range(NT):
        if i + 1 < NT:
            stage_load(i + 1)
        stage_b(i)
        if i + 1 < NT:
            stage_pre(i + 1)
            stage_a(i + 1)
        stage_c(i)


QT_ = 512
NW = 834
A_Q, A_K, A_V = 0, 64, 128
B_Q, B_K, B_V, B_O, B_I, B_F = 192, 256, 320, 384, 448, 449
C_Q, C_K, C_V = 450, 514, 578
D_Q, D_K, D_V = 642, 706, 770


class MixRes:
    def __init__(self, kb, S):
        self.S = S
        self.NB = S // 128
        self.NQ = S // QT_
        NB = self.NB
        self.Wm = kb.sbuf("Wm", [128, 8, NW], BF16); self.Wm_b = kb.buf()
        self.gm = kb.sbuf("gm", [128, 8], F32); self.gm_b = kb.buf()
        self.ident = kb.sbuf("identm", [128, 128], BF16); self.ident_b = kb.buf()
        self.ht = [kb.sbuf(f"ht{i}", [128, 8, QT_], BF16) for i in range(2)]; self.ht_b = [kb.buf() for _ in range(2)]
        self.QT = kb.sbuf("QT", [64, S], BF16); self.QT_b = kb.buf()
        self.KT = kb.sbuf("KT", [64, S], BF16); self.KT_b = kb.buf()
        self.Va = kb.sbuf("Va", [128, NB, 128], BF16); self.Va_b = kb.buf()
        self.par = kb.sbuf("par", [128, 32], F32); self.par_b = kb.buf()
        self.cst = kb.sbuf("cst", [128, 256], BF16); self.cst_b = kb.buf()
        self.sq = [kb.sbuf(f"sq{i}", [64, QT_], BF16) for i in range(2)]; self.sq_b = [kb.buf() for _ in range(2)]
        self.rr = [kb.sbuf(f"rr{i}", [64, QT_], F32) for i in range(2)]; self.rr_b = [kb.buf() for _ in range(2)]
        self.e32 = [kb.sbuf(f"e32_{i}", [128, 2, QT_], F32) for i in range(2)]; self.e32_b = [kb.buf() for _ in range(2)]
        self.wst = [self.e32[i][:, :, :].rearrange("p a b -> p (a b)")[:, 0:NW] for i in range(2)]; self.wst_b = self.e32_b
        self.e16 = [kb.sbuf(f"e16_{i}", [128, 2, QT_], BF16) for i in range(4)]; self.e16_b = [kb.buf() for _ in range(4)]
        self.spp = [kb.sbuf(f"spp_{i}", [128, 2, QT_], BF16) for i in range(2)]; self.spp_b = [kb.buf() for _ in range(2)]
        self.fin = [kb.sbuf(f"fin{i}", [64, QT_], F32) for i in range(3)]; self.fin_b = [kb.buf() for _ in range(3)]
        self.yo = [kb.sbuf(f"yo{i}", [64, QT_], BF16) for i in range(2)]; self.yo_b = [kb.buf() for _ in range(2)]
        self.mask = kb.sbuf("mask", [128, 4, QT_], BF16); self.mask_b = kb.buf()
        self.EB = kb.sbuf("EB", [128, 8, QT_], F32); self.EB_b = kb.buf()
        self.tri = kb.sbuf("tri", [128, 3, 128], BF16); self.tri_b = kb.buf()
        self.pp = [kb.psum(f"pp{i}", [128, 2, 512], F32) for i in range(4)]
        self.ps = [self.pp[i // 2][:, i % 2, :] for i in range(8)]
        self.ps_b = [kb.buf() for _ in range(8)]
        self.nyo = 0
        self.ne16 = 0


def mix_load_common(kb, R, wsel, gmix_lay, ident_d, cst_d):
    nc = kb.nc
    kb.dma("sp", R.gm[:], gmix_lay, writes=[R.gm_b])
    kb.dma("sp", R.ident[:], ident_d, writes=[R.ident_b])
    kb.dma("sp", R.cst[:], cst_d, writes=[R.cst_b])
    for kc in range(8):
        sb = kc % 2
        kb.dma("sp", R.wst[sb][:, :], wsel[kc * 128:(kc + 1) * 128, :], writes=[R.wst_b[sb]])
        kb.op("dve", lambda: nc.vector.tensor_scalar(out=R.Wm[:, kc, :], in0=R.wst[sb][:, :], scalar1=R.gm[:, kc:kc + 1],
                                                     scalar2=None, op0=ALU.mult),
              reads=[R.wst_b[sb], R.gm_b], writes=[R.Wm_b])


def load_ht(kb, R, hT, qt):
    bi = qt % 2
    if callable(hT):
        src = hT(qt).rearrange("c p t -> p c t")
    else:
        src = hT[:, qt * QT_:(qt + 1) * QT_].rearrange("(c p) t -> p c t", p=128)
    kb.dma("sp", R.ht[bi][:, :, :], src, writes=[R.ht_b[bi]])
    return R.ht[bi], R.ht_b[bi]


def proj_fm(kb, R, ht, ht_b, c0, ncol, pb):
    nc = kb.nc
    for kc in range(8):
        kb.op("pe", lambda kc=kc: nc.tensor.matmul(R.ps[pb][0:ncol, :], lhsT=R.Wm[:, kc, c0:c0 + ncol], rhs=ht[:, kc, :],
                                                   start=(kc == 0), stop=(kc == 7)),
              reads=[R.Wm_b, ht_b], writes=[R.ps_b[pb]], inc=(kc == 7))


def proj_tm(kb, R, ht, ht_b, c0, ncol, pb, s):
    nc = kb.nc
    for kc in range(8):
        kb.op("pe", lambda kc=kc: nc.tensor.matmul(R.ps[pb][:, s * 128:s * 128 + ncol], lhsT=ht[:, kc, s * 128:(s + 1) * 128],
                                                   rhs=R.Wm[:, kc, c0:c0 + ncol], start=(kc == 0), stop=(kc == 7)),
              reads=[R.Wm_b, ht_b], writes=[R.ps_b[pb]], inc=(kc == 7))


def qk_norm_store(kb, R, pb, pb2, dst, dst_b, qt, gcol, cmat, inv_n, i2):
    nc = kb.nc
    sq, sqb = R.sq[i2], R.sq_b[i2]
    rr, rrb = R.rr[i2], R.rr_b[i2]
    kb.op("act", lambda: nc.scalar.activation(out=sq[:, :], in_=R.ps[pb][0:64, :], func=AF.Square),
          reads=[R.ps_b[pb]], writes=[sqb])
    kb.op("pe", lambda: nc.tensor.matmul(R.ps[pb2][0:64, :], lhsT=cmat, rhs=sq[:, :], start=True, stop=True),
          reads=[sqb, R.cst_b], writes=[R.ps_b[pb2]])
    kb.op("act", lambda: nc.scalar.activation(out=rr[:, :], in_=R.ps[pb2][0:64, :], func=AF.Sqrt, bias=EPS, scale=inv_n),
          reads=[R.ps_b[pb2]], writes=[rrb])
    kb.op("dve", lambda: nc.vector.reciprocal(out=rr[:, :], in_=rr[:, :]), reads=[rrb], writes=[rrb])
    kb.op("dve", lambda: nc.vector.scalar_tensor_tensor(out=dst[:, qt * QT_:(qt + 1) * QT_], in0=R.ps[pb][0:64, :],
                                                        scalar=R.par[0:64, gcol:gcol + 1], in1=rr[:, :],
                                                        op0=ALU.mult, op1=ALU.mult),
          reads=[R.ps_b[pb], rrb, R.par_b], writes=[dst_b])


def v_store(kb, R, ht, ht_b, c0, qt, pb):
    nc = kb.nc
    for s in range(4):
        proj_tm(kb, R, ht, ht_b, c0, 64, pb, s)
    src = R.ps[pb][:, :].rearrange("p (s c) -> p s c", c=128)[:, :, 0:64]
    kb.op("act", lambda: nc.scalar.copy(out=R.Va[:, qt * 4:(qt + 1) * 4, 0:64], in_=src),
          reads=[R.ps_b[pb]], writes=[R.Va_b])


def ydst(yT_d, row0, qt):
    if callable(yT_d):
        return yT_d(row0, qt)
    return yT_d[row0:row0 + 64, qt * QT_:(qt + 1) * QT_]


def out_store(kb, R, yT_d, row0, qt, src_fn, reads):
    i = R.nyo % 2; R.nyo += 1
    src_fn(R.yo[i], R.yo_b[i])
    kb.dma("pool", ydst(yT_d, row0, qt), R.yo[i][:, :], reads=[R.yo_b[i]])


def mixer_c(kb, R, hT, yT_d, row0, masks_c):
    nc = kb.nc
    S, NB, NQ = R.S, R.NB, R.NQ
    kb.dma("sp", R.mask[:, :, :], masks_c[:, :, 0, :], writes=[R.mask_b])
    kb.op("pool", lambda: nc.gpsimd.memset(R.Va[:, :, 64:128], 1.0), writes=[R.Va_b])
    bd32 = R.cst[0:64, 64:128]
    ones64 = R.cst[0:64, 0:64]
    for qt in range(NQ):
        ht, htb = load_ht(kb, R, hT, qt)
        proj_fm(kb, R, ht, htb, C_Q, 64, 0)
        proj_fm(kb, R, ht, htb, C_K, 64, 2)
        qk_norm_store(kb, R, 0, 1, R.QT, R.QT_b, qt, 0, bd32, 1.0 / 32, 0)
        qk_norm_store(kb, R, 2, 3, R.KT, R.KT_b, qt, 1, bd32, 1.0 / 32, 1)
        v_store(kb, R, ht, htb, C_V, qt, 4 + (qt % 2))
    for qt in range(NQ):
        nkb = 4 * qt + 4
        O0, O1 = 6, 7
        estate = {}

        def s_step(kbk):
            pj = kbk % 3
            sb = 2 * pj
            for m in range(2):
                kb.op("pe", lambda m=m: nc.tensor.matmul(R.ps[sb + m],
                                                         lhsT=R.KT[m * 32:(m + 1) * 32, kbk * 128:(kbk + 1) * 128],
                                                         rhs=R.QT[m * 32:(m + 1) * 32, qt * QT_:(qt + 1) * QT_],
                                                         start=True, stop=True),
                      reads=[R.KT_b, R.QT_b], writes=[R.ps_b[sb + m]])
            ei = R.ne16 % 3; R.ne16 += 1
            e, eb = R.e16[ei], R.e16_b[ei]
            estate[kbk] = (e, eb)
            kb.op("act", lambda: nc.scalar.activation(out=e[:, :, :], in_=R.pp[pj][:, :, :], func=AF.Exp),
                  reads=[R.ps_b[sb], R.ps_b[sb + 1]], writes=[eb])
            r = kbk - 4 * qt
            if r >= 0:
                for m in range(2):
                    kb.op("dve", lambda m=m: nc.vector.tensor_tensor(out=e[:, m, :], in0=e[:, m, :], in1=R.mask[:, r, :], op=ALU.mult),
                          reads=[eb, R.mask_b], writes=[eb])

        def pv_step(kbk):
            e, eb = estate.pop(kbk)
            for m in range(2):
                kb.op("pe", lambda m=m: nc.tensor.matmul(R.ps[O0 + m][:, :], lhsT=R.Va[:, kbk, :], rhs=e[:, m, :],
                                                         start=(kbk == 0), stop=(kbk == nkb - 1)),
                      reads=[R.Va_b, eb], writes=[R.ps_b[O0 + m]])

        s_step(0)
        if nkb > 1:
            s_step(1)
        for kbk in range(nkb):
            if kbk + 2 < nkb:
                s_step(kbk + 2)
            pv_step(kbk)
        f0, f1, f2 = R.fin
        b0, b1, b2 = R.fin_b
        kb.op("dve", lambda: nc.vector.reciprocal(out=f0[:, :], in_=R.ps[O0][64:128, :]), reads=[R.ps_b[O0]], writes=[b0])
        kb.op("dve", lambda: nc.vector.tensor_tensor(out=f0[:, :], in0=R.ps[O0][0:64, :], in1=f0[:, :], op=ALU.mult),
              reads=[R.ps_b[O0], b0], writes=[b0])
        kb.op("dve", lambda: nc.vector.reciprocal(out=f1[:, :], in_=R.ps[O1][64:128, :]), reads=[R.ps_b[O1]], writes=[b1])
        kb.op("dve", lambda: nc.vector.tensor_tensor(out=f1[:, :], in0=R.ps[O1][0:64, :], in1=f1[:, :], op=ALU.mult),
              reads=[R.ps_b[O1], b1], writes=[b1])
        kb.op("dve", lambda: nc.vector.scalar_tensor_tensor(out=f2[:, :], in0=f1[:, :], scalar=R.par[0:64, 2:3], in1=f0[:, :],
                                                            op0=ALU.mult, op1=ALU.add),
              reads=[b0, b1, R.par_b], writes=[b2])
        kb.op("act", lambda: nc.scalar.activation(out=R.sq[0][:, :], in_=f2[:, :], func=AF.Square), reads=[b2], writes=[R.sq_b[0]])
        kb.op("pe", lambda: nc.tensor.matmul(R.ps[0][0:64, :], lhsT=ones64, rhs=R.sq[0][:, :], start=True, stop=True),
              reads=[R.sq_b[0], R.cst_b], writes=[R.ps_b[0]])
        kb.op("act", lambda: nc.scalar.activation(out=R.rr[0][:, :], in_=R.ps[0][0:64, :], func=AF.Sqrt, bias=EPS, scale=1.0 / 64),
              reads=[R.ps_b[0]], writes=[R.rr_b[0]])
        kb.op("dve", lambda: nc.vector.reciprocal(out=R.rr[0][:, :], in_=R.rr[0][:, :]), reads=[R.rr_b[0]], writes=[R.rr_b[0]])

        def fn(yo, yob):
            kb.op("dve", lambda: nc.vector.scalar_tensor_tensor(out=yo[:, :], in0=f2[:, :], scalar=R.par[0:64, 3:4], in1=R.rr[0][:, :],
                                                                op0=ALU.mult, op1=ALU.mult),
                  reads=[b2, R.rr_b[0], R.par_b], writes=[yob])
        out_store(kb, R, yT_d, row0, qt, fn, None)


def mixer_d(kb, R, hT, yT_d, row0, masks_d, tri_d):
    nc = kb.nc
    S, NB, NQ = R.S, R.NB, R.NQ
    kb.dma("sp", R.mask[:, :, :], masks_d, writes=[R.mask_b])
    kb.dma("sp", R.tri[:, :, :], tri_d, writes=[R.tri_b])
    for qt in range(NQ):
        ht, htb = load_ht(kb, R, hT, qt)
        proj_fm(kb, R, ht, htb, D_Q, 64, 0)
        proj_fm(kb, R, ht, htb, D_K, 64, 1)
        kb.op("act", lambda: nc.scalar.activation(out=R.QT[:, qt * QT_:(qt + 1) * QT_], in_=R.ps[0][0:64, :], func=AF.Copy, scale=0.125),
              reads=[R.ps_b[0]], writes=[R.QT_b])
        kb.op("dve", lambda: nc.vector.tensor_copy(out=R.KT[:, qt * QT_:(qt + 1) * QT_], in_=R.ps[1][0:64, :]),
              reads=[R.ps_b[1]], writes=[R.KT_b])
        v_store(kb, R, ht, htb, D_V, qt, 4 + (qt % 2))
    RA, RB, OB = 4, 5, 6
    ed_b = [kb.buf() for _ in range(2)]
    for qt in range(NQ):
        kbs = list(range(4 * qt + 3, -1, -1))
        npair = len(kbs) // 2
        qsl = slice(qt * QT_, (qt + 1) * QT_)

        def blocks(j):
            return kbs[2 * j], kbs[2 * j + 1]

        def z_mm(j):
            for h, kbk in enumerate(blocks(j)):
                zb = 2 * (j % 2) + h
                kb.op("pe", lambda kbk=kbk, zb=zb: nc.tensor.matmul(R.ps[zb], lhsT=R.KT[:, kbk * 128:(kbk + 1) * 128],
                                                                    rhs=R.QT[:, qsl], start=True, stop=True),
                      reads=[R.KT_b, R.QT_b], writes=[R.ps_b[zb]])

        def esp(j):
            zp = j % 2
            e, eb = R.e32[j % 2], ed_b[j % 2]
            sp, spb = R.spp[j % 2], R.spp_b[j % 2]
            kb.op("act", lambda: nc.scalar.activation(out=e[:, :, :], in_=R.pp[zp][:, :, :], func=AF.Exp),
                  reads=[R.ps_b[2 * zp], R.ps_b[2 * zp + 1]], writes=[eb])
            kb.op("act", lambda: nc.scalar.activation(out=sp[:, :, :], in_=e[:, :, :], func=AF.Ln, bias=1.0),
                  reads=[eb], writes=[spb])
            for h, kbk in enumerate(blocks(j)):
                r = kbk - 4 * qt
                if r >= 0:
                    kb.op("dve", lambda h=h, r=r: nc.vector.tensor_tensor(out=sp[:, h, :], in0=sp[:, h, :], in1=R.mask[:, r, :], op=ALU.mult),
                          reads=[spb, R.mask_b], writes=[spb])

        def mmR(bank, t_i, sp_ap, spb, start, stop=False):
            kb.op("pe", lambda: nc.tensor.matmul(R.ps[bank], lhsT=R.tri[:, t_i, :], rhs=sp_ap, start=start, stop=stop),
                  reads=[R.tri_b, spb], writes=[R.ps_b[bank]])

        def chain_a(j):
            sp, spb = R.spp[j % 2], R.spp_b[j % 2]
            mmR(RA, 0, sp[:, 0, :], spb, j == 0)
            mmR(RB, 2, sp[:, 0, :], spb, j == 0)
            mmR(RB, 0, sp[:, 1, :], spb, False)

        def chain_b(j):
            e, eb = R.e32[j % 2], ed_b[j % 2]
            sp, spb = R.spp[j % 2], R.spp_b[j % 2]
            tt, ttb = R.e16[j % 2], R.e16_b[j % 2]
            aa, aab = R.e16[2 + j % 2], R.e16_b[2 + j % 2]
            kb.op("act", lambda: nc.scalar.activation(out=tt[:, :, :], in_=R.pp[2][:, :, :], func=AF.Exp, scale=-1.0),
                  reads=[R.ps_b[RA], R.ps_b[RB]], writes=[ttb])
            mmR(RA, 1, sp[:, 0, :], spb, False)
            mmR(RA, 2, sp[:, 1, :], spb, False, j == npair - 1)
            mmR(RB, 1, sp[:, 1, :], spb, False, j == npair - 1)
            kb.op("dve", lambda: nc.vector.tensor_tensor(out=aa[:, :, :], in0=e[:, :, :], in1=tt[:, :, :], op=ALU.mult),
                  reads=[eb, ttb], writes=[aab])
            for h, kbk in enumerate(blocks(j)):
                r = kbk - 4 * qt
                if r >= 0:
                    kb.op("dve", lambda h=h, r=r: nc.vector.tensor_tensor(out=aa[:, h, :], in0=aa[:, h, :], in1=R.mask[:, r, :], op=ALU.mult),
                          reads=[aab, R.mask_b], writes=[aab])

        def pv(j):
            aa, aab = R.e16[2 + j % 2], R.e16_b[2 + j % 2]
            for h, kbk in enumerate(blocks(j)):
                kb.op("pe", lambda h=h, kbk=kbk: nc.tensor.matmul(R.ps[OB][0:64, :], lhsT=R.Va[:, kbk, 0:64], rhs=aa[:, h, :],
                                                                  start=(j == 0 and h == 0), stop=(j == npair - 1 and h == 1)),
                      reads=[R.Va_b, aab], writes=[R.ps_b[OB]])

        z_mm(0)
        if npair > 1:
            z_mm(1)
        esp(0)
        for j in range(npair):
            if j + 1 < npair:
                esp(j + 1)
            chain_a(j)
            if j + 2 < npair:
                z_mm(j + 2)
            chain_b(j)
            if j >= 1:
                pv(j - 1)
        pv(npair - 1)

        def fn(yo, yob):
            kb.op("dve", lambda: nc.vector.tensor_copy(out=yo[:, :], in_=R.ps[OB][0:64, :]), reads=[R.ps_b[OB]], writes=[yob])
        out_store(kb, R, yT_d, row0, qt, fn, None)


def mixer_a(kb, R, hT, yT_d, row0, biasT_d):
    nc = kb.nc
    S, NB, NQ = R.S, R.NB, R.NQ
    kb.dma("sp", R.EB[:, :, :], biasT_d, writes=[R.EB_b])
    for r in range(8):
        kb.op("act", lambda r=r: nc.scalar.activation(out=R.EB[:, r, :], in_=R.EB[:, r, :], func=AF.Exp),
              reads=[R.EB_b], writes=[R.EB_b])
    kb.op("pool", lambda: nc.gpsimd.memset(R.Va[:, :, 64:128], 1.0), writes=[R.Va_b])
    ones64 = R.cst[0:64, 0:64]
    for qt in range(NQ):
        ht, htb = load_ht(kb, R, hT, qt)
        proj_fm(kb, R, ht, htb, A_Q, 64, 0)
        proj_fm(kb, R, ht, htb, A_K, 64, 2)
        qk_norm_store(kb, R, 0, 1, R.QT, R.QT_b, qt, 4, ones64, 1.0 / 64, 0)
        qk_norm_store(kb, R, 2, 3, R.KT, R.KT_b, qt, 5, ones64, 1.0 / 64, 1)
        v_store(kb, R, ht, htb, A_V, qt, 4 + (qt % 2))
    OB = 4
    for qt in range(NQ):
        rs = [r for r in range(8) if 4 * qt - 4 + r >= 0]
        for j, r in enumerate(rs):
            kbk = 4 * qt - 4 + r
            sb = j % 2
            kb.op("pe", lambda: nc.tensor.matmul(R.ps[sb][:, :], lhsT=R.KT[:, kbk * 128:(kbk + 1) * 128],
                                                 rhs=R.QT[:, qt * QT_:(qt + 1) * QT_], start=True, stop=True),
                  reads=[R.KT_b, R.QT_b], writes=[R.ps_b[sb]])
            e, eb = R.e32[j % 2], R.e32_b[j % 2]
            p, pbuf = R.e16[j % 2], R.e16_b[j % 2]
            kb.op("act", lambda: nc.scalar.activation(out=e[:, 0, :], in_=R.ps[sb][:, :], func=AF.Exp),
                  reads=[R.ps_b[sb]], writes=[eb])
            kb.op("dve", lambda: nc.vector.tensor_tensor(out=p[:, 0, :], in0=e[:, 0, :], in1=R.EB[:, r, :], op=ALU.mult),
                  reads=[eb, R.EB_b], writes=[pbuf])
            kb.op("pe", lambda: nc.tensor.matmul(R.ps[OB][:, :], lhsT=R.Va[:, kbk, :], rhs=p[:, 0, :],
                                                 start=(j == 0), stop=(j == len(rs) - 1)),
                  reads=[R.Va_b, pbuf], writes=[R.ps_b[OB]])
        f0, b0 = R.fin[0], R.fin_b[0]
        kb.op("dve", lambda: nc.vector.reciprocal(out=f0[:, :], in_=R.ps[OB][64:128, :]), reads=[R.ps_b[OB]], writes=[b0])

        def fn(yo, yob):
            kb.op("dve", lambda: nc.vector.tensor_tensor(out=yo[:, :], in0=R.ps[OB][0:64, :], in1=f0[:, :], op=ALU.mult),
                  reads=[R.ps_b[OB], b0], writes=[yob])
        out_store(kb, R, yT_d, row0, qt, fn, None)


class MixResB:
    def __init__(self, kb, R):
        NB = R.NB
        self.Osig = R.EB[:, :, :].bitcast(BF16).rearrange("p a (b c) -> p (a b) c", c=64)[:, 0:NB, :]; self.Osig_b = R.EB_b
        self.G = kb.sbuf("Gates", [128, 8, NB], F32); self.G_b = kb.buf()
        self.trif = kb.sbuf("trif", [128, 2, 128], F32); self.trif_b = kb.buf()
        self.cw = kb.sbuf("convw", [128, 8], F32); self.cw_b = kb.buf()
        self.gob = kb.sbuf("gob", [128, 64], F32); self.gob_b = kb.buf()
        self.St = [kb.sbuf(f"St{i}", [64, 65], F32) for i in range(2)]; self.St_b = [kb.buf() for _ in range(2)]
        self.Sb = [kb.sbuf(f"Sb{i}", [64, 65], BF16) for i in range(2)]; self.Sb_b = [kb.buf() for _ in range(2)]
        self.tok = [kb.sbuf(f"tok{i}", [128, 3, 64], BF16) for i in range(2)]; self.tok_b = [kb.buf() for _ in range(2)]
        self.qkT = [kb.sbuf(f"qkT{i}", [64, 2, 128], BF16) for i in range(2)]; self.qkT_b = [kb.buf() for _ in range(2)]
        self.qkm = [kb.sbuf(f"qkm{i}", [128, 128], BF16) for i in range(2)]; self.qkm_b = [kb.buf() for _ in range(2)]
        self.cm = kb.sbuf("cmask", [128, 128], F32); self.cm_b = kb.buf()
        self.hn = [kb.sbuf(f"hn{i}", [128, 64], F32) for i in range(2)]; self.hn_b = [kb.buf() for _ in range(2)]
        self.hs = [kb.sbuf(f"hs{i}", [128, 8], F32) for i in range(2)]; self.hs_b = [kb.buf() for _ in range(2)]
        self.yb = [kb.sbuf(f"yb{i}", [128, 64], BF16) for i in range(2)]; self.yb_b = [kb.buf() for _ in range(2)]
        self.jk = kb.sbuf("jk", [128, 64], BF16); self.jk_b = kb.buf()


def mixer_b(kb, R, RB_, hT, yT_d, row0, bpar_d, trif_d, cmask_d, gob_d):
    nc = kb.nc
    S, NB, NQ = R.S, R.NB, R.NQ
    B = RB_
    kb.dma("sp", B.cw[:, :], bpar_d, writes=[B.cw_b])
    kb.dma("sp", B.trif[:, :, :], trif_d, writes=[B.trif_b])
    kb.dma("sp", B.cm[:, :], cmask_d, writes=[B.cm_b])
    kb.dma("sp", B.gob[:, :], gob_d, writes=[B.gob_b])
    kb.op("pool", lambda: nc.gpsimd.memset(R.Va[:, :, 64:65], 1.0), writes=[R.Va_b])
    kb.op("dve", lambda: nc.vector.tensor_scalar(out=B.cw[:, 7:8], in0=B.cw[:, 6:7], scalar1=-1.0, scalar2=None, op0=ALU.mult),
          reads=[B.cw_b], writes=[B.cw_b])
    cv = [R.e32[i][:, :, :].rearrange("p a b -> p (a b)") for i in range(2)]
    cvb = R.e32_b
    accA = R.e16[0][:, :, :].rearrange("p a b -> p (a b)").bitcast(F32); accB_ = R.e16[1][:, :, :].rearrange("p a b -> p (a b)").bitcast(F32)
    accA_b = R.e16_b[0]; accB_b = R.e16_b[1]
    for qt in range(NQ):
        ht, htb = load_ht(kb, R, hT, qt)
        ci = qt % 2
        proj_fm(kb, R, ht, htb, B_Q, 128, 0)
        if qt == 0:
            kb.op("dve", lambda: nc.vector.memset(cv[ci][:, 0:3], 0.0), writes=[cvb[ci]])
        else:
            kb.op("dve", lambda: nc.vector.tensor_copy(out=cv[ci][:, 0:3], in_=cv[1 - ci][:, 512:515]),
                  reads=[cvb[1 - ci]], writes=[cvb[ci]])
        kb.op("act", lambda: nc.scalar.copy(out=cv[ci][:, 3:515], in_=R.ps[0][:, :]), reads=[R.ps_b[0]], writes=[cvb[ci]])
        kb.op("dve", lambda: nc.vector.tensor_scalar(out=accA, in0=cv[ci][:, 3:515], scalar1=B.cw[:, 3:4], scalar2=B.cw[:, 4:5],
                                                     op0=ALU.mult, op1=ALU.add),
              reads=[cvb[ci], B.cw_b], writes=[accA_b])
        for j in (2, 1, 0):
            kb.op("dve", lambda j=j: nc.vector.scalar_tensor_tensor(out=accA, in0=cv[ci][:, j:j + 512], scalar=B.cw[:, j:j + 1],
                                                                    in1=accA, op0=ALU.mult, op1=ALU.add),
                  reads=[cvb[ci], B.cw_b, accA_b], writes=[accA_b])
        kb.op("act", lambda: nc.scalar.activation(out=accB_, in_=accA, func=AF.Sigmoid), reads=[accA_b], writes=[accB_b])
        kb.op("dve", lambda: nc.vector.tensor_tensor(out=accB_, in0=accA, in1=accB_, op=ALU.mult), reads=[accA_b, accB_b], writes=[accB_b])
        kb.op("act", lambda: nc.scalar.copy(out=R.QT[:, qt * QT_:(qt + 1) * QT_], in_=accB_[0:64, :]), reads=[accB_b], writes=[R.QT_b])
        kb.op("act", lambda: nc.scalar.copy(out=R.KT[:, qt * QT_:(qt + 1) * QT_], in_=accB_[64:128, :]), reads=[accB_b], writes=[R.KT_b])
        pb = 4 + (qt % 2)
        for s in range(4):
            nonlocal_pb = 3 + ((qt * 4 + s) % 4)
            for kc in range(8):
                kb.op("pe", lambda kc=kc: nc.tensor.matmul(R.ps[nonlocal_pb][:, 0:130], lhsT=ht[:, kc, s * 128:(s + 1) * 128],
                                                           rhs=R.Wm[:, kc, B_V:B_V + 130], start=(kc == 0), stop=(kc == 7)),
                      reads=[R.Wm_b, htb], writes=[R.ps_b[nonlocal_pb]], inc=(kc == 7))
            blk = qt * 4 + s
            kb.op("dve", lambda: nc.vector.tensor_copy(out=R.Va[:, blk, 0:64], in_=R.ps[nonlocal_pb][:, 0:64]),
                  reads=[R.ps_b[nonlocal_pb]], writes=[R.Va_b])
            kb.op("act", lambda: nc.scalar.activation(out=B.Osig[:, blk, :], in_=R.ps[nonlocal_pb][:, 64:128], func=AF.Sigmoid),
                  reads=[R.ps_b[nonlocal_pb]], writes=[B.Osig_b])
            kb.op("dve", lambda: nc.vector.tensor_copy(out=B.G[:, 0:2, blk], in_=R.ps[nonlocal_pb][:, 128:130]),
                  reads=[R.ps_b[nonlocal_pb]], writes=[B.G_b])
    G = B.G
    kb.op("act", lambda: nc.scalar.activation(out=G[:, 2, :], in_=G[:, 1, :], func=AF.Exp, scale=-1.0, bias=B.cw[:, 7:8]),
          reads=[B.G_b, B.cw_b], writes=[B.G_b])
    kb.op("act", lambda: nc.scalar.activation(out=G[:, 2, :], in_=G[:, 2, :], func=AF.Ln, bias=1.0), reads=[B.G_b], writes=[B.G_b])
    kb.op("dve", lambda: nc.vector.tensor_scalar(out=G[:, 2, :], in0=G[:, 2, :], scalar1=-1.0, scalar2=None, op0=ALU.mult),
          reads=[B.G_b], writes=[B.G_b])
    kb.op("pe", lambda: nc.tensor.matmul(R.ps[0][:, 0:NB], lhsT=B.trif[:, 0, :], rhs=G[:, 2, :], start=True, stop=True),
          reads=[B.trif_b, B.G_b], writes=[R.ps_b[0]])
    kb.op("pe", lambda: nc.tensor.matmul(R.ps[1][:, 0:NB], lhsT=B.trif[:, 1, :], rhs=G[:, 2, :], start=True, stop=True),
          reads=[B.trif_b, B.G_b], writes=[R.ps_b[1]])
    kb.op("dve", lambda: nc.vector.tensor_copy(out=G[:, 3, :], in_=R.ps[0][:, 0:NB]), reads=[R.ps_b[0]], writes=[B.G_b])
    kb.op("act", lambda: nc.scalar.activation(out=G[:, 4, :], in_=G[:, 3, :], func=AF.Exp), reads=[B.G_b], writes=[B.G_b])
    kb.op("dve", lambda: nc.vector.tensor_tensor(out=G[:, 5, :], in0=G[:, 0, :], in1=G[:, 3, :], op=ALU.subtract),
          reads=[B.G_b], writes=[B.G_b])
    kb.op("dve", lambda: nc.vector.tensor_tensor(out=G[:, 6, :], in0=G[:, 5, :], in1=R.ps[1][:, 0:NB], op=ALU.add),
          reads=[B.G_b, R.ps_b[1]], writes=[B.G_b])
    kb.op("act", lambda: nc.scalar.activation(out=G[:, 5, :], in_=G[:, 5, :], func=AF.Exp, bias=B.cw[:, 5:6]),
          reads=[B.G_b, B.cw_b], writes=[B.G_b])
    kb.op("act", lambda: nc.scalar.activation(out=G[:, 6, :], in_=G[:, 6, :], func=AF.Exp, bias=B.cw[:, 5:6]),
          reads=[B.G_b, B.cw_b], writes=[B.G_b])
    kb.op("dve", lambda: nc.vector.tensor_scalar(out=G[:, 5:7, :], in0=G[:, 5:7, :], scalar1=0.125, scalar2=None, op0=ALU.mult),
          reads=[B.G_b], writes=[B.G_b])
    kb.op("act", lambda: nc.scalar.activation(out=G[:, 7, :], in_=R.ps[1][:, 0:NB], func=AF.Exp), reads=[R.ps_b[1]], writes=[B.G_b])
    kb.op("dve", lambda: nc.vector.memset(B.St[0][:, :], 0.0), writes=[B.St_b[0]])
    kb.op("dve", lambda: nc.vector.memset(B.Sb[0][:, :], 0.0), writes=[B.Sb_b[0]])
    PT, PT2, PS_, PO, PU = 0, 1, 2, 3, 6
    for b in range(NB):
        i2 = b % 2
        tok, tokb = B.tok[i2], B.tok_b[i2]
        qkT, qkTb = B.qkT[i2], B.qkT_b[i2]
        tpA = R.ps[0][:, :].bitcast(BF16); tpq = tpA[:, 0:128]; tpk = tpA[:, 128:256]
        kb.op("pe", lambda: nc.tensor.transpose(out=tpq[:, 0:64], in_=R.QT[:, b * 128:(b + 1) * 128], identity=R.ident[0:64, 0:64]),
              reads=[R.QT_b, R.ident_b], writes=[R.ps_b[0]])
        kb.op("pe", lambda: nc.tensor.transpose(out=tpk[:, 0:64], in_=R.KT[:, b * 128:(b + 1) * 128], identity=R.ident[0:64, 0:64]),
              reads=[R.KT_b, R.ident_b], writes=[R.ps_b[0]])
        kb.op("dve", lambda: nc.vector.tensor_scalar(out=tok[:, 0, :], in0=tpq[:, 0:64], scalar1=G[:, 4, b:b + 1], scalar2=None, op0=ALU.mult),
              reads=[R.ps_b[0], B.G_b], writes=[tokb])
        kb.op("dve", lambda: nc.vector.tensor_scalar(out=tok[:, 1, :], in0=tpk[:, 0:64], scalar1=G[:, 5, b:b + 1], scalar2=None, op0=ALU.mult),
              reads=[R.ps_b[0], B.G_b], writes=[tokb])
        kb.op("dve", lambda: nc.vector.tensor_scalar(out=tok[:, 2, :], in0=tpk[:, 0:64], scalar1=G[:, 6, b:b + 1], scalar2=None, op0=ALU.mult),
              reads=[R.ps_b[0], B.G_b], writes=[tokb])
        tpB = R.ps[1][:, :].bitcast(BF16); tq2 = tpB[:, 0:128]; tk2 = tpB[:, 128:256]
        kb.op("pe", lambda: nc.tensor.transpose(out=tq2[0:64, :], in_=tok[:, 0, :], identity=R.ident[:, :]),
              reads=[tokb, R.ident_b], writes=[R.ps_b[1]])
        kb.op("pe", lambda: nc.tensor.transpose(out=tk2[0:64, :], in_=tok[:, 1, :], identity=R.ident[:, :]),
              reads=[tokb, R.ident_b], writes=[R.ps_b[1]])
        kb.op("act", lambda: nc.scalar.copy(out=qkT[:, 0, :], in_=tq2[0:64, :]), reads=[R.ps_b[1]], writes=[qkTb])
        kb.op("act", lambda: nc.scalar.copy(out=qkT[:, 1, :], in_=tk2[0:64, :]), reads=[R.ps_b[1]], writes=[qkTb])
        kb.op("pe", lambda: nc.tensor.matmul(R.ps[PS_][:, 0:128], lhsT=qkT[:, 1, :], rhs=qkT[:, 0, :], start=True, stop=True),
              reads=[qkTb], writes=[R.ps_b[PS_]])
        qkm, qkmb = B.qkm[i2], B.qkm_b[i2]
        kb.op("dve", lambda: nc.vector.tensor_tensor(out=qkm[:, :], in0=R.ps[PS_][:, 0:128], in1=B.cm[:, :], op=ALU.mult),
              reads=[R.ps_b[PS_], B.cm_b], writes=[qkmb])
        Sp, Spb = B.Sb[i2], B.Sb_b[i2]
        po = PO + (b % 2)
        kb.op("pe", lambda: nc.tensor.matmul(R.ps[po][:, 0:65], lhsT=qkm[:, :], rhs=R.Va[:, b, 0:65], start=True, stop=False),
              reads=[qkmb, R.Va_b], writes=[R.ps_b[po]], inc=False)
        kb.op("pe", lambda: nc.tensor.matmul(R.ps[po][:, 0:65], lhsT=qkT[:, 0, :], rhs=Sp[:, :], start=False, stop=True),
              reads=[qkTb, Spb], writes=[R.ps_b[po]])
        kb.op("pe", lambda: nc.tensor.matmul(R.ps[PU][0:64, 0:65], lhsT=tok[:, 2, :], rhs=R.Va[:, b, 0:65], start=True, stop=True),
              reads=[tokb, R.Va_b], writes=[R.ps_b[PU]])
        Sn, Snb = B.St[1 - i2], B.St_b[1 - i2]
        So, Sob = B.St[i2], B.St_b[i2]
        kb.op("dve", lambda: nc.vector.scalar_tensor_tensor(out=Sn[:, :], in0=So[:, :], scalar=G[0:64, 7, b:b + 1], in1=R.ps[PU][0:64, 0:65],
                                                            op0=ALU.mult, op1=ALU.add),
              reads=[Sob, B.G_b, R.ps_b[PU]], writes=[Snb])
        kb.op("act", lambda: nc.scalar.copy(out=B.Sb[1 - i2][:, :], in_=Sn[:, :]), reads=[Snb], writes=[B.Sb_b[1 - i2]])
        hs, hsb = B.hs[i2], B.hs_b[i2]
        hn, hnb = B.hn[i2], B.hn_b[i2]
        kb.op("act", lambda: nc.scalar.activation(out=hs[:, 5:6], in_=R.ps[po][:, 64:65], func=AF.Abs),
              reads=[R.ps_b[po]], writes=[hsb])
        kb.op("dve", lambda: nc.vector.tensor_scalar(out=hs[:, 0:1], in0=hs[:, 5:6], scalar1=1.0, scalar2=None, op0=ALU.max),
              reads=[hsb], writes=[hsb])
        kb.op("dve", lambda: nc.vector.reciprocal(out=hs[:, 1:2], in_=hs[:, 0:1]), reads=[hsb], writes=[hsb])
        kb.op("dve", lambda: nc.vector.tensor_scalar(out=hn[:, :], in0=R.ps[po][:, 0:64], scalar1=hs[:, 1:2], scalar2=None, op0=ALU.mult),
              reads=[R.ps_b[po], hsb], writes=[hnb])
        kb.op("act", lambda: nc.scalar.activation(out=B.jk[:, :], in_=hn[:, :], func=AF.Square, accum_out=hs[:, 2:3]),
              reads=[hnb], writes=[B.jk_b, hsb])
        kb.op("act", lambda: nc.scalar.activation(out=hs[:, 3:4], in_=hs[:, 2:3], func=AF.Sqrt, bias=EPS, scale=1.0 / 64),
              reads=[hsb], writes=[hsb])
        kb.op("dve", lambda: nc.vector.reciprocal(out=hs[:, 4:5], in_=hs[:, 3:4]), reads=[hsb], writes=[hsb])
        kb.op("dve", lambda: nc.vector.scalar_tensor_tensor(out=hn[:, :], in0=hn[:, :], scalar=hs[:, 4:5], in1=B.gob[:, :],
                                                            op0=ALU.mult, op1=ALU.mult),
              reads=[hnb, hsb, B.gob_b], writes=[hnb])
        yb, ybb = B.yb[i2], B.yb_b[i2]
        kb.op("dve", lambda: nc.vector.tensor_tensor(out=yb[:, :], in0=hn[:, :], in1=B.Osig[:, b, :], op=ALU.mult),
              reads=[hnb, B.Osig_b], writes=[ybb])
        ty = R.ps[5][:, :].bitcast(BF16)[:, 0:128]
        kb.op("pe", lambda: nc.tensor.transpose(out=ty[0:64, :], in_=yb[:, :], identity=R.ident[:, :]),
              reads=[ybb, R.ident_b], writes=[R.ps_b[5]])
        qt = b // 4
        if b % 4 == 0:
            R.cur_yo = R.nyo % 2; R.nyo += 1
        yo, yob = R.yo[R.cur_yo], R.yo_b[R.cur_yo]
        kb.op("act", lambda: nc.scalar.copy(out=yo[:, (b % 4) * 128:(b % 4 + 1) * 128], in_=ty[0:64, :]),
              reads=[R.ps_b[5]], writes=[yob])
        if b % 4 == 3:
            kb.dma("pool", ydst(yT_d, row0, qt), yo[:, :], reads=[yob])


def mix_params(kb, R, praw_d, clam_d):
    nc = kb.nc
    pr = R.par
    kb.dma("sp", pr[0:64, 16:24], praw_d, writes=[R.par_b])
    cl = R.rr[0][:, 0:128].rearrange("p (a b) -> p a b", a=4)
    kb.dma("sp", cl, clam_d, writes=[R.rr_b[0]])
    V = nc.vector
    kb.op("dve", lambda: V.tensor_scalar(out=pr[0:64, 0:1], in0=pr[0:64, 16:17], scalar1=32 ** -0.5, scalar2=None, op0=ALU.mult), reads=[R.par_b], writes=[R.par_b])
    kb.op("dve", lambda: V.tensor_copy(out=pr[0:64, 1:2], in_=pr[0:64, 17:18]), reads=[R.par_b], writes=[R.par_b])
    kb.op("dve", lambda: V.tensor_tensor(out=pr[0:64, 3:4], in0=pr[0:64, 18:19], in1=pr[0:64, 22:23], op=ALU.mult), reads=[R.par_b], writes=[R.par_b])
    kb.op("dve", lambda: V.tensor_scalar(out=pr[0:64, 4:5], in0=pr[0:64, 19:20], scalar1=0.125, scalar2=None, op0=ALU.mult), reads=[R.par_b], writes=[R.par_b])
    kb.op("dve", lambda: V.tensor_copy(out=pr[0:64, 5:6], in_=pr[0:64, 20:21]), reads=[R.par_b], writes=[R.par_b])
    pp = R.rr[1][:, 0:64].rearrange("p (a b) -> p a b", a=2)
    kb.op("dve", lambda: V.tensor_tensor(out=pp[:, 0, :], in0=cl[:, 0, :], in1=cl[:, 1, :], op=ALU.mult), reads=[R.rr_b[0]], writes=[R.rr_b[1]])
    kb.op("dve", lambda: V.tensor_tensor(out=pp[:, 1, :], in0=cl[:, 2, :], in1=cl[:, 3, :], op=ALU.mult), reads=[R.rr_b[0]], writes=[R.rr_b[1]])
    kb.op("dve", lambda: V.reduce_sum(out=pr[0:64, 8:10], in_=pp, axis=AX.X), reads=[R.rr_b[1]], writes=[R.par_b])
    kb.op("act", lambda: nc.scalar.activation(out=pr[0:64, 10:12], in_=pr[0:64, 8:10], func=AF.Exp), reads=[R.par_b], writes=[R.par_b])
    kb.op("dve", lambda: V.tensor_tensor(out=pr[0:64, 12:13], in0=pr[0:64, 11:12], in1=pr[0:64, 10:11], op=ALU.subtract), reads=[R.par_b], writes=[R.par_b])
    kb.op("dve", lambda: V.tensor_tensor(out=pr[0:64, 2:3], in0=pr[0:64, 12:13], in1=pr[0:64, 21:22], op=ALU.subtract), reads=[R.par_b], writes=[R.par_b])

import ml_dtypes
bf16 = ml_dtypes.bfloat16
GW = 256
OFF = dict(aq=0, ak=256, av=512, bqk=768, bv=1280, bo=1536, bi=1792, bf=1796, cq=1800, ck=2056, cv=2312, dq=2568, dk=2824, dv=3080)

def sel_cols(j):
    c = []
    r = lambda o: list(range(o + j * 64, o + j * 64 + 64))
    c += r(OFF['aq']) + r(OFF['ak']) + r(OFF['av'])
    c += r(OFF['bqk']) + r(OFF['bqk'] + 256) + r(OFF['bv']) + r(OFF['bo']) + [OFF['bi'] + j, OFF['bf'] + j]
    c += r(OFF['cq']) + r(OFF['ck']) + r(OFF['cv'])
    c += r(OFF['dq']) + r(OFF['dk']) + r(OFF['dv'])
    return np.array(c)

def const_inputs():
    d = {}
    d['ident'] = np.eye(128, dtype=np.float32).astype(bf16)
    cst = np.zeros((128, 256), np.float32)
    cst[0:64, 0:64] = 1.0
    cst[0:32, 64:96] = 1.0; cst[32:64, 96:128] = 1.0
    d['cst'] = cst.astype(bf16)
    s = np.arange(128)[:, None, None, None]; r = np.arange(4)[None, :, None, None]; t = np.arange(512)[None, None, None, :]
    mc = ((2 * r + (s >= 64)) <= (t // 64)).astype(np.float32)
    d['masks_c'] = np.broadcast_to(mc, (128, 4, 2, 512)).astype(bf16).copy()
    s = np.arange(128)[:, None, None]; r = np.arange(4)[None, :, None]; t = np.arange(512)[None, None, :]
    d['masks_d'] = ((128 * r + s) < t).astype(np.float32).astype(bf16)
    j = np.arange(128)[:, None]; s2 = np.arange(128)[None, :]
    tri = np.zeros((128, 3, 128), np.float32)
    tri[:, 0, :] = (j >= s2); tri[:, 1, :] = (j < s2); tri[:, 2, :] = 1.0
    d['tri'] = tri.astype(bf16)
    trif = np.zeros((128, 2, 128), np.float32)
    trif[:, 0, :] = (j <= s2); trif[:, 1, :] = 1.0
    d['trif'] = trif
    d['cmask'] = (j <= s2).astype(np.float32)
    return d

def bias_index():
    s = np.arange(128)[:, None, None]; r = np.arange(8)[None, :, None]; t = np.arange(512)[None, None, :]
    rel = t - s + 512 - 128 * r
    idx = np.clip(rel, -128, 128) + 128
    dd = t // 64 + 8 - 2 * r - s // 64
    vis = (dd >= 0) & (dd <= 8)
    return idx, vis

_IDX, _VIS = bias_index()

def layer_core_inputs(P, l, j, lam_init=None):
    d = {}
    d['wsel'] = np.ascontiguousarray(P['w_in'][l][:, sel_cols(j)])
    d['gmix'] = np.ascontiguousarray(P['mix_norm'][l].reshape(8, 128).T)
    praw = np.zeros((64, 8), np.float32)
    praw[:, 0] = np.tile(P['c_q_norm'][l], 2); praw[:, 1] = np.tile(P['c_k_norm'][l], 2)
    praw[:, 2] = P['c_out_norm'][l]; praw[:, 3] = P['a_q_norm'][l]; praw[:, 4] = P['a_k_norm'][l]
    if lam_init is None:
        lam_init = 0.8 - 0.6 * np.exp(-0.3 * l)
    praw[:, 5] = lam_init; praw[:, 6] = 1.0 - lam_init
    d['praw'] = praw
    d['clam'] = np.ascontiguousarray(np.broadcast_to(P['c_lambda'][l][None], (64, 4, 32))).astype(np.float32)
    rb = P['a_rel_bias'][l][j]
    d['biasT'] = np.where(_VIS, rb[_IDX], np.float32(-1e30)).astype(np.float32)
    bpar = np.zeros((128, 8), np.float32)
    ch = np.concatenate([np.arange(j * 64, j * 64 + 64), 256 + np.arange(j * 64, j * 64 + 64)])
    bpar[:, 0:4] = P['b_conv_w'][l][:, ch].T
    bpar[:, 4] = P['b_conv_b'][l][ch]
    bpar[:, 5] = P['b_gate_bias'][l][0, j]
    bpar[:, 6] = P['b_gate_bias'][l][1, j]
    d['bpar'] = bpar
    d['gob'] = np.ascontiguousarray(np.broadcast_to(P['b_out_norm'][l][j][None], (128, 64))).astype(np.float32)
    return d


from concourse.bass_utils import run_bass_kernel_spmd

SEQ = 16384
NCORE = 8
TPC = 4096
DEPTH = 2
GROUPS = [[0, 1, 2, 3], [4, 5, 6, 7]]


def _din(nc, name, shape, dt):
    return nc.dram_tensor(name, list(shape), dt, kind="ExternalInput").ap()


def _dout(nc, name, shape, dt):
    return nc.dram_tensor(name, list(shape), dt, kind="ExternalOutput").ap()


def _dint(nc, name, shape, dt):
    return nc.dram_tensor(name, list(shape), dt, kind="Internal").ap()


MIX_IN = dict(wsel=([D, NW], F32), gmix=([128, 8], F32), praw=([64, 8], F32), clam=([64, 4, 32], F32),
              biasT=([128, 8, 512], F32), bpar=([128, 8], F32), gob=([128, 64], F32))
CONST_IN = dict(ident=([128, 128], BF16), cst=([128, 256], BF16), masks_c=([128, 4, 2, 512], BF16),
                masks_d=([128, 4, 512], BF16), tri=([128, 3, 128], BF16), trif=([128, 2, 128], F32), cmask=([128, 128], F32))


def build_fused(S=SEQ, T=TPC):
    nc = bass.Bass("TRN2", target_bir_lowering=False)
    NQr = T // QT_
    x_in = _din(nc, "x_in", [T, D], F32)
    x_out = _dout(nc, "x_out", [T, D], F32)
    Cn = {k: _din(nc, k, sh, dt) for k, (sh, dt) in CONST_IN.items()}
    ffn = {}
    for l in range(DEPTH):
        for f in ("ffn1", "ffn2"):
            ffn[(f, l)] = dict(g=_din(nc, f"{f}_g{l}", [128, 8], F32), wg=_din(nc, f"{f}_wg{l}", [D, DFF], F32),
                               wu=_din(nc, f"{f}_wu{l}", [D, DFF], F32), wd=_din(nc, f"{f}_wd{l}", [DFF, D], F32))
    wo = [_din(nc, f"wo{l}", [D, D], F32) for l in range(DEPTH)]
    mx = [{k: _din(nc, f"{k}{l}", sh, dt) for k, (sh, dt) in MIX_IN.items()} for l in range(DEPTH)]
    xa = _dint(nc, "xa", [T, D], F32); xb = _dint(nc, "xb", [T, D], F32); xc = _dint(nc, "xc", [T, D], F32)
    hT_loc = _dint(nc, "hT_loc", [D, T], BF16)
    hT_all = _dint(nc, "hT_all", [4 * D, T], BF16)
    yT_loc = _dint(nc, "yT_loc", [D, T], BF16)
    yT_all = _dint(nc, "yT_all", [4 * D, T], BF16)
    yT_mine = _dint(nc, "yT_mine", [D, T], BF16)

    hv = hT_all.rearrange("(k r p) t -> k r p t", k=8, r=4)

    def hT_src(qt):
        r, o = qt // NQr, (qt % NQr) * QT_
        return hv[:, r, :, o:o + QT_]

    def y_dst(row0, qt):
        q, o = qt // NQr, (qt % NQr) * QT_
        return yT_loc[q * 256 + row0:q * 256 + row0 + 64, o:o + QT_]

    with ExitStack() as st:
        kb = KB(nc, st)
        pid = nc.sync.partition_id()
        qv = pid % 4

        ymine_b = kb.buf()

        def fetch_mine():
            yv2 = yT_all.rearrange("(q h j p) t -> q h j p t", q=4, h=2, j=4)
            for j in range(4):
                for h in range(2):
                    kb.dma("sp", yT_mine[j * 256 + h * 128:j * 256 + (h + 1) * 128, :],
                           yv2[bass.ds(qv, 1), h, j, :, :].rearrange("o p t -> (o p) t"), writes=[ymine_b])

        def tok_phase(passes):
            with ExitStack() as mem:
                kb.mem = mem
                R = TokRes(kb, any(p.get("wo") is not None for p in passes))
                load_consts(kb, R, Cn["ident"])
                prev_bufs = None
                for k, p in enumerate(passes):
                    w = p["ffn"]
                    load_ffn_weights(kb, R, w["g"], w["wg"], w["wu"], w["wd"], p.get("wo"))
                    ob = [kb.buf() for _ in range(T // TT)] if k + 1 < len(passes) else None
                    has_pre = p.get("wo") is not None
                    token_pass(kb, R, T, p["xi"], p["xo"], pre=(yT_mine if has_pre else None),
                               post=p.get("post"), in_bufs=prev_bufs, out_bufs=ob,
                               pre_bufs=([ymine_b] * (T // TT) if has_pre else None))
                    prev_bufs = ob
                kb.barrier()
            kb.mem = st

        def mix_phase(l):
            with ExitStack() as mem:
                kb.mem = mem
                R = MixRes(kb, S)
                RB = MixResB(kb, R)
                m = mx[l]
                mix_load_common(kb, R, m["wsel"], m["gmix"], Cn["ident"], Cn["cst"])
                mix_params(kb, R, m["praw"], m["clam"])
                mixer_a(kb, R, hT_src, y_dst, 0, m["biasT"])
                kb.barrier()
                mixer_b(kb, R, RB, hT_src, y_dst, 64, m["bpar"], Cn["trif"], Cn["cmask"], m["gob"])
                kb.barrier()
                mixer_c(kb, R, hT_src, y_dst, 128, Cn["masks_c"])
                kb.barrier()
                mixer_d(kb, R, hT_src, y_dst, 192, Cn["masks_d"], Cn["tri"])
                kb.barrier()
            kb.mem = st

        tok_phase([dict(ffn=ffn[("ffn1", 0)], xi=x_in, xo=xa, post=hT_loc)])
        kb.allgather(hT_loc, hT_all, GROUPS)
        mix_phase(0)
        kb.allgather(yT_loc, yT_all, GROUPS)
        fetch_mine()
        tok_phase([dict(ffn=ffn[("ffn2", 0)], wo=wo[0], xi=xa, xo=xb),
                   dict(ffn=ffn[("ffn1", 1)], xi=xb, xo=xc, post=hT_loc)])
        kb.allgather(hT_loc, hT_all, GROUPS)
        mix_phase(1)
        kb.allgather(yT_loc, yT_all, GROUPS)
        fetch_mine()
        tok_phase([dict(ffn=ffn[("ffn2", 1)], wo=wo[1], xi=xc, xo=x_out)])
        kb.finish()
    return nc


def build_mixer_prog(S=SEQ):
    nc = bass.Bass("TRN2", target_bir_lowering=False)
    hT = _din(nc, "hT", [D, S], BF16)
    Cn = {k: _din(nc, k, sh, dt) for k, (sh, dt) in CONST_IN.items()}
    m = {k: _din(nc, k, sh, dt) for k, (sh, dt) in MIX_IN.items()}
    yT = _dout(nc, "yT", [256, S], BF16)
    with ExitStack() as st:
        kb = KB(nc, st)
        R = MixRes(kb, S)
        RB = MixResB(kb, R)
        mix_load_common(kb, R, m["wsel"], m["gmix"], Cn["ident"], Cn["cst"])
        mix_params(kb, R, m["praw"], m["clam"])
        mixer_a(kb, R, hT, yT, 0, m["biasT"])
        kb.barrier()
        mixer_b(kb, R, RB, hT, yT, 64, m["bpar"], Cn["trif"], Cn["cmask"], m["gob"])
        kb.barrier()
        mixer_c(kb, R, hT, yT, 128, Cn["masks_c"])
        kb.barrier()
        mixer_d(kb, R, hT, yT, 192, Cn["masks_d"], Cn["tri"])
        kb.finish()
    return nc


def _lay(g):
    return np.ascontiguousarray(np.asarray(g, np.float32).reshape(8, 128).T)


def _wo_perm(w_out):
    idx = np.arange(1024).reshape(4, 4, 64)
    perm = idx.transpose(1, 0, 2).reshape(-1)
    return np.ascontiguousarray(w_out[perm, :])


def make_in_maps(P, TPC=TPC):
    x = np.ascontiguousarray(P["x"], dtype=np.float32).reshape(-1, D)
    C = const_inputs()
    shared = dict(C)
    for l in range(DEPTH):
        for f in ("ffn1", "ffn2"):
            shared[f"{f}_g{l}"] = _lay(P[f + "_norm"][l])
            shared[f"{f}_wg{l}"] = np.ascontiguousarray(P[f + "_wg"][l], dtype=np.float32)
            shared[f"{f}_wu{l}"] = np.ascontiguousarray(P[f + "_wu"][l], dtype=np.float32)
            shared[f"{f}_wd{l}"] = np.ascontiguousarray(P[f + "_wd"][l], dtype=np.float32)
        shared[f"wo{l}"] = _wo_perm(np.asarray(P["w_out"][l], np.float32))
    ims = []
    for c in range(NCORE):
        d = dict(shared)
        d["x_in"] = x[c * TPC:(c + 1) * TPC]
        j = c % 4
        for l in range(DEPTH):
            for k, v in layer_core_inputs(P, l, j).items():
                d[f"{k}{l}"] = v
        ims.append(d)
    return ims


def kernel(**inputs):
    P = {k: np.asarray(v) for k, v in inputs.items()}
    nc = build_fused()
    ims = make_in_maps(P)
    res = run_bass_kernel_spmd(nc, ims, core_ids=list(range(NCORE)))
    out = np.concatenate([r["x_out"] for r in res.results], axis=0).reshape(2, SEQ, D).astype(np.float32)
    return out
```

```python
import numpy as np
from contextlib import ExitStack
import concourse.bass as bass
import concourse.mybir as mybir

F32 = mybir.dt.float32
BF16 = mybir.dt.bfloat16
AF = mybir.ActivationFunctionType
ALU = mybir.AluOpType
AX = mybir.AxisListType

EPOCH = 4096


class Buf:
    __slots__ = ("w", "r", "name")

    def __init__(self, name=""):
        self.w = None
        self.r = {}
        self.name = name


class KB:
    def __init__(self, nc, stack):
        self.nc = nc
        self.st = stack
        self.E = {"pe": nc.tensor, "act": nc.scalar, "dve": nc.vector, "pool": nc.gpsimd, "sp": nc.sync}
        self.cnt = {e: 0 for e in self.E}
        self.sems = {e: [] for e in self.E}
        self.waited = {e: {} for e in self.E}
        self.ndma = 12
        self.dma_sems = {}
        self.dma_cnt = {}
        self.dma_rr = {}
        self.nsem = 0
        self.uid = 0
        self.mem = stack

    def sem(self, name):
        self.nsem += 1
        return self.st.enter_context(self.nc.semaphore(name))

    def sbuf(self, name, shape, dt):
        self.uid += 1
        return self.mem.enter_context(self.nc.sbuf_tensor(f"sb{self.uid}_" + name, list(shape), dt))

    def psum(self, name, shape, dt):
        self.uid += 1
        return self.mem.enter_context(self.nc.psum_tensor(f"ps{self.uid}_" + name, list(shape), dt))

    def buf(self, name=""):
        return Buf(name)

    def _esem(self, e, n):
        ep = (n - 1) // EPOCH
        while len(self.sems[e]) <= ep:
            self.sems[e].append(self.sem(f"c_{e}_{len(self.sems[e])}"))
        return self.sems[e][ep], (n - 1) % EPOCH + 1

    def _wait(self, e, ev):
        if ev[0] == "e":
            _, src, n = ev
            if src == e and e == "pe":
                return
            key = ("e", src)
            if self.waited[e].get(key, 0) >= n:
                return
            if src == e and n > self.cnt[e]:
                raise RuntimeError("self-wait on future event")
            s, v = self._esem(src, n)
            self.E[e].wait_ge(s, v)
            self.waited[e][key] = n
        else:
            _, q, i, k = ev
            key = ("d", q, i)
            if self.waited[e].get(key, 0) >= k:
                return
            self.E[e].wait_ge(self.dma_sems[q][i], 16 * k)
            self.waited[e][key] = k

    @staticmethod
    def _evkey(ev):
        return (ev[0], ev[1]) if ev[0] == "e" else (ev[0], ev[1], ev[2])

    def _collect(self, reads, writes):
        deps = []
        for b in reads:
            if b.w is not None:
                deps.append(b.w)
        for b in writes:
            if b.w is not None:
                deps.append(b.w)
            deps.extend(b.r.values())
        return deps

    def _record(self, ev, reads, writes):
        k = self._evkey(ev)
        for b in reads:
            b.r[k] = ev
        for b in writes:
            b.w = ev
            b.r = {}

    def op(self, e, fn, reads=(), writes=(), inc=True):
        for ev in self._collect(reads, writes):
            self._wait(e, ev)
        ins = fn()
        if inc:
            self.cnt[e] += 1
            s, v = self._esem(e, self.cnt[e])
            ins.then_inc(s, 1)
            ev = ("e", e, self.cnt[e])
        else:
            ev = ("e", e, self.cnt[e] + 1)
        self._record(ev, reads, writes)
        return ins

    def dma(self, q, out, in_, reads=(), writes=(), **kw):
        for ev in self._collect(reads, writes):
            self._wait(q, ev)
        if q not in self.dma_sems:
            self.dma_sems[q] = [self.sem(f"d_{q}_{i}") for i in range(self.ndma)]
            self.dma_cnt[q] = [0] * self.ndma
            self.dma_rr[q] = 0
        i = self.dma_rr[q]
        self.dma_rr[q] = (i + 1) % self.ndma
        if self.dma_cnt[q][i] > 0:
            self._wait(q, ("d", q, i, self.dma_cnt[q][i]))
        self.dma_cnt[q][i] += 1
        ins = self.E[q].dma_start(out=out, in_=in_, **kw)
        ins.then_inc(self.dma_sems[q][i], 16)
        ev = ("d", q, i, self.dma_cnt[q][i])
        self._record(ev, reads, writes)
        return ins

    def barrier(self, extra_sems=()):
        for e in self.E:
            for q in self.dma_sems:
                for i in range(self.ndma):
                    if self.dma_cnt[q][i] > 0:
                        self._wait(e, ("d", q, i, self.dma_cnt[q][i]))
            for src in ("pe", "act", "dve", "pool"):
                if self.cnt[src] > 0 and not (src == e and e == "pe"):
                    self._wait(e, ("e", src, self.cnt[src]))
            for (sm, v) in extra_sems:
                self.E[e].wait_ge(sm, v)

    def allgather(self, src2d, dst2d, groups, chunk_rows=128):
        self.barrier()
        R_ = src2d.shape[0]
        nk = R_ // chunk_rows
        ng = len(groups[0])
        if not hasattr(self, "cc_sem"):
            self.cc_sem = self.sem("ccsem")
            self.cc_cnt = 0
        for k in range(nk):
            self.nc.gpsimd.collective_compute("AllGather", ALU.bypass, replica_groups=groups,
                                              ins=[src2d[k * chunk_rows:(k + 1) * chunk_rows, :]],
                                              outs=[dst2d[k * ng * chunk_rows:(k + 1) * ng * chunk_rows, :]]).then_inc(self.cc_sem, 1)
            self.cc_cnt += 1
        for e in self.E:
            self.E[e].wait_ge(self.cc_sem, self.cc_cnt)

    def finish(self):
        for q in self.dma_sems:
            for i in range(self.ndma):
                if self.dma_cnt[q][i] > 0:
                    self._wait("sp", ("d", q, i, self.dma_cnt[q][i]))
        for e in ("pe", "act", "dve", "pool"):
            if self.cnt[e] > 0:
                self._wait("sp", ("e", e, self.cnt[e]))


D = 1024
DFF = 2816
NFC = DFF // 128
TT = 256
SUB = TT // 128
EPS = 1e-6


class TokRes:
    def __init__(self, kb, with_pre):
        nc = kb.nc
        self.kb = kb
        self.Wg = kb.sbuf("Wg", [128, 8, DFF], BF16); self.Wg_b = kb.buf()
        self.Wu = kb.sbuf("Wu", [128, 8, DFF], BF16); self.Wu_b = kb.buf()
        self.Wd = kb.sbuf("Wd", [128, NFC, D], BF16); self.Wd_b = kb.buf()
        self.stage = [kb.sbuf(f"stage{i}", [128, 1024], F32) for i in range(2)]
        self.stage_b = [kb.buf() for _ in range(2)]
        self.gt = kb.sbuf("gt", [128, 8], F32); self.gt_b = kb.buf()
        self.ident = kb.sbuf("ident", [128, 128], BF16); self.ident_b = kb.buf()
        self.xt = [kb.sbuf(f"xt{i}", [128, SUB, D], F32) for i in range(2)]
        self.xt_b = [kb.buf() for _ in range(2)]
        self.xn = kb.sbuf("xn", [128, D], BF16); self.xn_b = kb.buf()
        self.st = kb.sbuf("stat", [128, 8], F32); self.st_b = kb.buf()
        self.xnT = [kb.sbuf(f"xnT{i}", [128, 8, TT], BF16) for i in range(2)]
        self.xnT_b = [kb.buf() for _ in range(2)]
        self.hid = kb.sbuf("hid", [128, NFC, TT], BF16)
        self.hid_b = [kb.buf() for _ in range(NFC)]
        self.sg = [kb.sbuf(f"sg{i}", [128, TT], F32) for i in range(2)]
        self.sg_b = [kb.buf() for _ in range(2)]
        self.hTo = kb.sbuf("hTo", [128, 8, TT], BF16); self.hTo_b = kb.buf()
        self.with_pre = with_pre
        if with_pre:
            self.Wo = kb.sbuf("Wo", [128, 8, D], BF16); self.Wo_b = kb.buf()
            self.yt = [kb.sbuf(f"yt{i}", [128, 8, TT], BF16) for i in range(2)]
            self.yt_b = [kb.buf() for _ in range(2)]
        self.psg = [kb.psum(f"psg{i}", [128, 512], F32) for i in range(2)]
        self.psg_b = [kb.buf() for _ in range(2)]
        self.psd = [kb.psum(f"psd{i}", [128, 512], F32) for i in range(2)]
        self.psd_b = [kb.buf() for _ in range(2)]
        self.tp = [kb.psum(f"tp{i}", [128, 8, 128], BF16) for i in range(2)]
        self.tp_b = [kb.buf() for _ in range(2)]
        self.ntp = 0
        self.npsd = 0
        self.ncast = 0


def load_consts(kb, R, ident_d):
    kb.dma("sp", R.ident[:], ident_d, writes=[R.ident_b])


def load_ffn_weights(kb, R, g_lay, wg, wu, wd, w_out=None):
    nc = kb.nc
    kb.dma("sp", R.gt[:], g_lay, writes=[R.gt_b])

    def cast(dst_ap, dst_b, src_ap, src_b, scal):
        e = ("dve", "act", "dve", "act", "pool")[R.ncast % 5]
        R.ncast += 1
        E = kb.E[e]
        if e == "act":
            if scal is None:
                kb.op(e, lambda: E.copy(out=dst_ap, in_=src_ap), reads=[src_b], writes=[dst_b])
            else:
                kb.op(e, lambda: E.activation(out=dst_ap, in_=src_ap, func=AF.Copy, scale=scal),
                      reads=[src_b, R.gt_b], writes=[dst_b])
        elif scal is None:
            kb.op(e, lambda: E.tensor_copy(out=dst_ap, in_=src_ap), reads=[src_b], writes=[dst_b])
        else:
            kb.op(e, lambda: E.tensor_scalar(out=dst_ap, in0=src_ap, scalar1=scal, scalar2=None, op0=ALU.mult),
                  reads=[src_b, R.gt_b], writes=[dst_b])

    k = 0
    for (W, Wb, src) in ((R.Wg, R.Wg_b, wg), (R.Wu, R.Wu_b, wu)):
        for kc in range(8):
            for (c0, c1) in ((0, 1024), (1024, 2048), (2048, DFF)):
                sb = k % 2; k += 1
                kb.dma("sp", R.stage[sb][:, 0:c1 - c0], src[kc * 128:(kc + 1) * 128, c0:c1],
                       writes=[R.stage_b[sb]])
                cast(W[:, kc, c0:c1], Wb, R.stage[sb][:, 0:c1 - c0], R.stage_b[sb], R.gt[:, kc:kc + 1])
    for fc in range(NFC):
        sb = k % 2; k += 1
        kb.dma("sp", R.stage[sb][:, 0:D], wd[fc * 128:(fc + 1) * 128, :], writes=[R.stage_b[sb]])
        cast(R.Wd[:, fc, :], R.Wd_b, R.stage[sb][:, 0:D], R.stage_b[sb], None)
    if w_out is not None:
        for kc in range(8):
            sb = k % 2; k += 1
            kb.dma("sp", R.stage[sb][:, 0:D], w_out[kc * 128:(kc + 1) * 128, :], writes=[R.stage_b[sb]])
            cast(R.Wo[:, kc, :], R.Wo_b, R.stage[sb][:, 0:D], R.stage_b[sb], None)


def norm_transpose(kb, R, x_ap, x_b, dstT, dstT_b, s):
    nc = kb.nc
    ss = R.st[:, 0:1]; rs = R.st[:, 1:2]; rstd = R.st[:, 2:3]
    kb.op("act", lambda: nc.scalar.activation(out=R.xn[:], in_=x_ap, func=AF.Square, accum_out=ss),
          reads=[x_b], writes=[R.xn_b, R.st_b])
    kb.op("act", lambda: nc.scalar.activation(out=rs, in_=ss, func=AF.Sqrt, bias=EPS, scale=1.0 / D),
          reads=[R.st_b], writes=[R.st_b])
    kb.op("dve", lambda: nc.vector.reciprocal(out=rstd, in_=rs), reads=[R.st_b], writes=[R.st_b])
    kb.op("dve", lambda: nc.vector.tensor_scalar(out=R.xn[:], in0=x_ap, scalar1=rstd, scalar2=None, op0=ALU.mult),
          reads=[x_b, R.st_b], writes=[R.xn_b])
    ti = R.ntp % 2; R.ntp += 1
    tp = R.tp[ti]; tpb = R.tp_b[ti]
    for kc in range(8):
        kb.op("pe", lambda kc=kc: nc.tensor.transpose(out=tp[:, kc, :], in_=R.xn[:, kc * 128:(kc + 1) * 128],
                                                      identity=R.ident[:]),
              reads=[R.xn_b, R.ident_b], writes=[tpb], inc=(kc == 7))
    kb.op("act", lambda: nc.scalar.copy(out=dstT[:, :, s * 128:(s + 1) * 128], in_=tp[:, :, :]),
          reads=[tpb], writes=[dstT_b])


def token_pass(kb, R, T, x_in, x_out, pre=None, post=None, in_bufs=None, out_bufs=None, pre_bufs=None, post_bufs=None):
    nc = kb.nc
    NT = T // TT

    def stage_load(i):
        bi = i % 2
        kb.dma("sp", R.xt[bi][:, :, :], x_in[i * TT:(i + 1) * TT, :].rearrange("(s p) d -> p s d", p=128),
               reads=([in_bufs[i]] if in_bufs else []), writes=[R.xt_b[bi]])
        if pre is not None:
            kb.dma("sp", R.yt[bi][:, :, :], pre[:, i * TT:(i + 1) * TT].rearrange("(c p) t -> p c t", p=128),
                   reads=([pre_bufs[i]] if pre_bufs else []), writes=[R.yt_b[bi]])

    def stage_pre(i):
        bi = i % 2
        if pre is None:
            return
        for s in range(SUB):
            for h in range(2):
                pi = R.npsd % 2; R.npsd += 1
                for kc in range(8):
                    kb.op("pe", lambda kc=kc: nc.tensor.matmul(R.psd[pi][:, :], lhsT=R.yt[bi][:, kc, s * 128:(s + 1) * 128],
                                                               rhs=R.Wo[:, kc, h * 512:(h + 1) * 512],
                                                               start=(kc == 0), stop=(kc == 7)),
                          reads=[R.yt_b[bi], R.Wo_b], writes=[R.psd_b[pi]], inc=(kc == 7))
                xs = R.xt[bi][:, s, h * 512:(h + 1) * 512]
                kb.op("dve", lambda: nc.vector.tensor_tensor(out=xs, in0=R.psd[pi][:, :], in1=xs, op=ALU.add),
                      reads=[R.psd_b[pi], R.xt_b[bi]], writes=[R.xt_b[bi]])

    def stage_a(i):
        bi = i % 2
        for s in range(SUB):
            norm_transpose(kb, R, R.xt[bi][:, s, :], R.xt_b[bi], R.xnT[bi], R.xnT_b[bi], s)

    def stage_b(i):
        bi = i % 2
        for fc in range(NFC):
            gi = fc % 2
            for (W, Wb, off) in ((R.Wg, R.Wg_b, 0), (R.Wu, R.Wu_b, 256)):
                for kc in range(8):
                    kb.op("pe", lambda kc=kc, W=W, off=off: nc.tensor.matmul(
                        R.psg[gi][:, off:off + TT], lhsT=W[:, kc, fc * 128:(fc + 1) * 128], rhs=R.xnT[bi][:, kc, :],
                        start=(kc == 0), stop=(kc == 7)),
                          reads=[Wb, R.xnT_b[bi]], writes=[R.psg_b[gi]], inc=(kc == 7))
            kb.op("act", lambda: nc.scalar.activation(out=R.sg[gi][:, :], in_=R.psg[gi][:, 0:TT], func=AF.Silu),
                  reads=[R.psg_b[gi]], writes=[R.sg_b[gi]])
            kb.op("dve", lambda: nc.vector.tensor_tensor(out=R.hid[:, fc, :], in0=R.sg[gi][:, :],
                                                         in1=R.psg[gi][:, 256:256 + TT], op=ALU.mult),
                  reads=[R.sg_b[gi], R.psg_b[gi]], writes=[R.hid_b[fc]])

    def stage_c(i):
        bi = i % 2
        for s in range(SUB):
            for h in range(2):
                pi = R.npsd % 2; R.npsd += 1
                for fc in range(NFC):
                    kb.op("pe", lambda fc=fc: nc.tensor.matmul(R.psd[pi][:, :], lhsT=R.hid[:, fc, s * 128:(s + 1) * 128],
                                                               rhs=R.Wd[:, fc, h * 512:(h + 1) * 512],
                                                               start=(fc == 0), stop=(fc == NFC - 1)),
                          reads=[R.hid_b[fc], R.Wd_b], writes=[R.psd_b[pi]], inc=(fc == NFC - 1))
                xs = R.xt[bi][:, s, h * 512:(h + 1) * 512]
                kb.op("dve", lambda: nc.vector.scalar_tensor_tensor(out=xs, in0=R.psd[pi][:, :], scalar=0.5, in1=xs,
                                                                    op0=ALU.mult, op1=ALU.add),
                      reads=[R.psd_b[pi], R.xt_b[bi]], writes=[R.xt_b[bi]])
            if post is not None:
                norm_transpose(kb, R, R.xt[bi][:, s, :], R.xt_b[bi], R.hTo, R.hTo_b, s)
        kb.dma("pool", x_out[i * TT:(i + 1) * TT, :].rearrange("(s p) d -> p s d", p=128), R.xt[bi][:, :, :],
               reads=[R.xt_b[bi]], writes=([out_bufs[i]] if out_bufs else []))
        if post is not None:
            kb.dma("pool", post[:, i * TT:(i + 1) * TT].rearrange("(c p) t -> p c t", p=128), R.hTo[:, :, :],
                   reads=[R.hTo_b], writes=([post_bufs[i]] if post_bufs else []))

    stage_load(0)
    stage_pre(0)
    stage_a(0)
    for i in range(NT):
        if i + 1 < NT:
            stage_load(i + 1)
        stage_b(i)
        if i + 1 < NT:
            stage_pre(i + 1)
            stage_a(i + 1)
        stage_c(i)


QT_ = 512
NW = 834
A_Q, A_K, A_V = 0, 64, 128
B_Q, B_K, B_V, B_O, B_I, B_F = 192, 256, 320, 384, 448, 449
C_Q, C_K, C_V = 450, 514, 578
D_Q, D_K, D_V = 642, 706, 770


class MixRes:
    def __init__(self, kb, S):
        self.S = S
        self.NB = S // 128
        self.NQ = S // QT_
        NB = self.NB
        self.Wm = kb.sbuf("Wm", [128, 8, NW], BF16); self.Wm_b = kb.buf()
        self.gm = kb.sbuf("gm", [128, 8], F32); self.gm_b = kb.buf()
        self.ident = kb.sbuf("identm", [128, 128], BF16); self.ident_b = kb.buf()
        self.ht = [kb.sbuf(f"ht{i}", [128, 8, QT_], BF16) for i in range(2)]; self.ht_b = [kb.buf() for _ in range(2)]
        self.QT = kb.sbuf("QT", [128, S], BF16); self.QT_b = kb.buf()
        self.KT = kb.sbuf("KT", [128, S], BF16); self.KT_b = kb.buf()
        self.Va = kb.sbuf("Va", [128, NB, 128], BF16); self.Va_b = kb.buf()
        self.par = kb.sbuf("par", [128, 32], F32); self.par_b = kb.buf()
        self.cst = kb.sbuf("cst", [128, 256], BF16); self.cst_b = kb.buf()
        self.sq = [kb.sbuf(f"sq{i}", [64, QT_], BF16) for i in range(2)]; self.sq_b = [kb.buf() for _ in range(2)]
        self.rr = [kb.sbuf(f"rr{i}", [64, QT_], F32) for i in range(2)]; self.rr_b = [kb.buf() for _ in range(2)]
        self.e32 = [kb.sbuf(f"e32_{i}", [128, 2, QT_], F32) for i in range(2)]; self.e32_b = [kb.buf() for _ in range(2)]
        self.wst = [self.e32[i][:, :, :].rearrange("p a b -> p (a b)")[:, 0:NW] for i in range(2)]; self.wst_b = self.e32_b
        self.e16 = [kb.sbuf(f"e16_{i}", [128, 2, QT_], BF16) for i in range(4)]; self.e16_b = [kb.buf() for _ in range(4)]
        self.spp = [kb.sbuf(f"spp_{i}", [128, 2, QT_], BF16) for i in range(2)]; self.spp_b = [kb.buf() for _ in range(2)]
        self.fin = [kb.sbuf(f"fin{i}", [64, QT_], F32) for i in range(3)]; self.fin_b = [kb.buf() for _ in range(3)]
        self.yo = [kb.sbuf(f"yo{i}", [64, QT_], BF16) for i in range(2)]; self.yo_b = [kb.buf() for _ in range(2)]
        self.mask = kb.sbuf("mask", [128, 4, QT_], BF16); self.mask_b = kb.buf()
        self.EB = kb.sbuf("EB", [128, 8, QT_], F32); self.EB_b = kb.buf()
        self.tri = kb.sbuf("tri", [128, 3, 128], BF16); self.tri_b = kb.buf()
        self.pp = [kb.psum(f"pp{i}", [128, 2, 512], F32) for i in range(4)]
        self.ps = [self.pp[i // 2][:, i % 2, :] for i in range(8)]
        self.ps_b = [kb.buf() for _ in range(8)]
        self.nyo = 0
        self.ne16 = 0


def mix_load_common(kb, R, wsel, gmix_lay, ident_d, cst_d):
    nc = kb.nc
    kb.dma("sp", R.gm[:], gmix_lay, writes=[R.gm_b])
    kb.dma("sp", R.ident[:], ident_d, writes=[R.ident_b])
    kb.dma("sp", R.cst[:], cst_d, writes=[R.cst_b])
    for kc in range(8):
        sb = kc % 2
        kb.dma("sp", R.wst[sb][:, :], wsel[kc * 128:(kc + 1) * 128, :], writes=[R.wst_b[sb]])
        kb.op("dve", lambda: nc.vector.tensor_scalar(out=R.Wm[:, kc, :], in0=R.wst[sb][:, :], scalar1=R.gm[:, kc:kc + 1],
                                                     scalar2=None, op0=ALU.mult),
              reads=[R.wst_b[sb], R.gm_b], writes=[R.Wm_b])


def load_ht(kb, R, hT, qt):
    bi = qt % 2
    if callable(hT):
        src = hT(qt).rearrange("c p t -> p c t")
    else:
        src = hT[:, qt * QT_:(qt + 1) * QT_].rearrange("(c p) t -> p c t", p=128)
    kb.dma("sp", R.ht[bi][:, :, :], src, writes=[R.ht_b[bi]])
    return R.ht[bi], R.ht_b[bi]


def proj_fm(kb, R, ht, ht_b, c0, ncol, pb):
    nc = kb.nc
    for kc in range(8):
        kb.op("pe", lambda kc=kc: nc.tensor.matmul(R.ps[pb][0:ncol, :], lhsT=R.Wm[:, kc, c0:c0 + ncol], rhs=ht[:, kc, :],
                                                   start=(kc == 0), stop=(kc == 7)),
              reads=[R.Wm_b, ht_b], writes=[R.ps_b[pb]], inc=(kc == 7))


def proj_tm(kb, R, ht, ht_b, c0, ncol, pb, s):
    nc = kb.nc
    for kc in range(8):
        kb.op("pe", lambda kc=kc: nc.tensor.matmul(R.ps[pb][:, s * 128:s * 128 + ncol], lhsT=ht[:, kc, s * 128:(s + 1) * 128],
                                                   rhs=R.Wm[:, kc, c0:c0 + ncol], start=(kc == 0), stop=(kc == 7)),
              reads=[R.Wm_b, ht_b], writes=[R.ps_b[pb]], inc=(kc == 7))


def qk_norm_store(kb, R, pb, pb2, dst, dst_b, qt, gcol, cmat, inv_n, i2):
    nc = kb.nc
    sq, sqb = R.sq[i2], R.sq_b[i2]
    rr, rrb = R.rr[i2], R.rr_b[i2]
    kb.op("act", lambda: nc.scalar.activation(out=sq[:, :], in_=R.ps[pb][0:64, :], func=AF.Square),
          reads=[R.ps_b[pb]], writes=[sqb])
    kb.op("pe", lambda: nc.tensor.matmul(R.ps[pb2][0:64, :], lhsT=cmat, rhs=sq[:, :], start=True, stop=True),
          reads=[sqb, R.cst_b], writes=[R.ps_b[pb2]])
    kb.op("act", lambda: nc.scalar.activation(out=rr[:, :], in_=R.ps[pb2][0:64, :], func=AF.Sqrt, bias=EPS, scale=inv_n),
          reads=[R.ps_b[pb2]], writes=[rrb])
    kb.op("dve", lambda: nc.vector.reciprocal(out=rr[:, :], in_=rr[:, :]), reads=[rrb], writes=[rrb])
    kb.op("dve", lambda: nc.vector.scalar_tensor_tensor(out=dst[0:64, qt * QT_:(qt + 1) * QT_], in0=R.ps[pb][0:64, :],
                                                        scalar=R.par[0:64, gcol:gcol + 1], in1=rr[:, :],
                                                        op0=ALU.mult, op1=ALU.mult),
          reads=[R.ps_b[pb], rrb, R.par_b], writes=[dst_b])


def v_store(kb, R, ht, ht_b, c0, qt, pb):
    nc = kb.nc
    for s in range(4):
        proj_tm(kb, R, ht, ht_b, c0, 64, pb, s)
    src = R.ps[pb][:, :].rearrange("p (s c) -> p s c", c=128)[:, :, 0:64]
    kb.op("act", lambda: nc.scalar.copy(out=R.Va[:, qt * 4:(qt + 1) * 4, 0:64], in_=src),
          reads=[R.ps_b[pb]], writes=[R.Va_b])


def ydst(yT_d, row0, qt):
    if callable(yT_d):
        return yT_d(row0, qt)
    return yT_d[row0:row0 + 64, qt * QT_:(qt + 1) * QT_]


def out_store(kb, R, yT_d, row0, qt, src_fn, reads):
    i = R.nyo % 2; R.nyo += 1
    src_fn(R.yo[i], R.yo_b[i])
    kb.dma("pool", ydst(yT_d, row0, qt), R.yo[i][:, :], reads=[R.yo_b[i]])


def mixer_c(kb, R, hT, yT_d, row0, masks_c):
    nc = kb.nc
    S, NB, NQ = R.S, R.NB, R.NQ
    kb.dma("sp", R.mask[:, :, :], masks_c[:, :, 0, :], writes=[R.mask_b])
    kb.op("pool", lambda: nc.gpsimd.memset(R.Va[:, :, 64:128], 1.0), writes=[R.Va_b])
    bd32 = R.cst[0:64, 64:128]
    ones64 = R.cst[0:64, 0:64]
    for qt in range(NQ):
        ht, htb = load_ht(kb, R, hT, qt)
        proj_fm(kb, R, ht, htb, C_Q, 64, 0)
        proj_fm(kb, R, ht, htb, C_K, 64, 2)
        qk_norm_store(kb, R, 0, 1, R.QT, R.QT_b, qt, 0, bd32, 1.0 / 32, 0)
        qk_norm_store(kb, R, 2, 3, R.KT, R.KT_b, qt, 1, bd32, 1.0 / 32, 1)
        v_store(kb, R, ht, htb, C_V, qt, 4 + (qt % 2))
    for qt in range(NQ):
        nkb = 4 * qt + 4
        O0, O1 = 6, 7
        estate = {}

        def s_step(kbk):
            pj = kbk % 3
            sb = 2 * pj
            for m in range(2):
                kb.op("pe", lambda m=m: nc.tensor.matmul(R.ps[sb + m],
                                                         lhsT=R.KT[m * 32:(m + 1) * 32, kbk * 128:(kbk + 1) * 128],
                                                         rhs=R.QT[m * 32:(m + 1) * 32, qt * QT_:(qt + 1) * QT_],
                                                         start=True, stop=True),
                      reads=[R.KT_b, R.QT_b], writes=[R.ps_b[sb + m]])
            ei = R.ne16 % 3; R.ne16 += 1
            e, eb = R.e16[ei], R.e16_b[ei]
            estate[kbk] = (e, eb)
            kb.op("act", lambda: nc.scalar.activation(out=e[:, :, :], in_=R.pp[pj][:, :, :], func=AF.Exp),
                  reads=[R.ps_b[sb], R.ps_b[sb + 1]], writes=[eb])
            r = kbk - 4 * qt
            if r >= 0:
                for m in range(2):
                    kb.op("dve", lambda m=m: nc.vector.tensor_tensor(out=e[:, m, :], in0=e[:, m, :], in1=R.mask[:, r, :], op=ALU.mult),
                          reads=[eb, R.mask_b], writes=[eb])

        def pv_step(kbk):
            e, eb = estate.pop(kbk)
            for m in range(2):
                kb.op("pe", lambda m=m: nc.tensor.matmul(R.ps[O0 + m][:, :], lhsT=R.Va[:, kbk, :], rhs=e[:, m, :],
                                                         start=(kbk == 0), stop=(kbk == nkb - 1)),
                      reads=[R.Va_b, eb], writes=[R.ps_b[O0 + m]])

        s_step(0)
        if nkb > 1:
            s_step(1)
        for kbk in range(nkb):
            if kbk + 2 < nkb:
                s_step(kbk + 2)
            pv_step(kbk)
        f0, f1, f2 = R.fin
        b0, b1, b2 = R.fin_b
        kb.op("dve", lambda: nc.vector.reciprocal(out=f0[:, :], in_=R.ps[O0][64:128, :]), reads=[R.ps_b[O0]], writes=[b0])
        kb.op("dve", lambda: nc.vector.tensor_tensor(out=f0[:, :], in0=R.ps[O0][0:64, :], in1=f0[:, :], op=ALU.mult),
              reads=[R.ps_b[O0], b0], writes=[b0])
        kb.op("dve", lambda: nc.vector.reciprocal(out=f1[:, :], in_=R.ps[O1][64:128, :]), reads=[R.ps_b[O1]], writes=[b1])
        kb.op("dve", lambda: nc.vector.tensor_tensor(out=f1[:, :], in0=R.ps[O1][0:64, :], in1=f1[:, :], op=ALU.mult),
              reads=[R.ps_b[O1], b1], writes=[b1])
        kb.op("dve", lambda: nc.vector.scalar_tensor_tensor(out=f2[:, :], in0=f1[:, :], scalar=R.par[0:64, 2:3], in1=f0[:, :],
                                                            op0=ALU.mult, op1=ALU.add),
              reads=[b0, b1, R.par_b], writes=[b2])
        kb.op("act", lambda: nc.scalar.activation(out=R.sq[0][:, :], in_=f2[:, :], func=AF.Square), reads=[b2], writes=[R.sq_b[0]])
        kb.op("pe", lambda: nc.tensor.matmul(R.ps[0][0:64, :], lhsT=ones64, rhs=R.sq[0][:, :], start=True, stop=True),
              reads=[R.sq_b[0], R.cst_b], writes=[R.ps_b[0]])
        kb.op("act", lambda: nc.scalar.activation(out=R.rr[0][:, :], in_=R.ps[0][0:64, :], func=AF.Sqrt, bias=EPS, scale=1.0 / 64),
              reads=[R.ps_b[0]], writes=[R.rr_b[0]])
        kb.op("dve", lambda: nc.vector.reciprocal(out=R.rr[0][:, :], in_=R.rr[0][:, :]), reads=[R.rr_b[0]], writes=[R.rr_b[0]])

        def fn(yo, yob):
            kb.op("dve", lambda: nc.vector.scalar_tensor_tensor(out=yo[:, :], in0=f2[:, :], scalar=R.par[0:64, 3:4], in1=R.rr[0][:, :],
                                                                op0=ALU.mult, op1=ALU.mult),
                  reads=[b2, R.rr_b[0], R.par_b], writes=[yob])
        out_store(kb, R, yT_d, row0, qt, fn, None)


def mixer_d(kb, R, hT, yT_d, row0, masks_d, tri_d):
    nc = kb.nc
    S, NB, NQ = R.S, R.NB, R.NQ
    kb.dma("sp", R.mask[:, :, :], masks_d, writes=[R.mask_b])
    kb.dma("sp", R.tri[:, :, :], tri_d, writes=[R.tri_b])
    for qt in range(NQ):
        ht, htb = load_ht(kb, R, hT, qt)
        proj_fm(kb, R, ht, htb, D_Q, 64, 0)
        proj_fm(kb, R, ht, htb, D_K, 64, 1)
        for half in range(2):
            rows = slice(half * 64, half * 64 + 64)
            kb.op("act", lambda rows=rows: nc.scalar.activation(out=R.QT[rows, qt * QT_:(qt + 1) * QT_], in_=R.ps[0][0:64, :], func=AF.Copy, scale=0.125),
                  reads=[R.ps_b[0]], writes=[R.QT_b])
            kb.op("dve", lambda rows=rows: nc.vector.tensor_copy(out=R.KT[rows, qt * QT_:(qt + 1) * QT_], in_=R.ps[1][0:64, :]),
                  reads=[R.ps_b[1]], writes=[R.KT_b])
        v_store(kb, R, ht, htb, D_V, qt, 4 + (qt % 2))
    RA, RB, OB = 4, 5, 6
    ed_b = [kb.buf() for _ in range(2)]
    for qt in range(NQ):
        kbs = list(range(4 * qt + 3, -1, -1))
        npair = len(kbs) // 2
        qsl = slice(qt * QT_, (qt + 1) * QT_)

        def blocks(j):
            return kbs[2 * j], kbs[2 * j + 1]

        def z_mm(j):
            for h, kbk in enumerate(blocks(j)):
                zb = 2 * (j % 2) + h
                rows = slice(h * 64, h * 64 + 64)
                kb.op("pe", lambda kbk=kbk, zb=zb, rows=rows: nc.tensor.matmul(R.ps[zb], lhsT=R.KT[rows, kbk * 128:(kbk + 1) * 128],
                                                                               rhs=R.QT[rows, qsl], start=True, stop=True),
                      reads=[R.KT_b, R.QT_b], writes=[R.ps_b[zb]])

        def esp(j):
            zp = j % 2
            e, eb = R.e32[j % 2], ed_b[j % 2]
            sp, spb = R.spp[j % 2], R.spp_b[j % 2]
            kb.op("act", lambda: nc.scalar.activation(out=e[:, :, :], in_=R.pp[zp][:, :, :], func=AF.Exp),
                  reads=[R.ps_b[2 * zp], R.ps_b[2 * zp + 1]], writes=[eb])
            kb.op("act", lambda: nc.scalar.activation(out=sp[:, :, :], in_=e[:, :, :], func=AF.Ln, bias=1.0),
                  reads=[eb], writes=[spb])
            for h, kbk in enumerate(blocks(j)):
                r = kbk - 4 * qt
                if r >= 0:
                    kb.op("dve", lambda h=h, r=r: nc.vector.tensor_tensor(out=sp[:, h, :], in0=sp[:, h, :], in1=R.mask[:, r, :], op=ALU.mult),
                          reads=[spb, R.mask_b], writes=[spb])

        def mmR(bank, t_i, sp_ap, spb, start, stop=False):
            kb.op("pe", lambda: nc.tensor.matmul(R.ps[bank], lhsT=R.tri[:, t_i, :], rhs=sp_ap, start=start, stop=stop),
                  reads=[R.tri_b, spb], writes=[R.ps_b[bank]])

        def chain_a(j):
            sp, spb = R.spp[j % 2], R.spp_b[j % 2]
            mmR(RA, 0, sp[:, 0, :], spb, j == 0)
            mmR(RB, 2, sp[:, 0, :], spb, j == 0)
            mmR(RB, 0, sp[:, 1, :], spb, False)

        def chain_b(j):
            e, eb = R.e32[j % 2], ed_b[j % 2]
            sp, spb = R.spp[j % 2], R.spp_b[j % 2]
            tt, ttb = R.e16[j % 2], R.e16_b[j % 2]
            aa, aab = R.e16[2 + j % 2], R.e16_b[2 + j % 2]
            kb.op("act", lambda: nc.scalar.activation(out=tt[:, :, :], in_=R.pp[2][:, :, :], func=AF.Exp, scale=-1.0),
                  reads=[R.ps_b[RA], R.ps_b[RB]], writes=[ttb])
            mmR(RA, 1, sp[:, 0, :], spb, False)
            mmR(RA, 2, sp[:, 1, :], spb, False, j == npair - 1)
            mmR(RB, 1, sp[:, 1, :], spb, False, j == npair - 1)
            kb.op("dve", lambda: nc.vector.tensor_tensor(out=aa[:, :, :], in0=e[:, :, :], in1=tt[:, :, :], op=ALU.mult),
                  reads=[eb, ttb], writes=[aab])
            for h, kbk in enumerate(blocks(j)):
                r = kbk - 4 * qt
                if r >= 0:
                    kb.op("dve", lambda h=h, r=r: nc.vector.tensor_tensor(out=aa[:, h, :], in0=aa[:, h, :], in1=R.mask[:, r, :], op=ALU.mult),
                          reads=[aab, R.mask_b], writes=[aab])

        def pv(j):
            aa, aab = R.e16[2 + j % 2], R.e16_b[2 + j % 2]
            for h, kbk in enumerate(blocks(j)):
                kb.op("pe", lambda h=h, kbk=kbk: nc.tensor.matmul(R.ps[OB][0:64, :], lhsT=R.Va[:, kbk, 0:64], rhs=aa[:, h, :],
                                                                  start=(j == 0 and h == 0), stop=(j == npair - 1 and h == 1)),
                      reads=[R.Va_b, aab], writes=[R.ps_b[OB]])

        z_mm(0)
        if npair > 1:
            z_mm(1)
        esp(0)
        for j in range(npair):
            if j + 1 < npair:
                esp(j + 1)
            chain_a(j)
            if j + 2 < npair:
                z_mm(j + 2)
            if j >= 1:
                pv(j - 1)
            chain_b(j)
        pv(npair - 1)

        def fn(yo, yob):
            kb.op("dve", lambda: nc.vector.tensor_copy(out=yo[:, :], in_=R.ps[OB][0:64, :]), reads=[R.ps_b[OB]], writes=[yob])
        out_store(kb, R, yT_d, row0, qt, fn, None)


def mixer_a(kb, R, hT, yT_d, row0, biasT_d):
    nc = kb.nc
    S, NB, NQ = R.S, R.NB, R.NQ
    kb.dma("sp", R.EB[:, :, :], biasT_d, writes=[R.EB_b])
    for r in range(8):
        kb.op("act", lambda r=r: nc.scalar.activation(out=R.EB[:, r, :], in_=R.EB[:, r, :], func=AF.Exp),
              reads=[R.EB_b], writes=[R.EB_b])
    kb.op("pool", lambda: nc.gpsimd.memset(R.Va[:, :, 64:128], 1.0), writes=[R.Va_b])
    ones64 = R.cst[0:64, 0:64]
    for qt in range(NQ):
        ht, htb = load_ht(kb, R, hT, qt)
        proj_fm(kb, R, ht, htb, A_Q, 64, 0)
        proj_fm(kb, R, ht, htb, A_K, 64, 2)
        qk_norm_store(kb, R, 0, 1, R.QT, R.QT_b, qt, 4, ones64, 1.0 / 64, 0)
        qk_norm_store(kb, R, 2, 3, R.KT, R.KT_b, qt, 5, ones64, 1.0 / 64, 1)
        v_store(kb, R, ht, htb, A_V, qt, 4 + (qt % 2))
    OB = 4
    for qt in range(NQ):
        rs = [r for r in range(8) if 4 * qt - 4 + r >= 0]
        for j, r in enumerate(rs):
            kbk = 4 * qt - 4 + r
            sb = j % 2
            kb.op("pe", lambda: nc.tensor.matmul(R.ps[sb][:, :], lhsT=R.KT[0:64, kbk * 128:(kbk + 1) * 128],
                                                 rhs=R.QT[0:64, qt * QT_:(qt + 1) * QT_], start=True, stop=True),
                  reads=[R.KT_b, R.QT_b], writes=[R.ps_b[sb]])
            e, eb = R.e32[j % 2], R.e32_b[j % 2]
            p, pbuf = R.e16[j % 2], R.e16_b[j % 2]
            kb.op("act", lambda: nc.scalar.activation(out=e[:, 0, :], in_=R.ps[sb][:, :], func=AF.Exp),
                  reads=[R.ps_b[sb]], writes=[eb])
            kb.op("dve", lambda: nc.vector.tensor_tensor(out=p[:, 0, :], in0=e[:, 0, :], in1=R.EB[:, r, :], op=ALU.mult),
                  reads=[eb, R.EB_b], writes=[pbuf])
            kb.op("pe", lambda: nc.tensor.matmul(R.ps[OB][:, :], lhsT=R.Va[:, kbk, :], rhs=p[:, 0, :],
                                                 start=(j == 0), stop=(j == len(rs) - 1)),
                  reads=[R.Va_b, pbuf], writes=[R.ps_b[OB]])
        f0, b0 = R.fin[0], R.fin_b[0]
        kb.op("dve", lambda: nc.vector.reciprocal(out=f0[:, :], in_=R.ps[OB][64:128, :]), reads=[R.ps_b[OB]], writes=[b0])

        def fn(yo, yob):
            kb.op("dve", lambda: nc.vector.tensor_tensor(out=yo[:, :], in0=R.ps[OB][0:64, :], in1=f0[:, :], op=ALU.mult),
                  reads=[R.ps_b[OB], b0], writes=[yob])
        out_store(kb, R, yT_d, row0, qt, fn, None)


class MixResB:
    def __init__(self, kb, R):
        NB = R.NB
        self.Osig = R.EB[:, :, :].bitcast(BF16).rearrange("p a (b c) -> p (a b) c", c=64)[:, 0:NB, :]; self.Osig_b = R.EB_b
        self.G = kb.sbuf("Gates", [128, 8, NB], F32); self.G_b = kb.buf()
        self.trif = kb.sbuf("trif", [128, 2, 128], F32); self.trif_b = kb.buf()
        self.cw = kb.sbuf("convw", [128, 8], F32); self.cw_b = kb.buf()
        self.gob = kb.sbuf("gob", [128, 64], F32); self.gob_b = kb.buf()
        self.St = [kb.sbuf(f"St{i}", [64, 65], F32) for i in range(2)]; self.St_b = [kb.buf() for _ in range(2)]
        self.Sb = [kb.sbuf(f"Sb{i}", [64, 65], BF16) for i in range(2)]; self.Sb_b = [kb.buf() for _ in range(2)]
        self.tok = [kb.sbuf(f"tok{i}", [128, 3, 64], BF16) for i in range(2)]; self.tok_b = [kb.buf() for _ in range(2)]
        self.qkT = [kb.sbuf(f"qkT{i}", [64, 2, 128], BF16) for i in range(2)]; self.qkT_b = [kb.buf() for _ in range(2)]
        self.qkm = [kb.sbuf(f"qkm{i}", [128, 128], BF16) for i in range(2)]; self.qkm_b = [kb.buf() for _ in range(2)]
        self.cm = kb.sbuf("cmask", [128, 128], F32); self.cm_b = kb.buf()
        self.hn = [kb.sbuf(f"hn{i}", [128, 64], F32) for i in range(2)]; self.hn_b = [kb.buf() for _ in range(2)]
        self.hs = [kb.sbuf(f"hs{i}", [128, 8], F32) for i in range(2)]; self.hs_b = [kb.buf() for _ in range(2)]
        self.yb = [kb.sbuf(f"yb{i}", [128, 64], BF16) for i in range(2)]; self.yb_b = [kb.buf() for _ in range(2)]
        self.jk = kb.sbuf("jk", [128, 64], BF16); self.jk_b = kb.buf()


def mixer_b(kb, R, RB_, hT, yT_d, row0, bpar_d, trif_d, cmask_d, gob_d):
    nc = kb.nc
    S, NB, NQ = R.S, R.NB, R.NQ
    B = RB_
    kb.dma("sp", B.cw[:, :], bpar_d, writes=[B.cw_b])
    kb.dma("sp", B.trif[:, :, :], trif_d, writes=[B.trif_b])
    kb.dma("sp", B.cm[:, :], cmask_d, writes=[B.cm_b])
    kb.dma("sp", B.gob[:, :], gob_d, writes=[B.gob_b])
    kb.op("pool", lambda: nc.gpsimd.memset(R.Va[:, :, 64:65], 1.0), writes=[R.Va_b])
    kb.op("dve", lambda: nc.vector.tensor_scalar(out=B.cw[:, 7:8], in0=B.cw[:, 6:7], scalar1=-1.0, scalar2=None, op0=ALU.mult),
          reads=[B.cw_b], writes=[B.cw_b])
    cv = [R.e32[i][:, :, :].rearrange("p a b -> p (a b)") for i in range(2)]
    cvb = R.e32_b
    accA = R.e16[0][:, :, :].rearrange("p a b -> p (a b)").bitcast(F32); accB_ = R.e16[1][:, :, :].rearrange("p a b -> p (a b)").bitcast(F32)
    accA_b = R.e16_b[0]; accB_b = R.e16_b[1]
    for qt in range(NQ):
        ht, htb = load_ht(kb, R, hT, qt)
        ci = qt % 2
        proj_fm(kb, R, ht, htb, B_Q, 128, 0)
        if qt == 0:
            kb.op("dve", lambda: nc.vector.memset(cv[ci][:, 0:3], 0.0), writes=[cvb[ci]])
        else:
            kb.op("dve", lambda: nc.vector.tensor_copy(out=cv[ci][:, 0:3], in_=cv[1 - ci][:, 512:515]),
                  reads=[cvb[1 - ci]], writes=[cvb[ci]])
        kb.op("act", lambda: nc.scalar.copy(out=cv[ci][:, 3:515], in_=R.ps[0][:, :]), reads=[R.ps_b[0]], writes=[cvb[ci]])
        kb.op("dve", lambda: nc.vector.tensor_scalar(out=accA, in0=cv[ci][:, 3:515], scalar1=B.cw[:, 3:4], scalar2=B.cw[:, 4:5],
                                                     op0=ALU.mult, op1=ALU.add),
              reads=[cvb[ci], B.cw_b], writes=[accA_b])
        for j in (2, 1, 0):
            kb.op("dve", lambda j=j: nc.vector.scalar_tensor_tensor(out=accA, in0=cv[ci][:, j:j + 512], scalar=B.cw[:, j:j + 1],
                                                                    in1=accA, op0=ALU.mult, op1=ALU.add),
                  reads=[cvb[ci], B.cw_b, accA_b], writes=[accA_b])
        kb.op("act", lambda: nc.scalar.activation(out=accB_, in_=accA, func=AF.Sigmoid), reads=[accA_b], writes=[accB_b])
        kb.op("dve", lambda: nc.vector.tensor_tensor(out=accB_, in0=accA, in1=accB_, op=ALU.mult), reads=[accA_b, accB_b], writes=[accB_b])
        kb.op("act", lambda: nc.scalar.copy(out=R.QT[0:64, qt * QT_:(qt + 1) * QT_], in_=accB_[0:64, :]), reads=[accB_b], writes=[R.QT_b])
        kb.op("act", lambda: nc.scalar.copy(out=R.KT[0:64, qt * QT_:(qt + 1) * QT_], in_=accB_[64:128, :]), reads=[accB_b], writes=[R.KT_b])
        pb = 4 + (qt % 2)
        for s in range(4):
            nonlocal_pb = 3 + ((qt * 4 + s) % 4)
            for kc in range(8):
                kb.op("pe", lambda kc=kc: nc.tensor.matmul(R.ps[nonlocal_pb][:, 0:130], lhsT=ht[:, kc, s * 128:(s + 1) * 128],
                                                           rhs=R.Wm[:, kc, B_V:B_V + 130], start=(kc == 0), stop=(kc == 7)),
                      reads=[R.Wm_b, htb], writes=[R.ps_b[nonlocal_pb]], inc=(kc == 7))
            blk = qt * 4 + s
            kb.op("dve", lambda: nc.vector.tensor_copy(out=R.Va[:, blk, 0:64], in_=R.ps[nonlocal_pb][:, 0:64]),
                  reads=[R.ps_b[nonlocal_pb]], writes=[R.Va_b])
            kb.op("act", lambda: nc.scalar.activation(out=B.Osig[:, blk, :], in_=R.ps[nonlocal_pb][:, 64:128], func=AF.Sigmoid),
                  reads=[R.ps_b[nonlocal_pb]], writes=[B.Osig_b])
            kb.op("dve", lambda: nc.vector.tensor_copy(out=B.G[:, 0:2, blk], in_=R.ps[nonlocal_pb][:, 128:130]),
                  reads=[R.ps_b[nonlocal_pb]], writes=[B.G_b])
    G = B.G
    kb.op("act", lambda: nc.scalar.activation(out=G[:, 2, :], in_=G[:, 1, :], func=AF.Exp, scale=-1.0, bias=B.cw[:, 7:8]),
          reads=[B.G_b, B.cw_b], writes=[B.G_b])
    kb.op("act", lambda: nc.scalar.activation(out=G[:, 2, :], in_=G[:, 2, :], func=AF.Ln, bias=1.0), reads=[B.G_b], writes=[B.G_b])
    kb.op("dve", lambda: nc.vector.tensor_scalar(out=G[:, 2, :], in0=G[:, 2, :], scalar1=-1.0, scalar2=None, op0=ALU.mult),
          reads=[B.G_b], writes=[B.G_b])
    kb.op("pe", lambda: nc.tensor.matmul(R.ps[0][:, 0:NB], lhsT=B.trif[:, 0, :], rhs=G[:, 2, :], start=True, stop=True),
          reads=[B.trif_b, B.G_b], writes=[R.ps_b[0]])
    kb.op("pe", lambda: nc.tensor.matmul(R.ps[1][:, 0:NB], lhsT=B.trif[:, 1, :], rhs=G[:, 2, :], start=True, stop=True),
          reads=[B.trif_b, B.G_b], writes=[R.ps_b[1]])
    kb.op("dve", lambda: nc.vector.tensor_copy(out=G[:, 3, :], in_=R.ps[0][:, 0:NB]), reads=[R.ps_b[0]], writes=[B.G_b])
    kb.op("act", lambda: nc.scalar.activation(out=G[:, 4, :], in_=G[:, 3, :], func=AF.Exp), reads=[B.G_b], writes=[B.G_b])
    kb.op("dve", lambda: nc.vector.tensor_tensor(out=G[:, 5, :], in0=G[:, 0, :], in1=G[:, 3, :], op=ALU.subtract),
          reads=[B.G_b], writes=[B.G_b])
    kb.op("dve", lambda: nc.vector.tensor_tensor(out=G[:, 6, :], in0=G[:, 5, :], in1=R.ps[1][:, 0:NB], op=ALU.add),
          reads=[B.G_b, R.ps_b[1]], writes=[B.G_b])
    kb.op("act", lambda: nc.scalar.activation(out=G[:, 5, :], in_=G[:, 5, :], func=AF.Exp, bias=B.cw[:, 5:6]),
          reads=[B.G_b, B.cw_b], writes=[B.G_b])
    kb.op("act", lambda: nc.scalar.activation(out=G[:, 6, :], in_=G[:, 6, :], func=AF.Exp, bias=B.cw[:, 5:6]),
          reads=[B.G_b, B.cw_b], writes=[B.G_b])
    kb.op("dve", lambda: nc.vector.tensor_scalar(out=G[:, 5:7, :], in0=G[:, 5:7, :], scalar1=0.125, scalar2=None, op0=ALU.mult),
          reads=[B.G_b], writes=[B.G_b])
    kb.op("act", lambda: nc.scalar.activation(out=G[:, 7, :], in_=R.ps[1][:, 0:NB], func=AF.Exp), reads=[R.ps_b[1]], writes=[B.G_b])
    kb.op("dve", lambda: nc.vector.memset(B.St[0][:, :], 0.0), writes=[B.St_b[0]])
    kb.op("dve", lambda: nc.vector.memset(B.Sb[0][:, :], 0.0), writes=[B.Sb_b[0]])
    PT, PT2, PS_, PO, PU = 0, 1, 2, 3, 6
    for b in range(NB):
        i2 = b % 2
        tok, tokb = B.tok[i2], B.tok_b[i2]
        qkT, qkTb = B.qkT[i2], B.qkT_b[i2]
        tpA = R.ps[0][:, :].bitcast(BF16); tpq = tpA[:, 0:128]; tpk = tpA[:, 128:256]
        kb.op("pe", lambda: nc.tensor.transpose(out=tpq[:, 0:64], in_=R.QT[0:64, b * 128:(b + 1) * 128], identity=R.ident[0:64, 0:64]),
              reads=[R.QT_b, R.ident_b], writes=[R.ps_b[0]])
        kb.op("pe", lambda: nc.tensor.transpose(out=tpk[:, 0:64], in_=R.KT[0:64, b * 128:(b + 1) * 128], identity=R.ident[0:64, 0:64]),
              reads=[R.KT_b, R.ident_b], writes=[R.ps_b[0]])
        kb.op("dve", lambda: nc.vector.tensor_scalar(out=tok[:, 0, :], in0=tpq[:, 0:64], scalar1=G[:, 4, b:b + 1], scalar2=None, op0=ALU.mult),
              reads=[R.ps_b[0], B.G_b], writes=[tokb])
        kb.op("dve", lambda: nc.vector.tensor_scalar(out=tok[:, 1, :], in0=tpk[:, 0:64], scalar1=G[:, 5, b:b + 1], scalar2=None, op0=ALU.mult),
              reads=[R.ps_b[0], B.G_b], writes=[tokb])
        kb.op("dve", lambda: nc.vector.tensor_scalar(out=tok[:, 2, :], in0=tpk[:, 0:64], scalar1=G[:, 6, b:b + 1], scalar2=None, op0=ALU.mult),
              reads=[R.ps_b[0], B.G_b], writes=[tokb])
        tpB = R.ps[1][:, :].bitcast(BF16); tq2 = tpB[:, 0:128]; tk2 = tpB[:, 128:256]
        kb.op("pe", lambda: nc.tensor.transpose(out=tq2[0:64, :], in_=tok[:, 0, :], identity=R.ident[:, :]),
              reads=[tokb, R.ident_b], writes=[R.ps_b[1]])
        kb.op("pe", lambda: nc.tensor.transpose(out=tk2[0:64, :], in_=tok[:, 1, :], identity=R.ident[:, :]),
              reads=[tokb, R.ident_b], writes=[R.ps_b[1]])
        kb.op("act", lambda: nc.scalar.copy(out=qkT[:, 0, :], in_=tq2[0:64, :]), reads=[R.ps_b[1]], writes=[qkTb])
        kb.op("act", lambda: nc.scalar.copy(out=qkT[:, 1, :], in_=tk2[0:64, :]), reads=[R.ps_b[1]], writes=[qkTb])
        kb.op("pe", lambda: nc.tensor.matmul(R.ps[PS_][:, 0:128], lhsT=qkT[:, 1, :], rhs=qkT[:, 0, :], start=True, stop=True),
              reads=[qkTb], writes=[R.ps_b[PS_]])
        qkm, qkmb = B.qkm[i2], B.qkm_b[i2]
        kb.op("dve", lambda: nc.vector.tensor_tensor(out=qkm[:, :], in0=R.ps[PS_][:, 0:128], in1=B.cm[:, :], op=ALU.mult),
              reads=[R.ps_b[PS_], B.cm_b], writes=[qkmb])
        Sp, Spb = B.Sb[i2], B.Sb_b[i2]
        po = PO + (b % 2)
        kb.op("pe", lambda: nc.tensor.matmul(R.ps[po][:, 0:65], lhsT=qkm[:, :], rhs=R.Va[:, b, 0:65], start=True, stop=False),
              reads=[qkmb, R.Va_b], writes=[R.ps_b[po]], inc=False)
        kb.op("pe", lambda: nc.tensor.matmul(R.ps[po][:, 0:65], lhsT=qkT[:, 0, :], rhs=Sp[:, :], start=False, stop=True),
              reads=[qkTb, Spb], writes=[R.ps_b[po]])
        kb.op("pe", lambda: nc.tensor.matmul(R.ps[PU][0:64, 0:65], lhsT=tok[:, 2, :], rhs=R.Va[:, b, 0:65], start=True, stop=True),
              reads=[tokb, R.Va_b], writes=[R.ps_b[PU]])
        Sn, Snb = B.St[1 - i2], B.St_b[1 - i2]
        So, Sob = B.St[i2], B.St_b[i2]
        kb.op("dve", lambda: nc.vector.scalar_tensor_tensor(out=Sn[:, :], in0=So[:, :], scalar=G[0:64, 7, b:b + 1], in1=R.ps[PU][0:64, 0:65],
                                                            op0=ALU.mult, op1=ALU.add),
              reads=[Sob, B.G_b, R.ps_b[PU]], writes=[Snb])
        kb.op("act", lambda: nc.scalar.copy(out=B.Sb[1 - i2][:, :], in_=Sn[:, :]), reads=[Snb], writes=[B.Sb_b[1 - i2]])
        hs, hsb = B.hs[i2], B.hs_b[i2]
        hn, hnb = B.hn[i2], B.hn_b[i2]
        kb.op("act", lambda: nc.scalar.activation(out=hs[:, 5:6], in_=R.ps[po][:, 64:65], func=AF.Abs),
              reads=[R.ps_b[po]], writes=[hsb])
        kb.op("dve", lambda: nc.vector.tensor_scalar(out=hs[:, 0:1], in0=hs[:, 5:6], scalar1=1.0, scalar2=None, op0=ALU.max),
              reads=[hsb], writes=[hsb])
        kb.op("dve", lambda: nc.vector.reciprocal(out=hs[:, 1:2], in_=hs[:, 0:1]), reads=[hsb], writes=[hsb])
        kb.op("dve", lambda: nc.vector.tensor_scalar(out=hn[:, :], in0=R.ps[po][:, 0:64], scalar1=hs[:, 1:2], scalar2=None, op0=ALU.mult),
              reads=[R.ps_b[po], hsb], writes=[hnb])
        kb.op("act", lambda: nc.scalar.activation(out=B.jk[:, :], in_=hn[:, :], func=AF.Square, accum_out=hs[:, 2:3]),
              reads=[hnb], writes=[B.jk_b, hsb])
        kb.op("act", lambda: nc.scalar.activation(out=hs[:, 3:4], in_=hs[:, 2:3], func=AF.Sqrt, bias=EPS, scale=1.0 / 64),
              reads=[hsb], writes=[hsb])
        kb.op("dve", lambda: nc.vector.reciprocal(out=hs[:, 4:5], in_=hs[:, 3:4]), reads=[hsb], writes=[hsb])
        kb.op("dve", lambda: nc.vector.scalar_tensor_tensor(out=hn[:, :], in0=hn[:, :], scalar=hs[:, 4:5], in1=B.gob[:, :],
                                                            op0=ALU.mult, op1=ALU.mult),
              reads=[hnb, hsb, B.gob_b], writes=[hnb])
        yb, ybb = B.yb[i2], B.yb_b[i2]
        kb.op("dve", lambda: nc.vector.tensor_tensor(out=yb[:, :], in0=hn[:, :], in1=B.Osig[:, b, :], op=ALU.mult),
              reads=[hnb, B.Osig_b], writes=[ybb])
        ty = R.ps[5][:, :].bitcast(BF16)[:, 0:128]
        kb.op("pe", lambda: nc.tensor.transpose(out=ty[0:64, :], in_=yb[:, :], identity=R.ident[:, :]),
              reads=[ybb, R.ident_b], writes=[R.ps_b[5]])
        qt = b // 4
        if b % 4 == 0:
            R.cur_yo = R.nyo % 2; R.nyo += 1
        yo, yob = R.yo[R.cur_yo], R.yo_b[R.cur_yo]
        kb.op("act", lambda: nc.scalar.copy(out=yo[:, (b % 4) * 128:(b % 4 + 1) * 128], in_=ty[0:64, :]),
              reads=[R.ps_b[5]], writes=[yob])
        if b % 4 == 3:
            kb.dma("pool", ydst(yT_d, row0, qt), yo[:, :], reads=[yob])


def mix_params(kb, R, praw_d, clam_d):
    nc = kb.nc
    pr = R.par
    kb.dma("sp", pr[0:64, 16:24], praw_d, writes=[R.par_b])
    cl = R.rr[0][:, 0:128].rearrange("p (a b) -> p a b", a=4)
    kb.dma("sp", cl, clam_d, writes=[R.rr_b[0]])
    V = nc.vector
    kb.op("dve", lambda: V.tensor_scalar(out=pr[0:64, 0:1], in0=pr[0:64, 16:17], scalar1=32 ** -0.5, scalar2=None, op0=ALU.mult), reads=[R.par_b], writes=[R.par_b])
    kb.op("dve", lambda: V.tensor_copy(out=pr[0:64, 1:2], in_=pr[0:64, 17:18]), reads=[R.par_b], writes=[R.par_b])
    kb.op("dve", lambda: V.tensor_tensor(out=pr[0:64, 3:4], in0=pr[0:64, 18:19], in1=pr[0:64, 22:23], op=ALU.mult), reads=[R.par_b], writes=[R.par_b])
    kb.op("dve", lambda: V.tensor_scalar(out=pr[0:64, 4:5], in0=pr[0:64, 19:20], scalar1=0.125, scalar2=None, op0=ALU.mult), reads=[R.par_b], writes=[R.par_b])
    kb.op("dve", lambda: V.tensor_copy(out=pr[0:64, 5:6], in_=pr[0:64, 20:21]), reads=[R.par_b], writes=[R.par_b])
    pp = R.rr[1][:, 0:64].rearrange("p (a b) -> p a b", a=2)
    kb.op("dve", lambda: V.tensor_tensor(out=pp[:, 0, :], in0=cl[:, 0, :], in1=cl[:, 1, :], op=ALU.mult), reads=[R.rr_b[0]], writes=[R.rr_b[1]])
    kb.op("dve", lambda: V.tensor_tensor(out=pp[:, 1, :], in0=cl[:, 2, :], in1=cl[:, 3, :], op=ALU.mult), reads=[R.rr_b[0]], writes=[R.rr_b[1]])
    kb.op("dve", lambda: V.reduce_sum(out=pr[0:64, 8:10], in_=pp, axis=AX.X), reads=[R.rr_b[1]], writes=[R.par_b])
    kb.op("act", lambda: nc.scalar.activation(out=pr[0:64, 10:12], in_=pr[0:64, 8:10], func=AF.Exp), reads=[R.par_b], writes=[R.par_b])
    kb.op("dve", lambda: V.tensor_tensor(out=pr[0:64, 12:13], in0=pr[0:64, 11:12], in1=pr[0:64, 10:11], op=ALU.subtract), reads=[R.par_b], writes=[R.par_b])
    kb.op("dve", lambda: V.tensor_tensor(out=pr[0:64, 2:3], in0=pr[0:64, 12:13], in1=pr[0:64, 21:22], op=ALU.subtract), reads=[R.par_b], writes=[R.par_b])

import ml_dtypes
bf16 = ml_dtypes.bfloat16
GW = 256
OFF = dict(aq=0, ak=256, av=512, bqk=768, bv=1280, bo=1536, bi=1792, bf=1796, cq=1800, ck=2056, cv=2312, dq=2568, dk=2824, dv=3080)

def sel_cols(j):
    c = []
    r = lambda o: list(range(o + j * 64, o + j * 64 + 64))
    c += r(OFF['aq']) + r(OFF['ak']) + r(OFF['av'])
    c += r(OFF['bqk']) + r(OFF['bqk'] + 256) + r(OFF['bv']) + r(OFF['bo']) + [OFF['bi'] + j, OFF['bf'] + j]
    c += r(OFF['cq']) + r(OFF['ck']) + r(OFF['cv'])
    c += r(OFF['dq']) + r(OFF['dk']) + r(OFF['dv'])
    return np.array(c)

def const_inputs():
    d = {}
    d['ident'] = np.eye(128, dtype=np.float32).astype(bf16)
    cst = np.zeros((128, 256), np.float32)
    cst[0:64, 0:64] = 1.0
    cst[0:32, 64:96] = 1.0; cst[32:64, 96:128] = 1.0
    d['cst'] = cst.astype(bf16)
    s = np.arange(128)[:, None, None, None]; r = np.arange(4)[None, :, None, None]; t = np.arange(512)[None, None, None, :]
    mc = ((2 * r + (s >= 64)) <= (t // 64)).astype(np.float32)
    d['masks_c'] = np.broadcast_to(mc, (128, 4, 2, 512)).astype(bf16).copy()
    s = np.arange(128)[:, None, None]; r = np.arange(4)[None, :, None]; t = np.arange(512)[None, None, :]
    d['masks_d'] = ((128 * r + s) < t).astype(np.float32).astype(bf16)
    j = np.arange(128)[:, None]; s2 = np.arange(128)[None, :]
    tri = np.zeros((128, 3, 128), np.float32)
    tri[:, 0, :] = (j >= s2); tri[:, 1, :] = (j < s2); tri[:, 2, :] = 1.0
    d['tri'] = tri.astype(bf16)
    trif = np.zeros((128, 2, 128), np.float32)
    trif[:, 0, :] = (j <= s2); trif[:, 1, :] = 1.0
    d['trif'] = trif
    d['cmask'] = (j <= s2).astype(np.float32)
    return d

def bias_index():
    s = np.arange(128)[:, None, None]; r = np.arange(8)[None, :, None]; t = np.arange(512)[None, None, :]
    rel = t - s + 512 - 128 * r
    idx = np.clip(rel, -128, 128) + 128
    dd = t // 64 + 8 - 2 * r - s // 64
    vis = (dd >= 0) & (dd <= 8)
    return idx, vis

_IDX, _VIS = bias_index()

def layer_core_inputs(P, l, j, lam_init=None):
    d = {}
    d['wsel'] = np.ascontiguousarray(P['w_in'][l][:, sel_cols(j)])
    d['gmix'] = np.ascontiguousarray(P['mix_norm'][l].reshape(8, 128).T)
    praw = np.zeros((64, 8), np.float32)
    praw[:, 0] = np.tile(P['c_q_norm'][l], 2); praw[:, 1] = np.tile(P['c_k_norm'][l], 2)
    praw[:, 2] = P['c_out_norm'][l]; praw[:, 3] = P['a_q_norm'][l]; praw[:, 4] = P['a_k_norm'][l]
    if lam_init is None:
        lam_init = 0.8 - 0.6 * np.exp(-0.3 * l)
    praw[:, 5] = lam_init; praw[:, 6] = 1.0 - lam_init
    d['praw'] = praw
    d['clam'] = np.ascontiguousarray(np.broadcast_to(P['c_lambda'][l][None], (64, 4, 32))).astype(np.float32)
    rb = P['a_rel_bias'][l][j]
    d['biasT'] = np.where(_VIS, rb[_IDX], np.float32(-1e30)).astype(np.float32)
    bpar = np.zeros((128, 8), np.float32)
    ch = np.concatenate([np.arange(j * 64, j * 64 + 64), 256 + np.arange(j * 64, j * 64 + 64)])
    bpar[:, 0:4] = P['b_conv_w'][l][:, ch].T
    bpar[:, 4] = P['b_conv_b'][l][ch]
    bpar[:, 5] = P['b_gate_bias'][l][0, j]
    bpar[:, 6] = P['b_gate_bias'][l][1, j]
    d['bpar'] = bpar
    d['gob'] = np.ascontiguousarray(np.broadcast_to(P['b_out_norm'][l][j][None], (128, 64))).astype(np.float32)
    return d


from concourse.bass_utils import run_bass_kernel_spmd

SEQ = 16384
NCORE = 8
TPC = 4096
DEPTH = 2
GROUPS = [[0, 1, 2, 3], [4, 5, 6, 7]]


def _din(nc, name, shape, dt):
    return nc.dram_tensor(name, list(shape), dt, kind="ExternalInput").ap()


def _dout(nc, name, shape, dt):
    return nc.dram_tensor(name, list(shape), dt, kind="ExternalOutput").ap()


def _dint(nc, name, shape, dt):
    return nc.dram_tensor(name, list(shape), dt, kind="Internal").ap()


MIX_IN = dict(wsel=([D, NW], F32), gmix=([128, 8], F32), praw=([64, 8], F32), clam=([64, 4, 32], F32),
              biasT=([128, 8, 512], F32), bpar=([128, 8], F32), gob=([128, 64], F32))
CONST_IN = dict(ident=([128, 128], BF16), cst=([128, 256], BF16), masks_c=([128, 4, 2, 512], BF16),
                masks_d=([128, 4, 512], BF16), tri=([128, 3, 128], BF16), trif=([128, 2, 128], F32), cmask=([128, 128], F32))


def build_fused(S=SEQ, T=TPC):
    nc = bass.Bass("TRN2", target_bir_lowering=False)
    NQr = T // QT_
    x_in = _din(nc, "x_in", [T, D], F32)
    x_out = _dout(nc, "x_out", [T, D], F32)
    Cn = {k: _din(nc, k, sh, dt) for k, (sh, dt) in CONST_IN.items()}
    ffn = {}
    for l in range(DEPTH):
        for f in ("ffn1", "ffn2"):
            ffn[(f, l)] = dict(g=_din(nc, f"{f}_g{l}", [128, 8], F32), wg=_din(nc, f"{f}_wg{l}", [D, DFF], F32),
                               wu=_din(nc, f"{f}_wu{l}", [D, DFF], F32), wd=_din(nc, f"{f}_wd{l}", [DFF, D], F32))
    wo = [_din(nc, f"wo{l}", [D, D], F32) for l in range(DEPTH)]
    mx = [{k: _din(nc, f"{k}{l}", sh, dt) for k, (sh, dt) in MIX_IN.items()} for l in range(DEPTH)]
    xa = _dint(nc, "xa", [T, D], F32); xb = _dint(nc, "xb", [T, D], F32); xc = _dint(nc, "xc", [T, D], F32)
    hT_loc = _dint(nc, "hT_loc", [D, T], BF16)
    hT_all = _dint(nc, "hT_all", [4 * D, T], BF16)
    yT_loc = _dint(nc, "yT_loc", [D, T], BF16)
    yT_all = _dint(nc, "yT_all", [4 * D, T], BF16)
    yT_mine = _dint(nc, "yT_mine", [D, T], BF16)

    hv = hT_all.rearrange("(k r p) t -> k r p t", k=8, r=4)

    def hT_src(qt):
        r, o = qt // NQr, (qt % NQr) * QT_
        return hv[:, r, :, o:o + QT_]

    def y_dst(row0, qt):
        q, o = qt // NQr, (qt % NQr) * QT_
        return yT_loc[q * 256 + row0:q * 256 + row0 + 64, o:o + QT_]

    with ExitStack() as st:
        kb = KB(nc, st)
        pid = nc.sync.partition_id()
        qv = pid % 4

        ymine_b = kb.buf()

        def fetch_mine():
            yv2 = yT_all.rearrange("(q h j p) t -> q h j p t", q=4, h=2, j=4)
            for j in range(4):
                for h in range(2):
                    kb.dma("sp", yT_mine[j * 256 + h * 128:j * 256 + (h + 1) * 128, :],
                           yv2[bass.ds(qv, 1), h, j, :, :].rearrange("o p t -> (o p) t"), writes=[ymine_b])

        def tok_phase(passes):
            with ExitStack() as mem:
                kb.mem = mem
                R = TokRes(kb, any(p.get("wo") is not None for p in passes))
                load_consts(kb, R, Cn["ident"])
                prev_bufs = None
                for k, p in enumerate(passes):
                    w = p["ffn"]
                    load_ffn_weights(kb, R, w["g"], w["wg"], w["wu"], w["wd"], p.get("wo"))
                    ob = [kb.buf() for _ in range(T // TT)] if k + 1 < len(passes) else None
                    has_pre = p.get("wo") is not None
                    token_pass(kb, R, T, p["xi"], p["xo"], pre=(yT_mine if has_pre else None),
                               post=p.get("post"), in_bufs=prev_bufs, out_bufs=ob,
                               pre_bufs=([ymine_b] * (T // TT) if has_pre else None))
                    prev_bufs = ob
                kb.barrier()
            kb.mem = st

        def mix_phase(l):
            with ExitStack() as mem:
                kb.mem = mem
                R = MixRes(kb, S)
                RB = MixResB(kb, R)
                m = mx[l]
                mix_load_common(kb, R, m["wsel"], m["gmix"], Cn["ident"], Cn["cst"])
                mix_params(kb, R, m["praw"], m["clam"])
                mixer_a(kb, R, hT_src, y_dst, 0, m["biasT"])
                kb.barrier()
                mixer_b(kb, R, RB, hT_src, y_dst, 64, m["bpar"], Cn["trif"], Cn["cmask"], m["gob"])
                kb.barrier()
                mixer_c(kb, R, hT_src, y_dst, 128, Cn["masks_c"])
                kb.barrier()
                mixer_d(kb, R, hT_src, y_dst, 192, Cn["masks_d"], Cn["tri"])
                kb.barrier()
            kb.mem = st

        tok_phase([dict(ffn=ffn[("ffn1", 0)], xi=x_in, xo=xa, post=hT_loc)])
        kb.allgather(hT_loc, hT_all, GROUPS)
        mix_phase(0)
        kb.allgather(yT_loc, yT_all, GROUPS)
        fetch_mine()
        tok_phase([dict(ffn=ffn[("ffn2", 0)], wo=wo[0], xi=xa, xo=xb),
                   dict(ffn=ffn[("ffn1", 1)], xi=xb, xo=xc, post=hT_loc)])
        kb.allgather(hT_loc, hT_all, GROUPS)
        mix_phase(1)
        kb.allgather(yT_loc, yT_all, GROUPS)
        fetch_mine()
        tok_phase([dict(ffn=ffn[("ffn2", 1)], wo=wo[1], xi=xc, xo=x_out)])
        kb.finish()
    return nc


def build_mixer_prog(S=SEQ):
    nc = bass.Bass("TRN2", target_bir_lowering=False)
    hT = _din(nc, "hT", [D, S], BF16)
    Cn = {k: _din(nc, k, sh, dt) for k, (sh, dt) in CONST_IN.items()}
    m = {k: _din(nc, k, sh, dt) for k, (sh, dt) in MIX_IN.items()}
    yT = _dout(nc, "yT", [256, S], BF16)
    with ExitStack() as st:
        kb = KB(nc, st)
        R = MixRes(kb, S)
        RB = MixResB(kb, R)
        mix_load_common(kb, R, m["wsel"], m["gmix"], Cn["ident"], Cn["cst"])
        mix_params(kb, R, m["praw"], m["clam"])
        mixer_a(kb, R, hT, yT, 0, m["biasT"])
        kb.barrier()
        mixer_b(kb, R, RB, hT, yT, 64, m["bpar"], Cn["trif"], Cn["cmask"], m["gob"])
        kb.barrier()
        mixer_c(kb, R, hT, yT, 128, Cn["masks_c"])
        kb.barrier()
        mixer_d(kb, R, hT, yT, 192, Cn["masks_d"], Cn["tri"])
        kb.finish()
    return nc


def _lay(g):
    return np.ascontiguousarray(np.asarray(g, np.float32).reshape(8, 128).T)


def _wo_perm(w_out):
    idx = np.arange(1024).reshape(4, 4, 64)
    perm = idx.transpose(1, 0, 2).reshape(-1)
    return np.ascontiguousarray(w_out[perm, :])


def make_in_maps(P, TPC=TPC):
    x = np.ascontiguousarray(P["x"], dtype=np.float32).reshape(-1, D)
    C = const_inputs()
    shared = dict(C)
    for l in range(DEPTH):
        for f in ("ffn1", "ffn2"):
            shared[f"{f}_g{l}"] = _lay(P[f + "_norm"][l])
            shared[f"{f}_wg{l}"] = np.ascontiguousarray(P[f + "_wg"][l], dtype=np.float32)
            shared[f"{f}_wu{l}"] = np.ascontiguousarray(P[f + "_wu"][l], dtype=np.float32)
            shared[f"{f}_wd{l}"] = np.ascontiguousarray(P[f + "_wd"][l], dtype=np.float32)
        shared[f"wo{l}"] = _wo_perm(np.asarray(P["w_out"][l], np.float32))
    ims = []
    for c in range(NCORE):
        d = dict(shared)
        d["x_in"] = x[c * TPC:(c + 1) * TPC]
        j = c % 4
        for l in range(DEPTH):
            for k, v in layer_core_inputs(P, l, j).items():
                d[f"{k}{l}"] = v
        ims.append(d)
    return ims


def kernel(**inputs):
    P = {k: np.asarray(v) for k, v in inputs.items()}
    nc = build_fused()
    ims = make_in_maps(P)
    res = run_bass_kernel_spmd(nc, ims, core_ids=list(range(NCORE)))
    out = np.concatenate([r["x_out"] for r in res.results], axis=0).reshape(2, SEQ, D).astype(np.float32)
    return out
```

```python
import numpy as np
from contextlib import ExitStack
import concourse.bass as bass
import concourse.mybir as mybir

F32 = mybir.dt.float32
BF16 = mybir.dt.bfloat16
AF = mybir.ActivationFunctionType
ALU = mybir.AluOpType
AX = mybir.AxisListType

EPOCH = 4096


class Buf:
    __slots__ = ("w", "r", "name")

    def __init__(self, name=""):
        self.w = None
        self.r = {}
        self.name = name


class KB:
    def __init__(self, nc, stack):
        self.nc = nc
        self.st = stack
        self.E = {"pe": nc.tensor, "act": nc.scalar, "dve": nc.vector, "pool": nc.gpsimd, "sp": nc.sync}
        self.cnt = {e: 0 for e in self.E}
        self.sems = {e: [] for e in self.E}
        self.waited = {e: {} for e in self.E}
        self.ndma = 12
        self.dma_sems = {}
        self.dma_cnt = {}
        self.dma_rr = {}
        self.nsem = 0
        self.uid = 0
        self.mem = stack

    def sem(self, name):
        self.nsem += 1
        return self.st.enter_context(self.nc.semaphore(name))

    def sbuf(self, name, shape, dt):
        self.uid += 1
        return self.mem.enter_context(self.nc.sbuf_tensor(f"sb{self.uid}_" + name, list(shape), dt))

    def psum(self, name, shape, dt):
        self.uid += 1
        return self.mem.enter_context(self.nc.psum_tensor(f"ps{self.uid}_" + name, list(shape), dt))

    def buf(self, name=""):
        return Buf(name)

    def _esem(self, e, n):
        ep = (n - 1) // EPOCH
        while len(self.sems[e]) <= ep:
            self.sems[e].append(self.sem(f"c_{e}_{len(self.sems[e])}"))
        return self.sems[e][ep], (n - 1) % EPOCH + 1

    def _wait(self, e, ev):
        if ev[0] == "e":
            _, src, n = ev
            if src == e and e == "pe":
                return
            key = ("e", src)
            if self.waited[e].get(key, 0) >= n:
                return
            if src == e and n > self.cnt[e]:
                raise RuntimeError("self-wait on future event")
            s, v = self._esem(src, n)
            self.E[e].wait_ge(s, v)
            self.waited[e][key] = n
        else:
            _, q, i, k = ev
            key = ("d", q, i)
            if self.waited[e].get(key, 0) >= k:
                return
            self.E[e].wait_ge(self.dma_sems[q][i], 16 * k)
            self.waited[e][key] = k

    @staticmethod
    def _evkey(ev):
        return (ev[0], ev[1]) if ev[0] == "e" else (ev[0], ev[1], ev[2])

    def _collect(self, reads, writes):
        deps = []
        for b in reads:
            if b.w is not None:
                deps.append(b.w)
        for b in writes:
            if b.w is not None:
                deps.append(b.w)
            deps.extend(b.r.values())
        return deps

    def _record(self, ev, reads, writes):
        k = self._evkey(ev)
        for b in reads:
            b.r[k] = ev
        for b in writes:
            b.w = ev
            b.r = {}

    def op(self, e, fn, reads=(), writes=(), inc=True):
        for ev in self._collect(reads, writes):
            self._wait(e, ev)
        ins = fn()
        if inc:
            self.cnt[e] += 1
            s, v = self._esem(e, self.cnt[e])
            ins.then_inc(s, 1)
            ev = ("e", e, self.cnt[e])
        else:
            ev = ("e", e, self.cnt[e] + 1)
        self._record(ev, reads, writes)
        return ins

    def dma(self, q, out, in_, reads=(), writes=(), **kw):
        for ev in self._collect(reads, writes):
            self._wait(q, ev)
        if q not in self.dma_sems:
            self.dma_sems[q] = [self.sem(f"d_{q}_{i}") for i in range(self.ndma)]
            self.dma_cnt[q] = [0] * self.ndma
            self.dma_rr[q] = 0
        i = self.dma_rr[q]
        self.dma_rr[q] = (i + 1) % self.ndma
        if self.dma_cnt[q][i] > 0:
            self._wait(q, ("d", q, i, self.dma_cnt[q][i]))
        self.dma_cnt[q][i] += 1
        ins = self.E[q].dma_start(out=out, in_=in_, **kw)
        ins.then_inc(self.dma_sems[q][i], 16)
        ev = ("d", q, i, self.dma_cnt[q][i])
        self._record(ev, reads, writes)
        return ins

    def barrier(self, extra_sems=()):
        for e in self.E:
            for q in self.dma_sems:
                for i in range(self.ndma):
                    if self.dma_cnt[q][i] > 0:
                        self._wait(e, ("d", q, i, self.dma_cnt[q][i]))
            for src in ("pe", "act", "dve", "pool"):
                if self.cnt[src] > 0 and not (src == e and e == "pe"):
                    self._wait(e, ("e", src, self.cnt[src]))
            for (sm, v) in extra_sems:
                self.E[e].wait_ge(sm, v)

    def allgather(self, src2d, dst2d, groups, chunk_rows=128):
        self.barrier()
        R_ = src2d.shape[0]
        nk = R_ // chunk_rows
        ng = len(groups[0])
        if not hasattr(self, "cc_sem"):
            self.cc_sem = self.sem("ccsem")
            self.cc_cnt = 0
        for k in range(nk):
            self.nc.gpsimd.collective_compute("AllGather", ALU.bypass, replica_groups=groups,
                                              ins=[src2d[k * chunk_rows:(k + 1) * chunk_rows, :]],
                                              outs=[dst2d[k * ng * chunk_rows:(k + 1) * ng * chunk_rows, :]]).then_inc(self.cc_sem, 1)
            self.cc_cnt += 1
        for e in self.E:
            self.E[e].wait_ge(self.cc_sem, self.cc_cnt)

    def finish(self):
        for q in self.dma_sems:
            for i in range(self.ndma):
                if self.dma_cnt[q][i] > 0:
                    self._wait("sp", ("d", q, i, self.dma_cnt[q][i]))
        for e in ("pe", "act", "dve", "pool"):
            if self.cnt[e] > 0:
                self._wait("sp", ("e", e, self.cnt[e]))


D = 1024
DFF = 2816
NFC = DFF // 128
TT = 256
SUB = TT // 128
EPS = 1e-6


class TokRes:
    def __init__(self, kb, with_pre):
        nc = kb.nc
        self.kb = kb
        self.Wg = kb.sbuf("Wg", [128, 8, DFF], BF16); self.Wg_b = kb.buf()
        self.Wu = kb.sbuf("Wu", [128, 8, DFF], BF16); self.Wu_b = kb.buf()
        self.Wd = kb.sbuf("Wd", [128, NFC, D], BF16); self.Wd_b = kb.buf()
        self.stage = [kb.sbuf(f"stage{i}", [128, 1024], F32) for i in range(2)]
        self.stage_b = [kb.buf() for _ in range(2)]
        self.gt = kb.sbuf("gt", [128, 8], F32); self.gt_b = kb.buf()
        self.ident = kb.sbuf("ident", [128, 128], BF16); self.ident_b = kb.buf()
        self.xt = [kb.sbuf(f"xt{i}", [128, SUB, D], F32) for i in range(2)]
        self.xt_b = [kb.buf() for _ in range(2)]
        self.xn = kb.sbuf("xn", [128, D], BF16); self.xn_b = kb.buf()
        self.st = kb.sbuf("stat", [128, 8], F32); self.st_b = kb.buf()
        self.xnT = [kb.sbuf(f"xnT{i}", [128, 8, TT], BF16) for i in range(2)]
        self.xnT_b = [kb.buf() for _ in range(2)]
        self.hid = kb.sbuf("hid", [128, NFC, TT], BF16)
        self.hid_b = [kb.buf() for _ in range(NFC)]
        self.sg = [kb.sbuf(f"sg{i}", [128, TT], F32) for i in range(2)]
        self.sg_b = [kb.buf() for _ in range(2)]
        self.hTo = kb.sbuf("hTo", [128, 8, TT], BF16); self.hTo_b = kb.buf()
        self.with_pre = with_pre
        if with_pre:
            self.Wo = kb.sbuf("Wo", [128, 8, D], BF16); self.Wo_b = kb.buf()
            self.yt = [kb.sbuf(f"yt{i}", [128, 8, TT], BF16) for i in range(2)]
            self.yt_b = [kb.buf() for _ in range(2)]
        self.psg = [kb.psum(f"psg{i}", [128, 512], F32) for i in range(2)]
        self.psg_b = [kb.buf() for _ in range(2)]
        self.psd = [kb.psum(f"psd{i}", [128, 512], F32) for i in range(2)]
        self.psd_b = [kb.buf() for _ in range(2)]
        self.tp = [kb.psum(f"tp{i}", [128, 8, 128], BF16) for i in range(2)]
        self.tp_b = [kb.buf() for _ in range(2)]
        self.ntp = 0
        self.npsd = 0
        self.ncast = 0


def load_consts(kb, R, ident_d):
    kb.dma("sp", R.ident[:], ident_d, writes=[R.ident_b])


def load_ffn_weights(kb, R, g_lay, wg, wu, wd, w_out=None):
    nc = kb.nc
    kb.dma("sp", R.gt[:], g_lay, writes=[R.gt_b])

    def cast(dst_ap, dst_b, src_ap, src_b, scal):
        e = ("dve", "act", "dve", "act", "pool")[R.ncast % 5]
        R.ncast += 1
        E = kb.E[e]
        if e == "act":
            if scal is None:
                kb.op(e, lambda: E.copy(out=dst_ap, in_=src_ap), reads=[src_b], writes=[dst_b])
            else:
                kb.op(e, lambda: E.activation(out=dst_ap, in_=src_ap, func=AF.Copy, scale=scal),
                      reads=[src_b, R.gt_b], writes=[dst_b])
        elif scal is None:
            kb.op(e, lambda: E.tensor_copy(out=dst_ap, in_=src_ap), reads=[src_b], writes=[dst_b])
        else:
            kb.op(e, lambda: E.tensor_scalar(out=dst_ap, in0=src_ap, scalar1=scal, scalar2=None, op0=ALU.mult),
                  reads=[src_b, R.gt_b], writes=[dst_b])

    k = 0
    for (W, Wb, src) in ((R.Wg, R.Wg_b, wg), (R.Wu, R.Wu_b, wu)):
        for kc in range(8):
            for (c0, c1) in ((0, 1024), (1024, 2048), (2048, DFF)):
                sb = k % 2; k += 1
                kb.dma("sp", R.stage[sb][:, 0:c1 - c0], src[kc * 128:(kc + 1) * 128, c0:c1],
                       writes=[R.stage_b[sb]])
                cast(W[:, kc, c0:c1], Wb, R.stage[sb][:, 0:c1 - c0], R.stage_b[sb], R.gt[:, kc:kc + 1])
    for fc in range(NFC):
        sb = k % 2; k += 1
        kb.dma("sp", R.stage[sb][:, 0:D], wd[fc * 128:(fc + 1) * 128, :], writes=[R.stage_b[sb]])
        cast(R.Wd[:, fc, :], R.Wd_b, R.stage[sb][:, 0:D], R.stage_b[sb], None)
    if w_out is not None:
        for kc in range(8):
            sb = k % 2; k += 1
            kb.dma("sp", R.stage[sb][:, 0:D], w_out[kc * 128:(kc + 1) * 128, :], writes=[R.stage_b[sb]])
            cast(R.Wo[:, kc, :], R.Wo_b, R.stage[sb][:, 0:D], R.stage_b[sb], None)


def norm_transpose(kb, R, x_ap, x_b, dstT, dstT_b, s):
    nc = kb.nc
    ss = R.st[:, 0:1]; rs = R.st[:, 1:2]; rstd = R.st[:, 2:3]
    kb.op("act", lambda: nc.scalar.activation(out=R.xn[:], in_=x_ap, func=AF.Square, accum_out=ss),
          reads=[x_b], writes=[R.xn_b, R.st_b])
    kb.op("act", lambda: nc.scalar.activation(out=rs, in_=ss, func=AF.Sqrt, bias=EPS, scale=1.0 / D),
          reads=[R.st_b], writes=[R.st_b])
    kb.op("dve", lambda: nc.vector.reciprocal(out=rstd, in_=rs), reads=[R.st_b], writes=[R.st_b])
    kb.op("dve", lambda: nc.vector.tensor_scalar(out=R.xn[:], in0=x_ap, scalar1=rstd, scalar2=None, op0=ALU.mult),
          reads=[x_b, R.st_b], writes=[R.xn_b])
    ti = R.ntp % 2; R.ntp += 1
    tp = R.tp[ti]; tpb = R.tp_b[ti]
    for kc in range(8):
        kb.op("pe", lambda kc=kc: nc.tensor.transpose(out=tp[:, kc, :], in_=R.xn[:, kc * 128:(kc + 1) * 128],
                                                      identity=R.ident[:]),
              reads=[R.xn_b, R.ident_b], writes=[tpb], inc=(kc == 7))
    kb.op("act", lambda: nc.scalar.copy(out=dstT[:, :, s * 128:(s + 1) * 128], in_=tp[:, :, :]),
          reads=[tpb], writes=[dstT_b])


def token_pass(kb, R, T, x_in, x_out, pre=None, post=None, in_bufs=None, out_bufs=None, pre_bufs=None, post_bufs=None):
    nc = kb.nc
    NT = T // TT

    def stage_load(i):
        bi = i % 2
        kb.dma("sp", R.xt[bi][:, :, :], x_in[i * TT:(i + 1) * TT, :].rearrange("(s p) d -> p s d", p=128),
               reads=([in_bufs[i]] if in_bufs else []), writes=[R.xt_b[bi]])
        if pre is not None:
            kb.dma("sp", R.yt[bi][:, :, :], pre[:, i * TT:(i + 1) * TT].rearrange("(c p) t -> p c t", p=128),
                   reads=([pre_bufs[i]] if pre_bufs else []), writes=[R.yt_b[bi]])

    def stage_pre(i):
        bi = i % 2
        if pre is None:
            return
        for s in range(SUB):
            for h in range(2):
                pi = R.npsd % 2; R.npsd += 1
                for kc in range(8):
                    kb.op("pe", lambda kc=kc: nc.tensor.matmul(R.psd[pi][:, :], lhsT=R.yt[bi][:, kc, s * 128:(s + 1) * 128],
                                                               rhs=R.Wo[:, kc, h * 512:(h + 1) * 512],
                                                               start=(kc == 0), stop=(kc == 7)),
                          reads=[R.yt_b[bi], R.Wo_b], writes=[R.psd_b[pi]], inc=(kc == 7))
                xs = R.xt[bi][:, s, h * 512:(h + 1) * 512]
                kb.op("dve", lambda: nc.vector.tensor_tensor(out=xs, in0=R.psd[pi][:, :], in1=xs, op=ALU.add),
                      reads=[R.psd_b[pi], R.xt_b[bi]], writes=[R.xt_b[bi]])

    def stage_a(i):
        bi = i % 2
        for s in range(SUB):
            norm_transpose(kb, R, R.xt[bi][:, s, :], R.xt_b[bi], R.xnT[bi], R.xnT_b[bi], s)

    def stage_b(i):
        bi = i % 2
        for fc in range(NFC):
            gi = fc % 2
            for (W, Wb, off) in ((R.Wg, R.Wg_b, 0), (R.Wu, R.Wu_b, 256)):
                for kc in range(8):
                    kb.op("pe", lambda kc=kc, W=W, off=off: nc.tensor.matmul(
                        R.psg[gi][:, off:off + TT], lhsT=W[:, kc, fc * 128:(fc + 1) * 128], rhs=R.xnT[bi][:, kc, :],
                        start=(kc == 0), stop=(kc == 7)),
                          reads=[Wb, R.xnT_b[bi]], writes=[R.psg_b[gi]], inc=(kc == 7))
            kb.op("act", lambda: nc.scalar.activation(out=R.sg[gi][:, :], in_=R.psg[gi][:, 0:TT], func=AF.Silu),
                  reads=[R.psg_b[gi]], writes=[R.sg_b[gi]])
            kb.op("dve", lambda: nc.vector.tensor_tensor(out=R.hid[:, fc, :], in0=R.sg[gi][:, :],
                                                         in1=R.psg[gi][:, 256:256 + TT], op=ALU.mult),
                  reads=[R.sg_b[gi], R.psg_b[gi]], writes=[R.hid_b[fc]])

    def stage_c(i):
        bi = i % 2
        for s in range(SUB):
            for h in range(2):
                pi = R.npsd % 2; R.npsd += 1
                for fc in range(NFC):
                    kb.op("pe", lambda fc=fc: nc.tensor.matmul(R.psd[pi][:, :], lhsT=R.hid[:, fc, s * 128:(s + 1) * 128],
                                                               rhs=R.Wd[:, fc, h * 512:(h + 1) * 512],
                                                               start=(fc == 0), stop=(fc == NFC - 1)),
                          reads=[R.hid_b[fc], R.Wd_b], writes=[R.psd_b[pi]], inc=(fc == NFC - 1))
                xs = R.xt[bi][:, s, h * 512:(h + 1) * 512]
                kb.op("dve", lambda: nc.vector.scalar_tensor_tensor(out=xs, in0=R.psd[pi][:, :], scalar=0.5, in1=xs,
                                                                    op0=ALU.mult, op1=ALU.add),
                      reads=[R.psd_b[pi], R.xt_b[bi]], writes=[R.xt_b[bi]])
            if post is not None:
                norm_transpose(kb, R, R.xt[bi][:, s, :], R.xt_b[bi], R.hTo, R.hTo_b, s)
        kb.dma("pool", x_out[i * TT:(i + 1) * TT, :].rearrange("(s p) d -> p s d", p=128), R.xt[bi][:, :, :],
               reads=[R.xt_b[bi]], writes=([out_bufs[i]] if out_bufs else []))
        if post is not None:
            kb.dma("pool", post[:, i * TT:(i + 1) * TT].rearrange("(c p) t -> p c t", p=128), R.hTo[:, :, :],
                   reads=[R.hTo_b], writes=([post_bufs[i]] if post_bufs else []))

    stage_load(0)
    stage_pre(0)
    stage_a(0)
    for i in range(NT):
        if i + 1 < NT:
            stage_load(i + 1)
        stage_b(i)
        if i + 1 < NT:
            stage_pre(i + 1)
            stage_a(i + 1)
        stage_c(i)


QT_ = 512
NW = 834
A_Q, A_K, A_V = 0, 64, 128
B_Q, B_K, B_V, B_O, B_I, B_F = 192, 256, 320, 384, 448, 449
C_Q, C_K, C_V = 450, 514, 578
D_Q, D_K, D_V = 642, 706, 770


class MixRes:
    def __init__(self, kb, S):
        self.S = S
        self.NB = S // 128
        self.NQ = S // QT_
        NB = self.NB
        self.Wm = kb.sbuf("Wm", [128, 8, NW], BF16); self.Wm_b = kb.buf()
        self.gm = kb.sbuf("gm", [128, 8], F32); self.gm_b = kb.buf()
        self.ident = kb.sbuf("identm", [128, 128], BF16); self.ident_b = kb.buf()
        self.ht = [kb.sbuf(f"ht{i}", [128, 8, QT_], BF16) for i in range(2)]; self.ht_b = [kb.buf() for _ in range(2)]
        self.QT = kb.sbuf("QT", [128, S], BF16); self.QT_b = kb.buf()
        self.KT = kb.sbuf("KT", [128, S], BF16); self.KT_b = kb.buf()
        self.Va = kb.sbuf("Va", [128, NB, 128], BF16); self.Va_b = kb.buf()
        self.par = kb.sbuf("par", [128, 32], F32); self.par_b = kb.buf()
        self.cst = kb.sbuf("cst", [128, 256], BF16); self.cst_b = kb.buf()
        self.sq = [kb.sbuf(f"sq{i}", [64, QT_], BF16) for i in range(2)]; self.sq_b = [kb.buf() for _ in range(2)]
        self.rr = [kb.sbuf(f"rr{i}", [64, QT_], F32) for i in range(2)]; self.rr_b = [kb.buf() for _ in range(2)]
        self.e32 = [kb.sbuf(f"e32_{i}", [128, 2, QT_], F32) for i in range(2)]; self.e32_b = [kb.buf() for _ in range(2)]
        self.wst = [self.e32[i][:, :, :].rearrange("p a b -> p (a b)")[:, 0:NW] for i in range(2)]; self.wst_b = self.e32_b
        self.e32c = kb.sbuf("e32_c", [128, 2, QT_], F32)
        self.e16 = [kb.sbuf(f"e16_{i}", [128, 2, QT_], BF16) for i in range(4)]; self.e16_b = [kb.buf() for _ in range(4)]
        self.spp = [kb.sbuf(f"spp_{i}", [128, 2, QT_], BF16) for i in range(2)]; self.spp_b = [kb.buf() for _ in range(2)]
        self.fin = [kb.sbuf(f"fin{i}", [64, QT_], F32) for i in range(3)]; self.fin_b = [kb.buf() for _ in range(3)]
        self.yo = [kb.sbuf(f"yo{i}", [64, QT_], BF16) for i in range(2)]; self.yo_b = [kb.buf() for _ in range(2)]
        self.mask = kb.sbuf("mask", [128, 4, QT_], BF16); self.mask_b = kb.buf()
        self.EB = kb.sbuf("EB", [128, 8, QT_], F32); self.EB_b = kb.buf()
        self.tri = kb.sbuf("tri", [128, 3, 128], BF16); self.tri_b = kb.buf()
        self.pp = [kb.psum(f"pp{i}", [128, 2, 512], F32) for i in range(4)]
        self.ps = [self.pp[i // 2][:, i % 2, :] for i in range(8)]
        self.ps_b = [kb.buf() for _ in range(8)]
        self.nyo = 0
        self.ne16 = 0


def mix_load_common(kb, R, wsel, gmix_lay, ident_d, cst_d):
    nc = kb.nc
    kb.dma("sp", R.gm[:], gmix_lay, writes=[R.gm_b])
    kb.dma("sp", R.ident[:], ident_d, writes=[R.ident_b])
    kb.dma("sp", R.cst[:], cst_d, writes=[R.cst_b])
    for kc in range(8):
        sb = kc % 2
        kb.dma("sp", R.wst[sb][:, :], wsel[kc * 128:(kc + 1) * 128, :], writes=[R.wst_b[sb]])
        kb.op("dve", lambda: nc.vector.tensor_scalar(out=R.Wm[:, kc, :], in0=R.wst[sb][:, :], scalar1=R.gm[:, kc:kc + 1],
                                                     scalar2=None, op0=ALU.mult),
              reads=[R.wst_b[sb], R.gm_b], writes=[R.Wm_b])


def load_ht(kb, R, hT, qt):
    bi = qt % 2
    if callable(hT):
        src = hT(qt).rearrange("c p t -> p c t")
    else:
        src = hT[:, qt * QT_:(qt + 1) * QT_].rearrange("(c p) t -> p c t", p=128)
    kb.dma("sp", R.ht[bi][:, :, :], src, writes=[R.ht_b[bi]])
    return R.ht[bi], R.ht_b[bi]


def proj_fm(kb, R, ht, ht_b, c0, ncol, pb):
    nc = kb.nc
    for kc in range(8):
        kb.op("pe", lambda kc=kc: nc.tensor.matmul(R.ps[pb][0:ncol, :], lhsT=R.Wm[:, kc, c0:c0 + ncol], rhs=ht[:, kc, :],
                                                   start=(kc == 0), stop=(kc == 7)),
              reads=[R.Wm_b, ht_b], writes=[R.ps_b[pb]], inc=(kc == 7))


def proj_tm(kb, R, ht, ht_b, c0, ncol, pb, s):
    nc = kb.nc
    for kc in range(8):
        kb.op("pe", lambda kc=kc: nc.tensor.matmul(R.ps[pb][:, s * 128:s * 128 + ncol], lhsT=ht[:, kc, s * 128:(s + 1) * 128],
                                                   rhs=R.Wm[:, kc, c0:c0 + ncol], start=(kc == 0), stop=(kc == 7)),
              reads=[R.Wm_b, ht_b], writes=[R.ps_b[pb]], inc=(kc == 7))


def qk_norm_store(kb, R, pb, pb2, dst, dst_b, qt, gcol, cmat, inv_n, i2):
    nc = kb.nc
    sq, sqb = R.sq[i2], R.sq_b[i2]
    rr, rrb = R.rr[i2], R.rr_b[i2]
    kb.op("act", lambda: nc.scalar.activation(out=sq[:, :], in_=R.ps[pb][0:64, :], func=AF.Square),
          reads=[R.ps_b[pb]], writes=[sqb])
    kb.op("pe", lambda: nc.tensor.matmul(R.ps[pb2][0:64, :], lhsT=cmat, rhs=sq[:, :], start=True, stop=True),
          reads=[sqb, R.cst_b], writes=[R.ps_b[pb2]])
    kb.op("act", lambda: nc.scalar.activation(out=rr[:, :], in_=R.ps[pb2][0:64, :], func=AF.Sqrt, bias=EPS, scale=inv_n),
          reads=[R.ps_b[pb2]], writes=[rrb])
    kb.op("dve", lambda: nc.vector.reciprocal(out=rr[:, :], in_=rr[:, :]), reads=[rrb], writes=[rrb])
    kb.op("dve", lambda: nc.vector.scalar_tensor_tensor(out=dst[0:64, qt * QT_:(qt + 1) * QT_], in0=R.ps[pb][0:64, :],
                                                        scalar=R.par[0:64, gcol:gcol + 1], in1=rr[:, :],
                                                        op0=ALU.mult, op1=ALU.mult),
          reads=[R.ps_b[pb], rrb, R.par_b], writes=[dst_b])


def v_store(kb, R, ht, ht_b, c0, qt, pb):
    nc = kb.nc
    for s in range(4):
        proj_tm(kb, R, ht, ht_b, c0, 64, pb, s)
    src = R.ps[pb][:, :].rearrange("p (s c) -> p s c", c=128)[:, :, 0:64]
    kb.op("act", lambda: nc.scalar.copy(out=R.Va[:, qt * 4:(qt + 1) * 4, 0:64], in_=src),
          reads=[R.ps_b[pb]], writes=[R.Va_b])


def ydst(yT_d, row0, qt):
    if callable(yT_d):
        return yT_d(row0, qt)
    return yT_d[row0:row0 + 64, qt * QT_:(qt + 1) * QT_]


def out_store(kb, R, yT_d, row0, qt, src_fn, reads):
    i = R.nyo % 2; R.nyo += 1
    src_fn(R.yo[i], R.yo_b[i])
    kb.dma("pool", ydst(yT_d, row0, qt), R.yo[i][:, :], reads=[R.yo_b[i]])


def mixer_c(kb, R, hT, yT_d, row0, masks_c):
    nc = kb.nc
    S, NB, NQ = R.S, R.NB, R.NQ
    kb.dma("sp", R.mask[:, :, :], masks_c[:, :, 0, :], writes=[R.mask_b])
    kb.op("pool", lambda: nc.gpsimd.memset(R.Va[:, :, 64:128], 1.0), writes=[R.Va_b])
    bd32 = R.cst[0:64, 64:128]
    ones64 = R.cst[0:64, 0:64]
    for qt in range(NQ):
        ht, htb = load_ht(kb, R, hT, qt)
        proj_fm(kb, R, ht, htb, C_Q, 64, 0)
        proj_fm(kb, R, ht, htb, C_K, 64, 2)
        qk_norm_store(kb, R, 0, 1, R.QT, R.QT_b, qt, 0, bd32, 1.0 / 32, 0)
        qk_norm_store(kb, R, 2, 3, R.KT, R.KT_b, qt, 1, bd32, 1.0 / 32, 1)
        v_store(kb, R, ht, htb, C_V, qt, 4 + (qt % 2))
    for qt in range(NQ):
        nkb = 4 * qt + 4
        O0, O1 = 6, 7
        estate = {}

        def s_step(kbk):
            pj = kbk % 3
            sb = 2 * pj
            for m in range(2):
                kb.op("pe", lambda m=m: nc.tensor.matmul(R.ps[sb + m],
                                                         lhsT=R.KT[m * 32:(m + 1) * 32, kbk * 128:(kbk + 1) * 128],
                                                         rhs=R.QT[m * 32:(m + 1) * 32, qt * QT_:(qt + 1) * QT_],
                                                         start=True, stop=True),
                      reads=[R.KT_b, R.QT_b], writes=[R.ps_b[sb + m]])
            ei = R.ne16 % 3; R.ne16 += 1
            e, eb = R.e16[ei], R.e16_b[ei]
            estate[kbk] = (e, eb)
            kb.op("act", lambda: nc.scalar.activation(out=e[:, :, :], in_=R.pp[pj][:, :, :], func=AF.Exp),
                  reads=[R.ps_b[sb], R.ps_b[sb + 1]], writes=[eb])
            r = kbk - 4 * qt
            if r >= 0:
                for m in range(2):
                    kb.op("dve", lambda m=m: nc.vector.tensor_tensor(out=e[:, m, :], in0=e[:, m, :], in1=R.mask[:, r, :], op=ALU.mult),
                          reads=[eb, R.mask_b], writes=[eb])

        def pv_step(kbk):
            e, eb = estate.pop(kbk)
            for m in range(2):
                kb.op("pe", lambda m=m: nc.tensor.matmul(R.ps[O0 + m][:, :], lhsT=R.Va[:, kbk, :], rhs=e[:, m, :],
                                                         start=(kbk == 0), stop=(kbk == nkb - 1)),
                      reads=[R.Va_b, eb], writes=[R.ps_b[O0 + m]])

        s_step(0)
        if nkb > 1:
            s_step(1)
        for kbk in range(nkb):
            if kbk + 2 < nkb:
                s_step(kbk + 2)
            pv_step(kbk)
        f0, f1, f2 = R.fin
        b0, b1, b2 = R.fin_b
        kb.op("dve", lambda: nc.vector.reciprocal(out=f0[:, :], in_=R.ps[O0][64:128, :]), reads=[R.ps_b[O0]], writes=[b0])
        kb.op("dve", lambda: nc.vector.tensor_tensor(out=f0[:, :], in0=R.ps[O0][0:64, :], in1=f0[:, :], op=ALU.mult),
              reads=[R.ps_b[O0], b0], writes=[b0])
        kb.op("dve", lambda: nc.vector.reciprocal(out=f1[:, :], in_=R.ps[O1][64:128, :]), reads=[R.ps_b[O1]], writes=[b1])
        kb.op("dve", lambda: nc.vector.tensor_tensor(out=f1[:, :], in0=R.ps[O1][0:64, :], in1=f1[:, :], op=ALU.mult),
              reads=[R.ps_b[O1], b1], writes=[b1])
        kb.op("dve", lambda: nc.vector.scalar_tensor_tensor(out=f2[:, :], in0=f1[:, :], scalar=R.par[0:64, 2:3], in1=f0[:, :],
                                                            op0=ALU.mult, op1=ALU.add),
              reads=[b0, b1, R.par_b], writes=[b2])
        kb.op("act", lambda: nc.scalar.activation(out=R.sq[0][:, :], in_=f2[:, :], func=AF.Square), reads=[b2], writes=[R.sq_b[0]])
        kb.op("pe", lambda: nc.tensor.matmul(R.ps[0][0:64, :], lhsT=ones64, rhs=R.sq[0][:, :], start=True, stop=True),
              reads=[R.sq_b[0], R.cst_b], writes=[R.ps_b[0]])
        kb.op("act", lambda: nc.scalar.activation(out=R.rr[0][:, :], in_=R.ps[0][0:64, :], func=AF.Sqrt, bias=EPS, scale=1.0 / 64),
              reads=[R.ps_b[0]], writes=[R.rr_b[0]])
        kb.op("dve", lambda: nc.vector.reciprocal(out=R.rr[0][:, :], in_=R.rr[0][:, :]), reads=[R.rr_b[0]], writes=[R.rr_b[0]])

        def fn(yo, yob):
            kb.op("dve", lambda: nc.vector.scalar_tensor_tensor(out=yo[:, :], in0=f2[:, :], scalar=R.par[0:64, 3:4], in1=R.rr[0][:, :],
                                                                op0=ALU.mult, op1=ALU.mult),
                  reads=[b2, R.rr_b[0], R.par_b], writes=[yob])
        out_store(kb, R, yT_d, row0, qt, fn, None)


def mixer_d(kb, R, hT, yT_d, row0, masks_d, tri_d):
    nc = kb.nc
    S, NB, NQ = R.S, R.NB, R.NQ
    kb.dma("sp", R.mask[:, :, :], masks_d, writes=[R.mask_b])
    kb.dma("sp", R.tri[:, :, :], tri_d, writes=[R.tri_b])
    for qt in range(NQ):
        ht, htb = load_ht(kb, R, hT, qt)
        proj_fm(kb, R, ht, htb, D_Q, 64, 0)
        proj_fm(kb, R, ht, htb, D_K, 64, 1)
        for half in range(2):
            rows = slice(half * 64, half * 64 + 64)
            kb.op("act", lambda rows=rows: nc.scalar.activation(out=R.QT[rows, qt * QT_:(qt + 1) * QT_], in_=R.ps[0][0:64, :], func=AF.Copy, scale=0.125),
                  reads=[R.ps_b[0]], writes=[R.QT_b])
            kb.op("dve", lambda rows=rows: nc.vector.tensor_copy(out=R.KT[rows, qt * QT_:(qt + 1) * QT_], in_=R.ps[1][0:64, :]),
                  reads=[R.ps_b[1]], writes=[R.KT_b])
        v_store(kb, R, ht, htb, D_V, qt, 4 + (qt % 2))
    RA, RB, OB = 4, 5, 6
    ed_b = [kb.buf() for _ in range(3)]
    ed = [R.e32[0], R.e32[1], R.e32c]
    for qt in range(NQ):
        kbs = list(range(4 * qt + 3, -1, -1))
        npair = len(kbs) // 2
        qsl = slice(qt * QT_, (qt + 1) * QT_)

        def blocks(j):
            return kbs[2 * j], kbs[2 * j + 1]

        def z_mm(j):
            for h, kbk in enumerate(blocks(j)):
                zb = 2 * (j % 2) + h
                rows = slice(h * 64, h * 64 + 64)
                kb.op("pe", lambda kbk=kbk, zb=zb, rows=rows: nc.tensor.matmul(R.ps[zb], lhsT=R.KT[rows, kbk * 128:(kbk + 1) * 128],
                                                                               rhs=R.QT[rows, qsl], start=True, stop=True),
                      reads=[R.KT_b, R.QT_b], writes=[R.ps_b[zb]])

        def esp(j):
            zp = j % 2
            e, eb = ed[j % 3], ed_b[j % 3]
            sp, spb = R.spp[j % 2], R.spp_b[j % 2]
            kb.op("act", lambda: nc.scalar.activation(out=e[:, :, :], in_=R.pp[zp][:, :, :], func=AF.Exp),
                  reads=[R.ps_b[2 * zp], R.ps_b[2 * zp + 1]], writes=[eb])
            kb.op("act", lambda: nc.scalar.activation(out=sp[:, :, :], in_=e[:, :, :], func=AF.Ln, bias=1.0),
                  reads=[eb], writes=[spb])
            for h, kbk in enumerate(blocks(j)):
                r = kbk - 4 * qt
                if r >= 0:
                    kb.op("dve", lambda h=h, r=r: nc.vector.tensor_tensor(out=sp[:, h, :], in0=sp[:, h, :], in1=R.mask[:, r, :], op=ALU.mult),
                          reads=[spb, R.mask_b], writes=[spb])

        def mmR(bank, t_i, sp_ap, spb, start, stop=False):
            kb.op("pe", lambda: nc.tensor.matmul(R.ps[bank], lhsT=R.tri[:, t_i, :], rhs=sp_ap, start=start, stop=stop),
                  reads=[R.tri_b, spb], writes=[R.ps_b[bank]])

        def chain_a(j):
            sp, spb = R.spp[j % 2], R.spp_b[j % 2]
            mmR(RA, 0, sp[:, 0, :], spb, j == 0)
            mmR(RB, 2, sp[:, 0, :], spb, j == 0)
            mmR(RB, 0, sp[:, 1, :], spb, False)

        def chain_b(j):
            e, eb = ed[j % 3], ed_b[j % 3]
            sp, spb = R.spp[j % 2], R.spp_b[j % 2]
            tt, ttb = R.e16[j % 2], R.e16_b[j % 2]
            aa, aab = R.e16[2 + j % 2], R.e16_b[2 + j % 2]
            kb.op("act", lambda: nc.scalar.activation(out=tt[:, :, :], in_=R.pp[2][:, :, :], func=AF.Exp, scale=-1.0),
                  reads=[R.ps_b[RA], R.ps_b[RB]], writes=[ttb])
            mmR(RA, 1, sp[:, 0, :], spb, False)
            mmR(RA, 2, sp[:, 1, :], spb, False, j == npair - 1)
            mmR(RB, 1, sp[:, 1, :], spb, False, j == npair - 1)
            kb.op("dve", lambda: nc.vector.tensor_tensor(out=aa[:, :, :], in0=e[:, :, :], in1=tt[:, :, :], op=ALU.mult),
                  reads=[eb, ttb], writes=[aab])
            for h, kbk in enumerate(blocks(j)):
                r = kbk - 4 * qt
                if r >= 0:
                    kb.op("dve", lambda h=h, r=r: nc.vector.tensor_tensor(out=aa[:, h, :], in0=aa[:, h, :], in1=R.mask[:, r, :], op=ALU.mult),
                          reads=[aab, R.mask_b], writes=[aab])

        def pv(j):
            aa, aab = R.e16[2 + j % 2], R.e16_b[2 + j % 2]
            for h, kbk in enumerate(blocks(j)):
                kb.op("pe", lambda h=h, kbk=kbk: nc.tensor.matmul(R.ps[OB][0:64, :], lhsT=R.Va[:, kbk, 0:64], rhs=aa[:, h, :],
                                                                  start=(j == 0 and h == 0), stop=(j == npair - 1 and h == 1)),
                      reads=[R.Va_b, aab], writes=[R.ps_b[OB]])

        z_mm(0)
        if npair > 1:
            z_mm(1)
        esp(0)
        for j in range(npair):
            if j + 1 < npair:
                esp(j + 1)
            chain_a(j)
            if j + 2 < npair:
                z_mm(j + 2)
            if j >= 1:
                pv(j - 1)
            chain_b(j)
        pv(npair - 1)

        def fn(yo, yob):
            kb.op("dve", lambda: nc.vector.tensor_copy(out=yo[:, :], in_=R.ps[OB][0:64, :]), reads=[R.ps_b[OB]], writes=[yob])
        out_store(kb, R, yT_d, row0, qt, fn, None)


def mixer_a(kb, R, hT, yT_d, row0, biasT_d):
    nc = kb.nc
    S, NB, NQ = R.S, R.NB, R.NQ
    kb.dma("sp", R.EB[:, :, :], biasT_d, writes=[R.EB_b])
    for r in range(8):
        kb.op("act", lambda r=r: nc.scalar.activation(out=R.EB[:, r, :], in_=R.EB[:, r, :], func=AF.Exp),
              reads=[R.EB_b], writes=[R.EB_b])
    kb.op("pool", lambda: nc.gpsimd.memset(R.Va[:, :, 64:128], 1.0), writes=[R.Va_b])
    ones64 = R.cst[0:64, 0:64]
    for qt in range(NQ):
        ht, htb = load_ht(kb, R, hT, qt)
        proj_fm(kb, R, ht, htb, A_Q, 64, 0)
        proj_fm(kb, R, ht, htb, A_K, 64, 2)
        qk_norm_store(kb, R, 0, 1, R.QT, R.QT_b, qt, 4, ones64, 1.0 / 64, 0)
        qk_norm_store(kb, R, 2, 3, R.KT, R.KT_b, qt, 5, ones64, 1.0 / 64, 1)
        v_store(kb, R, ht, htb, A_V, qt, 4 + (qt % 2))
    OB = 4
    for qt in range(NQ):
        rs = [r for r in range(8) if 4 * qt - 4 + r >= 0]
        for j, r in enumerate(rs):
            kbk = 4 * qt - 4 + r
            sb = j % 2
            kb.op("pe", lambda: nc.tensor.matmul(R.ps[sb][:, :], lhsT=R.KT[0:64, kbk * 128:(kbk + 1) * 128],
                                                 rhs=R.QT[0:64, qt * QT_:(qt + 1) * QT_], start=True, stop=True),
                  reads=[R.KT_b, R.QT_b], writes=[R.ps_b[sb]])
            e, eb = R.e32[j % 2], R.e32_b[j % 2]
            p, pbuf = R.e16[j % 2], R.e16_b[j % 2]
            kb.op("act", lambda: nc.scalar.activation(out=e[:, 0, :], in_=R.ps[sb][:, :], func=AF.Exp),
                  reads=[R.ps_b[sb]], writes=[eb])
            kb.op("dve", lambda: nc.vector.tensor_tensor(out=p[:, 0, :], in0=e[:, 0, :], in1=R.EB[:, r, :], op=ALU.mult),
                  reads=[eb, R.EB_b], writes=[pbuf])
            kb.op("pe", lambda: nc.tensor.matmul(R.ps[OB][:, :], lhsT=R.Va[:, kbk, :], rhs=p[:, 0, :],
                                                 start=(j == 0), stop=(j == len(rs) - 1)),
                  reads=[R.Va_b, pbuf], writes=[R.ps_b[OB]])
        f0, b0 = R.fin[0], R.fin_b[0]
        kb.op("dve", lambda: nc.vector.reciprocal(out=f0[:, :], in_=R.ps[OB][64:128, :]), reads=[R.ps_b[OB]], writes=[b0])

        def fn(yo, yob):
            kb.op("dve", lambda: nc.vector.tensor_tensor(out=yo[:, :], in0=R.ps[OB][0:64, :], in1=f0[:, :], op=ALU.mult),
                  reads=[R.ps_b[OB], b0], writes=[yob])
        out_store(kb, R, yT_d, row0, qt, fn, None)


class MixResB:
    def __init__(self, kb, R):
        NB = R.NB
        self.Osig = R.EB[:, :, :].bitcast(BF16).rearrange("p a (b c) -> p (a b) c", c=64)[:, 0:NB, :]; self.Osig_b = R.EB_b
        self.G = kb.sbuf("Gates", [128, 8, NB], F32); self.G_b = kb.buf()
        self.trif = kb.sbuf("trif", [128, 2, 128], F32); self.trif_b = kb.buf()
        self.cw = kb.sbuf("convw", [128, 8], F32); self.cw_b = kb.buf()
        self.gob = kb.sbuf("gob", [128, 64], F32); self.gob_b = kb.buf()
        self.St = [kb.sbuf(f"St{i}", [64, 65], F32) for i in range(2)]; self.St_b = [kb.buf() for _ in range(2)]
        self.Sb = [kb.sbuf(f"Sb{i}", [64, 65], BF16) for i in range(2)]; self.Sb_b = [kb.buf() for _ in range(2)]
        self.tok = [kb.sbuf(f"tok{i}", [128, 3, 64], BF16) for i in range(2)]; self.tok_b = [kb.buf() for _ in range(2)]
        self.qkT = [kb.sbuf(f"qkT{i}", [64, 2, 128], BF16) for i in range(2)]; self.qkT_b = [kb.buf() for _ in range(2)]
        self.qkm = [kb.sbuf(f"qkm{i}", [128, 128], BF16) for i in range(2)]; self.qkm_b = [kb.buf() for _ in range(2)]
        self.cm = kb.sbuf("cmask", [128, 128], F32); self.cm_b = kb.buf()
        self.hn = [kb.sbuf(f"hn{i}", [128, 64], F32) for i in range(2)]; self.hn_b = [kb.buf() for _ in range(2)]
        self.hs = [kb.sbuf(f"hs{i}", [128, 8], F32) for i in range(2)]; self.hs_b = [kb.buf() for _ in range(2)]
        self.yb = [kb.sbuf(f"yb{i}", [128, 64], BF16) for i in range(2)]; self.yb_b = [kb.buf() for _ in range(2)]
        self.jk = kb.sbuf("jk", [128, 64], BF16); self.jk_b = kb.buf()


def mixer_b(kb, R, RB_, hT, yT_d, row0, bpar_d, trif_d, cmask_d, gob_d):
    nc = kb.nc
    S, NB, NQ = R.S, R.NB, R.NQ
    B = RB_
    kb.dma("sp", B.cw[:, :], bpar_d, writes=[B.cw_b])
    kb.dma("sp", B.trif[:, :, :], trif_d, writes=[B.trif_b])
    kb.dma("sp", B.cm[:, :], cmask_d, writes=[B.cm_b])
    kb.dma("sp", B.gob[:, :], gob_d, writes=[B.gob_b])
    kb.op("pool", lambda: nc.gpsimd.memset(R.Va[:, :, 64:65], 1.0), writes=[R.Va_b])
    kb.op("dve", lambda: nc.vector.tensor_scalar(out=B.cw[:, 7:8], in0=B.cw[:, 6:7], scalar1=-1.0, scalar2=None, op0=ALU.mult),
          reads=[B.cw_b], writes=[B.cw_b])
    cv = [R.e32[i][:, :, :].rearrange("p a b -> p (a b)") for i in range(2)]
    cvb = R.e32_b
    accA = R.e16[0][:, :, :].rearrange("p a b -> p (a b)").bitcast(F32); accB_ = R.e16[1][:, :, :].rearrange("p a b -> p (a b)").bitcast(F32)
    accA_b = R.e16_b[0]; accB_b = R.e16_b[1]
    for qt in range(NQ):
        ht, htb = load_ht(kb, R, hT, qt)
        ci = qt % 2
        proj_fm(kb, R, ht, htb, B_Q, 128, 0)
        if qt == 0:
            kb.op("dve", lambda: nc.vector.memset(cv[ci][:, 0:3], 0.0), writes=[cvb[ci]])
        else:
            kb.op("dve", lambda: nc.vector.tensor_copy(out=cv[ci][:, 0:3], in_=cv[1 - ci][:, 512:515]),
                  reads=[cvb[1 - ci]], writes=[cvb[ci]])
        kb.op("act", lambda: nc.scalar.copy(out=cv[ci][:, 3:515], in_=R.ps[0][:, :]), reads=[R.ps_b[0]], writes=[cvb[ci]])
        kb.op("dve", lambda: nc.vector.tensor_scalar(out=accA, in0=cv[ci][:, 3:515], scalar1=B.cw[:, 3:4], scalar2=B.cw[:, 4:5],
                                                     op0=ALU.mult, op1=ALU.add),
              reads=[cvb[ci], B.cw_b], writes=[accA_b])
        for j in (2, 1, 0):
            kb.op("dve", lambda j=j: nc.vector.scalar_tensor_tensor(out=accA, in0=cv[ci][:, j:j + 512], scalar=B.cw[:, j:j + 1],
                                                                    in1=accA, op0=ALU.mult, op1=ALU.add),
                  reads=[cvb[ci], B.cw_b, accA_b], writes=[accA_b])
        kb.op("act", lambda: nc.scalar.activation(out=accB_, in_=accA, func=AF.Sigmoid), reads=[accA_b], writes=[accB_b])
        kb.op("dve", lambda: nc.vector.tensor_tensor(out=accB_, in0=accA, in1=accB_, op=ALU.mult), reads=[accA_b, accB_b], writes=[accB_b])
        kb.op("act", lambda: nc.scalar.copy(out=R.QT[0:64, qt * QT_:(qt + 1) * QT_], in_=accB_[0:64, :]), reads=[accB_b], writes=[R.QT_b])
        kb.op("act", lambda: nc.scalar.copy(out=R.KT[0:64, qt * QT_:(qt + 1) * QT_], in_=accB_[64:128, :]), reads=[accB_b], writes=[R.KT_b])
        pb = 4 + (qt % 2)
        for s in range(4):
            nonlocal_pb = 3 + ((qt * 4 + s) % 4)
            for kc in range(8):
                kb.op("pe", lambda kc=kc: nc.tensor.matmul(R.ps[nonlocal_pb][:, 0:130], lhsT=ht[:, kc, s * 128:(s + 1) * 128],
                                                           rhs=R.Wm[:, kc, B_V:B_V + 130], start=(kc == 0), stop=(kc == 7)),
                      reads=[R.Wm_b, htb], writes=[R.ps_b[nonlocal_pb]], inc=(kc == 7))
            blk = qt * 4 + s
            kb.op("dve", lambda: nc.vector.tensor_copy(out=R.Va[:, blk, 0:64], in_=R.ps[nonlocal_pb][:, 0:64]),
                  reads=[R.ps_b[nonlocal_pb]], writes=[R.Va_b])
            kb.op("act", lambda: nc.scalar.activation(out=B.Osig[:, blk, :], in_=R.ps[nonlocal_pb][:, 64:128], func=AF.Sigmoid),
                  reads=[R.ps_b[nonlocal_pb]], writes=[B.Osig_b])
            kb.op("dve", lambda: nc.vector.tensor_copy(out=B.G[:, 0:2, blk], in_=R.ps[nonlocal_pb][:, 128:130]),
                  reads=[R.ps_b[nonlocal_pb]], writes=[B.G_b])
    G = B.G
    kb.op("act", lambda: nc.scalar.activation(out=G[:, 2, :], in_=G[:, 1, :], func=AF.Exp, scale=-1.0, bias=B.cw[:, 7:8]),
          reads=[B.G_b, B.cw_b], writes=[B.G_b])
    kb.op("act", lambda: nc.scalar.activation(out=G[:, 2, :], in_=G[:, 2, :], func=AF.Ln, bias=1.0), reads=[B.G_b], writes=[B.G_b])
    kb.op("dve", lambda: nc.vector.tensor_scalar(out=G[:, 2, :], in0=G[:, 2, :], scalar1=-1.0, scalar2=None, op0=ALU.mult),
          reads=[B.G_b], writes=[B.G_b])
    kb.op("pe", lambda: nc.tensor.matmul(R.ps[0][:, 0:NB], lhsT=B.trif[:, 0, :], rhs=G[:, 2, :], start=True, stop=True),
          reads=[B.trif_b, B.G_b], writes=[R.ps_b[0]])
    kb.op("pe", lambda: nc.tensor.matmul(R.ps[1][:, 0:NB], lhsT=B.trif[:, 1, :], rhs=G[:, 2, :], start=True, stop=True),
          reads=[B.trif_b, B.G_b], writes=[R.ps_b[1]])
    kb.op("dve", lambda: nc.vector.tensor_copy(out=G[:, 3, :], in_=R.ps[0][:, 0:NB]), reads=[R.ps_b[0]], writes=[B.G_b])
    kb.op("act", lambda: nc.scalar.activation(out=G[:, 4, :], in_=G[:, 3, :], func=AF.Exp), reads=[B.G_b], writes=[B.G_b])
    kb.op("dve", lambda: nc.vector.tensor_tensor(out=G[:, 5, :], in0=G[:, 0, :], in1=G[:, 3, :], op=ALU.subtract),
          reads=[B.G_b], writes=[B.G_b])
    kb.op("dve", lambda: nc.vector.tensor_tensor(out=G[:, 6, :], in0=G[:, 5, :], in1=R.ps[1][:, 0:NB], op=ALU.add),
          reads=[B.G_b, R.ps_b[1]], writes=[B.G_b])
    kb.op("act", lambda: nc.scalar.activation(out=G[:, 5, :], in_=G[:, 5, :], func=AF.Exp, bias=B.cw[:, 5:6]),
          reads=[B.G_b, B.cw_b], writes=[B.G_b])
    kb.op("act", lambda: nc.scalar.activation(out=G[:, 6, :], in_=G[:, 6, :], func=AF.Exp, bias=B.cw[:, 5:6]),
          reads=[B.G_b, B.cw_b], writes=[B.G_b])
    kb.op("dve", lambda: nc.vector.tensor_scalar(out=G[:, 5:7, :], in0=G[:, 5:7, :], scalar1=0.125, scalar2=None, op0=ALU.mult),
          reads=[B.G_b], writes=[B.G_b])
    kb.op("act", lambda: nc.scalar.activation(out=G[:, 7, :], in_=R.ps[1][:, 0:NB], func=AF.Exp), reads=[R.ps_b[1]], writes=[B.G_b])
    kb.op("dve", lambda: nc.vector.memset(B.St[0][:, :], 0.0), writes=[B.St_b[0]])
    kb.op("dve", lambda: nc.vector.memset(B.Sb[0][:, :], 0.0), writes=[B.Sb_b[0]])
    PT, PT2, PS_, PO, PU = 0, 1, 2, 3, 6
    for b in range(NB):
        i2 = b % 2
        tok, tokb = B.tok[i2], B.tok_b[i2]
        qkT, qkTb = B.qkT[i2], B.qkT_b[i2]
        tpA = R.ps[0][:, :].bitcast(BF16); tpq = tpA[:, 0:128]; tpk = tpA[:, 128:256]
        kb.op("pe", lambda: nc.tensor.transpose(out=tpq[:, 0:64], in_=R.QT[0:64, b * 128:(b + 1) * 128], identity=R.ident[0:64, 0:64]),
              reads=[R.QT_b, R.ident_b], writes=[R.ps_b[0]])
        kb.op("pe", lambda: nc.tensor.transpose(out=tpk[:, 0:64], in_=R.KT[0:64, b * 128:(b + 1) * 128], identity=R.ident[0:64, 0:64]),
              reads=[R.KT_b, R.ident_b], writes=[R.ps_b[0]])
        kb.op("dve", lambda: nc.vector.tensor_scalar(out=tok[:, 0, :], in0=tpq[:, 0:64], scalar1=G[:, 4, b:b + 1], scalar2=None, op0=ALU.mult),
              reads=[R.ps_b[0], B.G_b], writes=[tokb])
        kb.op("dve", lambda: nc.vector.tensor_scalar(out=tok[:, 1, :], in0=tpk[:, 0:64], scalar1=G[:, 5, b:b + 1], scalar2=None, op0=ALU.mult),
              reads=[R.ps_b[0], B.G_b], writes=[tokb])
        kb.op("dve", lambda: nc.vector.tensor_scalar(out=tok[:, 2, :], in0=tpk[:, 0:64], scalar1=G[:, 6, b:b + 1], scalar2=None, op0=ALU.mult),
              reads=[R.ps_b[0], B.G_b], writes=[tokb])
        tpB = R.ps[1][:, :].bitcast(BF16); tq2 = tpB[:, 0:128]; tk2 = tpB[:, 128:256]
        kb.op("pe", lambda: nc.tensor.transpose(out=tq2[0:64, :], in_=tok[:, 0, :], identity=R.ident[:, :]),
              reads=[tokb, R.ident_b], writes=[R.ps_b[1]])
        kb.op("pe", lambda: nc.tensor.transpose(out=tk2[0:64, :], in_=tok[:, 1, :], identity=R.ident[:, :]),
              reads=[tokb, R.ident_b], writes=[R.ps_b[1]])
        kb.op("act", lambda: nc.scalar.copy(out=qkT[:, 0, :], in_=tq2[0:64, :]), reads=[R.ps_b[1]], writes=[qkTb])
        kb.op("act", lambda: nc.scalar.copy(out=qkT[:, 1, :], in_=tk2[0:64, :]), reads=[R.ps_b[1]], writes=[qkTb])
        kb.op("pe", lambda: nc.tensor.matmul(R.ps[PS_][:, 0:128], lhsT=qkT[:, 1, :], rhs=qkT[:, 0, :], start=True, stop=True),
              reads=[qkTb], writes=[R.ps_b[PS_]])
        qkm, qkmb = B.qkm[i2], B.qkm_b[i2]
        kb.op("dve", lambda: nc.vector.tensor_tensor(out=qkm[:, :], in0=R.ps[PS_][:, 0:128], in1=B.cm[:, :], op=ALU.mult),
              reads=[R.ps_b[PS_], B.cm_b], writes=[qkmb])
        Sp, Spb = B.Sb[i2], B.Sb_b[i2]
        po = PO + (b % 2)
        kb.op("pe", lambda: nc.tensor.matmul(R.ps[po][:, 0:65], lhsT=qkm[:, :], rhs=R.Va[:, b, 0:65], start=True, stop=False),
              reads=[qkmb, R.Va_b], writes=[R.ps_b[po]], inc=False)
        kb.op("pe", lambda: nc.tensor.matmul(R.ps[po][:, 0:65], lhsT=qkT[:, 0, :], rhs=Sp[:, :], start=False, stop=True),
              reads=[qkTb, Spb], writes=[R.ps_b[po]])
        kb.op("pe", lambda: nc.tensor.matmul(R.ps[PU][0:64, 0:65], lhsT=tok[:, 2, :], rhs=R.Va[:, b, 0:65], start=True, stop=True),
              reads=[tokb, R.Va_b], writes=[R.ps_b[PU]])
        Sn, Snb = B.St[1 - i2], B.St_b[1 - i2]
        So, Sob = B.St[i2], B.St_b[i2]
        kb.op("dve", lambda: nc.vector.scalar_tensor_tensor(out=Sn[:, :], in0=So[:, :], scalar=G[0:64, 7, b:b + 1], in1=R.ps[PU][0:64, 0:65],
                                                            op0=ALU.mult, op1=ALU.add),
              reads=[Sob, B.G_b, R.ps_b[PU]], writes=[Snb])
        kb.op("act", lambda: nc.scalar.copy(out=B.Sb[1 - i2][:, :], in_=Sn[:, :]), reads=[Snb], writes=[B.Sb_b[1 - i2]])
        hs, hsb = B.hs[i2], B.hs_b[i2]
        hn, hnb = B.hn[i2], B.hn_b[i2]
        kb.op("act", lambda: nc.scalar.activation(out=hs[:, 5:6], in_=R.ps[po][:, 64:65], func=AF.Abs),
              reads=[R.ps_b[po]], writes=[hsb])
        kb.op("dve", lambda: nc.vector.tensor_scalar(out=hs[:, 0:1], in0=hs[:, 5:6], scalar1=1.0, scalar2=None, op0=ALU.max),
              reads=[hsb], writes=[hsb])
        kb.op("dve", lambda: nc.vector.reciprocal(out=hs[:, 1:2], in_=hs[:, 0:1]), reads=[hsb], writes=[hsb])
        kb.op("dve", lambda: nc.vector.tensor_scalar(out=hn[:, :], in0=R.ps[po][:, 0:64], scalar1=hs[:, 1:2], scalar2=None, op0=ALU.mult),
              reads=[R.ps_b[po], hsb], writes=[hnb])
        kb.op("act", lambda: nc.scalar.activation(out=B.jk[:, :], in_=hn[:, :], func=AF.Square, accum_out=hs[:, 2:3]),
              reads=[hnb], writes=[B.jk_b, hsb])
        kb.op("act", lambda: nc.scalar.activation(out=hs[:, 3:4], in_=hs[:, 2:3], func=AF.Sqrt, bias=EPS, scale=1.0 / 64),
              reads=[hsb], writes=[hsb])
        kb.op("dve", lambda: nc.vector.reciprocal(out=hs[:, 4:5], in_=hs[:, 3:4]), reads=[hsb], writes=[hsb])
        kb.op("dve", lambda: nc.vector.scalar_tensor_tensor(out=hn[:, :], in0=hn[:, :], scalar=hs[:, 4:5], in1=B.gob[:, :],
                                                            op0=ALU.mult, op1=ALU.mult),
              reads=[hnb, hsb, B.gob_b], writes=[hnb])
        yb, ybb = B.yb[i2], B.yb_b[i2]
        kb.op("dve", lambda: nc.vector.tensor_tensor(out=yb[:, :], in0=hn[:, :], in1=B.Osig[:, b, :], op=ALU.mult),
              reads=[hnb, B.Osig_b], writes=[ybb])
        ty = R.ps[5][:, :].bitcast(BF16)[:, 0:128]
        kb.op("pe", lambda: nc.tensor.transpose(out=ty[0:64, :], in_=yb[:, :], identity=R.ident[:, :]),
              reads=[ybb, R.ident_b], writes=[R.ps_b[5]])
        qt = b // 4
        if b % 4 == 0:
            R.cur_yo = R.nyo % 2; R.nyo += 1
        yo, yob = R.yo[R.cur_yo], R.yo_b[R.cur_yo]
        kb.op("act", lambda: nc.scalar.copy(out=yo[:, (b % 4) * 128:(b % 4 + 1) * 128], in_=ty[0:64, :]),
              reads=[R.ps_b[5]], writes=[yob])
        if b % 4 == 3:
            kb.dma("pool", ydst(yT_d, row0, qt), yo[:, :], reads=[yob])


def mix_params(kb, R, praw_d, clam_d):
    nc = kb.nc
    pr = R.par
    kb.dma("sp", pr[0:64, 16:24], praw_d, writes=[R.par_b])
    cl = R.rr[0][:, 0:128].rearrange("p (a b) -> p a b", a=4)
    kb.dma("sp", cl, clam_d, writes=[R.rr_b[0]])
    V = nc.vector
    kb.op("dve", lambda: V.tensor_scalar(out=pr[0:64, 0:1], in0=pr[0:64, 16:17], scalar1=32 ** -0.5, scalar2=None, op0=ALU.mult), reads=[R.par_b], writes=[R.par_b])
    kb.op("dve", lambda: V.tensor_copy(out=pr[0:64, 1:2], in_=pr[0:64, 17:18]), reads=[R.par_b], writes=[R.par_b])
    kb.op("dve", lambda: V.tensor_tensor(out=pr[0:64, 3:4], in0=pr[0:64, 18:19], in1=pr[0:64, 22:23], op=ALU.mult), reads=[R.par_b], writes=[R.par_b])
    kb.op("dve", lambda: V.tensor_scalar(out=pr[0:64, 4:5], in0=pr[0:64, 19:20], scalar1=0.125, scalar2=None, op0=ALU.mult), reads=[R.par_b], writes=[R.par_b])
    kb.op("dve", lambda: V.tensor_copy(out=pr[0:64, 5:6], in_=pr[0:64, 20:21]), reads=[R.par_b], writes=[R.par_b])
    pp = R.rr[1][:, 0:64].rearrange("p (a b) -> p a b", a=2)
    kb.op("dve", lambda: V.tensor_tensor(out=pp[:, 0, :], in0=cl[:, 0, :], in1=cl[:, 1, :], op=ALU.mult), reads=[R.rr_b[0]], writes=[R.rr_b[1]])
    kb.op("dve", lambda: V.tensor_tensor(out=pp[:, 1, :], in0=cl[:, 2, :], in1=cl[:, 3, :], op=ALU.mult), reads=[R.rr_b[0]], writes=[R.rr_b[1]])
    kb.op("dve", lambda: V.reduce_sum(out=pr[0:64, 8:10], in_=pp, axis=AX.X), reads=[R.rr_b[1]], writes=[R.par_b])
    kb.op("act", lambda: nc.scalar.activation(out=pr[0:64, 10:12], in_=pr[0:64, 8:10], func=AF.Exp), reads=[R.par_b], writes=[R.par_b])
    kb.op("dve", lambda: V.tensor_tensor(out=pr[0:64, 12:13], in0=pr[0:64, 11:12], in1=pr[0:64, 10:11], op=ALU.subtract), reads=[R.par_b], writes=[R.par_b])
    kb.op("dve", lambda: V.tensor_tensor(out=pr[0:64, 2:3], in0=pr[0:64, 12:13], in1=pr[0:64, 21:22], op=ALU.subtract), reads=[R.par_b], writes=[R.par_b])

import ml_dtypes
bf16 = ml_dtypes.bfloat16
GW = 256
OFF = dict(aq=0, ak=256, av=512, bqk=768, bv=1280, bo=1536, bi=1792, bf=1796, cq=1800, ck=2056, cv=2312, dq=2568, dk=2824, dv=3080)

def sel_cols(j):
    c = []
    r = lambda o: list(range(o + j * 64, o + j * 64 + 64))
    c += r(OFF['aq']) + r(OFF['ak']) + r(OFF['av'])
    c += r(OFF['bqk']) + r(OFF['bqk'] + 256) + r(OFF['bv']) + r(OFF['bo']) + [OFF['bi'] + j, OFF['bf'] + j]
    c += r(OFF['cq']) + r(OFF['ck']) + r(OFF['cv'])
    c += r(OFF['dq']) + r(OFF['dk']) + r(OFF['dv'])
    return np.array(c)

def const_inputs():
    d = {}
    d['ident'] = np.eye(128, dtype=np.float32).astype(bf16)
    cst = np.zeros((128, 256), np.float32)
    cst[0:64, 0:64] = 1.0
    cst[0:32, 64:96] = 1.0; cst[32:64, 96:128] = 1.0
    d['cst'] = cst.astype(bf16)
    s = np.arange(128)[:, None, None, None]; r = np.arange(4)[None, :, None, None]; t = np.arange(512)[None, None, None, :]
    mc = ((2 * r + (s >= 64)) <= (t // 64)).astype(np.float32)
    d['masks_c'] = np.broadcast_to(mc, (128, 4, 2, 512)).astype(bf16).copy()
    s = np.arange(128)[:, None, None]; r = np.arange(4)[None, :, None]; t = np.arange(512)[None, None, :]
    d['masks_d'] = ((128 * r + s) < t).astype(np.float32).astype(bf16)
    j = np.arange(128)[:, None]; s2 = np.arange(128)[None, :]
    tri = np.zeros((128, 3, 128), np.float32)
    tri[:, 0, :] = (j >= s2); tri[:, 1, :] = (j < s2); tri[:, 2, :] = 1.0
    d['tri'] = tri.astype(bf16)
    trif = np.zeros((128, 2, 128), np.float32)
    trif[:, 0, :] = (j <= s2); trif[:, 1, :] = 1.0
    d['trif'] = trif
    d['cmask'] = (j <= s2).astype(np.float32)
    return d

def bias_index():
    s = np.arange(128)[:, None, None]; r = np.arange(8)[None, :, None]; t = np.arange(512)[None, None, :]
    rel = t - s + 512 - 128 * r
    idx = np.clip(rel, -128, 128) + 128
    dd = t // 64 + 8 - 2 * r - s // 64
    vis = (dd >= 0) & (dd <= 8)
    return idx, vis

_IDX, _VIS = bias_index()

def layer_core_inputs(P, l, j, lam_init=None):
    d = {}
    d['wsel'] = np.ascontiguousarray(P['w_in'][l][:, sel_cols(j)])
    d['gmix'] = np.ascontiguousarray(P['mix_norm'][l].reshape(8, 128).T)
    praw = np.zeros((64, 8), np.float32)
    praw[:, 0] = np.tile(P['c_q_norm'][l], 2); praw[:, 1] = np.tile(P['c_k_norm'][l], 2)
    praw[:, 2] = P['c_out_norm'][l]; praw[:, 3] = P['a_q_norm'][l]; praw[:, 4] = P['a_k_norm'][l]
    if lam_init is None:
        lam_init = 0.8 - 0.6 * np.exp(-0.3 * l)
    praw[:, 5] = lam_init; praw[:, 6] = 1.0 - lam_init
    d['praw'] = praw
    d['clam'] = np.ascontiguousarray(np.broadcast_to(P['c_lambda'][l][None], (64, 4, 32))).astype(np.float32)
    rb = P['a_rel_bias'][l][j]
    d['biasT'] = np.where(_VIS, rb[_IDX], np.float32(-1e30)).astype(np.float32)
    bpar = np.zeros((128, 8), np.float32)
    ch = np.concatenate([np.arange(j * 64, j * 64 + 64), 256 + np.arange(j * 64, j * 64 + 64)])
    bpar[:, 0:4] = P['b_conv_w'][l][:, ch].T
    bpar[:, 4] = P['b_conv_b'][l][ch]
    bpar[:, 5] = P['b_gate_bias'][l][0, j]
    bpar[:, 6] = P['b_gate_bias'][l][1, j]
    d['bpar'] = bpar
    d['gob'] = np.ascontiguousarray(np.broadcast_to(P['b_out_norm'][l][j][None], (128, 64))).astype(np.float32)
    return d


from concourse.bass_utils import run_bass_kernel_spmd

SEQ = 16384
NCORE = 8
TPC = 4096
DEPTH = 2
GROUPS = [[0, 1, 2, 3], [4, 5, 6, 7]]


def _din(nc, name, shape, dt):
    return nc.dram_tensor(name, list(shape), dt, kind="ExternalInput").ap()


def _dout(nc, name, shape, dt):
    return nc.dram_tensor(name, list(shape), dt, kind="ExternalOutput").ap()


def _dint(nc, name, shape, dt):
    return nc.dram_tensor(name, list(shape), dt, kind="Internal").ap()


MIX_IN = dict(wsel=([D, NW], F32), gmix=([128, 8], F32), praw=([64, 8], F32), clam=([64, 4, 32], F32),
              biasT=([128, 8, 512], F32), bpar=([128, 8], F32), gob=([128, 64], F32))
CONST_IN = dict(ident=([128, 128], BF16), cst=([128, 256], BF16), masks_c=([128, 4, 2, 512], BF16),
                masks_d=([128, 4, 512], BF16), tri=([128, 3, 128], BF16), trif=([128, 2, 128], F32), cmask=([128, 128], F32))


def build_fused(S=SEQ, T=TPC):
    nc = bass.Bass("TRN2", target_bir_lowering=False)
    NQr = T // QT_
    x_in = _din(nc, "x_in", [T, D], F32)
    x_out = _dout(nc, "x_out", [T, D], F32)
    Cn = {k: _din(nc, k, sh, dt) for k, (sh, dt) in CONST_IN.items()}
    ffn = {}
    for l in range(DEPTH):
        for f in ("ffn1", "ffn2"):
            ffn[(f, l)] = dict(g=_din(nc, f"{f}_g{l}", [128, 8], F32), wg=_din(nc, f"{f}_wg{l}", [D, DFF], F32),
                               wu=_din(nc, f"{f}_wu{l}", [D, DFF], F32), wd=_din(nc, f"{f}_wd{l}", [DFF, D], F32))
    wo = [_din(nc, f"wo{l}", [D, D], F32) for l in range(DEPTH)]
    mx = [{k: _din(nc, f"{k}{l}", sh, dt) for k, (sh, dt) in MIX_IN.items()} for l in range(DEPTH)]
    xa = _dint(nc, "xa", [T, D], F32); xb = _dint(nc, "xb", [T, D], F32); xc = _dint(nc, "xc", [T, D], F32)
    hT_loc = _dint(nc, "hT_loc", [D, T], BF16)
    hT_all = _dint(nc, "hT_all", [4 * D, T], BF16)
    yT_loc = _dint(nc, "yT_loc", [D, T], BF16)
    yT_all = _dint(nc, "yT_all", [4 * D, T], BF16)
    yT_mine = _dint(nc, "yT_mine", [D, T], BF16)

    hv = hT_all.rearrange("(k r p) t -> k r p t", k=8, r=4)

    def hT_src(qt):
        r, o = qt // NQr, (qt % NQr) * QT_
        return hv[:, r, :, o:o + QT_]

    def y_dst(row0, qt):
        q, o = qt // NQr, (qt % NQr) * QT_
        return yT_loc[q * 256 + row0:q * 256 + row0 + 64, o:o + QT_]

    with ExitStack() as st:
        kb = KB(nc, st)
        pid = nc.sync.partition_id()
        qv = pid % 4

        ymine_b = kb.buf()

        def fetch_mine():
            yv2 = yT_all.rearrange("(q h j p) t -> q h j p t", q=4, h=2, j=4)
            for j in range(4):
                for h in range(2):
                    kb.dma("sp", yT_mine[j * 256 + h * 128:j * 256 + (h + 1) * 128, :],
                           yv2[bass.ds(qv, 1), h, j, :, :].rearrange("o p t -> (o p) t"), writes=[ymine_b])

        def tok_phase(passes):
            with ExitStack() as mem:
                kb.mem = mem
                R = TokRes(kb, any(p.get("wo") is not None for p in passes))
                load_consts(kb, R, Cn["ident"])
                prev_bufs = None
                for k, p in enumerate(passes):
                    w = p["ffn"]
                    load_ffn_weights(kb, R, w["g"], w["wg"], w["wu"], w["wd"], p.get("wo"))
                    ob = [kb.buf() for _ in range(T // TT)] if k + 1 < len(passes) else None
                    has_pre = p.get("wo") is not None
                    token_pass(kb, R, T, p["xi"], p["xo"], pre=(yT_mine if has_pre else None),
                               post=p.get("post"), in_bufs=prev_bufs, out_bufs=ob,
                               pre_bufs=([ymine_b] * (T // TT) if has_pre else None))
                    prev_bufs = ob
                kb.barrier()
            kb.mem = st

        def mix_phase(l):
            with ExitStack() as mem:
                kb.mem = mem
                R = MixRes(kb, S)
                RB = MixResB(kb, R)
                m = mx[l]
                mix_load_common(kb, R, m["wsel"], m["gmix"], Cn["ident"], Cn["cst"])
                mix_params(kb, R, m["praw"], m["clam"])
                mixer_a(kb, R, hT_src, y_dst, 0, m["biasT"])
                kb.barrier()
                mixer_b(kb, R, RB, hT_src, y_dst, 64, m["bpar"], Cn["trif"], Cn["cmask"], m["gob"])
                kb.barrier()
                mixer_c(kb, R, hT_src, y_dst, 128, Cn["masks_c"])
                kb.barrier()
                mixer_d(kb, R, hT_src, y_dst, 192, Cn["masks_d"], Cn["tri"])
                kb.barrier()
            kb.mem = st

        tok_phase([dict(ffn=ffn[("ffn1", 0)], xi=x_in, xo=xa, post=hT_loc)])
        kb.allgather(hT_loc, hT_all, GROUPS)
        mix_phase(0)
        kb.allgather(yT_loc, yT_all, GROUPS)
        fetch_mine()
        tok_phase([dict(ffn=ffn[("ffn2", 0)], wo=wo[0], xi=xa, xo=xb),
                   dict(ffn=ffn[("ffn1", 1)], xi=xb, xo=xc, post=hT_loc)])
        kb.allgather(hT_loc, hT_all, GROUPS)
        mix_phase(1)
        kb.allgather(yT_loc, yT_all, GROUPS)
        fetch_mine()
        tok_phase([dict(ffn=ffn[("ffn2", 1)], wo=wo[1], xi=xc, xo=x_out)])
        kb.finish()
    return nc


def build_mixer_prog(S=SEQ):
    nc = bass.Bass("TRN2", target_bir_lowering=False)
    hT = _din(nc, "hT", [D, S], BF16)
    Cn = {k: _din(nc, k, sh, dt) for k, (sh, dt) in CONST_IN.items()}
    m = {k: _din(nc, k, sh, dt) for k, (sh, dt) in MIX_IN.items()}
    yT = _dout(nc, "yT", [256, S], BF16)
    with ExitStack() as st:
        kb = KB(nc, st)
        R = MixRes(kb, S)
        RB = MixResB(kb, R)
        mix_load_common(kb, R, m["wsel"], m["gmix"], Cn["ident"], Cn["cst"])
        mix_params(kb, R, m["praw"], m["clam"])
        mixer_a(kb, R, hT, yT, 0, m["biasT"])
        kb.barrier()
        mixer_b(kb, R, RB, hT, yT, 64, m["bpar"], Cn["trif"], Cn["cmask"], m["gob"])
        kb.barrier()
        mixer_c(kb, R, hT, yT, 128, Cn["masks_c"])
        kb.barrier()
        mixer_d(kb, R, hT, yT, 192, Cn["masks_d"], Cn["tri"])
        kb.finish()
    return nc


def _lay(g):
    return np.ascontiguousarray(np.asarray(g, np.float32).reshape(8, 128).T)


def _wo_perm(w_out):
    idx = np.arange(1024).reshape(4, 4, 64)
    perm = idx.transpose(1, 0, 2).reshape(-1)
    return np.ascontiguousarray(w_out[perm, :])


def make_in_maps(P, TPC=TPC):
    x = np.ascontiguousarray(P["x"], dtype=np.float32).reshape(-1, D)
    C = const_inputs()
    shared = dict(C)
    for l in range(DEPTH):
        for f in ("ffn1", "ffn2"):
            shared[f"{f}_g{l}"] = _lay(P[f + "_norm"][l])
            shared[f"{f}_wg{l}"] = np.ascontiguousarray(P[f + "_wg"][l], dtype=np.float32)
            shared[f"{f}_wu{l}"] = np.ascontiguousarray(P[f + "_wu"][l], dtype=np.float32)
            shared[f"{f}_wd{l}"] = np.ascontiguousarray(P[f + "_wd"][l], dtype=np.float32)
        shared[f"wo{l}"] = _wo_perm(np.asarray(P["w_out"][l], np.float32))
    ims = []
    for c in range(NCORE):
        d = dict(shared)
        d["x_in"] = x[c * TPC:(c + 1) * TPC]
        j = c % 4
        for l in range(DEPTH):
            for k, v in layer_core_inputs(P, l, j).items():
                d[f"{k}{l}"] = v
        ims.append(d)
    return ims


def kernel(**inputs):
    P = {k: np.asarray(v) for k, v in inputs.items()}
    nc = build_fused()
    ims = make_in_maps(P)
    res = run_bass_kernel_spmd(nc, ims, core_ids=list(range(NCORE)))
    out = np.concatenate([r["x_out"] for r in res.results], axis=0).reshape(2, SEQ, D).astype(np.float32)
    return out
```

```python
import numpy as np
from contextlib import ExitStack
import concourse.bass as bass
import concourse.mybir as mybir

F32 = mybir.dt.float32
BF16 = mybir.dt.bfloat16
AF = mybir.ActivationFunctionType
ALU = mybir.AluOpType
AX = mybir.AxisListType

EPOCH = 4096


class Buf:
    __slots__ = ("w", "r", "name")

    def __init__(self, name=""):
        self.w = None
        self.r = {}
        self.name = name


class KB:
    def __init__(self, nc, stack):
        self.nc = nc
        self.st = stack
        self.E = {"pe": nc.tensor, "act": nc.scalar, "dve": nc.vector, "pool": nc.gpsimd, "sp": nc.sync}
        self.cnt = {e: 0 for e in self.E}
        self.sems = {e: [] for e in self.E}
        self.waited = {e: {} for e in self.E}
        self.ndma = 12
        self.dma_sems = {}
        self.dma_cnt = {}
        self.dma_rr = {}
        self.nsem = 0
        self.uid = 0
        self.mem = stack

    def sem(self, name):
        self.nsem += 1
        return self.st.enter_context(self.nc.semaphore(name))

    def sbuf(self, name, shape, dt):
        self.uid += 1
        return self.mem.enter_context(self.nc.sbuf_tensor(f"sb{self.uid}_" + name, list(shape), dt))

    def psum(self, name, shape, dt):
        self.uid += 1
        return self.mem.enter_context(self.nc.psum_tensor(f"ps{self.uid}_" + name, list(shape), dt))

    def buf(self, name=""):
        return Buf(name)

    def _esem(self, e, n):
        ep = (n - 1) // EPOCH
        while len(self.sems[e]) <= ep:
            self.sems[e].append(self.sem(f"c_{e}_{len(self.sems[e])}"))
        return self.sems[e][ep], (n - 1) % EPOCH + 1

    def _wait(self, e, ev):
        if ev[0] == "e":
            _, src, n = ev
            if src == e and e == "pe":
                return
            key = ("e", src)
            if self.waited[e].get(key, 0) >= n:
                return
            if src == e and n > self.cnt[e]:
                raise RuntimeError("self-wait on future event")
            s, v = self._esem(src, n)
            self.E[e].wait_ge(s, v)
            self.waited[e][key] = n
        else:
            _, q, i, k = ev
            key = ("d", q, i)
            if self.waited[e].get(key, 0) >= k:
                return
            self.E[e].wait_ge(self.dma_sems[q][i], 16 * k)
            self.waited[e][key] = k

    @staticmethod
    def _evkey(ev):
        return (ev[0], ev[1]) if ev[0] == "e" else (ev[0], ev[1], ev[2])

    def _collect(self, reads, writes):
        deps = []
        for b in reads:
            if b.w is not None:
                deps.append(b.w)
        for b in writes:
            if b.w is not None:
                deps.append(b.w)
            deps.extend(b.r.values())
        return deps

    def _record(self, ev, reads, writes):
        k = self._evkey(ev)
        for b in reads:
            b.r[k] = ev
        for b in writes:
            b.w = ev
            b.r = {}

    def op(self, e, fn, reads=(), writes=(), inc=True):
        for ev in self._collect(reads, writes):
            self._wait(e, ev)
        ins = fn()
        if inc:
            self.cnt[e] += 1
            s, v = self._esem(e, self.cnt[e])
            ins.then_inc(s, 1)
            ev = ("e", e, self.cnt[e])
        else:
            ev = ("e", e, self.cnt[e] + 1)
        self._record(ev, reads, writes)
        return ins

    def dma(self, q, out, in_, reads=(), writes=(), **kw):
        for ev in self._collect(reads, writes):
            self._wait(q, ev)
        if q not in self.dma_sems:
            self.dma_sems[q] = [self.sem(f"d_{q}_{i}") for i in range(self.ndma)]
            self.dma_cnt[q] = [0] * self.ndma
            self.dma_rr[q] = 0
        i = self.dma_rr[q]
        self.dma_rr[q] = (i + 1) % self.ndma
        if self.dma_cnt[q][i] > 0:
            self._wait(q, ("d", q, i, self.dma_cnt[q][i]))
        self.dma_cnt[q][i] += 1
        ins = self.E[q].dma_start(out=out, in_=in_, **kw)
        ins.then_inc(self.dma_sems[q][i], 16)
        ev = ("d", q, i, self.dma_cnt[q][i])
        self._record(ev, reads, writes)
        return ins

    def barrier(self, extra_sems=()):
        for e in self.E:
            for q in self.dma_sems:
                for i in range(self.ndma):
                    if self.dma_cnt[q][i] > 0:
                        self._wait(e, ("d", q, i, self.dma_cnt[q][i]))
            for src in ("pe", "act", "dve", "pool"):
                if self.cnt[src] > 0 and not (src == e and e == "pe"):
                    self._wait(e, ("e", src, self.cnt[src]))
            for (sm, v) in extra_sems:
                self.E[e].wait_ge(sm, v)

    def allgather(self, src2d, dst2d, groups, chunk_rows=128):
        self.barrier()
        R_ = src2d.shape[0]
        nk = R_ // chunk_rows
        ng = len(groups[0])
        if not hasattr(self, "cc_sem"):
            self.cc_sem = self.sem("ccsem")
            self.cc_cnt = 0
        for k in range(nk):
            self.nc.gpsimd.collective_compute("AllGather", ALU.bypass, replica_groups=groups,
                                              ins=[src2d[k * chunk_rows:(k + 1) * chunk_rows, :]],
                                              outs=[dst2d[k * ng * chunk_rows:(k + 1) * ng * chunk_rows, :]]).then_inc(self.cc_sem, 1)
            self.cc_cnt += 1
        for e in self.E:
            self.E[e].wait_ge(self.cc_sem, self.cc_cnt)

    def finish(self):
        for q in self.dma_sems:
            for i in range(self.ndma):
                if self.dma_cnt[q][i] > 0:
                    self._wait("sp", ("d", q, i, self.dma_cnt[q][i]))
        for e in ("pe", "act", "dve", "pool"):
            if self.cnt[e] > 0:
                self._wait("sp", ("e", e, self.cnt[e]))


D = 1024
DFF = 2816
NFC = DFF // 128
TT = 256
SUB = TT // 128
EPS = 1e-6


class TokRes:
    def __init__(self, kb, with_pre):
        nc = kb.nc
        self.kb = kb
        self.Wg = kb.sbuf("Wg", [128, 8, DFF], BF16); self.Wg_b = kb.buf()
        self.Wu = kb.sbuf("Wu", [128, 8, DFF], BF16); self.Wu_b = kb.buf()
        self.Wd = kb.sbuf("Wd", [128, NFC, D], BF16); self.Wd_b = kb.buf()
        self.stage = [kb.sbuf(f"stage{i}", [128, 1024], F32) for i in range(2)]
        self.stage_b = [kb.buf() for _ in range(2)]
        self.gt = kb.sbuf("gt", [128, 8], F32); self.gt_b = kb.buf()
        self.ident = kb.sbuf("ident", [128, 128], BF16); self.ident_b = kb.buf()
        self.xt = [kb.sbuf(f"xt{i}", [128, SUB, D], F32) for i in range(2)]
        self.xt_b = [kb.buf() for _ in range(2)]
        self.xn = kb.sbuf("xn", [128, D], BF16); self.xn_b = kb.buf()
        self.st = kb.sbuf("stat", [128, 8], F32); self.st_b = kb.buf()
        self.xnT = [kb.sbuf(f"xnT{i}", [128, 8, TT], BF16) for i in range(2)]
        self.xnT_b = [kb.buf() for _ in range(2)]
        self.hid = kb.sbuf("hid", [128, NFC, TT], BF16)
        self.hid_b = [kb.buf() for _ in range(NFC)]
        self.sg = [kb.sbuf(f"sg{i}", [128, TT], F32) for i in range(2)]
        self.sg_b = [kb.buf() for _ in range(2)]
        self.hTo = kb.sbuf("hTo", [128, 8, TT], BF16); self.hTo_b = kb.buf()
        self.with_pre = with_pre
        if with_pre:
            self.Wo = kb.sbuf("Wo", [128, 8, D], BF16); self.Wo_b = kb.buf()
            self.yt = [kb.sbuf(f"yt{i}", [128, 8, TT], BF16) for i in range(2)]
            self.yt_b = [kb.buf() for _ in range(2)]
        self.psg = [kb.psum(f"psg{i}", [128, 512], F32) for i in range(2)]
        self.psg_b = [kb.buf() for _ in range(2)]
        self.psd = [kb.psum(f"psd{i}", [128, 512], F32) for i in range(2)]
        self.psd_b = [kb.buf() for _ in range(2)]
        self.tp = [kb.psum(f"tp{i}", [128, 8, 128], BF16) for i in range(2)]
        self.tp_b = [kb.buf() for _ in range(2)]
        self.ntp = 0
        self.npsd = 0
        self.ncast = 0


def load_consts(kb, R, ident_d):
    kb.dma("sp", R.ident[:], ident_d, writes=[R.ident_b])


def load_ffn_weights(kb, R, g_lay, wg, wu, wd, w_out=None):
    nc = kb.nc
    kb.dma("sp", R.gt[:], g_lay, writes=[R.gt_b])

    def cast(dst_ap, dst_b, src_ap, src_b, scal):
        e = ("dve", "act", "dve", "act", "pool")[R.ncast % 5]
        R.ncast += 1
        E = kb.E[e]
        if e == "act":
            if scal is None:
                kb.op(e, lambda: E.copy(out=dst_ap, in_=src_ap), reads=[src_b], writes=[dst_b])
            else:
                kb.op(e, lambda: E.activation(out=dst_ap, in_=src_ap, func=AF.Copy, scale=scal),
                      reads=[src_b, R.gt_b], writes=[dst_b])
        elif scal is None:
            kb.op(e, lambda: E.tensor_copy(out=dst_ap, in_=src_ap), reads=[src_b], writes=[dst_b])
        else:
            kb.op(e, lambda: E.tensor_scalar(out=dst_ap, in0=src_ap, scalar1=scal, scalar2=None, op0=ALU.mult),
                  reads=[src_b, R.gt_b], writes=[dst_b])

    k = 0
    for (W, Wb, src) in ((R.Wg, R.Wg_b, wg), (R.Wu, R.Wu_b, wu)):
        for kc in range(8):
            for (c0, c1) in ((0, 1024), (1024, 2048), (2048, DFF)):
                sb = k % 2; k += 1
                kb.dma("sp", R.stage[sb][:, 0:c1 - c0], src[kc * 128:(kc + 1) * 128, c0:c1],
                       writes=[R.stage_b[sb]])
                cast(W[:, kc, c0:c1], Wb, R.stage[sb][:, 0:c1 - c0], R.stage_b[sb], R.gt[:, kc:kc + 1])
    for fc in range(NFC):
        sb = k % 2; k += 1
        kb.dma("sp", R.stage[sb][:, 0:D], wd[fc * 128:(fc + 1) * 128, :], writes=[R.stage_b[sb]])
        cast(R.Wd[:, fc, :], R.Wd_b, R.stage[sb][:, 0:D], R.stage_b[sb], None)
    if w_out is not None:
        for kc in range(8):
            sb = k % 2; k += 1
            kb.dma("sp", R.stage[sb][:, 0:D], w_out[kc * 128:(kc + 1) * 128, :], writes=[R.stage_b[sb]])
            cast(R.Wo[:, kc, :], R.Wo_b, R.stage[sb][:, 0:D], R.stage_b[sb], None)


def norm_transpose(kb, R, x_ap, x_b, dstT, dstT_b, s):
    nc = kb.nc
    ss = R.st[:, 0:1]; rs = R.st[:, 1:2]; rstd = R.st[:, 2:3]
    kb.op("act", lambda: nc.scalar.activation(out=R.xn[:], in_=x_ap, func=AF.Square, accum_out=ss),
          reads=[x_b], writes=[R.xn_b, R.st_b])
    kb.op("act", lambda: nc.scalar.activation(out=rs, in_=ss, func=AF.Sqrt, bias=EPS, scale=1.0 / D),
          reads=[R.st_b], writes=[R.st_b])
    kb.op("dve", lambda: nc.vector.reciprocal(out=rstd, in_=rs), reads=[R.st_b], writes=[R.st_b])
    kb.op("dve", lambda: nc.vector.tensor_scalar(out=R.xn[:], in0=x_ap, scalar1=rstd, scalar2=None, op0=ALU.mult),
          reads=[x_b, R.st_b], writes=[R.xn_b])
    ti = R.ntp % 2; R.ntp += 1
    tp = R.tp[ti]; tpb = R.tp_b[ti]
    for kc in range(8):
        kb.op("pe", lambda kc=kc: nc.tensor.transpose(out=tp[:, kc, :], in_=R.xn[:, kc * 128:(kc + 1) * 128],
                                                      identity=R.ident[:]),
              reads=[R.xn_b, R.ident_b], writes=[tpb], inc=(kc == 7))
    kb.op("act", lambda: nc.scalar.copy(out=dstT[:, :, s * 128:(s + 1) * 128], in_=tp[:, :, :]),
          reads=[tpb], writes=[dstT_b])


def token_pass(kb, R, T, x_in, x_out, pre=None, post=None, in_bufs=None, out_bufs=None, pre_bufs=None, post_bufs=None):
    nc = kb.nc
    NT = T // TT

    def stage_load(i):
        bi = i % 2
        kb.dma("sp", R.xt[bi][:, :, :], x_in[i * TT:(i + 1) * TT, :].rearrange("(s p) d -> p s d", p=128),
               reads=([in_bufs[i]] if in_bufs else []), writes=[R.xt_b[bi]])
        if pre is not None:
            kb.dma("sp", R.yt[bi][:, :, :], pre[:, i * TT:(i + 1) * TT].rearrange("(c p) t -> p c t", p=128),
                   reads=([pre_bufs[i]] if pre_bufs else []), writes=[R.yt_b[bi]])

    def stage_pre(i):
        bi = i % 2
        if pre is None:
            return
        for s in range(SUB):
            for h in range(2):
                pi = R.npsd % 2; R.npsd += 1
                for kc in range(8):
                    kb.op("pe", lambda kc=kc: nc.tensor.matmul(R.psd[pi][:, :], lhsT=R.yt[bi][:, kc, s * 128:(s + 1) * 128],
                                                               rhs=R.Wo[:, kc, h * 512:(h + 1) * 512],
                                                               start=(kc == 0), stop=(kc == 7)),
                          reads=[R.yt_b[bi], R.Wo_b], writes=[R.psd_b[pi]], inc=(kc == 7))
                xs = R.xt[bi][:, s, h * 512:(h + 1) * 512]
                kb.op("dve", lambda: nc.vector.tensor_tensor(out=xs, in0=R.psd[pi][:, :], in1=xs, op=ALU.add),
                      reads=[R.psd_b[pi], R.xt_b[bi]], writes=[R.xt_b[bi]])

    def stage_a(i):
        bi = i % 2
        for s in range(SUB):
            norm_transpose(kb, R, R.xt[bi][:, s, :], R.xt_b[bi], R.xnT[bi], R.xnT_b[bi], s)

    def stage_b(i):
        bi = i % 2
        for fc in range(NFC):
            gi = fc % 2
            for (W, Wb, off) in ((R.Wg, R.Wg_b, 0), (R.Wu, R.Wu_b, 256)):
                for kc in range(8):
                    kb.op("pe", lambda kc=kc, W=W, off=off: nc.tensor.matmul(
                        R.psg[gi][:, off:off + TT], lhsT=W[:, kc, fc * 128:(fc + 1) * 128], rhs=R.xnT[bi][:, kc, :],
                        start=(kc == 0), stop=(kc == 7)),
                          reads=[Wb, R.xnT_b[bi]], writes=[R.psg_b[gi]], inc=(kc == 7))
            kb.op("act", lambda: nc.scalar.activation(out=R.sg[gi][:, :], in_=R.psg[gi][:, 0:TT], func=AF.Silu),
                  reads=[R.psg_b[gi]], writes=[R.sg_b[gi]])
            kb.op("dve", lambda: nc.vector.tensor_tensor(out=R.hid[:, fc, :], in0=R.sg[gi][:, :],
                                                         in1=R.psg[gi][:, 256:256 + TT], op=ALU.mult),
                  reads=[R.sg_b[gi], R.psg_b[gi]], writes=[R.hid_b[fc]])

    def stage_c(i):
        bi = i % 2
        for s in range(SUB):
            for h in range(2):
                pi = R.npsd % 2; R.npsd += 1
                for fc in range(NFC):
                    kb.op("pe", lambda fc=fc: nc.tensor.matmul(R.psd[pi][:, :], lhsT=R.hid[:, fc, s * 128:(s + 1) * 128],
                                                               rhs=R.Wd[:, fc, h * 512:(h + 1) * 512],
                                                               start=(fc == 0), stop=(fc == NFC - 1)),
                          reads=[R.hid_b[fc], R.Wd_b], writes=[R.psd_b[pi]], inc=(fc == NFC - 1))
                xs = R.xt[bi][:, s, h * 512:(h + 1) * 512]
                kb.op("dve", lambda: nc.vector.scalar_tensor_tensor(out=xs, in0=R.psd[pi][:, :], scalar=0.5, in1=xs,
                                                                    op0=ALU.mult, op1=ALU.add),
                      reads=[R.psd_b[pi], R.xt_b[bi]], writes=[R.xt_b[bi]])
            if post is not None:
                norm_transpose(kb, R, R.xt[bi][:, s, :], R.xt_b[bi], R.hTo, R.hTo_b, s)
        kb.dma("pool", x_out[i * TT:(i + 1) * TT, :].rearrange("(s p) d -> p s d", p=128), R.xt[bi][:, :, :],
               reads=[R.xt_b[bi]], writes=([out_bufs[i]] if out_bufs else []))
        if post is not None:
            kb.dma("pool", post[:, i * TT:(i + 1) * TT].rearrange("(c p) t -> p c t", p=128), R.hTo[:, :, :],
                   reads=[R.hTo_b], writes=([post_bufs[i]] if post_bufs else []))

    stage_load(0)
    stage_pre(0)
    stage_a(0)
    for i in range(NT):
        if i + 1 < NT:
            stage_load(i + 1)
        stage_b(i)
        if i + 1 < NT:
            stage_pre(i + 1)
            stage_a(i + 1)
        stage_c(i)


QT_ = 512
NW = 834
A_Q, A_K, A_V = 0, 64, 128
B_Q, B_K, B_V, B_O, B_I, B_F = 192, 256, 320, 384, 448, 449
C_Q, C_K, C_V = 450, 514, 578
D_Q, D_K, D_V = 642, 706, 770


class MixRes:
    def __init__(self, kb, S):
        self.S = S
        self.NB = S // 128
        self.NQ = S // QT_
        NB = self.NB
        self.Wm = kb.sbuf("Wm", [128, 8, NW], BF16); self.Wm_b = kb.buf()
        self.gm = kb.sbuf("gm", [128, 8], F32); self.gm_b = kb.buf()
        self.ident = kb.sbuf("identm", [128, 128], BF16); self.ident_b = kb.buf()
        self.ht = [kb.sbuf(f"ht{i}", [128, 8, QT_], BF16) for i in range(2)]; self.ht_b = [kb.buf() for _ in range(2)]
        self.QT = kb.sbuf("QT", [128, S], BF16); self.QT_b = kb.buf()
        self.KT = kb.sbuf("KT", [128, S], BF16); self.KT_b = kb.buf()
        self.Va = kb.sbuf("Va", [128, NB, 128], BF16); self.Va_b = kb.buf()
        self.par = kb.sbuf("par", [128, 32], F32); self.par_b = kb.buf()
        self.cst = kb.sbuf("cst", [128, 256], BF16); self.cst_b = kb.buf()
        self.sq = [kb.sbuf(f"sq{i}", [64, QT_], BF16) for i in range(2)]; self.sq_b = [kb.buf() for _ in range(2)]
        self.rr = [kb.sbuf(f"rr{i}", [64, QT_], F32) for i in range(2)]; self.rr_b = [kb.buf() for _ in range(2)]
        self.e32 = [kb.sbuf(f"e32_{i}", [128, 2, QT_], F32) for i in range(2)]; self.e32_b = [kb.buf() for _ in range(2)]
        self.wst = [self.e32[i][:, :, :].rearrange("p a b -> p (a b)")[:, 0:NW] for i in range(2)]; self.wst_b = self.e32_b
        self.e32c = kb.sbuf("e32_c", [128, 2, QT_], F32)
        self.e16 = [kb.sbuf(f"e16_{i}", [128, 2, QT_], BF16) for i in range(4)]; self.e16_b = [kb.buf() for _ in range(4)]
        self.spp = [kb.sbuf(f"spp_{i}", [128, 2, QT_], BF16) for i in range(2)]; self.spp_b = [kb.buf() for _ in range(2)]
        self.fin = [kb.sbuf(f"fin{i}", [64, QT_], F32) for i in range(3)]; self.fin_b = [kb.buf() for _ in range(3)]
        self.yo = [kb.sbuf(f"yo{i}", [64, QT_], BF16) for i in range(2)]; self.yo_b = [kb.buf() for _ in range(2)]
        self.mask = kb.sbuf("mask", [128, 4, QT_], BF16); self.mask_b = kb.buf()
        self.EB = kb.sbuf("EB", [128, 8, QT_], F32); self.EB_b = kb.buf()
        self.tri = kb.sbuf("tri", [128, 3, 128], BF16); self.tri_b = kb.buf()
        self.pp = [kb.psum(f"pp{i}", [128, 2, 512], F32) for i in range(4)]
        self.ps = [self.pp[i // 2][:, i % 2, :] for i in range(8)]
        self.ps_b = [kb.buf() for _ in range(8)]
        self.nyo = 0
        self.ne16 = 0


def mix_load_common(kb, R, wsel, gmix_lay, ident_d, cst_d):
    nc = kb.nc
    kb.dma("sp", R.gm[:], gmix_lay, writes=[R.gm_b])
    kb.dma("sp", R.ident[:], ident_d, writes=[R.ident_b])
    kb.dma("sp", R.cst[:], cst_d, writes=[R.cst_b])
    for kc in range(8):
        sb = kc % 2
        kb.dma("sp", R.wst[sb][:, :], wsel[kc * 128:(kc + 1) * 128, :], writes=[R.wst_b[sb]])
        kb.op("dve", lambda: nc.vector.tensor_scalar(out=R.Wm[:, kc, :], in0=R.wst[sb][:, :], scalar1=R.gm[:, kc:kc + 1],
                                                     scalar2=None, op0=ALU.mult),
              reads=[R.wst_b[sb], R.gm_b], writes=[R.Wm_b])


def load_ht(kb, R, hT, qt):
    bi = qt % 2
    if callable(hT):
        src = hT(qt).rearrange("c p t -> p c t")
    else:
        src = hT[:, qt * QT_:(qt + 1) * QT_].rearrange("(c p) t -> p c t", p=128)
    kb.dma("sp", R.ht[bi][:, :, :], src, writes=[R.ht_b[bi]])
    return R.ht[bi], R.ht_b[bi]


def proj_fm(kb, R, ht, ht_b, c0, ncol, pb):
    nc = kb.nc
    for kc in range(8):
        kb.op("pe", lambda kc=kc: nc.tensor.matmul(R.ps[pb][0:ncol, :], lhsT=R.Wm[:, kc, c0:c0 + ncol], rhs=ht[:, kc, :],
                                                   start=(kc == 0), stop=(kc == 7)),
              reads=[R.Wm_b, ht_b], writes=[R.ps_b[pb]], inc=(kc == 7))


def proj_tm(kb, R, ht, ht_b, c0, ncol, pb, s):
    nc = kb.nc
    for kc in range(8):
        kb.op("pe", lambda kc=kc: nc.tensor.matmul(R.ps[pb][:, s * 128:s * 128 + ncol], lhsT=ht[:, kc, s * 128:(s + 1) * 128],
                                                   rhs=R.Wm[:, kc, c0:c0 + ncol], start=(kc == 0), stop=(kc == 7)),
              reads=[R.Wm_b, ht_b], writes=[R.ps_b[pb]], inc=(kc == 7))


def qk_norm_store(kb, R, pb, pb2, dst, dst_b, qt, gcol, cmat, inv_n, i2):
    nc = kb.nc
    sq, sqb = R.sq[i2], R.sq_b[i2]
    rr, rrb = R.rr[i2], R.rr_b[i2]
    kb.op("act", lambda: nc.scalar.activation(out=sq[:, :], in_=R.ps[pb][0:64, :], func=AF.Square),
          reads=[R.ps_b[pb]], writes=[sqb])
    kb.op("pe", lambda: nc.tensor.matmul(R.ps[pb2][0:64, :], lhsT=cmat, rhs=sq[:, :], start=True, stop=True),
          reads=[sqb, R.cst_b], writes=[R.ps_b[pb2]])
    kb.op("act", lambda: nc.scalar.activation(out=rr[:, :], in_=R.ps[pb2][0:64, :], func=AF.Ln, bias=R.par[0:64, 13:14], scale=inv_n),
          reads=[R.ps_b[pb2], R.par_b], writes=[rrb])
    kb.op("act", lambda: nc.scalar.activation(out=rr[:, :], in_=rr[:, :], func=AF.Exp, scale=-0.5), reads=[rrb], writes=[rrb])
    kb.op("dve", lambda: nc.vector.scalar_tensor_tensor(out=dst[0:64, qt * QT_:(qt + 1) * QT_], in0=R.ps[pb][0:64, :],
                                                        scalar=R.par[0:64, gcol:gcol + 1], in1=rr[:, :],
                                                        op0=ALU.mult, op1=ALU.mult),
          reads=[R.ps_b[pb], rrb, R.par_b], writes=[dst_b])


def v_store(kb, R, ht, ht_b, c0, qt, pb):
    nc = kb.nc
    for s in range(4):
        proj_tm(kb, R, ht, ht_b, c0, 64, pb, s)
    src = R.ps[pb][:, :].rearrange("p (s c) -> p s c", c=128)[:, :, 0:64]
    kb.op("act", lambda: nc.scalar.copy(out=R.Va[:, qt * 4:(qt + 1) * 4, 0:64], in_=src),
          reads=[R.ps_b[pb]], writes=[R.Va_b])


def ydst(yT_d, row0, qt):
    if callable(yT_d):
        return yT_d(row0, qt)
    return yT_d[row0:row0 + 64, qt * QT_:(qt + 1) * QT_]


def out_store(kb, R, yT_d, row0, qt, src_fn, reads):
    i = R.nyo % 2; R.nyo += 1
    src_fn(R.yo[i], R.yo_b[i])
    kb.dma("pool", ydst(yT_d, row0, qt), R.yo[i][:, :], reads=[R.yo_b[i]])


def mixer_c(kb, R, hT, yT_d, row0, masks_c):
    nc = kb.nc
    S, NB, NQ = R.S, R.NB, R.NQ
    kb.dma("sp", R.mask[:, :, :], masks_c[:, :, 0, :], writes=[R.mask_b])
    kb.op("pool", lambda: nc.gpsimd.memset(R.Va[:, :, 64:128], 1.0), writes=[R.Va_b])
    bd32 = R.cst[0:64, 64:128]
    ones64 = R.cst[0:64, 0:64]
    for qt in range(NQ):
        ht, htb = load_ht(kb, R, hT, qt)
        proj_fm(kb, R, ht, htb, C_Q, 64, 0)
        proj_fm(kb, R, ht, htb, C_K, 64, 2)
        qk_norm_store(kb, R, 0, 1, R.QT, R.QT_b, qt, 0, bd32, 1.0 / 32, 0)
        qk_norm_store(kb, R, 2, 3, R.KT, R.KT_b, qt, 1, bd32, 1.0 / 32, 1)
        v_store(kb, R, ht, htb, C_V, qt, 4 + (qt % 2))
    for qt in range(NQ):
        nkb = 4 * qt + 4
        O0, O1 = 6, 7
        estate = {}

        def s_step(kbk):
            pj = kbk % 3
            sb = 2 * pj
            for m in range(2):
                kb.op("pe", lambda m=m: nc.tensor.matmul(R.ps[sb + m],
                                                         lhsT=R.KT[m * 32:(m + 1) * 32, kbk * 128:(kbk + 1) * 128],
                                                         rhs=R.QT[m * 32:(m + 1) * 32, qt * QT_:(qt + 1) * QT_],
                                                         start=True, stop=True),
                      reads=[R.KT_b, R.QT_b], writes=[R.ps_b[sb + m]])
            ei = R.ne16 % 3; R.ne16 += 1
            e, eb = R.e16[ei], R.e16_b[ei]
            estate[kbk] = (e, eb)
            kb.op("act", lambda: nc.scalar.activation(out=e[:, :, :], in_=R.pp[pj][:, :, :], func=AF.Exp),
                  reads=[R.ps_b[sb], R.ps_b[sb + 1]], writes=[eb])
            r = kbk - 4 * qt
            if r >= 0:
                for m in range(2):
                    kb.op("dve", lambda m=m: nc.vector.tensor_tensor(out=e[:, m, :], in0=e[:, m, :], in1=R.mask[:, r, :], op=ALU.mult),
                          reads=[eb, R.mask_b], writes=[eb])

        def pv_step(kbk):
            e, eb = estate.pop(kbk)
            for m in range(2):
                kb.op("pe", lambda m=m: nc.tensor.matmul(R.ps[O0 + m][:, :], lhsT=R.Va[:, kbk, :], rhs=e[:, m, :],
                                                         start=(kbk == 0), stop=(kbk == nkb - 1)),
                      reads=[R.Va_b, eb], writes=[R.ps_b[O0 + m]])

        s_step(0)
        if nkb > 1:
            s_step(1)
        for kbk in range(nkb):
            if kbk + 2 < nkb:
                s_step(kbk + 2)
            pv_step(kbk)
        f0, f1, f2 = R.fin
        b0, b1, b2 = R.fin_b
        kb.op("dve", lambda: nc.vector.reciprocal(out=f0[:, :], in_=R.ps[O0][64:128, :]), reads=[R.ps_b[O0]], writes=[b0])
        kb.op("dve", lambda: nc.vector.tensor_tensor(out=f0[:, :], in0=R.ps[O0][0:64, :], in1=f0[:, :], op=ALU.mult),
              reads=[R.ps_b[O0], b0], writes=[b0])
        kb.op("dve", lambda: nc.vector.reciprocal(out=f1[:, :], in_=R.ps[O1][64:128, :]), reads=[R.ps_b[O1]], writes=[b1])
        kb.op("dve", lambda: nc.vector.tensor_tensor(out=f1[:, :], in0=R.ps[O1][0:64, :], in1=f1[:, :], op=ALU.mult),
              reads=[R.ps_b[O1], b1], writes=[b1])
        kb.op("dve", lambda: nc.vector.scalar_tensor_tensor(out=f2[:, :], in0=f1[:, :], scalar=R.par[0:64, 2:3], in1=f0[:, :],
                                                            op0=ALU.mult, op1=ALU.add),
              reads=[b0, b1, R.par_b], writes=[b2])
        kb.op("act", lambda: nc.scalar.activation(out=R.sq[0][:, :], in_=f2[:, :], func=AF.Square), reads=[b2], writes=[R.sq_b[0]])
        kb.op("pe", lambda: nc.tensor.matmul(R.ps[0][0:64, :], lhsT=ones64, rhs=R.sq[0][:, :], start=True, stop=True),
              reads=[R.sq_b[0], R.cst_b], writes=[R.ps_b[0]])
        kb.op("act", lambda: nc.scalar.activation(out=R.rr[0][:, :], in_=R.ps[0][0:64, :], func=AF.Ln, bias=R.par[0:64, 13:14], scale=1.0 / 64),
              reads=[R.ps_b[0], R.par_b], writes=[R.rr_b[0]])
        kb.op("act", lambda: nc.scalar.activation(out=R.rr[0][:, :], in_=R.rr[0][:, :], func=AF.Exp, scale=-0.5), reads=[R.rr_b[0]], writes=[R.rr_b[0]])

        def fn(yo, yob):
            kb.op("dve", lambda: nc.vector.scalar_tensor_tensor(out=yo[:, :], in0=f2[:, :], scalar=R.par[0:64, 3:4], in1=R.rr[0][:, :],
                                                                op0=ALU.mult, op1=ALU.mult),
                  reads=[b2, R.rr_b[0], R.par_b], writes=[yob])
        out_store(kb, R, yT_d, row0, qt, fn, None)


def mixer_d(kb, R, hT, yT_d, row0, masks_d, tri_d):
    nc = kb.nc
    S, NB, NQ = R.S, R.NB, R.NQ
    kb.dma("sp", R.mask[:, :, :], masks_d, writes=[R.mask_b])
    kb.dma("sp", R.tri[:, :, :], tri_d, writes=[R.tri_b])
    for qt in range(NQ):
        ht, htb = load_ht(kb, R, hT, qt)
        proj_fm(kb, R, ht, htb, D_Q, 64, 0)
        proj_fm(kb, R, ht, htb, D_K, 64, 1)
        for half in range(2):
            rows = slice(half * 64, half * 64 + 64)
            kb.op("act", lambda rows=rows: nc.scalar.activation(out=R.QT[rows, qt * QT_:(qt + 1) * QT_], in_=R.ps[0][0:64, :], func=AF.Copy, scale=0.125),
                  reads=[R.ps_b[0]], writes=[R.QT_b])
            kb.op("dve", lambda rows=rows: nc.vector.tensor_copy(out=R.KT[rows, qt * QT_:(qt + 1) * QT_], in_=R.ps[1][0:64, :]),
                  reads=[R.ps_b[1]], writes=[R.KT_b])
        v_store(kb, R, ht, htb, D_V, qt, 4 + (qt % 2))
    RA, RB, OB = 4, 5, 6
    ed_b = [kb.buf() for _ in range(3)]
    ed = [R.e32[0], R.e32[1], R.e32c]
    for qt in range(NQ):
        kbs = list(range(4 * qt + 3, -1, -1))
        npair = len(kbs) // 2
        qsl = slice(qt * QT_, (qt + 1) * QT_)

        def blocks(j):
            return kbs[2 * j], kbs[2 * j + 1]

        def z_mm(j):
            for h, kbk in enumerate(blocks(j)):
                zb = 2 * (j % 2) + h
                rows = slice(h * 64, h * 64 + 64)
                kb.op("pe", lambda kbk=kbk, zb=zb, rows=rows: nc.tensor.matmul(R.ps[zb], lhsT=R.KT[rows, kbk * 128:(kbk + 1) * 128],
                                                                               rhs=R.QT[rows, qsl], start=True, stop=True),
                      reads=[R.KT_b, R.QT_b], writes=[R.ps_b[zb]])

        def esp(j):
            zp = j % 2
            e, eb = ed[j % 3], ed_b[j % 3]
            sp, spb = R.spp[j % 2], R.spp_b[j % 2]
            kb.op("act", lambda: nc.scalar.activation(out=e[:, :, :], in_=R.pp[zp][:, :, :], func=AF.Exp),
                  reads=[R.ps_b[2 * zp], R.ps_b[2 * zp + 1]], writes=[eb])
            kb.op("act", lambda: nc.scalar.activation(out=sp[:, :, :], in_=e[:, :, :], func=AF.Ln, bias=1.0),
                  reads=[eb], writes=[spb])
            for h, kbk in enumerate(blocks(j)):
                r = kbk - 4 * qt
                if r >= 0:
                    kb.op("dve", lambda h=h, r=r: nc.vector.tensor_tensor(out=sp[:, h, :], in0=sp[:, h, :], in1=R.mask[:, r, :], op=ALU.mult),
                          reads=[spb, R.mask_b], writes=[spb])

        def mmR(bank, t_i, sp_ap, spb, start, stop=False):
            kb.op("pe", lambda: nc.tensor.matmul(R.ps[bank], lhsT=R.tri[:, t_i, :], rhs=sp_ap, start=start, stop=stop),
                  reads=[R.tri_b, spb], writes=[R.ps_b[bank]])

        def chain_a(j):
            sp, spb = R.spp[j % 2], R.spp_b[j % 2]
            mmR(RA, 0, sp[:, 0, :], spb, j == 0)
            mmR(RB, 2, sp[:, 0, :], spb, j == 0)
            mmR(RB, 0, sp[:, 1, :], spb, False)

        def chain_b(j):
            e, eb = ed[j % 3], ed_b[j % 3]
            sp, spb = R.spp[j % 2], R.spp_b[j % 2]
            tt, ttb = R.e16[j % 2], R.e16_b[j % 2]
            aa, aab = R.e16[2 + j % 2], R.e16_b[2 + j % 2]
            kb.op("act", lambda: nc.scalar.activation(out=tt[:, :, :], in_=R.pp[2][:, :, :], func=AF.Exp, scale=-1.0),
                  reads=[R.ps_b[RA], R.ps_b[RB]], writes=[ttb])
            mmR(RA, 1, sp[:, 0, :], spb, False)
            mmR(RA, 2, sp[:, 1, :], spb, False, j == npair - 1)
            mmR(RB, 1, sp[:, 1, :], spb, False, j == npair - 1)
            kb.op("dve", lambda: nc.vector.tensor_tensor(out=aa[:, :, :], in0=e[:, :, :], in1=tt[:, :, :], op=ALU.mult),
                  reads=[eb, ttb], writes=[aab])
            for h, kbk in enumerate(blocks(j)):
                r = kbk - 4 * qt
                if r >= 0:
                    kb.op("dve", lambda h=h, r=r: nc.vector.tensor_tensor(out=aa[:, h, :], in0=aa[:, h, :], in1=R.mask[:, r, :], op=ALU.mult),
                          reads=[aab, R.mask_b], writes=[aab])

        def pv(j):
            aa, aab = R.e16[2 + j % 2], R.e16_b[2 + j % 2]
            for h, kbk in enumerate(blocks(j)):
                kb.op("pe", lambda h=h, kbk=kbk: nc.tensor.matmul(R.ps[OB][0:64, :], lhsT=R.Va[:, kbk, 0:64], rhs=aa[:, h, :],
                                                                  start=(j == 0 and h == 0), stop=(j == npair - 1 and h == 1)),
                      reads=[R.Va_b, aab], writes=[R.ps_b[OB]])

        z_mm(0)
        if npair > 1:
            z_mm(1)
        esp(0)
        for j in range(npair):
            if j + 1 < npair:
                esp(j + 1)
            chain_a(j)
            if j + 2 < npair:
                z_mm(j + 2)
            if j >= 1:
                pv(j - 1)
            chain_b(j)
        pv(npair - 1)

        def fn(yo, yob):
            kb.op("dve", lambda: nc.vector.tensor_copy(out=yo[:, :], in_=R.ps[OB][0:64, :]), reads=[R.ps_b[OB]], writes=[yob])
        out_store(kb, R, yT_d, row0, qt, fn, None)


def mixer_a(kb, R, hT, yT_d, row0, biasT_d):
    nc = kb.nc
    S, NB, NQ = R.S, R.NB, R.NQ
    kb.dma("sp", R.EB[:, :, :], biasT_d, writes=[R.EB_b])
    for r in range(8):
        kb.op("act", lambda r=r: nc.scalar.activation(out=R.EB[:, r, :], in_=R.EB[:, r, :], func=AF.Exp),
              reads=[R.EB_b], writes=[R.EB_b])
    kb.op("pool", lambda: nc.gpsimd.memset(R.Va[:, :, 64:128], 1.0), writes=[R.Va_b])
    ones64 = R.cst[0:64, 0:64]
    for qt in range(NQ):
        ht, htb = load_ht(kb, R, hT, qt)
        proj_fm(kb, R, ht, htb, A_Q, 64, 0)
        proj_fm(kb, R, ht, htb, A_K, 64, 2)
        qk_norm_store(kb, R, 0, 1, R.QT, R.QT_b, qt, 4, ones64, 1.0 / 64, 0)
        qk_norm_store(kb, R, 2, 3, R.KT, R.KT_b, qt, 5, ones64, 1.0 / 64, 1)
        v_store(kb, R, ht, htb, A_V, qt, 4 + (qt % 2))
    OB = 4
    for qt in range(NQ):
        rs = [r for r in range(8) if 4 * qt - 4 + r >= 0]
        for j, r in enumerate(rs):
            kbk = 4 * qt - 4 + r
            sb = j % 2
            kb.op("pe", lambda: nc.tensor.matmul(R.ps[sb][:, :], lhsT=R.KT[0:64, kbk * 128:(kbk + 1) * 128],
                                                 rhs=R.QT[0:64, qt * QT_:(qt + 1) * QT_], start=True, stop=True),
                  reads=[R.KT_b, R.QT_b], writes=[R.ps_b[sb]])
            e, eb = R.e32[j % 2], R.e32_b[j % 2]
            p, pbuf = R.e16[j % 2], R.e16_b[j % 2]
            kb.op("act", lambda: nc.scalar.activation(out=e[:, 0, :], in_=R.ps[sb][:, :], func=AF.Exp),
                  reads=[R.ps_b[sb]], writes=[eb])
            kb.op("dve", lambda: nc.vector.tensor_tensor(out=p[:, 0, :], in0=e[:, 0, :], in1=R.EB[:, r, :], op=ALU.mult),
                  reads=[eb, R.EB_b], writes=[pbuf])
            kb.op("pe", lambda: nc.tensor.matmul(R.ps[OB][:, :], lhsT=R.Va[:, kbk, :], rhs=p[:, 0, :],
                                                 start=(j == 0), stop=(j == len(rs) - 1)),
                  reads=[R.Va_b, pbuf], writes=[R.ps_b[OB]])
        f0, b0 = R.fin[0], R.fin_b[0]
        kb.op("dve", lambda: nc.vector.reciprocal(out=f0[:, :], in_=R.ps[OB][64:128, :]), reads=[R.ps_b[OB]], writes=[b0])

        def fn(yo, yob):
            kb.op("dve", lambda: nc.vector.tensor_tensor(out=yo[:, :], in0=R.ps[OB][0:64, :], in1=f0[:, :], op=ALU.mult),
                  reads=[R.ps_b[OB], b0], writes=[yob])
        out_store(kb, R, yT_d, row0, qt, fn, None)


class MixResB:
    def __init__(self, kb, R):
        NB = R.NB
        self.Osig = R.EB[:, :, :].bitcast(BF16).rearrange("p a (b c) -> p (a b) c", c=64)[:, 0:NB, :]; self.Osig_b = R.EB_b
        self.G = kb.sbuf("Gates", [128, 8, NB], F32); self.G_b = kb.buf()
        self.trif = kb.sbuf("trif", [128, 2, 128], F32); self.trif_b = kb.buf()
        self.cw = kb.sbuf("convw", [128, 8], F32); self.cw_b = kb.buf()
        self.gob = kb.sbuf("gob", [128, 64], F32); self.gob_b = kb.buf()
        self.St = [kb.sbuf(f"St{i}", [64, 65], F32) for i in range(2)]; self.St_b = [kb.buf() for _ in range(2)]
        self.Sb = [kb.sbuf(f"Sb{i}", [64, 65], BF16) for i in range(2)]; self.Sb_b = [kb.buf() for _ in range(2)]
        self.tok = [kb.sbuf(f"tok{i}", [128, 3, 64], BF16) for i in range(2)]; self.tok_b = [kb.buf() for _ in range(2)]
        self.qkT = [kb.sbuf(f"qkT{i}", [64, 2, 128], BF16) for i in range(2)]; self.qkT_b = [kb.buf() for _ in range(2)]
        self.qkm = [kb.sbuf(f"qkm{i}", [128, 128], BF16) for i in range(2)]; self.qkm_b = [kb.buf() for _ in range(2)]
        self.cm = kb.sbuf("cmask", [128, 128], F32); self.cm_b = kb.buf()
        self.hn = [kb.sbuf(f"hn{i}", [128, 64], F32) for i in range(2)]; self.hn_b = [kb.buf() for _ in range(2)]
        self.hs = [kb.sbuf(f"hs{i}", [128, 8], F32) for i in range(2)]; self.hs_b = [kb.buf() for _ in range(2)]
        self.yb = [kb.sbuf(f"yb{i}", [128, 64], BF16) for i in range(2)]; self.yb_b = [kb.buf() for _ in range(2)]
        self.jk = kb.sbuf("jk", [128, 64], BF16); self.jk_b = kb.buf()


def mixer_b(kb, R, RB_, hT, yT_d, row0, bpar_d, trif_d, cmask_d, gob_d):
    nc = kb.nc
    S, NB, NQ = R.S, R.NB, R.NQ
    B = RB_
    kb.dma("sp", B.cw[:, :], bpar_d, writes=[B.cw_b])
    kb.dma("sp", B.trif[:, :, :], trif_d, writes=[B.trif_b])
    kb.dma("sp", B.cm[:, :], cmask_d, writes=[B.cm_b])
    kb.dma("sp", B.gob[:, :], gob_d, writes=[B.gob_b])
    kb.op("pool", lambda: nc.gpsimd.memset(R.Va[:, :, 64:65], 1.0), writes=[R.Va_b])
    kb.op("dve", lambda: nc.vector.tensor_scalar(out=B.cw[:, 7:8], in0=B.cw[:, 6:7], scalar1=-1.0, scalar2=None, op0=ALU.mult),
          reads=[B.cw_b], writes=[B.cw_b])
    cv = [R.e32[i][:, :, :].rearrange("p a b -> p (a b)") for i in range(2)]
    cvb = R.e32_b
    accA = R.e16[0][:, :, :].rearrange("p a b -> p (a b)").bitcast(F32); accB_ = R.e16[1][:, :, :].rearrange("p a b -> p (a b)").bitcast(F32)
    accA_b = R.e16_b[0]; accB_b = R.e16_b[1]
    for qt in range(NQ):
        ht, htb = load_ht(kb, R, hT, qt)
        ci = qt % 2
        proj_fm(kb, R, ht, htb, B_Q, 128, 0)
        if qt == 0:
            kb.op("dve", lambda: nc.vector.memset(cv[ci][:, 0:3], 0.0), writes=[cvb[ci]])
        else:
            kb.op("dve", lambda: nc.vector.tensor_copy(out=cv[ci][:, 0:3], in_=cv[1 - ci][:, 512:515]),
                  reads=[cvb[1 - ci]], writes=[cvb[ci]])
        kb.op("act", lambda: nc.scalar.copy(out=cv[ci][:, 3:515], in_=R.ps[0][:, :]), reads=[R.ps_b[0]], writes=[cvb[ci]])
        kb.op("dve", lambda: nc.vector.tensor_scalar(out=accA, in0=cv[ci][:, 3:515], scalar1=B.cw[:, 3:4], scalar2=B.cw[:, 4:5],
                                                     op0=ALU.mult, op1=ALU.add),
              reads=[cvb[ci], B.cw_b], writes=[accA_b])
        for j in (2, 1, 0):
            kb.op("dve", lambda j=j: nc.vector.scalar_tensor_tensor(out=accA, in0=cv[ci][:, j:j + 512], scalar=B.cw[:, j:j + 1],
                                                                    in1=accA, op0=ALU.mult, op1=ALU.add),
                  reads=[cvb[ci], B.cw_b, accA_b], writes=[accA_b])
        kb.op("act", lambda: nc.scalar.activation(out=accB_, in_=accA, func=AF.Sigmoid), reads=[accA_b], writes=[accB_b])
        kb.op("dve", lambda: nc.vector.tensor_tensor(out=accB_, in0=accA, in1=accB_, op=ALU.mult), reads=[accA_b, accB_b], writes=[accB_b])
        kb.op("act", lambda: nc.scalar.copy(out=R.QT[0:64, qt * QT_:(qt + 1) * QT_], in_=accB_[0:64, :]), reads=[accB_b], writes=[R.QT_b])
        kb.op("act", lambda: nc.scalar.copy(out=R.KT[0:64, qt * QT_:(qt + 1) * QT_], in_=accB_[64:128, :]), reads=[accB_b], writes=[R.KT_b])
        pb = 4 + (qt % 2)
        for s in range(4):
            nonlocal_pb = 3 + ((qt * 4 + s) % 4)
            for kc in range(8):
                kb.op("pe", lambda kc=kc: nc.tensor.matmul(R.ps[nonlocal_pb][:, 0:130], lhsT=ht[:, kc, s * 128:(s + 1) * 128],
                                                           rhs=R.Wm[:, kc, B_V:B_V + 130], start=(kc == 0), stop=(kc == 7)),
                      reads=[R.Wm_b, htb], writes=[R.ps_b[nonlocal_pb]], inc=(kc == 7))
            blk = qt * 4 + s
            kb.op("dve", lambda: nc.vector.tensor_copy(out=R.Va[:, blk, 0:64], in_=R.ps[nonlocal_pb][:, 0:64]),
                  reads=[R.ps_b[nonlocal_pb]], writes=[R.Va_b])
            kb.op("act", lambda: nc.scalar.activation(out=B.Osig[:, blk, :], in_=R.ps[nonlocal_pb][:, 64:128], func=AF.Sigmoid),
                  reads=[R.ps_b[nonlocal_pb]], writes=[B.Osig_b])
            kb.op("dve", lambda: nc.vector.tensor_copy(out=B.G[:, 0:2, blk], in_=R.ps[nonlocal_pb][:, 128:130]),
                  reads=[R.ps_b[nonlocal_pb]], writes=[B.G_b])
    G = B.G
    kb.op("act", lambda: nc.scalar.activation(out=G[:, 2, :], in_=G[:, 1, :], func=AF.Exp, scale=-1.0, bias=B.cw[:, 7:8]),
          reads=[B.G_b, B.cw_b], writes=[B.G_b])
    kb.op("act", lambda: nc.scalar.activation(out=G[:, 2, :], in_=G[:, 2, :], func=AF.Ln, bias=1.0), reads=[B.G_b], writes=[B.G_b])
    kb.op("dve", lambda: nc.vector.tensor_scalar(out=G[:, 2, :], in0=G[:, 2, :], scalar1=-1.0, scalar2=None, op0=ALU.mult),
          reads=[B.G_b], writes=[B.G_b])
    kb.op("pe", lambda: nc.tensor.matmul(R.ps[0][:, 0:NB], lhsT=B.trif[:, 0, :], rhs=G[:, 2, :], start=True, stop=True),
          reads=[B.trif_b, B.G_b], writes=[R.ps_b[0]])
    kb.op("pe", lambda: nc.tensor.matmul(R.ps[1][:, 0:NB], lhsT=B.trif[:, 1, :], rhs=G[:, 2, :], start=True, stop=True),
          reads=[B.trif_b, B.G_b], writes=[R.ps_b[1]])
    kb.op("dve", lambda: nc.vector.tensor_copy(out=G[:, 3, :], in_=R.ps[0][:, 0:NB]), reads=[R.ps_b[0]], writes=[B.G_b])
    kb.op("act", lambda: nc.scalar.activation(out=G[:, 4, :], in_=G[:, 3, :], func=AF.Exp), reads=[B.G_b], writes=[B.G_b])
    kb.op("dve", lambda: nc.vector.tensor_tensor(out=G[:, 5, :], in0=G[:, 0, :], in1=G[:, 3, :], op=ALU.subtract),
          reads=[B.G_b], writes=[B.G_b])
    kb.op("dve", lambda: nc.vector.tensor_tensor(out=G[:, 6, :], in0=G[:, 5, :], in1=R.ps[1][:, 0:NB], op=ALU.add),
          reads=[B.G_b, R.ps_b[1]], writes=[B.G_b])
    kb.op("act", lambda: nc.scalar.activation(out=G[:, 5, :], in_=G[:, 5, :], func=AF.Exp, bias=B.cw[:, 5:6]),
          reads=[B.G_b, B.cw_b], writes=[B.G_b])
    kb.op("act", lambda: nc.scalar.activation(out=G[:, 6, :], in_=G[:, 6, :], func=AF.Exp, bias=B.cw[:, 5:6]),
          reads=[B.G_b, B.cw_b], writes=[B.G_b])
    kb.op("dve", lambda: nc.vector.tensor_scalar(out=G[:, 5:7, :], in0=G[:, 5:7, :], scalar1=0.125, scalar2=None, op0=ALU.mult),
          reads=[B.G_b], writes=[B.G_b])
    kb.op("act", lambda: nc.scalar.activation(out=G[:, 7, :], in_=R.ps[1][:, 0:NB], func=AF.Exp), reads=[R.ps_b[1]], writes=[B.G_b])
    kb.op("dve", lambda: nc.vector.memset(B.St[0][:, :], 0.0), writes=[B.St_b[0]])
    kb.op("dve", lambda: nc.vector.memset(B.Sb[0][:, :], 0.0), writes=[B.Sb_b[0]])
    PT, PT2, PS_, PO, PU = 0, 1, 2, 3, 6
    for b in range(NB):
        i2 = b % 2
        tok, tokb = B.tok[i2], B.tok_b[i2]
        qkT, qkTb = B.qkT[i2], B.qkT_b[i2]
        tpA = R.ps[0][:, :].bitcast(BF16); tpq = tpA[:, 0:128]; tpk = tpA[:, 128:256]
        kb.op("pe", lambda: nc.tensor.transpose(out=tpq[:, 0:64], in_=R.QT[0:64, b * 128:(b + 1) * 128], identity=R.ident[0:64, 0:64]),
              reads=[R.QT_b, R.ident_b], writes=[R.ps_b[0]])
        kb.op("pe", lambda: nc.tensor.transpose(out=tpk[:, 0:64], in_=R.KT[0:64, b * 128:(b + 1) * 128], identity=R.ident[0:64, 0:64]),
              reads=[R.KT_b, R.ident_b], writes=[R.ps_b[0]])
        kb.op("dve", lambda: nc.vector.tensor_scalar(out=tok[:, 0, :], in0=tpq[:, 0:64], scalar1=G[:, 4, b:b + 1], scalar2=None, op0=ALU.mult),
              reads=[R.ps_b[0], B.G_b], writes=[tokb])
        kb.op("dve", lambda: nc.vector.tensor_scalar(out=tok[:, 1, :], in0=tpk[:, 0:64], scalar1=G[:, 5, b:b + 1], scalar2=None, op0=ALU.mult),
              reads=[R.ps_b[0], B.G_b], writes=[tokb])
        kb.op("dve", lambda: nc.vector.tensor_scalar(out=tok[:, 2, :], in0=tpk[:, 0:64], scalar1=G[:, 6, b:b + 1], scalar2=None, op0=ALU.mult),
              reads=[R.ps_b[0], B.G_b], writes=[tokb])
        tpB = R.ps[1][:, :].bitcast(BF16); tq2 = tpB[:, 0:128]; tk2 = tpB[:, 128:256]
        kb.op("pe", lambda: nc.tensor.transpose(out=tq2[0:64, :], in_=tok[:, 0, :], identity=R.ident[:, :]),
              reads=[tokb, R.ident_b], writes=[R.ps_b[1]])
        kb.op("pe", lambda: nc.tensor.transpose(out=tk2[0:64, :], in_=tok[:, 1, :], identity=R.ident[:, :]),
              reads=[tokb, R.ident_b], writes=[R.ps_b[1]])
        kb.op("act", lambda: nc.scalar.copy(out=qkT[:, 0, :], in_=tq2[0:64, :]), reads=[R.ps_b[1]], writes=[qkTb])
        kb.op("act", lambda: nc.scalar.copy(out=qkT[:, 1, :], in_=tk2[0:64, :]), reads=[R.ps_b[1]], writes=[qkTb])
        kb.op("pe", lambda: nc.tensor.matmul(R.ps[PS_][:, 0:128], lhsT=qkT[:, 1, :], rhs=qkT[:, 0, :], start=True, stop=True),
              reads=[qkTb], writes=[R.ps_b[PS_]])
        qkm, qkmb = B.qkm[i2], B.qkm_b[i2]
        kb.op("dve", lambda: nc.vector.tensor_tensor(out=qkm[:, :], in0=R.ps[PS_][:, 0:128], in1=B.cm[:, :], op=ALU.mult),
              reads=[R.ps_b[PS_], B.cm_b], writes=[qkmb])
        Sp, Spb = B.Sb[i2], B.Sb_b[i2]
        po = PO + (b % 2)
        kb.op("pe", lambda: nc.tensor.matmul(R.ps[po][:, 0:65], lhsT=qkm[:, :], rhs=R.Va[:, b, 0:65], start=True, stop=False),
              reads=[qkmb, R.Va_b], writes=[R.ps_b[po]], inc=False)
        kb.op("pe", lambda: nc.tensor.matmul(R.ps[po][:, 0:65], lhsT=qkT[:, 0, :], rhs=Sp[:, :], start=False, stop=True),
              reads=[qkTb, Spb], writes=[R.ps_b[po]])
        kb.op("pe", lambda: nc.tensor.matmul(R.ps[PU][0:64, 0:65], lhsT=tok[:, 2, :], rhs=R.Va[:, b, 0:65], start=True, stop=True),
              reads=[tokb, R.Va_b], writes=[R.ps_b[PU]])
        Sn, Snb = B.St[1 - i2], B.St_b[1 - i2]
        So, Sob = B.St[i2], B.St_b[i2]
        kb.op("dve", lambda: nc.vector.scalar_tensor_tensor(out=Sn[:, :], in0=So[:, :], scalar=G[0:64, 7, b:b + 1], in1=R.ps[PU][0:64, 0:65],
                                                            op0=ALU.mult, op1=ALU.add),
              reads=[Sob, B.G_b, R.ps_b[PU]], writes=[Snb])
        kb.op("act", lambda: nc.scalar.copy(out=B.Sb[1 - i2][:, :], in_=Sn[:, :]), reads=[Snb], writes=[B.Sb_b[1 - i2]])
        hs, hsb = B.hs[i2], B.hs_b[i2]
        hn, hnb = B.hn[i2], B.hn_b[i2]
        kb.op("act", lambda: nc.scalar.activation(out=hs[:, 5:6], in_=R.ps[po][:, 64:65], func=AF.Abs),
              reads=[R.ps_b[po]], writes=[hsb])
        kb.op("dve", lambda: nc.vector.tensor_scalar(out=hs[:, 0:1], in0=hs[:, 5:6], scalar1=1.0, scalar2=None, op0=ALU.max),
              reads=[hsb], writes=[hsb])
        kb.op("dve", lambda: nc.vector.reciprocal(out=hs[:, 1:2], in_=hs[:, 0:1]), reads=[hsb], writes=[hsb])
        kb.op("dve", lambda: nc.vector.tensor_scalar(out=hn[:, :], in0=R.ps[po][:, 0:64], scalar1=hs[:, 1:2], scalar2=None, op0=ALU.mult),
              reads=[R.ps_b[po], hsb], writes=[hnb])
        kb.op("act", lambda: nc.scalar.activation(out=B.jk[:, :], in_=hn[:, :], func=AF.Square, accum_out=hs[:, 2:3]),
              reads=[hnb], writes=[B.jk_b, hsb])
        kb.op("act", lambda: nc.scalar.activation(out=hs[:, 3:4], in_=hs[:, 2:3], func=AF.Sqrt, bias=EPS, scale=1.0 / 64),
              reads=[hsb], writes=[hsb])
        kb.op("dve", lambda: nc.vector.reciprocal(out=hs[:, 4:5], in_=hs[:, 3:4]), reads=[hsb], writes=[hsb])
        kb.op("dve", lambda: nc.vector.scalar_tensor_tensor(out=hn[:, :], in0=hn[:, :], scalar=hs[:, 4:5], in1=B.gob[:, :],
                                                            op0=ALU.mult, op1=ALU.mult),
              reads=[hnb, hsb, B.gob_b], writes=[hnb])
        yb, ybb = B.yb[i2], B.yb_b[i2]
        kb.op("dve", lambda: nc.vector.tensor_tensor(out=yb[:, :], in0=hn[:, :], in1=B.Osig[:, b, :], op=ALU.mult),
              reads=[hnb, B.Osig_b], writes=[ybb])
        ty = R.ps[5][:, :].bitcast(BF16)[:, 0:128]
        kb.op("pe", lambda: nc.tensor.transpose(out=ty[0:64, :], in_=yb[:, :], identity=R.ident[:, :]),
              reads=[ybb, R.ident_b], writes=[R.ps_b[5]])
        qt = b // 4
        if b % 4 == 0:
            R.cur_yo = R.nyo % 2; R.nyo += 1
        yo, yob = R.yo[R.cur_yo], R.yo_b[R.cur_yo]
        kb.op("act", lambda: nc.scalar.copy(out=yo[:, (b % 4) * 128:(b % 4 + 1) * 128], in_=ty[0:64, :]),
              reads=[R.ps_b[5]], writes=[yob])
        if b % 4 == 3:
            kb.dma("pool", ydst(yT_d, row0, qt), yo[:, :], reads=[yob])


def mix_params(kb, R, praw_d, clam_d):
    nc = kb.nc
    pr = R.par
    kb.dma("sp", pr[0:64, 16:24], praw_d, writes=[R.par_b])
    cl = R.rr[0][:, 0:128].rearrange("p (a b) -> p a b", a=4)
    kb.dma("sp", cl, clam_d, writes=[R.rr_b[0]])
    V = nc.vector
    kb.op("dve", lambda: V.memset(pr[0:64, 13:14], EPS), writes=[R.par_b])
    kb.op("dve", lambda: V.tensor_scalar(out=pr[0:64, 0:1], in0=pr[0:64, 16:17], scalar1=32 ** -0.5, scalar2=None, op0=ALU.mult), reads=[R.par_b], writes=[R.par_b])
    kb.op("dve", lambda: V.tensor_copy(out=pr[0:64, 1:2], in_=pr[0:64, 17:18]), reads=[R.par_b], writes=[R.par_b])
    kb.op("dve", lambda: V.tensor_tensor(out=pr[0:64, 3:4], in0=pr[0:64, 18:19], in1=pr[0:64, 22:23], op=ALU.mult), reads=[R.par_b], writes=[R.par_b])
    kb.op("dve", lambda: V.tensor_scalar(out=pr[0:64, 4:5], in0=pr[0:64, 19:20], scalar1=0.125, scalar2=None, op0=ALU.mult), reads=[R.par_b], writes=[R.par_b])
    kb.op("dve", lambda: V.tensor_copy(out=pr[0:64, 5:6], in_=pr[0:64, 20:21]), reads=[R.par_b], writes=[R.par_b])
    pp = R.rr[1][:, 0:64].rearrange("p (a b) -> p a b", a=2)
    kb.op("dve", lambda: V.tensor_tensor(out=pp[:, 0, :], in0=cl[:, 0, :], in1=cl[:, 1, :], op=ALU.mult), reads=[R.rr_b[0]], writes=[R.rr_b[1]])
    kb.op("dve", lambda: V.tensor_tensor(out=pp[:, 1, :], in0=cl[:, 2, :], in1=cl[:, 3, :], op=ALU.mult), reads=[R.rr_b[0]], writes=[R.rr_b[1]])
    kb.op("dve", lambda: V.reduce_sum(out=pr[0:64, 8:10], in_=pp, axis=AX.X), reads=[R.rr_b[1]], writes=[R.par_b])
    kb.op("act", lambda: nc.scalar.activation(out=pr[0:64, 10:12], in_=pr[0:64, 8:10], func=AF.Exp), reads=[R.par_b], writes=[R.par_b])
    kb.op("dve", lambda: V.tensor_tensor(out=pr[0:64, 12:13], in0=pr[0:64, 11:12], in1=pr[0:64, 10:11], op=ALU.subtract), reads=[R.par_b], writes=[R.par_b])
    kb.op("dve", lambda: V.tensor_tensor(out=pr[0:64, 2:3], in0=pr[0:64, 12:13], in1=pr[0:64, 21:22], op=ALU.subtract), reads=[R.par_b], writes=[R.par_b])

import ml_dtypes
bf16 = ml_dtypes.bfloat16
GW = 256
OFF = dict(aq=0, ak=256, av=512, bqk=768, bv=1280, bo=1536, bi=1792, bf=1796, cq=1800, ck=2056, cv=2312, dq=2568, dk=2824, dv=3080)

def sel_cols(j):
    c = []
    r = lambda o: list(range(o + j * 64, o + j * 64 + 64))
    c += r(OFF['aq']) + r(OFF['ak']) + r(OFF['av'])
    c += r(OFF['bqk']) + r(OFF['bqk'] + 256) + r(OFF['bv']) + r(OFF['bo']) + [OFF['bi'] + j, OFF['bf'] + j]
    c += r(OFF['cq']) + r(OFF['ck']) + r(OFF['cv'])
    c += r(OFF['dq']) + r(OFF['dk']) + r(OFF['dv'])
    return np.array(c)

def const_inputs():
    d = {}
    d['ident'] = np.eye(128, dtype=np.float32).astype(bf16)
    cst = np.zeros((128, 256), np.float32)
    cst[0:64, 0:64] = 1.0
    cst[0:32, 64:96] = 1.0; cst[32:64, 96:128] = 1.0
    d['cst'] = cst.astype(bf16)
    s = np.arange(128)[:, None, None, None]; r = np.arange(4)[None, :, None, None]; t = np.arange(512)[None, None, None, :]
    mc = ((2 * r + (s >= 64)) <= (t // 64)).astype(np.float32)
    d['masks_c'] = np.broadcast_to(mc, (128, 4, 2, 512)).astype(bf16).copy()
    s = np.arange(128)[:, None, None]; r = np.arange(4)[None, :, None]; t = np.arange(512)[None, None, :]
    d['masks_d'] = ((128 * r + s) < t).astype(np.float32).astype(bf16)
    j = np.arange(128)[:, None]; s2 = np.arange(128)[None, :]
    tri = np.zeros((128, 3, 128), np.float32)
    tri[:, 0, :] = (j >= s2); tri[:, 1, :] = (j < s2); tri[:, 2, :] = 1.0
    d['tri'] = tri.astype(bf16)
    trif = np.zeros((128, 2, 128), np.float32)
    trif[:, 0, :] = (j <= s2); trif[:, 1, :] = 1.0
    d['trif'] = trif
    d['cmask'] = (j <= s2).astype(np.float32)
    return d

def bias_index():
    s = np.arange(128)[:, None, None]; r = np.arange(8)[None, :, None]; t = np.arange(512)[None, None, :]
    rel = t - s + 512 - 128 * r
    idx = np.clip(rel, -128, 128) + 128
    dd = t // 64 + 8 - 2 * r - s // 64
    vis = (dd >= 0) & (dd <= 8)
    return idx, vis

_IDX, _VIS = bias_index()

def layer_core_inputs(P, l, j, lam_init=None):
    d = {}
    d['wsel'] = np.ascontiguousarray(P['w_in'][l][:, sel_cols(j)])
    d['gmix'] = np.ascontiguousarray(P['mix_norm'][l].reshape(8, 128).T)
    praw = np.zeros((64, 8), np.float32)
    praw[:, 0] = np.tile(P['c_q_norm'][l], 2); praw[:, 1] = np.tile(P['c_k_norm'][l], 2)
    praw[:, 2] = P['c_out_norm'][l]; praw[:, 3] = P['a_q_norm'][l]; praw[:, 4] = P['a_k_norm'][l]
    if lam_init is None:
        lam_init = 0.8 - 0.6 * np.exp(-0.3 * l)
    praw[:, 5] = lam_init; praw[:, 6] = 1.0 - lam_init
    d['praw'] = praw
    d['clam'] = np.ascontiguousarray(np.broadcast_to(P['c_lambda'][l][None], (64, 4, 32))).astype(np.float32)
    rb = P['a_rel_bias'][l][j]
    d['biasT'] = np.where(_VIS, rb[_IDX], np.float32(-1e30)).astype(np.float32)
    bpar = np.zeros((128, 8), np.float32)
    ch = np.concatenate([np.arange(j * 64, j * 64 + 64), 256 + np.arange(j * 64, j * 64 + 64)])
    bpar[:, 0:4] = P['b_conv_w'][l][:, ch].T
    bpar[:, 4] = P['b_conv_b'][l][ch]
    bpar[:, 5] = P['b_gate_bias'][l][0, j]
    bpar[:, 6] = P['b_gate_bias'][l][1, j]
    d['bpar'] = bpar
    d['gob'] = np.ascontiguousarray(np.broadcast_to(P['b_out_norm'][l][j][None], (128, 64))).astype(np.float32)
    return d


from concourse.bass_utils import run_bass_kernel_spmd

SEQ = 16384
NCORE = 8
TPC = 4096
DEPTH = 2
GROUPS = [[0, 1, 2, 3], [4, 5, 6, 7]]


def _din(nc, name, shape, dt):
    return nc.dram_tensor(name, list(shape), dt, kind="ExternalInput").ap()


def _dout(nc, name, shape, dt):
    return nc.dram_tensor(name, list(shape), dt, kind="ExternalOutput").ap()


def _dint(nc, name, shape, dt):
    return nc.dram_tensor(name, list(shape), dt, kind="Internal").ap()


MIX_IN = dict(wsel=([D, NW], F32), gmix=([128, 8], F32), praw=([64, 8], F32), clam=([64, 4, 32], F32),
              biasT=([128, 8, 512], F32), bpar=([128, 8], F32), gob=([128, 64], F32))
CONST_IN = dict(ident=([128, 128], BF16), cst=([128, 256], BF16), masks_c=([128, 4, 2, 512], BF16),
                masks_d=([128, 4, 512], BF16), tri=([128, 3, 128], BF16), trif=([128, 2, 128], F32), cmask=([128, 128], F32))


def build_fused(S=SEQ, T=TPC):
    nc = bass.Bass("TRN2", target_bir_lowering=False)
    NQr = T // QT_
    x_in = _din(nc, "x_in", [T, D], F32)
    x_out = _dout(nc, "x_out", [T, D], F32)
    Cn = {k: _din(nc, k, sh, dt) for k, (sh, dt) in CONST_IN.items()}
    ffn = {}
    for l in range(DEPTH):
        for f in ("ffn1", "ffn2"):
            ffn[(f, l)] = dict(g=_din(nc, f"{f}_g{l}", [128, 8], F32), wg=_din(nc, f"{f}_wg{l}", [D, DFF], F32),
                               wu=_din(nc, f"{f}_wu{l}", [D, DFF], F32), wd=_din(nc, f"{f}_wd{l}", [DFF, D], F32))
    wo = [_din(nc, f"wo{l}", [D, D], F32) for l in range(DEPTH)]
    mx = [{k: _din(nc, f"{k}{l}", sh, dt) for k, (sh, dt) in MIX_IN.items()} for l in range(DEPTH)]
    xa = _dint(nc, "xa", [T, D], F32); xb = _dint(nc, "xb", [T, D], F32); xc = _dint(nc, "xc", [T, D], F32)
    hT_loc = _dint(nc, "hT_loc", [D, T], BF16)
    hT_all = _dint(nc, "hT_all", [4 * D, T], BF16)
    yT_loc = _dint(nc, "yT_loc", [D, T], BF16)
    yT_all = _dint(nc, "yT_all", [4 * D, T], BF16)
    yT_mine = _dint(nc, "yT_mine", [D, T], BF16)

    hv = hT_all.rearrange("(k r p) t -> k r p t", k=8, r=4)

    def hT_src(qt):
        r, o = qt // NQr, (qt % NQr) * QT_
        return hv[:, r, :, o:o + QT_]

    def y_dst(row0, qt):
        q, o = qt // NQr, (qt % NQr) * QT_
        return yT_loc[q * 256 + row0:q * 256 + row0 + 64, o:o + QT_]

    with ExitStack() as st:
        kb = KB(nc, st)
        pid = nc.sync.partition_id()
        qv = pid % 4

        ymine_b = kb.buf()

        def fetch_mine():
            yv2 = yT_all.rearrange("(q h j p) t -> q h j p t", q=4, h=2, j=4)
            for j in range(4):
                for h in range(2):
                    kb.dma("sp", yT_mine[j * 256 + h * 128:j * 256 + (h + 1) * 128, :],
                           yv2[bass.ds(qv, 1), h, j, :, :].rearrange("o p t -> (o p) t"), writes=[ymine_b])

        def tok_phase(passes):
            with ExitStack() as mem:
                kb.mem = mem
                R = TokRes(kb, any(p.get("wo") is not None for p in passes))
                load_consts(kb, R, Cn["ident"])
                prev_bufs = None
                for k, p in enumerate(passes):
                    w = p["ffn"]
                    load_ffn_weights(kb, R, w["g"], w["wg"], w["wu"], w["wd"], p.get("wo"))
                    ob = [kb.buf() for _ in range(T // TT)] if k + 1 < len(passes) else None
                    has_pre = p.get("wo") is not None
                    token_pass(kb, R, T, p["xi"], p["xo"], pre=(yT_mine if has_pre else None),
                               post=p.get("post"), in_bufs=prev_bufs, out_bufs=ob,
                               pre_bufs=([ymine_b] * (T // TT) if has_pre else None))
                    prev_bufs = ob
                kb.barrier()
            kb.mem = st

        def mix_phase(l):
            with ExitStack() as mem:
                kb.mem = mem
                R = MixRes(kb, S)
                RB = MixResB(kb, R)
                m = mx[l]
                mix_load_common(kb, R, m["wsel"], m["gmix"], Cn["ident"], Cn["cst"])
                mix_params(kb, R, m["praw"], m["clam"])
                mixer_a(kb, R, hT_src, y_dst, 0, m["biasT"])
                kb.barrier()
                mixer_b(kb, R, RB, hT_src, y_dst, 64, m["bpar"], Cn["trif"], Cn["cmask"], m["gob"])
                kb.barrier()
                mixer_c(kb, R, hT_src, y_dst, 128, Cn["masks_c"])
                kb.barrier()
                mixer_d(kb, R, hT_src, y_dst, 192, Cn["masks_d"], Cn["tri"])
                kb.barrier()
            kb.mem = st

        tok_phase([dict(ffn=ffn[("ffn1", 0)], xi=x_in, xo=xa, post=hT_loc)])
        kb.allgather(hT_loc, hT_all, GROUPS)
        mix_phase(0)
        kb.allgather(yT_loc, yT_all, GROUPS)
        fetch_mine()
        tok_phase([dict(ffn=ffn[("ffn2", 0)], wo=wo[0], xi=xa, xo=xb),
                   dict(ffn=ffn[("ffn1", 1)], xi=xb, xo=xc, post=hT_loc)])
        kb.allgather(hT_loc, hT_all, GROUPS)
        mix_phase(1)
        kb.allgather(yT_loc, yT_all, GROUPS)
        fetch_mine()
        tok_phase([dict(ffn=ffn[("ffn2", 1)], wo=wo[1], xi=xc, xo=x_out)])
        kb.finish()
    return nc


def build_mixer_prog(S=SEQ):
    nc = bass.Bass("TRN2", target_bir_lowering=False)
    hT = _din(nc, "hT", [D, S], BF16)
    Cn = {k: _din(nc, k, sh, dt) for k, (sh, dt) in CONST_IN.items()}
    m = {k: _din(nc, k, sh, dt) for k, (sh, dt) in MIX_IN.items()}
    yT = _dout(nc, "yT", [256, S], BF16)
    with ExitStack() as st:
        kb = KB(nc, st)
        R = MixRes(kb, S)
        RB = MixResB(kb, R)
        mix_load_common(kb, R, m["wsel"], m["gmix"], Cn["ident"], Cn["cst"])
        mix_params(kb, R, m["praw"], m["clam"])
        mixer_a(kb, R, hT, yT, 0, m["biasT"])
        kb.barrier()
        mixer_b(kb, R, RB, hT, yT, 64, m["bpar"], Cn["trif"], Cn["cmask"], m["gob"])
        kb.barrier()
        mixer_c(kb, R, hT, yT, 128, Cn["masks_c"])
        kb.barrier()
        mixer_d(kb, R, hT, yT, 192, Cn["masks_d"], Cn["tri"])
        kb.finish()
    return nc


def _lay(g):
    return np.ascontiguousarray(np.asarray(g, np.float32).reshape(8, 128).T)


def _wo_perm(w_out):
    idx = np.arange(1024).reshape(4, 4, 64)
    perm = idx.transpose(1, 0, 2).reshape(-1)
    return np.ascontiguousarray(w_out[perm, :])


def make_in_maps(P, TPC=TPC):
    x = np.ascontiguousarray(P["x"], dtype=np.float32).reshape(-1, D)
    C = const_inputs()
    shared = dict(C)
    for l in range(DEPTH):
        for f in ("ffn1", "ffn2"):
            shared[f"{f}_g{l}"] = _lay(P[f + "_norm"][l])
            shared[f"{f}_wg{l}"] = np.ascontiguousarray(P[f + "_wg"][l], dtype=np.float32)
            shared[f"{f}_wu{l}"] = np.ascontiguousarray(P[f + "_wu"][l], dtype=np.float32)
            shared[f"{f}_wd{l}"] = np.ascontiguousarray(P[f + "_wd"][l], dtype=np.float32)
        shared[f"wo{l}"] = _wo_perm(np.asarray(P["w_out"][l], np.float32))
    ims = []
    for c in range(NCORE):
        d = dict(shared)
        d["x_in"] = x[c * TPC:(c + 1) * TPC]
        j = c % 4
        for l in range(DEPTH):
            for k, v in layer_core_inputs(P, l, j).items():
                d[f"{k}{l}"] = v
        ims.append(d)
    return ims


def kernel(**inputs):
    P = {k: np.asarray(v) for k, v in inputs.items()}
    nc = build_fused()
    ims = make_in_maps(P)
    res = run_bass_kernel_spmd(nc, ims, core_ids=list(range(NCORE)))
    out = np.concatenate([r["x_out"] for r in res.results], axis=0).reshape(2, SEQ, D).astype(np.float32)
    return out
```

```python
import numpy as np
from contextlib import ExitStack
import concourse.bass as bass
import concourse.mybir as mybir

F32 = mybir.dt.float32
BF16 = mybir.dt.bfloat16
AF = mybir.ActivationFunctionType
ALU = mybir.AluOpType
AX = mybir.AxisListType

EPOCH = 4096


class Buf:
    __slots__ = ("w", "r", "name")

    def __init__(self, name=""):
        self.w = None
        self.r = {}
        self.name = name


class KB:
    def __init__(self, nc, stack):
        self.nc = nc
        self.st = stack
        self.E = {"pe": nc.tensor, "act": nc.scalar, "dve": nc.vector, "pool": nc.gpsimd, "sp": nc.sync}
        self.cnt = {e: 0 for e in self.E}
        self.sems = {e: [] for e in self.E}
        self.waited = {e: {} for e in self.E}
        self.ndma = 12
        self.dma_sems = {}
        self.dma_cnt = {}
        self.dma_rr = {}
        self.nsem = 0
        self.uid = 0
        self.mem = stack

    def sem(self, name):
        self.nsem += 1
        return self.st.enter_context(self.nc.semaphore(name))

    def sbuf(self, name, shape, dt):
        self.uid += 1
        return self.mem.enter_context(self.nc.sbuf_tensor(f"sb{self.uid}_" + name, list(shape), dt))

    def psum(self, name, shape, dt):
        self.uid += 1
        return self.mem.enter_context(self.nc.psum_tensor(f"ps{self.uid}_" + name, list(shape), dt))

    def buf(self, name=""):
        return Buf(name)

    def _esem(self, e, n):
        ep = (n - 1) // EPOCH
        while len(self.sems[e]) <= ep:
            self.sems[e].append(self.sem(f"c_{e}_{len(self.sems[e])}"))
        return self.sems[e][ep], (n - 1) % EPOCH + 1

    def _wait(self, e, ev):
        if ev[0] == "e":
            _, src, n = ev
            if src == e and e == "pe":
                return
            key = ("e", src)
            if self.waited[e].get(key, 0) >= n:
                return
            if src == e and n > self.cnt[e]:
                raise RuntimeError("self-wait on future event")
            s, v = self._esem(src, n)
            self.E[e].wait_ge(s, v)
            self.waited[e][key] = n
        else:
            _, q, i, k = ev
            key = ("d", q, i)
            if self.waited[e].get(key, 0) >= k:
                return
            self.E[e].wait_ge(self.dma_sems[q][i], 16 * k)
            self.waited[e][key] = k

    @staticmethod
    def _evkey(ev):
        return (ev[0], ev[1]) if ev[0] == "e" else (ev[0], ev[1], ev[2])

    def _collect(self, reads, writes):
        deps = []
        for b in reads:
            if b.w is not None:
                deps.append(b.w)
        for b in writes:
            if b.w is not None:
                deps.append(b.w)
            deps.extend(b.r.values())
        return deps

    def _record(self, ev, reads, writes):
        k = self._evkey(ev)
        for b in reads:
            b.r[k] = ev
        for b in writes:
            b.w = ev
            b.r = {}

    def op(self, e, fn, reads=(), writes=(), inc=True):
        for ev in self._collect(reads, writes):
            self._wait(e, ev)
        ins = fn()
        if inc:
            self.cnt[e] += 1
            s, v = self._esem(e, self.cnt[e])
            ins.then_inc(s, 1)
            ev = ("e", e, self.cnt[e])
        else:
            ev = ("e", e, self.cnt[e] + 1)
        self._record(ev, reads, writes)
        return ins

    def dma(self, q, out, in_, reads=(), writes=(), **kw):
        for ev in self._collect(reads, writes):
            self._wait(q, ev)
        if q not in self.dma_sems:
            self.dma_sems[q] = [self.sem(f"d_{q}_{i}") for i in range(self.ndma)]
            self.dma_cnt[q] = [0] * self.ndma
            self.dma_rr[q] = 0
        i = self.dma_rr[q]
        self.dma_rr[q] = (i + 1) % self.ndma
        if self.dma_cnt[q][i] > 0:
            self._wait(q, ("d", q, i, self.dma_cnt[q][i]))
        self.dma_cnt[q][i] += 1
        ins = self.E[q].dma_start(out=out, in_=in_, **kw)
        ins.then_inc(self.dma_sems[q][i], 16)
        ev = ("d", q, i, self.dma_cnt[q][i])
        self._record(ev, reads, writes)
        return ins

    def barrier(self, extra_sems=()):
        for e in self.E:
            for q in self.dma_sems:
                for i in range(self.ndma):
                    if self.dma_cnt[q][i] > 0:
                        self._wait(e, ("d", q, i, self.dma_cnt[q][i]))
            for src in ("pe", "act", "dve", "pool"):
                if self.cnt[src] > 0 and not (src == e and e == "pe"):
                    self._wait(e, ("e", src, self.cnt[src]))
            for (sm, v) in extra_sems:
                self.E[e].wait_ge(sm, v)

    def allgather(self, src2d, dst2d, groups, chunk_rows=128):
        self.barrier()
        R_ = src2d.shape[0]
        nk = R_ // chunk_rows
        ng = len(groups[0])
        if not hasattr(self, "cc_sem"):
            self.cc_sem = self.sem("ccsem")
            self.cc_cnt = 0
        for k in range(nk):
            self.nc.gpsimd.collective_compute("AllGather", ALU.bypass, replica_groups=groups,
                                              ins=[src2d[k * chunk_rows:(k + 1) * chunk_rows, :]],
                                              outs=[dst2d[k * ng * chunk_rows:(k + 1) * ng * chunk_rows, :]]).then_inc(self.cc_sem, 1)
            self.cc_cnt += 1
        for e in self.E:
            self.E[e].wait_ge(self.cc_sem, self.cc_cnt)

    def finish(self):
        for q in self.dma_sems:
            for i in range(self.ndma):
                if self.dma_cnt[q][i] > 0:
                    self._wait("sp", ("d", q, i, self.dma_cnt[q][i]))
        for e in ("pe", "act", "dve", "pool"):
            if self.cnt[e] > 0:
                self._wait("sp", ("e", e, self.cnt[e]))


D = 1024
DFF = 2816
NFC = DFF // 128
TT = 256
SUB = TT // 128
EPS = 1e-6


class TokRes:
    def __init__(self, kb, with_pre):
        nc = kb.nc
        self.kb = kb
        self.Wg = kb.sbuf("Wg", [128, 8, DFF], BF16); self.Wg_b = kb.buf()
        self.Wu = kb.sbuf("Wu", [128, 8, DFF], BF16); self.Wu_b = kb.buf()
        self.Wd = kb.sbuf("Wd", [128, NFC, D], BF16); self.Wd_b = kb.buf()
        self.stage = [kb.sbuf(f"stage{i}", [128, 1024], F32) for i in range(2)]
        self.stage_b = [kb.buf() for _ in range(2)]
        self.gt = kb.sbuf("gt", [128, 8], F32); self.gt_b = kb.buf()
        self.ident = kb.sbuf("ident", [128, 128], BF16); self.ident_b = kb.buf()
        self.xt = [kb.sbuf(f"xt{i}", [128, SUB, D], F32) for i in range(2)]
        self.xt_b = [kb.buf() for _ in range(2)]
        self.xn = kb.sbuf("xn", [128, D], BF16); self.xn_b = kb.buf()
        self.st = kb.sbuf("stat", [128, 8], F32); self.st_b = kb.buf()
        self.xnT = [kb.sbuf(f"xnT{i}", [128, 8, TT], BF16) for i in range(2)]
        self.xnT_b = [kb.buf() for _ in range(2)]
        self.hid = kb.sbuf("hid", [128, NFC, TT], BF16)
        self.hid_b = [kb.buf() for _ in range(NFC)]
        self.sg = [kb.sbuf(f"sg{i}", [128, TT], F32) for i in range(2)]
        self.sg_b = [kb.buf() for _ in range(2)]
        self.hTo = kb.sbuf("hTo", [128, 8, TT], BF16); self.hTo_b = kb.buf()
        self.with_pre = with_pre
        if with_pre:
            self.Wo = kb.sbuf("Wo", [128, 8, D], BF16); self.Wo_b = kb.buf()
            self.yt = [kb.sbuf(f"yt{i}", [128, 8, TT], BF16) for i in range(2)]
            self.yt_b = [kb.buf() for _ in range(2)]
        self.psg = [kb.psum(f"psg{i}", [128, 512], F32) for i in range(2)]
        self.psg_b = [kb.buf() for _ in range(2)]
        self.psd = [kb.psum(f"psd{i}", [128, 512], F32) for i in range(2)]
        self.psd_b = [kb.buf() for _ in range(2)]
        self.tp = [kb.psum(f"tp{i}", [128, 8, 128], BF16) for i in range(2)]
        self.tp_b = [kb.buf() for _ in range(2)]
        self.ntp = 0
        self.npsd = 0
        self.ncast = 0


def load_consts(kb, R, ident_d):
    kb.dma("sp", R.ident[:], ident_d, writes=[R.ident_b])


def load_ffn_weights(kb, R, g_lay, wg, wu, wd, w_out=None):
    nc = kb.nc
    kb.dma("sp", R.gt[:], g_lay, writes=[R.gt_b])

    def cast(dst_ap, dst_b, src_ap, src_b, scal):
        e = ("dve", "act", "dve", "act", "pool")[R.ncast % 5]
        R.ncast += 1
        E = kb.E[e]
        if e == "act":
            if scal is None:
                kb.op(e, lambda: E.copy(out=dst_ap, in_=src_ap), reads=[src_b], writes=[dst_b])
            else:
                kb.op(e, lambda: E.activation(out=dst_ap, in_=src_ap, func=AF.Copy, scale=scal),
                      reads=[src_b, R.gt_b], writes=[dst_b])
        elif scal is None:
            kb.op(e, lambda: E.tensor_copy(out=dst_ap, in_=src_ap), reads=[src_b], writes=[dst_b])
        else:
            kb.op(e, lambda: E.tensor_scalar(out=dst_ap, in0=src_ap, scalar1=scal, scalar2=None, op0=ALU.mult),
                  reads=[src_b, R.gt_b], writes=[dst_b])

    k = 0
    for (W, Wb, src) in ((R.Wg, R.Wg_b, wg), (R.Wu, R.Wu_b, wu)):
        for kc in range(8):
            for (c0, c1) in ((0, 1024), (1024, 2048), (2048, DFF)):
                sb = k % 2; k += 1
                kb.dma("sp", R.stage[sb][:, 0:c1 - c0], src[kc * 128:(kc + 1) * 128, c0:c1],
                       writes=[R.stage_b[sb]])
                cast(W[:, kc, c0:c1], Wb, R.stage[sb][:, 0:c1 - c0], R.stage_b[sb], R.gt[:, kc:kc + 1])
    for fc in range(NFC):
        sb = k % 2; k += 1
        kb.dma("sp", R.stage[sb][:, 0:D], wd[fc * 128:(fc + 1) * 128, :], writes=[R.stage_b[sb]])
        cast(R.Wd[:, fc, :], R.Wd_b, R.stage[sb][:, 0:D], R.stage_b[sb], None)
    if w_out is not None:
        for kc in range(8):
            sb = k % 2; k += 1
            kb.dma("sp", R.stage[sb][:, 0:D], w_out[kc * 128:(kc + 1) * 128, :], writes=[R.stage_b[sb]])
            cast(R.Wo[:, kc, :], R.Wo_b, R.stage[sb][:, 0:D], R.stage_b[sb], None)


def norm_transpose(kb, R, x_ap, x_b, dstT, dstT_b, s):
    nc = kb.nc
    ss = R.st[:, 0:1]; rs = R.st[:, 1:2]; rstd = R.st[:, 2:3]
    kb.op("act", lambda: nc.scalar.activation(out=R.xn[:], in_=x_ap, func=AF.Square, accum_out=ss),
          reads=[x_b], writes=[R.xn_b, R.st_b])
    kb.op("act", lambda: nc.scalar.activation(out=rs, in_=ss, func=AF.Sqrt, bias=EPS, scale=1.0 / D),
          reads=[R.st_b], writes=[R.st_b])
    kb.op("dve", lambda: nc.vector.reciprocal(out=rstd, in_=rs), reads=[R.st_b], writes=[R.st_b])
    kb.op("dve", lambda: nc.vector.tensor_scalar(out=R.xn[:], in0=x_ap, scalar1=rstd, scalar2=None, op0=ALU.mult),
          reads=[x_b, R.st_b], writes=[R.xn_b])
    ti = R.ntp % 2; R.ntp += 1
    tp = R.tp[ti]; tpb = R.tp_b[ti]
    for kc in range(8):
        kb.op("pe", lambda kc=kc: nc.tensor.transpose(out=tp[:, kc, :], in_=R.xn[:, kc * 128:(kc + 1) * 128],
                                                      identity=R.ident[:]),
              reads=[R.xn_b, R.ident_b], writes=[tpb], inc=(kc == 7))
    kb.op("act", lambda: nc.scalar.copy(out=dstT[:, :, s * 128:(s + 1) * 128], in_=tp[:, :, :]),
          reads=[tpb], writes=[dstT_b])


def token_pass(kb, R, T, x_in, x_out, pre=None, post=None, in_bufs=None, out_bufs=None, pre_bufs=None, post_bufs=None):
    nc = kb.nc
    NT = T // TT

    def stage_load(i):
        bi = i % 2
        kb.dma("sp", R.xt[bi][:, :, :], x_in[i * TT:(i + 1) * TT, :].rearrange("(s p) d -> p s d", p=128),
               reads=([in_bufs[i]] if in_bufs else []), writes=[R.xt_b[bi]])
        if pre is not None:
            kb.dma("sp", R.yt[bi][:, :, :], pre[:, i * TT:(i + 1) * TT].rearrange("(c p) t -> p c t", p=128),
                   reads=([pre_bufs[i]] if pre_bufs else []), writes=[R.yt_b[bi]])

    def stage_pre(i):
        bi = i % 2
        if pre is None:
            return
        for s in range(SUB):
            for h in range(2):
                pi = R.npsd % 2; R.npsd += 1
                for kc in range(8):
                    kb.op("pe", lambda kc=kc: nc.tensor.matmul(R.psd[pi][:, :], lhsT=R.yt[bi][:, kc, s * 128:(s + 1) * 128],
                                                               rhs=R.Wo[:, kc, h * 512:(h + 1) * 512],
                                                               start=(kc == 0), stop=(kc == 7)),
                          reads=[R.yt_b[bi], R.Wo_b], writes=[R.psd_b[pi]], inc=(kc == 7))
                xs = R.xt[bi][:, s, h * 512:(h + 1) * 512]
                kb.op("dve", lambda: nc.vector.tensor_tensor(out=xs, in0=R.psd[pi][:, :], in1=xs, op=ALU.add),
                      reads=[R.psd_b[pi], R.xt_b[bi]], writes=[R.xt_b[bi]])

    def stage_a(i):
        bi = i % 2
        for s in range(SUB):
            norm_transpose(kb, R, R.xt[bi][:, s, :], R.xt_b[bi], R.xnT[bi], R.xnT_b[bi], s)

    def stage_b(i):
        bi = i % 2
        for fc in range(NFC):
            gi = fc % 2
            for (W, Wb, off) in ((R.Wg, R.Wg_b, 0), (R.Wu, R.Wu_b, 256)):
                for kc in range(8):
                    kb.op("pe", lambda kc=kc, W=W, off=off: nc.tensor.matmul(
                        R.psg[gi][:, off:off + TT], lhsT=W[:, kc, fc * 128:(fc + 1) * 128], rhs=R.xnT[bi][:, kc, :],
                        start=(kc == 0), stop=(kc == 7)),
                          reads=[Wb, R.xnT_b[bi]], writes=[R.psg_b[gi]], inc=(kc == 7))
            kb.op("act", lambda: nc.scalar.activation(out=R.sg[gi][:, :], in_=R.psg[gi][:, 0:TT], func=AF.Silu),
                  reads=[R.psg_b[gi]], writes=[R.sg_b[gi]])
            kb.op("dve", lambda: nc.vector.tensor_tensor(out=R.hid[:, fc, :], in0=R.sg[gi][:, :],
                                                         in1=R.psg[gi][:, 256:256 + TT], op=ALU.mult),
                  reads=[R.sg_b[gi], R.psg_b[gi]], writes=[R.hid_b[fc]])

    def stage_c(i):
        bi = i % 2
        for s in range(SUB):
            for h in range(2):
                pi = R.npsd % 2; R.npsd += 1
                for fc in range(NFC):
                    kb.op("pe", lambda fc=fc: nc.tensor.matmul(R.psd[pi][:, :], lhsT=R.hid[:, fc, s * 128:(s + 1) * 128],
                                                               rhs=R.Wd[:, fc, h * 512:(h + 1) * 512],
                                                               start=(fc == 0), stop=(fc == NFC - 1)),
                          reads=[R.hid_b[fc], R.Wd_b], writes=[R.psd_b[pi]], inc=(fc == NFC - 1))
                xs = R.xt[bi][:, s, h * 512:(h + 1) * 512]
                kb.op("dve", lambda: nc.vector.scalar_tensor_tensor(out=xs, in0=R.psd[pi][:, :], scalar=0.5, in1=xs,
                                                                    op0=ALU.mult, op1=ALU.add),
                      reads=[R.psd_b[pi], R.xt_b[bi]], writes=[R.xt_b[bi]])
            if post is not None:
                norm_transpose(kb, R, R.xt[bi][:, s, :], R.xt_b[bi], R.hTo, R.hTo_b, s)
        kb.dma("pool", x_out[i * TT:(i + 1) * TT, :].rearrange("(s p) d -> p s d", p=128), R.xt[bi][:, :, :],
               reads=[R.xt_b[bi]], writes=([out_bufs[i]] if out_bufs else []))
        if post is not None:
            kb.dma("pool", post[:, i * TT:(i + 1) * TT].rearrange("(c p) t -> p c t", p=128), R.hTo[:, :, :],
                   reads=[R.hTo_b], writes=([post_bufs[i]] if post_bufs else []))

    stage_load(0)
    stage_pre(0)
    stage_a(0)
    for i in range(NT):
        if i + 1 < NT:
            stage_load(i + 1)
        stage_b(i)
        if i + 1 < NT:
            stage_pre(i + 1)
            stage_a(i + 1)
        stage_c(i)


QT_ = 512
NW = 834
A_Q, A_K, A_V = 0, 64, 128
B_Q, B_K, B_V, B_O, B_I, B_F = 192, 256, 320, 384, 448, 449
C_Q, C_K, C_V = 450, 514, 578
D_Q, D_K, D_V = 642, 706, 770


class MixRes:
    def __init__(self, kb, S):
        self.S = S
        self.NB = S // 128
        self.NQ = S // QT_
        NB = self.NB
        self.Wm = kb.sbuf("Wm", [128, 8, NW], BF16); self.Wm_b = kb.buf()
        self.gm = kb.sbuf("gm", [128, 8], F32); self.gm_b = kb.buf()
        self.ident = kb.sbuf("identm", [128, 128], BF16); self.ident_b = kb.buf()
        self.ht = [kb.sbuf(f"ht{i}", [128, 8, QT_], BF16) for i in range(2)]; self.ht_b = [kb.buf() for _ in range(2)]
        self.QT = kb.sbuf("QT", [128, S], BF16); self.QT_b = kb.buf()
        self.KT = kb.sbuf("KT", [128, S], BF16); self.KT_b = kb.buf()
        self.Va = kb.sbuf("Va", [128, NB, 128], BF16); self.Va_b = kb.buf()
        self.par = kb.sbuf("par", [128, 32], F32); self.par_b = kb.buf()
        self.cst = kb.sbuf("cst", [128, 256], BF16); self.cst_b = kb.buf()
        self.sq = [kb.sbuf(f"sq{i}", [64, QT_], BF16) for i in range(2)]; self.sq_b = [kb.buf() for _ in range(2)]
        self.rr = [kb.sbuf(f"rr{i}", [64, QT_], F32) for i in range(2)]; self.rr_b = [kb.buf() for _ in range(2)]
        self.e32 = [kb.sbuf(f"e32_{i}", [128, 2, QT_], F32) for i in range(2)]; self.e32_b = [kb.buf() for _ in range(2)]
        self.wst = [self.e32[i][:, :, :].rearrange("p a b -> p (a b)")[:, 0:NW] for i in range(2)]; self.wst_b = self.e32_b
        self.e32c = kb.sbuf("e32_c", [128, 2, QT_], F32)
        self.e16 = [kb.sbuf(f"e16_{i}", [128, 2, QT_], BF16) for i in range(4)]; self.e16_b = [kb.buf() for _ in range(4)]
        self.spp = [kb.sbuf(f"spp_{i}", [128, 2, QT_], BF16) for i in range(2)]; self.spp_b = [kb.buf() for _ in range(2)]
        self.fin = [kb.sbuf(f"fin{i}", [64, QT_], F32) for i in range(3)]; self.fin_b = [kb.buf() for _ in range(3)]
        self.yo = [kb.sbuf(f"yo{i}", [64, QT_], BF16) for i in range(2)]; self.yo_b = [kb.buf() for _ in range(2)]
        self.mask = kb.sbuf("mask", [128, 4, QT_], BF16); self.mask_b = kb.buf()
        self.EB = kb.sbuf("EB", [128, 8, QT_], F32); self.EB_b = kb.buf()
        self.tri = kb.sbuf("tri", [128, 3, 128], BF16); self.tri_b = kb.buf()
        self.pp = [kb.psum(f"pp{i}", [128, 2, 512], F32) for i in range(4)]
        self.ps = [self.pp[i // 2][:, i % 2, :] for i in range(8)]
        self.ps_b = [kb.buf() for _ in range(8)]
        self.nyo = 0
        self.ne16 = 0


def mix_load_common(kb, R, wsel, gmix_lay, ident_d, cst_d):
    nc = kb.nc
    kb.dma("sp", R.gm[:], gmix_lay, writes=[R.gm_b])
    kb.dma("sp", R.ident[:], ident_d, writes=[R.ident_b])
    kb.dma("sp", R.cst[:], cst_d, writes=[R.cst_b])
    for kc in range(8):
        sb = kc % 2
        kb.dma("sp", R.wst[sb][:, :], wsel[kc * 128:(kc + 1) * 128, :], writes=[R.wst_b[sb]])
        kb.op("dve", lambda: nc.vector.tensor_scalar(out=R.Wm[:, kc, :], in0=R.wst[sb][:, :], scalar1=R.gm[:, kc:kc + 1],
                                                     scalar2=None, op0=ALU.mult),
              reads=[R.wst_b[sb], R.gm_b], writes=[R.Wm_b])


def load_ht(kb, R, hT, qt):
    bi = qt % 2
    if callable(hT):
        src = hT(qt).rearrange("c p t -> p c t")
    else:
        src = hT[:, qt * QT_:(qt + 1) * QT_].rearrange("(c p) t -> p c t", p=128)
    kb.dma("sp", R.ht[bi][:, :, :], src, writes=[R.ht_b[bi]])
    return R.ht[bi], R.ht_b[bi]


def proj_fm(kb, R, ht, ht_b, c0, ncol, pb):
    nc = kb.nc
    for kc in range(8):
        kb.op("pe", lambda kc=kc: nc.tensor.matmul(R.ps[pb][0:ncol, :], lhsT=R.Wm[:, kc, c0:c0 + ncol], rhs=ht[:, kc, :],
                                                   start=(kc == 0), stop=(kc == 7)),
              reads=[R.Wm_b, ht_b], writes=[R.ps_b[pb]], inc=(kc == 7))


def proj_tm(kb, R, ht, ht_b, c0, ncol, pb, s):
    nc = kb.nc
    for kc in range(8):
        kb.op("pe", lambda kc=kc: nc.tensor.matmul(R.ps[pb][:, s * 128:s * 128 + ncol], lhsT=ht[:, kc, s * 128:(s + 1) * 128],
                                                   rhs=R.Wm[:, kc, c0:c0 + ncol], start=(kc == 0), stop=(kc == 7)),
              reads=[R.Wm_b, ht_b], writes=[R.ps_b[pb]], inc=(kc == 7))


def qk_norm_store(kb, R, pb, pb2, dst, dst_b, qt, gcol, cmat, inv_n, i2):
    nc = kb.nc
    sq, sqb = R.sq[i2], R.sq_b[i2]
    rr, rrb = R.rr[i2], R.rr_b[i2]
    kb.op("act", lambda: nc.scalar.activation(out=sq[:, :], in_=R.ps[pb][0:64, :], func=AF.Square),
          reads=[R.ps_b[pb]], writes=[sqb])
    kb.op("pe", lambda: nc.tensor.matmul(R.ps[pb2][0:64, :], lhsT=cmat, rhs=sq[:, :], start=True, stop=True),
          reads=[sqb, R.cst_b], writes=[R.ps_b[pb2]])
    kb.op("act", lambda: nc.scalar.activation(out=rr[:, :], in_=R.ps[pb2][0:64, :], func=AF.Ln, bias=R.par[0:64, 13:14], scale=inv_n),
          reads=[R.ps_b[pb2], R.par_b], writes=[rrb])
    kb.op("act", lambda: nc.scalar.activation(out=rr[:, :], in_=rr[:, :], func=AF.Exp, scale=-0.5), reads=[rrb], writes=[rrb])
    kb.op("dve", lambda: nc.vector.scalar_tensor_tensor(out=dst[0:64, qt * QT_:(qt + 1) * QT_], in0=R.ps[pb][0:64, :],
                                                        scalar=R.par[0:64, gcol:gcol + 1], in1=rr[:, :],
                                                        op0=ALU.mult, op1=ALU.mult),
          reads=[R.ps_b[pb], rrb, R.par_b], writes=[dst_b])


def v_store(kb, R, ht, ht_b, c0, qt, pb):
    nc = kb.nc
    for s in range(4):
        proj_tm(kb, R, ht, ht_b, c0, 64, pb, s)
    src = R.ps[pb][:, :].rearrange("p (s c) -> p s c", c=128)[:, :, 0:64]
    kb.op("act", lambda: nc.scalar.copy(out=R.Va[:, qt * 4:(qt + 1) * 4, 0:64], in_=src),
          reads=[R.ps_b[pb]], writes=[R.Va_b])


def ydst(yT_d, row0, qt):
    if callable(yT_d):
        return yT_d(row0, qt)
    return yT_d[row0:row0 + 64, qt * QT_:(qt + 1) * QT_]


def out_store(kb, R, yT_d, row0, qt, src_fn, reads):
    i = R.nyo % 2; R.nyo += 1
    src_fn(R.yo[i], R.yo_b[i])
    kb.dma("pool", ydst(yT_d, row0, qt), R.yo[i][:, :], reads=[R.yo_b[i]])


def mixer_c(kb, R, hT, yT_d, row0, masks_c):
    nc = kb.nc
    S, NB, NQ = R.S, R.NB, R.NQ
    kb.dma("sp", R.mask[:, :, :], masks_c[:, :, 0, :], writes=[R.mask_b])
    kb.op("pool", lambda: nc.gpsimd.memset(R.Va[:, :, 64:128], 1.0), writes=[R.Va_b])
    bd32 = R.cst[0:64, 64:128]
    ones64 = R.cst[0:64, 0:64]
    for qt in range(NQ):
        ht, htb = load_ht(kb, R, hT, qt)
        proj_fm(kb, R, ht, htb, C_Q, 64, 0)
        proj_fm(kb, R, ht, htb, C_K, 64, 2)
        qk_norm_store(kb, R, 0, 1, R.QT, R.QT_b, qt, 0, bd32, 1.0 / 32, 0)
        qk_norm_store(kb, R, 2, 3, R.KT, R.KT_b, qt, 1, bd32, 1.0 / 32, 1)
        v_store(kb, R, ht, htb, C_V, qt, 4 + (qt % 2))
    for qt in range(NQ):
        nkb = 4 * qt + 4
        O0, O1 = 6, 7
        estate = {}

        def s_step(kbk):
            pj = kbk % 3
            sb = 2 * pj
            for m in range(2):
                kb.op("pe", lambda m=m: nc.tensor.matmul(R.ps[sb + m],
                                                         lhsT=R.KT[m * 32:(m + 1) * 32, kbk * 128:(kbk + 1) * 128],
                                                         rhs=R.QT[m * 32:(m + 1) * 32, qt * QT_:(qt + 1) * QT_],
                                                         start=True, stop=True),
                      reads=[R.KT_b, R.QT_b], writes=[R.ps_b[sb + m]])
            ei = R.ne16 % 3; R.ne16 += 1
            e, eb = R.e16[ei], R.e16_b[ei]
            estate[kbk] = (e, eb)
            kb.op("act", lambda: nc.scalar.activation(out=e[:, :, :], in_=R.pp[pj][:, :, :], func=AF.Exp),
                  reads=[R.ps_b[sb], R.ps_b[sb + 1]], writes=[eb])
            r = kbk - 4 * qt
            if r >= 0:
                for m in range(2):
                    kb.op("dve", lambda m=m: nc.vector.tensor_tensor(out=e[:, m, :], in0=e[:, m, :], in1=R.mask[:, r, :], op=ALU.mult),
                          reads=[eb, R.mask_b], writes=[eb])

        def pv_step(kbk):
            e, eb = estate.pop(kbk)
            for m in range(2):
                kb.op("pe", lambda m=m: nc.tensor.matmul(R.ps[O0 + m][:, :], lhsT=R.Va[:, kbk, :], rhs=e[:, m, :],
                                                         start=(kbk == 0), stop=(kbk == nkb - 1)),
                      reads=[R.Va_b, eb], writes=[R.ps_b[O0 + m]])

        s_step(0)
        if nkb > 1:
            s_step(1)
        for kbk in range(nkb):
            if kbk + 2 < nkb:
                s_step(kbk + 2)
            pv_step(kbk)
        f0, f1, f2 = R.fin
        b0, b1, b2 = R.fin_b
        kb.op("dve", lambda: nc.vector.reciprocal(out=f0[:, :], in_=R.ps[O0][64:128, :]), reads=[R.ps_b[O0]], writes=[b0])
        kb.op("dve", lambda: nc.vector.tensor_tensor(out=f0[:, :], in0=R.ps[O0][0:64, :], in1=f0[:, :], op=ALU.mult),
              reads=[R.ps_b[O0], b0], writes=[b0])
        kb.op("dve", lambda: nc.vector.reciprocal(out=f1[:, :], in_=R.ps[O1][64:128, :]), reads=[R.ps_b[O1]], writes=[b1])
        kb.op("dve", lambda: nc.vector.tensor_tensor(out=f1[:, :], in0=R.ps[O1][0:64, :], in1=f1[:, :], op=ALU.mult),
              reads=[R.ps_b[O1], b1], writes=[b1])
        kb.op("dve", lambda: nc.vector.scalar_tensor_tensor(out=f2[:, :], in0=f1[:, :], scalar=R.par[0:64, 2:3], in1=f0[:, :],
                                                            op0=ALU.mult, op1=ALU.add),
              reads=[b0, b1, R.par_b], writes=[b2])
        kb.op("act", lambda: nc.scalar.activation(out=R.sq[0][:, :], in_=f2[:, :], func=AF.Square), reads=[b2], writes=[R.sq_b[0]])
        kb.op("pe", lambda: nc.tensor.matmul(R.ps[0][0:64, :], lhsT=ones64, rhs=R.sq[0][:, :], start=True, stop=True),
              reads=[R.sq_b[0], R.cst_b], writes=[R.ps_b[0]])
        kb.op("act", lambda: nc.scalar.activation(out=R.rr[0][:, :], in_=R.ps[0][0:64, :], func=AF.Ln, bias=R.par[0:64, 13:14], scale=1.0 / 64),
              reads=[R.ps_b[0], R.par_b], writes=[R.rr_b[0]])
        kb.op("act", lambda: nc.scalar.activation(out=R.rr[0][:, :], in_=R.rr[0][:, :], func=AF.Exp, scale=-0.5), reads=[R.rr_b[0]], writes=[R.rr_b[0]])

        def fn(yo, yob):
            kb.op("dve", lambda: nc.vector.scalar_tensor_tensor(out=yo[:, :], in0=f2[:, :], scalar=R.par[0:64, 3:4], in1=R.rr[0][:, :],
                                                                op0=ALU.mult, op1=ALU.mult),
                  reads=[b2, R.rr_b[0], R.par_b], writes=[yob])
        out_store(kb, R, yT_d, row0, qt, fn, None)


def mixer_d(kb, R, hT, yT_d, row0, masks_d, tri_d):
    nc = kb.nc
    S, NB, NQ = R.S, R.NB, R.NQ
    kb.dma("sp", R.mask[:, :, :], masks_d, writes=[R.mask_b])
    kb.dma("sp", R.tri[:, :, :], tri_d, writes=[R.tri_b])
    for qt in range(NQ):
        ht, htb = load_ht(kb, R, hT, qt)
        proj_fm(kb, R, ht, htb, D_Q, 64, 0)
        proj_fm(kb, R, ht, htb, D_K, 64, 1)
        for half in range(2):
            rows = slice(half * 64, half * 64 + 64)
            kb.op("act", lambda rows=rows: nc.scalar.activation(out=R.QT[rows, qt * QT_:(qt + 1) * QT_], in_=R.ps[0][0:64, :], func=AF.Copy, scale=0.125),
                  reads=[R.ps_b[0]], writes=[R.QT_b])
            kb.op("dve", lambda rows=rows: nc.vector.tensor_copy(out=R.KT[rows, qt * QT_:(qt + 1) * QT_], in_=R.ps[1][0:64, :]),
                  reads=[R.ps_b[1]], writes=[R.KT_b])
        v_store(kb, R, ht, htb, D_V, qt, 4 + (qt % 2))
    RA, RB, OB = 4, 5, 6
    ed_b = [kb.buf() for _ in range(3)]
    ed = [R.e32[0], R.e32[1], R.e32c]
    for qt in range(NQ):
        kbs = list(range(4 * qt + 3, -1, -1))
        npair = len(kbs) // 2
        qsl = slice(qt * QT_, (qt + 1) * QT_)

        def blocks(j):
            return kbs[2 * j], kbs[2 * j + 1]

        def z_mm(j):
            for h, kbk in enumerate(blocks(j)):
                zb = 2 * (j % 2) + h
                rows = slice(h * 64, h * 64 + 64)
                kb.op("pe", lambda kbk=kbk, zb=zb, rows=rows: nc.tensor.matmul(R.ps[zb], lhsT=R.KT[rows, kbk * 128:(kbk + 1) * 128],
                                                                               rhs=R.QT[rows, qsl], start=True, stop=True),
                      reads=[R.KT_b, R.QT_b], writes=[R.ps_b[zb]])

        def esp(j):
            zp = j % 2
            e, eb = ed[j % 3], ed_b[j % 3]
            sp, spb = R.spp[j % 2], R.spp_b[j % 2]
            kb.op("act", lambda: nc.scalar.activation(out=e[:, :, :], in_=R.pp[zp][:, :, :], func=AF.Exp),
                  reads=[R.ps_b[2 * zp], R.ps_b[2 * zp + 1]], writes=[eb])
            kb.op("act", lambda: nc.scalar.activation(out=sp[:, :, :], in_=e[:, :, :], func=AF.Ln, bias=1.0),
                  reads=[eb], writes=[spb])
            for h, kbk in enumerate(blocks(j)):
                r = kbk - 4 * qt
                if r >= 0:
                    kb.op("dve", lambda h=h, r=r: nc.vector.tensor_tensor(out=sp[:, h, :], in0=sp[:, h, :], in1=R.mask[:, r, :], op=ALU.mult),
                          reads=[spb, R.mask_b], writes=[spb])

        def mmR(bank, t_i, sp_ap, spb, start, stop=False):
            kb.op("pe", lambda: nc.tensor.matmul(R.ps[bank], lhsT=R.tri[:, t_i, :], rhs=sp_ap, start=start, stop=stop),
                  reads=[R.tri_b, spb], writes=[R.ps_b[bank]])

        def chain_a(j):
            sp, spb = R.spp[j % 2], R.spp_b[j % 2]
            mmR(RA, 0, sp[:, 0, :], spb, j == 0)
            mmR(RB, 2, sp[:, 0, :], spb, j == 0)
            mmR(RB, 0, sp[:, 1, :], spb, False)

        def chain_b(j):
            e, eb = ed[j % 3], ed_b[j % 3]
            sp, spb = R.spp[j % 2], R.spp_b[j % 2]
            tt, ttb = R.e16[j % 2], R.e16_b[j % 2]
            aa, aab = R.e16[2 + j % 2], R.e16_b[2 + j % 2]
            kb.op("act", lambda: nc.scalar.activation(out=tt[:, :, :], in_=R.pp[2][:, :, :], func=AF.Exp, scale=-1.0),
                  reads=[R.ps_b[RA], R.ps_b[RB]], writes=[ttb])
            mmR(RA, 1, sp[:, 0, :], spb, False)
            mmR(RA, 2, sp[:, 1, :], spb, False, j == npair - 1)
            mmR(RB, 1, sp[:, 1, :], spb, False, j == npair - 1)
            kb.op("dve", lambda: nc.vector.tensor_tensor(out=aa[:, :, :], in0=e[:, :, :], in1=tt[:, :, :], op=ALU.mult),
                  reads=[eb, ttb], writes=[aab])
            for h, kbk in enumerate(blocks(j)):
                r = kbk - 4 * qt
                if r >= 0:
                    kb.op("dve", lambda h=h, r=r: nc.vector.tensor_tensor(out=aa[:, h, :], in0=aa[:, h, :], in1=R.mask[:, r, :], op=ALU.mult),
                          reads=[aab, R.mask_b], writes=[aab])

        def pv(j):
            aa, aab = R.e16[2 + j % 2], R.e16_b[2 + j % 2]
            for h, kbk in enumerate(blocks(j)):
                kb.op("pe", lambda h=h, kbk=kbk: nc.tensor.matmul(R.ps[OB][0:64, :], lhsT=R.Va[:, kbk, 0:64], rhs=aa[:, h, :],
                                                                  start=(j == 0 and h == 0), stop=(j == npair - 1 and h == 1)),
                      reads=[R.Va_b, aab], writes=[R.ps_b[OB]])

        z_mm(0)
        if npair > 1:
            z_mm(1)
        esp(0)
        for j in range(npair):
            if j + 1 < npair:
                esp(j + 1)
            chain_a(j)
            if j + 2 < npair:
                z_mm(j + 2)
            if j >= 1:
                pv(j - 1)
            chain_b(j)
        pv(npair - 1)

        def fn(yo, yob):
            kb.op("dve", lambda: nc.vector.tensor_copy(out=yo[:, :], in_=R.ps[OB][0:64, :]), reads=[R.ps_b[OB]], writes=[yob])
        out_store(kb, R, yT_d, row0, qt, fn, None)


def mixer_a(kb, R, hT, yT_d, row0, biasT_d):
    nc = kb.nc
    S, NB, NQ = R.S, R.NB, R.NQ
    kb.dma("sp", R.EB[:, :, :], biasT_d, writes=[R.EB_b])
    for r in range(8):
        kb.op("act", lambda r=r: nc.scalar.activation(out=R.EB[:, r, :], in_=R.EB[:, r, :], func=AF.Exp),
              reads=[R.EB_b], writes=[R.EB_b])
    kb.op("pool", lambda: nc.gpsimd.memset(R.Va[:, :, 64:128], 1.0), writes=[R.Va_b])
    ones64 = R.cst[0:64, 0:64]
    for qt in range(NQ):
        ht, htb = load_ht(kb, R, hT, qt)
        proj_fm(kb, R, ht, htb, A_Q, 64, 0)
        proj_fm(kb, R, ht, htb, A_K, 64, 2)
        qk_norm_store(kb, R, 0, 1, R.QT, R.QT_b, qt, 4, ones64, 1.0 / 64, 0)
        qk_norm_store(kb, R, 2, 3, R.KT, R.KT_b, qt, 5, ones64, 1.0 / 64, 1)
        v_store(kb, R, ht, htb, A_V, qt, 4 + (qt % 2))
    OB = 4
    for qt in range(NQ):
        rs = [r for r in range(8) if 4 * qt - 4 + r >= 0]
        for j, r in enumerate(rs):
            kbk = 4 * qt - 4 + r
            sb = j % 2
            kb.op("pe", lambda: nc.tensor.matmul(R.ps[sb][:, :], lhsT=R.KT[0:64, kbk * 128:(kbk + 1) * 128],
                                                 rhs=R.QT[0:64, qt * QT_:(qt + 1) * QT_], start=True, stop=True),
                  reads=[R.KT_b, R.QT_b], writes=[R.ps_b[sb]])
            e, eb = R.e32[j % 2], R.e32_b[j % 2]
            p, pbuf = R.e16[j % 2], R.e16_b[j % 2]
            kb.op("act", lambda: nc.scalar.activation(out=e[:, 0, :], in_=R.ps[sb][:, :], func=AF.Exp),
                  reads=[R.ps_b[sb]], writes=[eb])
            kb.op("dve", lambda: nc.vector.tensor_tensor(out=p[:, 0, :], in0=e[:, 0, :], in1=R.EB[:, r, :], op=ALU.mult),
                  reads=[eb, R.EB_b], writes=[pbuf])
            kb.op("pe", lambda: nc.tensor.matmul(R.ps[OB][:, :], lhsT=R.Va[:, kbk, :], rhs=p[:, 0, :],
                                                 start=(j == 0), stop=(j == len(rs) - 1)),
                  reads=[R.Va_b, pbuf], writes=[R.ps_b[OB]])
        f0, b0 = R.fin[0], R.fin_b[0]
        kb.op("dve", lambda: nc.vector.reciprocal(out=f0[:, :], in_=R.ps[OB][64:128, :]), reads=[R.ps_b[OB]], writes=[b0])

        def fn(yo, yob):
            kb.op("dve", lambda: nc.vector.tensor_tensor(out=yo[:, :], in0=R.ps[OB][0:64, :], in1=f0[:, :], op=ALU.mult),
                  reads=[R.ps_b[OB], b0], writes=[yob])
        out_store(kb, R, yT_d, row0, qt, fn, None)


class MixResB:
    def __init__(self, kb, R):
        NB = R.NB
        self.Osig = R.EB[:, :, :].bitcast(BF16).rearrange("p a (b c) -> p (a b) c", c=64)[:, 0:NB, :]; self.Osig_b = R.EB_b
        self.G = kb.sbuf("Gates", [128, 8, NB], F32); self.G_b = kb.buf()
        self.trif = kb.sbuf("trif", [128, 2, 128], F32); self.trif_b = kb.buf()
        self.cw = kb.sbuf("convw", [128, 8], F32); self.cw_b = kb.buf()
        self.gob = kb.sbuf("gob", [128, 64], F32); self.gob_b = kb.buf()
        self.St = [kb.sbuf(f"St{i}", [64, 65], F32) for i in range(2)]; self.St_b = [kb.buf() for _ in range(2)]
        self.Sb = [kb.sbuf(f"Sb{i}", [64, 65], BF16) for i in range(3)]; self.Sb_b = [kb.buf() for _ in range(3)]
        self.tok = [kb.sbuf(f"tok{i}", [128, 3, 64], BF16) for i in range(2)]; self.tok_b = [kb.buf() for _ in range(2)]
        self.qkT = [kb.sbuf(f"qkT{i}", [64, 2, 128], BF16) for i in range(2)]; self.qkT_b = [kb.buf() for _ in range(2)]
        self.qkm = [kb.sbuf(f"qkm{i}", [128, 128], BF16) for i in range(2)]; self.qkm_b = [kb.buf() for _ in range(2)]
        self.cm = kb.sbuf("cmask", [128, 128], F32); self.cm_b = kb.buf()
        self.hn = [kb.sbuf(f"hn{i}", [128, 64], F32) for i in range(2)]; self.hn_b = [kb.buf() for _ in range(2)]
        self.hs = [kb.sbuf(f"hs{i}", [128, 8], F32) for i in range(2)]; self.hs_b = [kb.buf() for _ in range(2)]
        self.yb = [kb.sbuf(f"yb{i}", [128, 64], BF16) for i in range(2)]; self.yb_b = [kb.buf() for _ in range(2)]
        self.jk = kb.sbuf("jk", [128, 64], BF16); self.jk_b = kb.buf()


def mixer_b(kb, R, RB_, hT, yT_d, row0, bpar_d, trif_d, cmask_d, gob_d):
    nc = kb.nc
    S, NB, NQ = R.S, R.NB, R.NQ
    B = RB_
    kb.dma("sp", B.cw[:, :], bpar_d, writes=[B.cw_b])
    kb.dma("sp", B.trif[:, :, :], trif_d, writes=[B.trif_b])
    kb.dma("sp", B.cm[:, :], cmask_d, writes=[B.cm_b])
    kb.dma("sp", B.gob[:, :], gob_d, writes=[B.gob_b])
    kb.op("pool", lambda: nc.gpsimd.memset(R.Va[:, :, 64:65], 1.0), writes=[R.Va_b])
    kb.op("dve", lambda: nc.vector.tensor_scalar(out=B.cw[:, 7:8], in0=B.cw[:, 6:7], scalar1=-1.0, scalar2=None, op0=ALU.mult),
          reads=[B.cw_b], writes=[B.cw_b])
    cv = [R.e32[i][:, :, :].rearrange("p a b -> p (a b)") for i in range(2)]
    cvb = R.e32_b
    accA = R.e16[0][:, :, :].rearrange("p a b -> p (a b)").bitcast(F32); accB_ = R.e16[1][:, :, :].rearrange("p a b -> p (a b)").bitcast(F32)
    accA_b = R.e16_b[0]; accB_b = R.e16_b[1]
    for qt in range(NQ):
        ht, htb = load_ht(kb, R, hT, qt)
        ci = qt % 2
        proj_fm(kb, R, ht, htb, B_Q, 128, 0)
        if qt == 0:
            kb.op("dve", lambda: nc.vector.memset(cv[ci][:, 0:3], 0.0), writes=[cvb[ci]])
        else:
            kb.op("dve", lambda: nc.vector.tensor_copy(out=cv[ci][:, 0:3], in_=cv[1 - ci][:, 512:515]),
                  reads=[cvb[1 - ci]], writes=[cvb[ci]])
        kb.op("act", lambda: nc.scalar.copy(out=cv[ci][:, 3:515], in_=R.ps[0][:, :]), reads=[R.ps_b[0]], writes=[cvb[ci]])
        kb.op("dve", lambda: nc.vector.tensor_scalar(out=accA, in0=cv[ci][:, 3:515], scalar1=B.cw[:, 3:4], scalar2=B.cw[:, 4:5],
                                                     op0=ALU.mult, op1=ALU.add),
              reads=[cvb[ci], B.cw_b], writes=[accA_b])
        for j in (2, 1, 0):
            kb.op("dve", lambda j=j: nc.vector.scalar_tensor_tensor(out=accA, in0=cv[ci][:, j:j + 512], scalar=B.cw[:, j:j + 1],
                                                                    in1=accA, op0=ALU.mult, op1=ALU.add),
                  reads=[cvb[ci], B.cw_b, accA_b], writes=[accA_b])
        kb.op("act", lambda: nc.scalar.activation(out=accB_, in_=accA, func=AF.Sigmoid), reads=[accA_b], writes=[accB_b])
        kb.op("dve", lambda: nc.vector.tensor_tensor(out=accB_, in0=accA, in1=accB_, op=ALU.mult), reads=[accA_b, accB_b], writes=[accB_b])
        kb.op("act", lambda: nc.scalar.copy(out=R.QT[0:64, qt * QT_:(qt + 1) * QT_], in_=accB_[0:64, :]), reads=[accB_b], writes=[R.QT_b])
        kb.op("act", lambda: nc.scalar.copy(out=R.KT[0:64, qt * QT_:(qt + 1) * QT_], in_=accB_[64:128, :]), reads=[accB_b], writes=[R.KT_b])
        pb = 4 + (qt % 2)
        for s in range(4):
            nonlocal_pb = 3 + ((qt * 4 + s) % 4)
            for kc in range(8):
                kb.op("pe", lambda kc=kc: nc.tensor.matmul(R.ps[nonlocal_pb][:, 0:130], lhsT=ht[:, kc, s * 128:(s + 1) * 128],
                                                           rhs=R.Wm[:, kc, B_V:B_V + 130], start=(kc == 0), stop=(kc == 7)),
                      reads=[R.Wm_b, htb], writes=[R.ps_b[nonlocal_pb]], inc=(kc == 7))
            blk = qt * 4 + s
            kb.op("dve", lambda: nc.vector.tensor_copy(out=R.Va[:, blk, 0:64], in_=R.ps[nonlocal_pb][:, 0:64]),
                  reads=[R.ps_b[nonlocal_pb]], writes=[R.Va_b])
            kb.op("act", lambda: nc.scalar.activation(out=B.Osig[:, blk, :], in_=R.ps[nonlocal_pb][:, 64:128], func=AF.Sigmoid),
                  reads=[R.ps_b[nonlocal_pb]], writes=[B.Osig_b])
            kb.op("dve", lambda: nc.vector.tensor_copy(out=B.G[:, 0:2, blk], in_=R.ps[nonlocal_pb][:, 128:130]),
                  reads=[R.ps_b[nonlocal_pb]], writes=[B.G_b])
    G = B.G
    kb.op("act", lambda: nc.scalar.activation(out=G[:, 2, :], in_=G[:, 1, :], func=AF.Exp, scale=-1.0, bias=B.cw[:, 7:8]),
          reads=[B.G_b, B.cw_b], writes=[B.G_b])
    kb.op("act", lambda: nc.scalar.activation(out=G[:, 2, :], in_=G[:, 2, :], func=AF.Ln, bias=1.0), reads=[B.G_b], writes=[B.G_b])
    kb.op("dve", lambda: nc.vector.tensor_scalar(out=G[:, 2, :], in0=G[:, 2, :], scalar1=-1.0, scalar2=None, op0=ALU.mult),
          reads=[B.G_b], writes=[B.G_b])
    kb.op("pe", lambda: nc.tensor.matmul(R.ps[0][:, 0:NB], lhsT=B.trif[:, 0, :], rhs=G[:, 2, :], start=True, stop=True),
          reads=[B.trif_b, B.G_b], writes=[R.ps_b[0]])
    kb.op("pe", lambda: nc.tensor.matmul(R.ps[1][:, 0:NB], lhsT=B.trif[:, 1, :], rhs=G[:, 2, :], start=True, stop=True),
          reads=[B.trif_b, B.G_b], writes=[R.ps_b[1]])
    kb.op("dve", lambda: nc.vector.tensor_copy(out=G[:, 3, :], in_=R.ps[0][:, 0:NB]), reads=[R.ps_b[0]], writes=[B.G_b])
    kb.op("act", lambda: nc.scalar.activation(out=G[:, 4, :], in_=G[:, 3, :], func=AF.Exp), reads=[B.G_b], writes=[B.G_b])
    kb.op("dve", lambda: nc.vector.tensor_tensor(out=G[:, 5, :], in0=G[:, 0, :], in1=G[:, 3, :], op=ALU.subtract),
          reads=[B.G_b], writes=[B.G_b])
    kb.op("dve", lambda: nc.vector.tensor_tensor(out=G[:, 6, :], in0=G[:, 5, :], in1=R.ps[1][:, 0:NB], op=ALU.add),
          reads=[B.G_b, R.ps_b[1]], writes=[B.G_b])
    kb.op("act", lambda: nc.scalar.activation(out=G[:, 5, :], in_=G[:, 5, :], func=AF.Exp, bias=B.cw[:, 5:6]),
          reads=[B.G_b, B.cw_b], writes=[B.G_b])
    kb.op("act", lambda: nc.scalar.activation(out=G[:, 6, :], in_=G[:, 6, :], func=AF.Exp, bias=B.cw[:, 5:6]),
          reads=[B.G_b, B.cw_b], writes=[B.G_b])
    kb.op("dve", lambda: nc.vector.tensor_scalar(out=G[:, 5:7, :], in0=G[:, 5:7, :], scalar1=0.125, scalar2=None, op0=ALU.mult),
          reads=[B.G_b], writes=[B.G_b])
    kb.op("act", lambda: nc.scalar.activation(out=G[:, 7, :], in_=R.ps[1][:, 0:NB], func=AF.Exp), reads=[R.ps_b[1]], writes=[B.G_b])
    kb.op("dve", lambda: nc.vector.memset(B.St[0][:, :], 0.0), writes=[B.St_b[0]])
    kb.op("dve", lambda: nc.vector.memset(B.Sb[0][:, :], 0.0), writes=[B.Sb_b[0]])
    PT, PT2, PS_, PO, PU = 0, 1, 2, 3, 6
    def front(b):
        i2 = b % 2
        tok, tokb = B.tok[i2], B.tok_b[i2]
        qkT, qkTb = B.qkT[i2], B.qkT_b[i2]
        tpA = R.ps[0][:, :].bitcast(BF16); tpq = tpA[:, 0:128]; tpk = tpA[:, 128:256]
        kb.op("pe", lambda: nc.tensor.transpose(out=tpq[:, 0:64], in_=R.QT[0:64, b * 128:(b + 1) * 128], identity=R.ident[0:64, 0:64]),
              reads=[R.QT_b, R.ident_b], writes=[R.ps_b[0]])
        kb.op("pe", lambda: nc.tensor.transpose(out=tpk[:, 0:64], in_=R.KT[0:64, b * 128:(b + 1) * 128], identity=R.ident[0:64, 0:64]),
              reads=[R.KT_b, R.ident_b], writes=[R.ps_b[0]])
        kb.op("dve", lambda: nc.vector.tensor_scalar(out=tok[:, 0, :], in0=tpq[:, 0:64], scalar1=G[:, 4, b:b + 1], scalar2=None, op0=ALU.mult),
              reads=[R.ps_b[0], B.G_b], writes=[tokb])
        kb.op("dve", lambda: nc.vector.tensor_scalar(out=tok[:, 1, :], in0=tpk[:, 0:64], scalar1=G[:, 5, b:b + 1], scalar2=None, op0=ALU.mult),
              reads=[R.ps_b[0], B.G_b], writes=[tokb])
        kb.op("dve", lambda: nc.vector.tensor_scalar(out=tok[:, 2, :], in0=tpk[:, 0:64], scalar1=G[:, 6, b:b + 1], scalar2=None, op0=ALU.mult),
              reads=[R.ps_b[0], B.G_b], writes=[tokb])
        tpB = R.ps[1][:, :].bitcast(BF16); tq2 = tpB[:, 0:128]; tk2 = tpB[:, 128:256]
        kb.op("pe", lambda: nc.tensor.transpose(out=tq2[0:64, :], in_=tok[:, 0, :], identity=R.ident[:, :]),
              reads=[tokb, R.ident_b], writes=[R.ps_b[1]])
        kb.op("pe", lambda: nc.tensor.transpose(out=tk2[0:64, :], in_=tok[:, 1, :], identity=R.ident[:, :]),
              reads=[tokb, R.ident_b], writes=[R.ps_b[1]])
        kb.op("act", lambda: nc.scalar.copy(out=qkT[:, 0, :], in_=tq2[0:64, :]), reads=[R.ps_b[1]], writes=[qkTb])
        kb.op("act", lambda: nc.scalar.copy(out=qkT[:, 1, :], in_=tk2[0:64, :]), reads=[R.ps_b[1]], writes=[qkTb])
        kb.op("pe", lambda: nc.tensor.matmul(R.ps[PS_][:, 0:128], lhsT=qkT[:, 1, :], rhs=qkT[:, 0, :], start=True, stop=True),
              reads=[qkTb], writes=[R.ps_b[PS_]])
        qkm, qkmb = B.qkm[i2], B.qkm_b[i2]
        kb.op("dve", lambda: nc.vector.tensor_tensor(out=qkm[:, :], in0=R.ps[PS_][:, 0:128], in1=B.cm[:, :], op=ALU.mult),
              reads=[R.ps_b[PS_], B.cm_b], writes=[qkmb])
        kb.op("pe", lambda: nc.tensor.matmul(R.ps[PU][0:64, 0:65], lhsT=tok[:, 2, :], rhs=R.Va[:, b, 0:65], start=True, stop=True),
              reads=[tokb, R.Va_b], writes=[R.ps_b[PU]])
        Sn, Snb = B.St[1 - i2], B.St_b[1 - i2]
        So, Sob = B.St[i2], B.St_b[i2]
        kb.op("dve", lambda: nc.vector.scalar_tensor_tensor(out=Sn[:, :], in0=So[:, :], scalar=G[0:64, 7, b:b + 1], in1=R.ps[PU][0:64, 0:65],
                                                            op0=ALU.mult, op1=ALU.add),
              reads=[Sob, B.G_b, R.ps_b[PU]], writes=[Snb])
        kb.op("act", lambda: nc.scalar.copy(out=B.Sb[(b + 1) % 3][:, :], in_=Sn[:, :]), reads=[Snb], writes=[B.Sb_b[(b + 1) % 3]])

    def back(b):
        i2 = b % 2
        tok, tokb = B.tok[i2], B.tok_b[i2]
        qkT, qkTb = B.qkT[i2], B.qkT_b[i2]
        qkm, qkmb = B.qkm[i2], B.qkm_b[i2]
        po = PO + (b % 2)
        Sp, Spb = B.Sb[b % 3], B.Sb_b[b % 3]
        po = PO + (b % 2)
        kb.op("pe", lambda: nc.tensor.matmul(R.ps[po][:, 0:65], lhsT=qkm[:, :], rhs=R.Va[:, b, 0:65], start=True, stop=False),
              reads=[qkmb, R.Va_b], writes=[R.ps_b[po]], inc=False)
        kb.op("pe", lambda: nc.tensor.matmul(R.ps[po][:, 0:65], lhsT=qkT[:, 0, :], rhs=Sp[:, :], start=False, stop=True),
              reads=[qkTb, Spb], writes=[R.ps_b[po]])
        hs, hsb = B.hs[i2], B.hs_b[i2]
        hn, hnb = B.hn[i2], B.hn_b[i2]
        kb.op("act", lambda: nc.scalar.activation(out=hs[:, 5:6], in_=R.ps[po][:, 64:65], func=AF.Abs),
              reads=[R.ps_b[po]], writes=[hsb])
        kb.op("dve", lambda: nc.vector.tensor_scalar(out=hs[:, 0:1], in0=hs[:, 5:6], scalar1=1.0, scalar2=None, op0=ALU.max),
              reads=[hsb], writes=[hsb])
        kb.op("dve", lambda: nc.vector.reciprocal(out=hs[:, 1:2], in_=hs[:, 0:1]), reads=[hsb], writes=[hsb])
        kb.op("dve", lambda: nc.vector.tensor_scalar(out=hn[:, :], in0=R.ps[po][:, 0:64], scalar1=hs[:, 1:2], scalar2=None, op0=ALU.mult),
              reads=[R.ps_b[po], hsb], writes=[hnb])
        kb.op("act", lambda: nc.scalar.activation(out=B.jk[:, :], in_=hn[:, :], func=AF.Square, accum_out=hs[:, 2:3]),
              reads=[hnb], writes=[B.jk_b, hsb])
        kb.op("act", lambda: nc.scalar.activation(out=hs[:, 3:4], in_=hs[:, 2:3], func=AF.Sqrt, bias=EPS, scale=1.0 / 64),
              reads=[hsb], writes=[hsb])
        kb.op("dve", lambda: nc.vector.reciprocal(out=hs[:, 4:5], in_=hs[:, 3:4]), reads=[hsb], writes=[hsb])
        kb.op("dve", lambda: nc.vector.scalar_tensor_tensor(out=hn[:, :], in0=hn[:, :], scalar=hs[:, 4:5], in1=B.gob[:, :],
                                                            op0=ALU.mult, op1=ALU.mult),
              reads=[hnb, hsb, B.gob_b], writes=[hnb])
        yb, ybb = B.yb[i2], B.yb_b[i2]
        kb.op("dve", lambda: nc.vector.tensor_tensor(out=yb[:, :], in0=hn[:, :], in1=B.Osig[:, b, :], op=ALU.mult),
              reads=[hnb, B.Osig_b], writes=[ybb])
        ty = R.ps[5][:, :].bitcast(BF16)[:, 0:128]
        kb.op("pe", lambda: nc.tensor.transpose(out=ty[0:64, :], in_=yb[:, :], identity=R.ident[:, :]),
              reads=[ybb, R.ident_b], writes=[R.ps_b[5]])
        qt = b // 4
        if b % 4 == 0:
            R.cur_yo = R.nyo % 2; R.nyo += 1
        yo, yob = R.yo[R.cur_yo], R.yo_b[R.cur_yo]
        kb.op("act", lambda: nc.scalar.copy(out=yo[:, (b % 4) * 128:(b % 4 + 1) * 128], in_=ty[0:64, :]),
              reads=[R.ps_b[5]], writes=[yob])
        if b % 4 == 3:
            kb.dma("pool", ydst(yT_d, row0, qt), yo[:, :], reads=[yob])

    front(0)
    for b in range(NB):
        if b + 1 < NB:
            front(b + 1)
        back(b)


def mix_params(kb, R, praw_d, clam_d):
    nc = kb.nc
    pr = R.par
    kb.dma("sp", pr[0:64, 16:24], praw_d, writes=[R.par_b])
    cl = R.rr[0][:, 0:128].rearrange("p (a b) -> p a b", a=4)
    kb.dma("sp", cl, clam_d, writes=[R.rr_b[0]])
    V = nc.vector
    kb.op("dve", lambda: V.memset(pr[0:64, 13:14], EPS), writes=[R.par_b])
    kb.op("dve", lambda: V.tensor_scalar(out=pr[0:64, 0:1], in0=pr[0:64, 16:17], scalar1=32 ** -0.5, scalar2=None, op0=ALU.mult), reads=[R.par_b], writes=[R.par_b])
    kb.op("dve", lambda: V.tensor_copy(out=pr[0:64, 1:2], in_=pr[0:64, 17:18]), reads=[R.par_b], writes=[R.par_b])
    kb.op("dve", lambda: V.tensor_tensor(out=pr[0:64, 3:4], in0=pr[0:64, 18:19], in1=pr[0:64, 22:23], op=ALU.mult), reads=[R.par_b], writes=[R.par_b])
    kb.op("dve", lambda: V.tensor_scalar(out=pr[0:64, 4:5], in0=pr[0:64, 19:20], scalar1=0.125, scalar2=None, op0=ALU.mult), reads=[R.par_b], writes=[R.par_b])
    kb.op("dve", lambda: V.tensor_copy(out=pr[0:64, 5:6], in_=pr[0:64, 20:21]), reads=[R.par_b], writes=[R.par_b])
    pp = R.rr[1][:, 0:64].rearrange("p (a b) -> p a b", a=2)
    kb.op("dve", lambda: V.tensor_tensor(out=pp[:, 0, :], in0=cl[:, 0, :], in1=cl[:, 1, :], op=ALU.mult), reads=[R.rr_b[0]], writes=[R.rr_b[1]])
    kb.op("dve", lambda: V.tensor_tensor(out=pp[:, 1, :], in0=cl[:, 2, :], in1=cl[:, 3, :], op=ALU.mult), reads=[R.rr_b[0]], writes=[R.rr_b[1]])
    kb.op("dve", lambda: V.reduce_sum(out=pr[0:64, 8:10], in_=pp, axis=AX.X), reads=[R.rr_b[1]], writes=[R.par_b])
    kb.op("act", lambda: nc.scalar.activation(out=pr[0:64, 10:12], in_=pr[0:64, 8:10], func=AF.Exp), reads=[R.par_b], writes=[R.par_b])
    kb.op("dve", lambda: V.tensor_tensor(out=pr[0:64, 12:13], in0=pr[0:64, 11:12], in1=pr[0:64, 10:11], op=ALU.subtract), reads=[R.par_b], writes=[R.par_b])
    kb.op("dve", lambda: V.tensor_tensor(out=pr[0:64, 2:3], in0=pr[0:64, 12:13], in1=pr[0:64, 21:22], op=ALU.subtract), reads=[R.par_b], writes=[R.par_b])

import ml_dtypes
bf16 = ml_dtypes.bfloat16
GW = 256
OFF = dict(aq=0, ak=256, av=512, bqk=768, bv=1280, bo=1536, bi=1792, bf=1796, cq=1800, ck=2056, cv=2312, dq=2568, dk=2824, dv=3080)

def sel_cols(j):
    c = []
    r = lambda o: list(range(o + j * 64, o + j * 64 + 64))
    c += r(OFF['aq']) + r(OFF['ak']) + r(OFF['av'])
    c += r(OFF['bqk']) + r(OFF['bqk'] + 256) + r(OFF['bv']) + r(OFF['bo']) + [OFF['bi'] + j, OFF['bf'] + j]
    c += r(OFF['cq']) + r(OFF['ck']) + r(OFF['cv'])
    c += r(OFF['dq']) + r(OFF['dk']) + r(OFF['dv'])
    return np.array(c)

def const_inputs():
    d = {}
    d['ident'] = np.eye(128, dtype=np.float32).astype(bf16)
    cst = np.zeros((128, 256), np.float32)
    cst[0:64, 0:64] = 1.0
    cst[0:32, 64:96] = 1.0; cst[32:64, 96:128] = 1.0
    d['cst'] = cst.astype(bf16)
    s = np.arange(128)[:, None, None, None]; r = np.arange(4)[None, :, None, None]; t = np.arange(512)[None, None, None, :]
    mc = ((2 * r + (s >= 64)) <= (t // 64)).astype(np.float32)
    d['masks_c'] = np.broadcast_to(mc, (128, 4, 2, 512)).astype(bf16).copy()
    s = np.arange(128)[:, None, None]; r = np.arange(4)[None, :, None]; t = np.arange(512)[None, None, :]
    d['masks_d'] = ((128 * r + s) < t).astype(np.float32).astype(bf16)
    j = np.arange(128)[:, None]; s2 = np.arange(128)[None, :]
    tri = np.zeros((128, 3, 128), np.float32)
    tri[:, 0, :] = (j >= s2); tri[:, 1, :] = (j < s2); tri[:, 2, :] = 1.0
    d['tri'] = tri.astype(bf16)
    trif = np.zeros((128, 2, 128), np.float32)
    trif[:, 0, :] = (j <= s2); trif[:, 1, :] = 1.0
    d['trif'] = trif
    d['cmask'] = (j <= s2).astype(np.float32)
    return d

def bias_index():
    s = np.arange(128)[:, None, None]; r = np.arange(8)[None, :, None]; t = np.arange(512)[None, None, :]
    rel = t - s + 512 - 128 * r
    idx = np.clip(rel, -128, 128) + 128
    dd = t // 64 + 8 - 2 * r - s // 64
    vis = (dd >= 0) & (dd <= 8)
    return idx, vis

_IDX, _VIS = bias_index()

def layer_core_inputs(P, l, j, lam_init=None):
    d = {}
    d['wsel'] = np.ascontiguousarray(P['w_in'][l][:, sel_cols(j)])
    d['gmix'] = np.ascontiguousarray(P['mix_norm'][l].reshape(8, 128).T)
    praw = np.zeros((64, 8), np.float32)
    praw[:, 0] = np.tile(P['c_q_norm'][l], 2); praw[:, 1] = np.tile(P['c_k_norm'][l], 2)
    praw[:, 2] = P['c_out_norm'][l]; praw[:, 3] = P['a_q_norm'][l]; praw[:, 4] = P['a_k_norm'][l]
    if lam_init is None:
        lam_init = 0.8 - 0.6 * np.exp(-0.3 * l)
    praw[:, 5] = lam_init; praw[:, 6] = 1.0 - lam_init
    d['praw'] = praw
    d['clam'] = np.ascontiguousarray(np.broadcast_to(P['c_lambda'][l][None], (64, 4, 32))).astype(np.float32)
    rb = P['a_rel_bias'][l][j]
    d['biasT'] = np.where(_VIS, rb[_IDX], np.float32(-1e30)).astype(np.float32)
    bpar = np.zeros((128, 8), np.float32)
    ch = np.concatenate([np.arange(j * 64, j * 64 + 64), 256 + np.arange(j * 64, j * 64 + 64)])
    bpar[:, 0:4] = P['b_conv_w'][l][:, ch].T
    bpar[:, 4] = P['b_conv_b'][l][ch]
    bpar[:, 5] = P['b_gate_bias'][l][0, j]
    bpar[:, 6] = P['b_gate_bias'][l][1, j]
    d['bpar'] = bpar
    d['gob'] = np.ascontiguousarray(np.broadcast_to(P['b_out_norm'][l][j][None], (128, 64))).astype(np.float32)
    return d


from concourse.bass_utils import run_bass_kernel_spmd

SEQ = 16384
NCORE = 8
TPC = 4096
DEPTH = 2
GROUPS = [[0, 1, 2, 3], [4, 5, 6, 7]]


def _din(nc, name, shape, dt):
    return nc.dram_tensor(name, list(shape), dt, kind="ExternalInput").ap()


def _dout(nc, name, shape, dt):
    return nc.dram_tensor(name, list(shape), dt, kind="ExternalOutput").ap()


def _dint(nc, name, shape, dt):
    return nc.dram_tensor(name, list(shape), dt, kind="Internal").ap()


MIX_IN = dict(wsel=([D, NW], F32), gmix=([128, 8], F32), praw=([64, 8], F32), clam=([64, 4, 32], F32),
              biasT=([128, 8, 512], F32), bpar=([128, 8], F32), gob=([128, 64], F32))
CONST_IN = dict(ident=([128, 128], BF16), cst=([128, 256], BF16), masks_c=([128, 4, 2, 512], BF16),
                masks_d=([128, 4, 512], BF16), tri=([128, 3, 128], BF16), trif=([128, 2, 128], F32), cmask=([128, 128], F32))


def build_fused(S=SEQ, T=TPC):
    nc = bass.Bass("TRN2", target_bir_lowering=False)
    NQr = T // QT_
    x_in = _din(nc, "x_in", [T, D], F32)
    x_out = _dout(nc, "x_out", [T, D], F32)
    Cn = {k: _din(nc, k, sh, dt) for k, (sh, dt) in CONST_IN.items()}
    ffn = {}
    for l in range(DEPTH):
        for f in ("ffn1", "ffn2"):
            ffn[(f, l)] = dict(g=_din(nc, f"{f}_g{l}", [128, 8], F32), wg=_din(nc, f"{f}_wg{l}", [D, DFF], F32),
                               wu=_din(nc, f"{f}_wu{l}", [D, DFF], F32), wd=_din(nc, f"{f}_wd{l}", [DFF, D], F32))
    wo = [_din(nc, f"wo{l}", [D, D], F32) for l in range(DEPTH)]
    mx = [{k: _din(nc, f"{k}{l}", sh, dt) for k, (sh, dt) in MIX_IN.items()} for l in range(DEPTH)]
    xa = _dint(nc, "xa", [T, D], F32); xb = _dint(nc, "xb", [T, D], F32); xc = _dint(nc, "xc", [T, D], F32)
    hT_loc = _dint(nc, "hT_loc", [D, T], BF16)
    hT_all = _dint(nc, "hT_all", [4 * D, T], BF16)
    yT_loc = _dint(nc, "yT_loc", [D, T], BF16)
    yT_all = _dint(nc, "yT_all", [4 * D, T], BF16)
    yT_mine = _dint(nc, "yT_mine", [D, T], BF16)

    hv = hT_all.rearrange("(k r p) t -> k r p t", k=8, r=4)

    def hT_src(qt):
        r, o = qt // NQr, (qt % NQr) * QT_
        return hv[:, r, :, o:o + QT_]

    def y_dst(row0, qt):
        q, o = qt // NQr, (qt % NQr) * QT_
        return yT_loc[q * 256 + row0:q * 256 + row0 + 64, o:o + QT_]

    with ExitStack() as st:
        kb = KB(nc, st)
        pid = nc.sync.partition_id()
        qv = pid % 4

        ymine_b = kb.buf()

        def fetch_mine():
            yv2 = yT_all.rearrange("(q h j p) t -> q h j p t", q=4, h=2, j=4)
            for j in range(4):
                for h in range(2):
                    kb.dma("sp", yT_mine[j * 256 + h * 128:j * 256 + (h + 1) * 128, :],
                           yv2[bass.ds(qv, 1), h, j, :, :].rearrange("o p t -> (o p) t"), writes=[ymine_b])

        def tok_phase(passes):
            with ExitStack() as mem:
                kb.mem = mem
                R = TokRes(kb, any(p.get("wo") is not None for p in passes))
                load_consts(kb, R, Cn["ident"])
                prev_bufs = None
                for k, p in enumerate(passes):
                    w = p["ffn"]
                    load_ffn_weights(kb, R, w["g"], w["wg"], w["wu"], w["wd"], p.get("wo"))
                    ob = [kb.buf() for _ in range(T // TT)] if k + 1 < len(passes) else None
                    has_pre = p.get("wo") is not None
                    token_pass(kb, R, T, p["xi"], p["xo"], pre=(yT_mine if has_pre else None),
                               post=p.get("post"), in_bufs=prev_bufs, out_bufs=ob,
                               pre_bufs=([ymine_b] * (T // TT) if has_pre else None))
                    prev_bufs = ob
                kb.barrier()
            kb.mem = st

        def mix_phase(l):
            with ExitStack() as mem:
                kb.mem = mem
                R = MixRes(kb, S)
                RB = MixResB(kb, R)
                m = mx[l]
                mix_load_common(kb, R, m["wsel"], m["gmix"], Cn["ident"], Cn["cst"])
                mix_params(kb, R, m["praw"], m["clam"])
                mixer_a(kb, R, hT_src, y_dst, 0, m["biasT"])
                kb.barrier()
                mixer_b(kb, R, RB, hT_src, y_dst, 64, m["bpar"], Cn["trif"], Cn["cmask"], m["gob"])
                kb.barrier()
                mixer_c(kb, R, hT_src, y_dst, 128, Cn["masks_c"])
                kb.barrier()
                mixer_d(kb, R, hT_src, y_dst, 192, Cn["masks_d"], Cn["tri"])
                kb.barrier()
            kb.mem = st

        tok_phase([dict(ffn=ffn[("ffn1", 0)], xi=x_in, xo=xa, post=hT_loc)])
        kb.allgather(hT_loc, hT_all, GROUPS)
        mix_phase(0)
        kb.allgather(yT_loc, yT_all, GROUPS)
        fetch_mine()
        tok_phase([dict(ffn=ffn[("ffn2", 0)], wo=wo[0], xi=xa, xo=xb),
                   dict(ffn=ffn[("ffn1", 1)], xi=xb, xo=xc, post=hT_loc)])
        kb.allgather(hT_loc, hT_all, GROUPS)
        mix_phase(1)
        kb.allgather(yT_loc, yT_all, GROUPS)
        fetch_mine()
        tok_phase([dict(ffn=ffn[("ffn2", 1)], wo=wo[1], xi=xc, xo=x_out)])
        kb.finish()
    return nc


def build_mixer_prog(S=SEQ):
    nc = bass.Bass("TRN2", target_bir_lowering=False)
    hT = _din(nc, "hT", [D, S], BF16)
    Cn = {k: _din(nc, k, sh, dt) for k, (sh, dt) in CONST_IN.items()}
    m = {k: _din(nc, k, sh, dt) for k, (sh, dt) in MIX_IN.items()}
    yT = _dout(nc, "yT", [256, S], BF16)
    with ExitStack() as st:
        kb = KB(nc, st)
        R = MixRes(kb, S)
        RB = MixResB(kb, R)
        mix_load_common(kb, R, m["wsel"], m["gmix"], Cn["ident"], Cn["cst"])
        mix_params(kb, R, m["praw"], m["clam"])
        mixer_a(kb, R, hT, yT, 0, m["biasT"])
        kb.barrier()
        mixer_b(kb, R, RB, hT, yT, 64, m["bpar"], Cn["trif"], Cn["cmask"], m["gob"])
        kb.barrier()
        mixer_c(kb, R, hT, yT, 128, Cn["masks_c"])
        kb.barrier()
        mixer_d(kb, R, hT, yT, 192, Cn["masks_d"], Cn["tri"])
        kb.finish()
    return nc


def _lay(g):
    return np.ascontiguousarray(np.asarray(g, np.float32).reshape(8, 128).T)


def _wo_perm(w_out):
    idx = np.arange(1024).reshape(4, 4, 64)
    perm = idx.transpose(1, 0, 2).reshape(-1)
    return np.ascontiguousarray(w_out[perm, :])


def make_in_maps(P, TPC=TPC):
    x = np.ascontiguousarray(P["x"], dtype=np.float32).reshape(-1, D)
    C = const_inputs()
    shared = dict(C)
    for l in range(DEPTH):
        for f in ("ffn1", "ffn2"):
            shared[f"{f}_g{l}"] = _lay(P[f + "_norm"][l])
            shared[f"{f}_wg{l}"] = np.ascontiguousarray(P[f + "_wg"][l], dtype=np.float32)
            shared[f"{f}_wu{l}"] = np.ascontiguousarray(P[f + "_wu"][l], dtype=np.float32)
            shared[f"{f}_wd{l}"] = np.ascontiguousarray(P[f + "_wd"][l], dtype=np.float32)
        shared[f"wo{l}"] = _wo_perm(np.asarray(P["w_out"][l], np.float32))
    ims = []
    for c in range(NCORE):
        d = dict(shared)
        d["x_in"] = x[c * TPC:(c + 1) * TPC]
        j = c % 4
        for l in range(DEPTH):
            for k, v in layer_core_inputs(P, l, j).items():
                d[f"{k}{l}"] = v
        ims.append(d)
    return ims


def kernel(**inputs):
    P = {k: np.asarray(v) for k, v in inputs.items()}
    nc = build_fused()
    ims = make_in_maps(P)
    res = run_bass_kernel_spmd(nc, ims, core_ids=list(range(NCORE)))
    out = np.concatenate([r["x_out"] for r in res.results], axis=0).reshape(2, SEQ, D).astype(np.float32)
    return out
```

```python
import numpy as np
from contextlib import ExitStack
import concourse.bass as bass
import concourse.mybir as mybir

F32 = mybir.dt.float32
BF16 = mybir.dt.bfloat16
AF = mybir.ActivationFunctionType
ALU = mybir.AluOpType
AX = mybir.AxisListType

EPOCH = 4096


class Buf:
    __slots__ = ("w", "r", "name")

    def __init__(self, name=""):
        self.w = None
        self.r = {}
        self.name = name


class KB:
    def __init__(self, nc, stack):
        self.nc = nc
        self.st = stack
        self.E = {"pe": nc.tensor, "act": nc.scalar, "dve": nc.vector, "pool": nc.gpsimd, "sp": nc.sync}
        self.cnt = {e: 0 for e in self.E}
        self.sems = {e: [] for e in self.E}
        self.waited = {e: {} for e in self.E}
        self.ndma = 12
        self.dma_sems = {}
        self.dma_cnt = {}
        self.dma_rr = {}
        self.nsem = 0
        self.uid = 0
        self.mem = stack

    def sem(self, name):
        self.nsem += 1
        return self.st.enter_context(self.nc.semaphore(name))

    def sbuf(self, name, shape, dt):
        self.uid += 1
        return self.mem.enter_context(self.nc.sbuf_tensor(f"sb{self.uid}_" + name, list(shape), dt))

    def psum(self, name, shape, dt):
        self.uid += 1
        return self.mem.enter_context(self.nc.psum_tensor(f"ps{self.uid}_" + name, list(shape), dt))

    def buf(self, name=""):
        return Buf(name)

    def _esem(self, e, n):
        ep = (n - 1) // EPOCH
        while len(self.sems[e]) <= ep:
            self.sems[e].append(self.sem(f"c_{e}_{len(self.sems[e])}"))
        return self.sems[e][ep], (n - 1) % EPOCH + 1

    def _wait(self, e, ev):
        if ev[0] == "e":
            _, src, n = ev
            if src == e and e == "pe":
                return
            key = ("e", src)
            if self.waited[e].get(key, 0) >= n:
                return
            if src == e and n > self.cnt[e]:
                raise RuntimeError("self-wait on future event")
            s, v = self._esem(src, n)
            self.E[e].wait_ge(s, v)
            self.waited[e][key] = n
        else:
            _, q, i, k = ev
            key = ("d", q, i)
            if self.waited[e].get(key, 0) >= k:
                return
            self.E[e].wait_ge(self.dma_sems[q][i], 16 * k)
            self.waited[e][key] = k

    @staticmethod
    def _evkey(ev):
        return (ev[0], ev[1]) if ev[0] == "e" else (ev[0], ev[1], ev[2])

    def _collect(self, reads, writes):
        deps = []
        for b in reads:
            if b.w is not None:
                deps.append(b.w)
        for b in writes:
            if b.w is not None:
                deps.append(b.w)
            deps.extend(b.r.values())
        return deps

    def _record(self, ev, reads, writes):
        k = self._evkey(ev)
        for b in reads:
            b.r[k] = ev
        for b in writes:
            b.w = ev
            b.r = {}

    def op(self, e, fn, reads=(), writes=(), inc=True):
        for ev in self._collect(reads, writes):
            self._wait(e, ev)
        ins = fn()
        if inc:
            self.cnt[e] += 1
            s, v = self._esem(e, self.cnt[e])
            ins.then_inc(s, 1)
            ev = ("e", e, self.cnt[e])
        else:
            ev = ("e", e, self.cnt[e] + 1)
        self._record(ev, reads, writes)
        return ins

    def dma(self, q, out, in_, reads=(), writes=(), **kw):
        for ev in self._collect(reads, writes):
            self._wait(q, ev)
        if q not in self.dma_sems:
            self.dma_sems[q] = [self.sem(f"d_{q}_{i}") for i in range(self.ndma)]
            self.dma_cnt[q] = [0] * self.ndma
            self.dma_rr[q] = 0
        i = self.dma_rr[q]
        self.dma_rr[q] = (i + 1) % self.ndma
        if self.dma_cnt[q][i] > 0:
            self._wait(q, ("d", q, i, self.dma_cnt[q][i]))
        self.dma_cnt[q][i] += 1
        ins = self.E[q].dma_start(out=out, in_=in_, **kw)
        ins.then_inc(self.dma_sems[q][i], 16)
        ev = ("d", q, i, self.dma_cnt[q][i])
        self._record(ev, reads, writes)
        return ins

    def barrier(self, extra_sems=()):
        for e in self.E:
            for q in self.dma_sems:
                for i in range(self.ndma):
                    if self.dma_cnt[q][i] > 0:
                        self._wait(e, ("d", q, i, self.dma_cnt[q][i]))
            for src in ("pe", "act", "dve", "pool"):
                if self.cnt[src] > 0 and not (src == e and e == "pe"):
                    self._wait(e, ("e", src, self.cnt[src]))
            for (sm, v) in extra_sems:
                self.E[e].wait_ge(sm, v)

    def allgather(self, src2d, dst2d, groups, chunk_rows=128):
        self.barrier()
        R_ = src2d.shape[0]
        nk = R_ // chunk_rows
        ng = len(groups[0])
        if not hasattr(self, "cc_sem"):
            self.cc_sem = self.sem("ccsem")
            self.cc_cnt = 0
        for k in range(nk):
            self.nc.gpsimd.collective_compute("AllGather", ALU.bypass, replica_groups=groups,
                                              ins=[src2d[k * chunk_rows:(k + 1) * chunk_rows, :]],
                                              outs=[dst2d[k * ng * chunk_rows:(k + 1) * ng * chunk_rows, :]]).then_inc(self.cc_sem, 1)
            self.cc_cnt += 1
        for e in self.E:
            self.E[e].wait_ge(self.cc_sem, self.cc_cnt)

    def finish(self):
        for q in self.dma_sems:
            for i in range(self.ndma):
                if self.dma_cnt[q][i] > 0:
                    self._wait("sp", ("d", q, i, self.dma_cnt[q][i]))
        for e in ("pe", "act", "dve", "pool"):
            if self.cnt[e] > 0:
                self._wait("sp", ("e", e, self.cnt[e]))


D = 1024
DFF = 2816
NFC = DFF // 128
TT = 256
SUB = TT // 128
EPS = 1e-6


class TokRes:
    def __init__(self, kb, with_pre):
        nc = kb.nc
        self.kb = kb
        self.Wg = kb.sbuf("Wg", [128, 8, DFF], BF16); self.Wg_b = kb.buf()
        self.Wu = kb.sbuf("Wu", [128, 8, DFF], BF16); self.Wu_b = kb.buf()
        self.Wd = kb.sbuf("Wd", [128, NFC, D], BF16); self.Wd_b = kb.buf()
        self.stage = [kb.sbuf(f"stage{i}", [128, 1024], F32) for i in range(2)]
        self.stage_b = [kb.buf() for _ in range(2)]
        self.gt = kb.sbuf("gt", [128, 8], F32); self.gt_b = kb.buf()
        self.ident = kb.sbuf("ident", [128, 128], BF16); self.ident_b = kb.buf()
        self.xt = [kb.sbuf(f"xt{i}", [128, SUB, D], F32) for i in range(2)]
        self.xt_b = [kb.buf() for _ in range(2)]
        self.xn = kb.sbuf("xn", [128, D], BF16); self.xn_b = kb.buf()
        self.st = kb.sbuf("stat", [128, 8], F32); self.st_b = kb.buf()
        self.xnT = [kb.sbuf(f"xnT{i}", [128, 8, TT], BF16) for i in range(2)]
        self.xnT_b = [kb.buf() for _ in range(2)]
        self.hid = kb.sbuf("hid", [128, NFC, TT], BF16)
        self.hid_b = [kb.buf() for _ in range(NFC)]
        self.sg = [kb.sbuf(f"sg{i}", [128, TT], F32) for i in range(2)]
        self.sg_b = [kb.buf() for _ in range(2)]
        self.hTo = kb.sbuf("hTo", [128, 8, TT], BF16); self.hTo_b = kb.buf()
        self.with_pre = with_pre
        if with_pre:
            self.Wo = kb.sbuf("Wo", [128, 8, D], BF16); self.Wo_b = kb.buf()
            self.yt = [kb.sbuf(f"yt{i}", [128, 8, TT], BF16) for i in range(2)]
            self.yt_b = [kb.buf() for _ in range(2)]
        self.psg = [kb.psum(f"psg{i}", [128, 512], F32) for i in range(2)]
        self.psg_b = [kb.buf() for _ in range(2)]
        self.psd = [kb.psum(f"psd{i}", [128, 512], F32) for i in range(2)]
        self.psd_b = [kb.buf() for _ in range(2)]
        self.tp = [kb.psum(f"tp{i}", [128, 8, 128], BF16) for i in range(2)]
        self.tp_b = [kb.buf() for _ in range(2)]
        self.ntp = 0
        self.npsd = 0
        self.ncast = 0


def load_consts(kb, R, ident_d):
    kb.dma("sp", R.ident[:], ident_d, writes=[R.ident_b])


def load_ffn_weights(kb, R, g_lay, wg, wu, wd, w_out=None):
    nc = kb.nc
    kb.dma("sp", R.gt[:], g_lay, writes=[R.gt_b])

    def cast(dst_ap, dst_b, src_ap, src_b, scal):
        e = ("dve", "act", "dve", "act", "pool")[R.ncast % 5]
        R.ncast += 1
        E = kb.E[e]
        if e == "act":
            if scal is None:
                kb.op(e, lambda: E.copy(out=dst_ap, in_=src_ap), reads=[src_b], writes=[dst_b])
            else:
                kb.op(e, lambda: E.activation(out=dst_ap, in_=src_ap, func=AF.Copy, scale=scal),
                      reads=[src_b, R.gt_b], writes=[dst_b])
        elif scal is None:
            kb.op(e, lambda: E.tensor_copy(out=dst_ap, in_=src_ap), reads=[src_b], writes=[dst_b])
        else:
            kb.op(e, lambda: E.tensor_scalar(out=dst_ap, in0=src_ap, scalar1=scal, scalar2=None, op0=ALU.mult),
                  reads=[src_b, R.gt_b], writes=[dst_b])

    k = 0
    for (W, Wb, src) in ((R.Wg, R.Wg_b, wg), (R.Wu, R.Wu_b, wu)):
        for kc in range(8):
            for (c0, c1) in ((0, 1024), (1024, 2048), (2048, DFF)):
                sb = k % 2; k += 1
                kb.dma("sp", R.stage[sb][:, 0:c1 - c0], src[kc * 128:(kc + 1) * 128, c0:c1],
                       writes=[R.stage_b[sb]])
                cast(W[:, kc, c0:c1], Wb, R.stage[sb][:, 0:c1 - c0], R.stage_b[sb], R.gt[:, kc:kc + 1])
    for fc in range(NFC):
        sb = k % 2; k += 1
        kb.dma("sp", R.stage[sb][:, 0:D], wd[fc * 128:(fc + 1) * 128, :], writes=[R.stage_b[sb]])
        cast(R.Wd[:, fc, :], R.Wd_b, R.stage[sb][:, 0:D], R.stage_b[sb], None)
    if w_out is not None:
        for kc in range(8):
            sb = k % 2; k += 1
            kb.dma("sp", R.stage[sb][:, 0:D], w_out[kc * 128:(kc + 1) * 128, :], writes=[R.stage_b[sb]])
            cast(R.Wo[:, kc, :], R.Wo_b, R.stage[sb][:, 0:D], R.stage_b[sb], None)


def norm_transpose(kb, R, x_ap, x_b, dstT, dstT_b, s):
    nc = kb.nc
    ss = R.st[:, 0:1]; rs = R.st[:, 1:2]; rstd = R.st[:, 2:3]
    kb.op("act", lambda: nc.scalar.activation(out=R.xn[:], in_=x_ap, func=AF.Square, accum_out=ss),
          reads=[x_b], writes=[R.xn_b, R.st_b])
    kb.op("act", lambda: nc.scalar.activation(out=rs, in_=ss, func=AF.Sqrt, bias=EPS, scale=1.0 / D),
          reads=[R.st_b], writes=[R.st_b])
    kb.op("dve", lambda: nc.vector.reciprocal(out=rstd, in_=rs), reads=[R.st_b], writes=[R.st_b])
    kb.op("dve", lambda: nc.vector.tensor_scalar(out=R.xn[:], in0=x_ap, scalar1=rstd, scalar2=None, op0=ALU.mult),
          reads=[x_b, R.st_b], writes=[R.xn_b])
    ti = R.ntp % 2; R.ntp += 1
    tp = R.tp[ti]; tpb = R.tp_b[ti]
    for kc in range(8):
        kb.op("pe", lambda kc=kc: nc.tensor.transpose(out=tp[:, kc, :], in_=R.xn[:, kc * 128:(kc + 1) * 128],
                                                      identity=R.ident[:]),
              reads=[R.xn_b, R.ident_b], writes=[tpb], inc=(kc == 7))
    kb.op("act", lambda: nc.scalar.copy(out=dstT[:, :, s * 128:(s + 1) * 128], in_=tp[:, :, :]),
          reads=[tpb], writes=[dstT_b])


def token_pass(kb, R, T, x_in, x_out, pre=None, post=None, in_bufs=None, out_bufs=None, pre_bufs=None, post_bufs=None):
    nc = kb.nc
    NT = T // TT

    def stage_load(i):
        bi = i % 2
        kb.dma("sp", R.xt[bi][:, :, :], x_in[i * TT:(i + 1) * TT, :].rearrange("(s p) d -> p s d", p=128),
               reads=([in_bufs[i]] if in_bufs else []), writes=[R.xt_b[bi]])
        if pre is not None:
            kb.dma("sp", R.yt[bi][:, :, :], pre[:, i * TT:(i + 1) * TT].rearrange("(c p) t -> p c t", p=128),
                   reads=([pre_bufs[i]] if pre_bufs else []), writes=[R.yt_b[bi]])

    def stage_pre(i):
        bi = i % 2
        if pre is None:
            return
        for s in range(SUB):
            for h in range(2):
                pi = R.npsd % 2; R.npsd += 1
                for kc in range(8):
                    kb.op("pe", lambda kc=kc: nc.tensor.matmul(R.psd[pi][:, :], lhsT=R.yt[bi][:, kc, s * 128:(s + 1) * 128],
                                                               rhs=R.Wo[:, kc, h * 512:(h + 1) * 512],
                                                               start=(kc == 0), stop=(kc == 7)),
                          reads=[R.yt_b[bi], R.Wo_b], writes=[R.psd_b[pi]], inc=(kc == 7))
                xs = R.xt[bi][:, s, h * 512:(h + 1) * 512]
                kb.op("dve", lambda: nc.vector.tensor_tensor(out=xs, in0=R.psd[pi][:, :], in1=xs, op=ALU.add),
                      reads=[R.psd_b[pi], R.xt_b[bi]], writes=[R.xt_b[bi]])

    def stage_a(i):
        bi = i % 2
        for s in range(SUB):
            norm_transpose(kb, R, R.xt[bi][:, s, :], R.xt_b[bi], R.xnT[bi], R.xnT_b[bi], s)

    def stage_b(i):
        bi = i % 2
        for fc in range(NFC):
            gi = fc % 2
            for (W, Wb, off) in ((R.Wg, R.Wg_b, 0), (R.Wu, R.Wu_b, 256)):
                for kc in range(8):
                    kb.op("pe", lambda kc=kc, W=W, off=off: nc.tensor.matmul(
                        R.psg[gi][:, off:off + TT], lhsT=W[:, kc, fc * 128:(fc + 1) * 128], rhs=R.xnT[bi][:, kc, :],
                        start=(kc == 0), stop=(kc == 7)),
                          reads=[Wb, R.xnT_b[bi]], writes=[R.psg_b[gi]], inc=(kc == 7))
            kb.op("act", lambda: nc.scalar.activation(out=R.sg[gi][:, :], in_=R.psg[gi][:, 0:TT], func=AF.Silu),
                  reads=[R.psg_b[gi]], writes=[R.sg_b[gi]])
            kb.op("dve", lambda: nc.vector.tensor_tensor(out=R.hid[:, fc, :], in0=R.sg[gi][:, :],
                                                         in1=R.psg[gi][:, 256:256 + TT], op=ALU.mult),
                  reads=[R.sg_b[gi], R.psg_b[gi]], writes=[R.hid_b[fc]])

    def stage_c(i):
        bi = i % 2
        for s in range(SUB):
            for h in range(2):
                pi = R.npsd % 2; R.npsd += 1
                for fc in range(NFC):
                    kb.op("pe", lambda fc=fc: nc.tensor.matmul(R.psd[pi][:, :], lhsT=R.hid[:, fc, s * 128:(s + 1) * 128],
                                                               rhs=R.Wd[:, fc, h * 512:(h + 1) * 512],
                                                               start=(fc == 0), stop=(fc == NFC - 1)),
                          reads=[R.hid_b[fc], R.Wd_b], writes=[R.psd_b[pi]], inc=(fc == NFC - 1))
                xs = R.xt[bi][:, s, h * 512:(h + 1) * 512]
                kb.op("dve", lambda: nc.vector.scalar_tensor_tensor(out=xs, in0=R.psd[pi][:, :], scalar=0.5, in1=xs,
                                                                    op0=ALU.mult, op1=ALU.add),
                      reads=[R.psd_b[pi], R.xt_b[bi]], writes=[R.xt_b[bi]])
            if post is not None:
                norm_transpose(kb, R, R.xt[bi][:, s, :], R.xt_b[bi], R.hTo, R.hTo_b, s)
        kb.dma("pool", x_out[i * TT:(i + 1) * TT, :].rearrange("(s p) d -> p s d", p=128), R.xt[bi][:, :, :],
               reads=[R.xt_b[bi]], writes=([out_bufs[i]] if out_bufs else []))
        if post is not None:
            kb.dma("pool", post[:, i * TT:(i + 1) * TT].rearrange("(c p) t -> p c t", p=128), R.hTo[:, :, :],
                   reads=[R.hTo_b], writes=([post_bufs[i]] if post_bufs else []))

    stage_load(0)
    stage_pre(0)
    stage_a(0)
    for i in range(NT):
        if i + 1 < NT:
            stage_load(i + 1)
        stage_b(i)
        if i + 1 < NT:
            stage_pre(i + 1)
            stage_a(i + 1)
        stage_c(i)


QT_ = 512
NW = 834
A_Q, A_K, A_V = 0, 64, 128
B_Q, B_K, B_V, B_O, B_I, B_F = 192, 256, 320, 384, 448, 449
C_Q, C_K, C_V = 450, 514, 578
D_Q, D_K, D_V = 642, 706, 770


class MixRes:
    def __init__(self, kb, S):
        self.S = S
        self.NB = S // 128
        self.NQ = S // QT_
        NB = self.NB
        self.Wm = kb.sbuf("Wm", [128, 8, NW], BF16); self.Wm_b = kb.buf()
        self.gm = kb.sbuf("gm", [128, 8], F32); self.gm_b = kb.buf()
        self.ident = kb.sbuf("identm", [128, 128], BF16); self.ident_b = kb.buf()
        self.ht = [kb.sbuf(f"ht{i}", [128, 8, QT_], BF16) for i in range(2)]; self.ht_b = [kb.buf() for _ in range(2)]
        self.QT = kb.sbuf("QT", [128, S], BF16); self.QT_b = kb.buf()
        self.KT = kb.sbuf("KT", [128, S], BF16); self.KT_b = kb.buf()
        self.Va = kb.sbuf("Va", [128, NB, 128], BF16); self.Va_b = kb.buf()
        self.par = kb.sbuf("par", [128, 32], F32); self.par_b = kb.buf()
        self.cst = kb.sbuf("cst", [128, 256], BF16); self.cst_b = kb.buf()
        self.sq = [kb.sbuf(f"sq{i}", [64, QT_], BF16) for i in range(2)]; self.sq_b = [kb.buf() for _ in range(2)]
        self.rr = [kb.sbuf(f"rr{i}", [64, QT_], F32) for i in range(2)]; self.rr_b = [kb.buf() for _ in range(2)]
        self.e32 = [kb.sbuf(f"e32_{i}", [128, 2, QT_], F32) for i in range(2)]; self.e32_b = [kb.buf() for _ in range(2)]
        self.wst = [self.e32[i][:, :, :].rearrange("p a b -> p (a b)")[:, 0:NW] for i in range(2)]; self.wst_b = self.e32_b
        self.e32c = kb.sbuf("e32_c", [128, 2, QT_], F32)
        self.e16 = [kb.sbuf(f"e16_{i}", [128, 2, QT_], BF16) for i in range(4)]; self.e16_b = [kb.buf() for _ in range(4)]
        self.spp = [kb.sbuf(f"spp_{i}", [128, 2, QT_], BF16) for i in range(2)]; self.spp_b = [kb.buf() for _ in range(2)]
        self.fin = [kb.sbuf(f"fin{i}", [64, QT_], F32) for i in range(3)]; self.fin_b = [kb.buf() for _ in range(3)]
        self.yo = [kb.sbuf(f"yo{i}", [64, QT_], BF16) for i in range(2)]; self.yo_b = [kb.buf() for _ in range(2)]
        self.mask = kb.sbuf("mask", [128, 4, QT_], BF16); self.mask_b = kb.buf()
        self.EB = kb.sbuf("EB", [128, 8, QT_], F32); self.EB_b = kb.buf()
        self.tri = kb.sbuf("tri", [128, 3, 128], BF16); self.tri_b = kb.buf()
        self.pp = [kb.psum(f"pp{i}", [128, 2, 512], F32) for i in range(4)]
        self.ps = [self.pp[i // 2][:, i % 2, :] for i in range(8)]
        self.ps_b = [kb.buf() for _ in range(8)]
        self.nyo = 0
        self.ne16 = 0


def mix_load_common(kb, R, wsel, gmix_lay, ident_d, cst_d):
    nc = kb.nc
    kb.dma("sp", R.gm[:], gmix_lay, writes=[R.gm_b])
    kb.dma("sp", R.ident[:], ident_d, writes=[R.ident_b])
    kb.dma("sp", R.cst[:], cst_d, writes=[R.cst_b])
    for kc in range(8):
        sb = kc % 2
        kb.dma("sp", R.wst[sb][:, :], wsel[kc * 128:(kc + 1) * 128, :], writes=[R.wst_b[sb]])
        kb.op("dve", lambda: nc.vector.tensor_scalar(out=R.Wm[:, kc, :], in0=R.wst[sb][:, :], scalar1=R.gm[:, kc:kc + 1],
                                                     scalar2=None, op0=ALU.mult),
              reads=[R.wst_b[sb], R.gm_b], writes=[R.Wm_b])


def load_ht(kb, R, hT, qt):
    bi = qt % 2
    if callable(hT):
        src = hT(qt).rearrange("c p t -> p c t")
    else:
        src = hT[:, qt * QT_:(qt + 1) * QT_].rearrange("(c p) t -> p c t", p=128)
    kb.dma("sp", R.ht[bi][:, :, :], src, writes=[R.ht_b[bi]])
    return R.ht[bi], R.ht_b[bi]


def proj_fm(kb, R, ht, ht_b, c0, ncol, pb):
    nc = kb.nc
    for kc in range(8):
        kb.op("pe", lambda kc=kc: nc.tensor.matmul(R.ps[pb][0:ncol, :], lhsT=R.Wm[:, kc, c0:c0 + ncol], rhs=ht[:, kc, :],
                                                   start=(kc == 0), stop=(kc == 7)),
              reads=[R.Wm_b, ht_b], writes=[R.ps_b[pb]], inc=(kc == 7))


def proj_tm(kb, R, ht, ht_b, c0, ncol, pb, s):
    nc = kb.nc
    for kc in range(8):
        kb.op("pe", lambda kc=kc: nc.tensor.matmul(R.ps[pb][:, s * 128:s * 128 + ncol], lhsT=ht[:, kc, s * 128:(s + 1) * 128],
                                                   rhs=R.Wm[:, kc, c0:c0 + ncol], start=(kc == 0), stop=(kc == 7)),
              reads=[R.Wm_b, ht_b], writes=[R.ps_b[pb]], inc=(kc == 7))


def qk_norm_store(kb, R, pb, pb2, dst, dst_b, qt, gcol, cmat, inv_n, i2):
    nc = kb.nc
    sq, sqb = R.sq[i2], R.sq_b[i2]
    rr, rrb = R.rr[i2], R.rr_b[i2]
    kb.op("act", lambda: nc.scalar.activation(out=sq[:, :], in_=R.ps[pb][0:64, :], func=AF.Square),
          reads=[R.ps_b[pb]], writes=[sqb])
    kb.op("pe", lambda: nc.tensor.matmul(R.ps[pb2][0:64, :], lhsT=cmat, rhs=sq[:, :], start=True, stop=True),
          reads=[sqb, R.cst_b], writes=[R.ps_b[pb2]])
    kb.op("act", lambda: nc.scalar.activation(out=rr[:, :], in_=R.ps[pb2][0:64, :], func=AF.Ln, bias=R.par[0:64, 13:14], scale=inv_n),
          reads=[R.ps_b[pb2], R.par_b], writes=[rrb])
    kb.op("act", lambda: nc.scalar.activation(out=rr[:, :], in_=rr[:, :], func=AF.Exp, scale=-0.5), reads=[rrb], writes=[rrb])
    kb.op("dve", lambda: nc.vector.scalar_tensor_tensor(out=dst[0:64, qt * QT_:(qt + 1) * QT_], in0=R.ps[pb][0:64, :],
                                                        scalar=R.par[0:64, gcol:gcol + 1], in1=rr[:, :],
                                                        op0=ALU.mult, op1=ALU.mult),
          reads=[R.ps_b[pb], rrb, R.par_b], writes=[dst_b])


def v_store(kb, R, ht, ht_b, c0, qt, pb):
    nc = kb.nc
    for s in range(4):
        proj_tm(kb, R, ht, ht_b, c0, 64, pb, s)
    src = R.ps[pb][:, :].rearrange("p (s c) -> p s c", c=128)[:, :, 0:64]
    kb.op("act", lambda: nc.scalar.copy(out=R.Va[:, qt * 4:(qt + 1) * 4, 0:64], in_=src),
          reads=[R.ps_b[pb]], writes=[R.Va_b])


def ydst(yT_d, row0, qt):
    if callable(yT_d):
        return yT_d(row0, qt)
    return yT_d[row0:row0 + 64, qt * QT_:(qt + 1) * QT_]


def out_store(kb, R, yT_d, row0, qt, src_fn, reads):
    i = R.nyo % 2; R.nyo += 1
    src_fn(R.yo[i], R.yo_b[i])
    kb.dma("pool", ydst(yT_d, row0, qt), R.yo[i][:, :], reads=[R.yo_b[i]])


def mixer_c(kb, R, hT, yT_d, row0, masks_c):
    nc = kb.nc
    S, NB, NQ = R.S, R.NB, R.NQ
    kb.dma("sp", R.mask[:, :, :], masks_c[:, :, 0, :], writes=[R.mask_b])
    kb.op("pool", lambda: nc.gpsimd.memset(R.Va[:, :, 64:128], 1.0), writes=[R.Va_b])
    bd32 = R.cst[0:64, 64:128]
    ones64 = R.cst[0:64, 0:64]
    for qt in range(NQ):
        ht, htb = load_ht(kb, R, hT, qt)
        proj_fm(kb, R, ht, htb, C_Q, 64, 0)
        proj_fm(kb, R, ht, htb, C_K, 64, 2)
        qk_norm_store(kb, R, 0, 1, R.QT, R.QT_b, qt, 0, bd32, 1.0 / 32, 0)
        qk_norm_store(kb, R, 2, 3, R.KT, R.KT_b, qt, 1, bd32, 1.0 / 32, 1)
        v_store(kb, R, ht, htb, C_V, qt, 4 + (qt % 2))
    for qt in range(NQ):
        nkb = 4 * qt + 4
        O0, O1 = 6, 7
        estate = {}

        def s_step(kbk):
            pj = kbk % 3
            sb = 2 * pj
            for m in range(2):
                kb.op("pe", lambda m=m: nc.tensor.matmul(R.ps[sb + m],
                                                         lhsT=R.KT[m * 32:(m + 1) * 32, kbk * 128:(kbk + 1) * 128],
                                                         rhs=R.QT[m * 32:(m + 1) * 32, qt * QT_:(qt + 1) * QT_],
                                                         start=True, stop=True),
                      reads=[R.KT_b, R.QT_b], writes=[R.ps_b[sb + m]])
            ei = R.ne16 % 3; R.ne16 += 1
            e, eb = R.e16[ei], R.e16_b[ei]
            estate[kbk] = (e, eb)
            kb.op("act", lambda: nc.scalar.activation(out=e[:, :, :], in_=R.pp[pj][:, :, :], func=AF.Exp),
                  reads=[R.ps_b[sb], R.ps_b[sb + 1]], writes=[eb])
            r = kbk - 4 * qt
            if r >= 0:
                for m in range(2):
                    kb.op("dve", lambda m=m: nc.vector.tensor_tensor(out=e[:, m, :], in0=e[:, m, :], in1=R.mask[:, r, :], op=ALU.mult),
                          reads=[eb, R.mask_b], writes=[eb])

        def pv_step(kbk):
            e, eb = estate.pop(kbk)
            for m in range(2):
                kb.op("pe", lambda m=m: nc.tensor.matmul(R.ps[O0 + m][:, :], lhsT=R.Va[:, kbk, :], rhs=e[:, m, :],
                                                         start=(kbk == 0), stop=(kbk == nkb - 1)),
                      reads=[R.Va_b, eb], writes=[R.ps_b[O0 + m]])

        s_step(0)
        if nkb > 1:
            s_step(1)
        for kbk in range(nkb):
            if kbk + 2 < nkb:
                s_step(kbk + 2)
            pv_step(kbk)
        f0, f1, f2 = R.fin
        b0, b1, b2 = R.fin_b
        kb.op("dve", lambda: nc.vector.reciprocal(out=f0[:, :], in_=R.ps[O0][64:128, :]), reads=[R.ps_b[O0]], writes=[b0])
        kb.op("dve", lambda: nc.vector.tensor_tensor(out=f0[:, :], in0=R.ps[O0][0:64, :], in1=f0[:, :], op=ALU.mult),
              reads=[R.ps_b[O0], b0], writes=[b0])
        kb.op("dve", lambda: nc.vector.reciprocal(out=f1[:, :], in_=R.ps[O1][64:128, :]), reads=[R.ps_b[O1]], writes=[b1])
        kb.op("dve", lambda: nc.vector.tensor_tensor(out=f1[:, :], in0=R.ps[O1][0:64, :], in1=f1[:, :], op=ALU.mult),
              reads=[R.ps_b[O1], b1], writes=[b1])
        kb.op("dve", lambda: nc.vector.scalar_tensor_tensor(out=f2[:, :], in0=f1[:, :], scalar=R.par[0:64, 2:3], in1=f0[:, :],
                                                            op0=ALU.mult, op1=ALU.add),
              reads=[b0, b1, R.par_b], writes=[b2])
        kb.op("act", lambda: nc.scalar.activation(out=R.sq[0][:, :], in_=f2[:, :], func=AF.Square), reads=[b2], writes=[R.sq_b[0]])
        kb.op("pe", lambda: nc.tensor.matmul(R.ps[0][0:64, :], lhsT=ones64, rhs=R.sq[0][:, :], start=True, stop=True),
              reads=[R.sq_b[0], R.cst_b], writes=[R.ps_b[0]])
        kb.op("act", lambda: nc.scalar.activation(out=R.rr[0][:, :], in_=R.ps[0][0:64, :], func=AF.Ln, bias=R.par[0:64, 13:14], scale=1.0 / 64),
              reads=[R.ps_b[0], R.par_b], writes=[R.rr_b[0]])
        kb.op("act", lambda: nc.scalar.activation(out=R.rr[0][:, :], in_=R.rr[0][:, :], func=AF.Exp, scale=-0.5), reads=[R.rr_b[0]], writes=[R.rr_b[0]])

        def fn(yo, yob):
            kb.op("dve", lambda: nc.vector.scalar_tensor_tensor(out=yo[:, :], in0=f2[:, :], scalar=R.par[0:64, 3:4], in1=R.rr[0][:, :],
                                                                op0=ALU.mult, op1=ALU.mult),
                  reads=[b2, R.rr_b[0], R.par_b], writes=[yob])
        out_store(kb, R, yT_d, row0, qt, fn, None)


def mixer_d(kb, R, hT, yT_d, row0, masks_d, tri_d):
    nc = kb.nc
    S, NB, NQ = R.S, R.NB, R.NQ
    kb.dma("sp", R.mask[:, :, :], masks_d, writes=[R.mask_b])
    kb.dma("sp", R.tri[:, :, :], tri_d, writes=[R.tri_b])
    for qt in range(NQ):
        ht, htb = load_ht(kb, R, hT, qt)
        proj_fm(kb, R, ht, htb, D_Q, 64, 0)
        proj_fm(kb, R, ht, htb, D_K, 64, 1)
        for half in range(2):
            rows = slice(half * 64, half * 64 + 64)
            kb.op("act", lambda rows=rows: nc.scalar.activation(out=R.QT[rows, qt * QT_:(qt + 1) * QT_], in_=R.ps[0][0:64, :], func=AF.Copy, scale=0.125),
                  reads=[R.ps_b[0]], writes=[R.QT_b])
            kb.op("dve", lambda rows=rows: nc.vector.tensor_copy(out=R.KT[rows, qt * QT_:(qt + 1) * QT_], in_=R.ps[1][0:64, :]),
                  reads=[R.ps_b[1]], writes=[R.KT_b])
        v_store(kb, R, ht, htb, D_V, qt, 4 + (qt % 2))
    RA, RB, OB = 4, 5, 6
    ed_b = [kb.buf() for _ in range(3)]
    ed = [R.e32[0], R.e32[1], R.e32c]
    for qt in range(NQ):
        kbs = list(range(4 * qt + 3, -1, -1))
        npair = len(kbs) // 2
        qsl = slice(qt * QT_, (qt + 1) * QT_)

        def blocks(j):
            return kbs[2 * j], kbs[2 * j + 1]

        def z_mm(j):
            for h, kbk in enumerate(blocks(j)):
                zb = 2 * (j % 2) + h
                rows = slice(h * 64, h * 64 + 64)
                kb.op("pe", lambda kbk=kbk, zb=zb, rows=rows: nc.tensor.matmul(R.ps[zb], lhsT=R.KT[rows, kbk * 128:(kbk + 1) * 128],
                                                                               rhs=R.QT[rows, qsl], start=True, stop=True),
                      reads=[R.KT_b, R.QT_b], writes=[R.ps_b[zb]])

        def esp(j):
            zp = j % 2
            e, eb = ed[j % 3], ed_b[j % 3]
            sp, spb = R.spp[j % 2], R.spp_b[j % 2]
            kb.op("act", lambda: nc.scalar.activation(out=e[:, :, :], in_=R.pp[zp][:, :, :], func=AF.Exp),
                  reads=[R.ps_b[2 * zp], R.ps_b[2 * zp + 1]], writes=[eb])
            kb.op("act", lambda: nc.scalar.activation(out=sp[:, :, :], in_=e[:, :, :], func=AF.Ln, bias=1.0),
                  reads=[eb], writes=[spb])
            for h, kbk in enumerate(blocks(j)):
                r = kbk - 4 * qt
                if r >= 0:
                    kb.op("dve", lambda h=h, r=r: nc.vector.tensor_tensor(out=sp[:, h, :], in0=sp[:, h, :], in1=R.mask[:, r, :], op=ALU.mult),
                          reads=[spb, R.mask_b], writes=[spb])

        def mmR(bank, t_i, sp_ap, spb, start, stop=False):
            kb.op("pe", lambda: nc.tensor.matmul(R.ps[bank], lhsT=R.tri[:, t_i, :], rhs=sp_ap, start=start, stop=stop),
                  reads=[R.tri_b, spb], writes=[R.ps_b[bank]])

        def chain_a(j):
            sp, spb = R.spp[j % 2], R.spp_b[j % 2]
            mmR(RA, 0, sp[:, 0, :], spb, j == 0)
            mmR(RB, 2, sp[:, 0, :], spb, j == 0)
            mmR(RB, 0, sp[:, 1, :], spb, False)

        def chain_b(j):
            e, eb = ed[j % 3], ed_b[j % 3]
            sp, spb = R.spp[j % 2], R.spp_b[j % 2]
            tt, ttb = R.e16[j % 2], R.e16_b[j % 2]
            aa, aab = R.e16[2 + j % 2], R.e16_b[2 + j % 2]
            kb.op("act", lambda: nc.scalar.activation(out=tt[:, :, :], in_=R.pp[2][:, :, :], func=AF.Exp, scale=-1.0),
                  reads=[R.ps_b[RA], R.ps_b[RB]], writes=[ttb])
            mmR(RA, 1, sp[:, 0, :], spb, False)
            mmR(RA, 2, sp[:, 1, :], spb, False, j == npair - 1)
            mmR(RB, 1, sp[:, 1, :], spb, False, j == npair - 1)
            kb.op("dve", lambda: nc.vector.tensor_tensor(out=aa[:, :, :], in0=e[:, :, :], in1=tt[:, :, :], op=ALU.mult),
                  reads=[eb, ttb], writes=[aab])
            for h, kbk in enumerate(blocks(j)):
                r = kbk - 4 * qt
                if r >= 0:
                    kb.op("dve", lambda h=h, r=r: nc.vector.tensor_tensor(out=aa[:, h, :], in0=aa[:, h, :], in1=R.mask[:, r, :], op=ALU.mult),
                          reads=[aab, R.mask_b], writes=[aab])

        def pv(j):
            aa, aab = R.e16[2 + j % 2], R.e16_b[2 + j % 2]
            for h, kbk in enumerate(blocks(j)):
                kb.op("pe", lambda h=h, kbk=kbk: nc.tensor.matmul(R.ps[OB][0:64, :], lhsT=R.Va[:, kbk, 0:64], rhs=aa[:, h, :],
                                                                  start=(j == 0 and h == 0), stop=(j == npair - 1 and h == 1)),
                      reads=[R.Va_b, aab], writes=[R.ps_b[OB]])

        z_mm(0)
        if npair > 1:
            z_mm(1)
        esp(0)
        for j in range(npair):
            if j + 1 < npair:
                esp(j + 1)
            chain_a(j)
            if j + 2 < npair:
                z_mm(j + 2)
            if j >= 1:
                pv(j - 1)
            chain_b(j)
        pv(npair - 1)

        def fn(yo, yob):
            kb.op("dve", lambda: nc.vector.tensor_copy(out=yo[:, :], in_=R.ps[OB][0:64, :]), reads=[R.ps_b[OB]], writes=[yob])
        out_store(kb, R, yT_d, row0, qt, fn, None)


def mixer_a(kb, R, hT, yT_d, row0, biasT_d):
    nc = kb.nc
    S, NB, NQ = R.S, R.NB, R.NQ
    kb.dma("sp", R.EB[:, :, :], biasT_d, writes=[R.EB_b])
    for r in range(8):
        kb.op("act", lambda r=r: nc.scalar.activation(out=R.EB[:, r, :], in_=R.EB[:, r, :], func=AF.Exp),
              reads=[R.EB_b], writes=[R.EB_b])
    kb.op("pool", lambda: nc.gpsimd.memset(R.Va[:, :, 64:128], 1.0), writes=[R.Va_b])
    ones64 = R.cst[0:64, 0:64]
    for qt in range(NQ):
        ht, htb = load_ht(kb, R, hT, qt)
        proj_fm(kb, R, ht, htb, A_Q, 64, 0)
        proj_fm(kb, R, ht, htb, A_K, 64, 2)
        qk_norm_store(kb, R, 0, 1, R.QT, R.QT_b, qt, 4, ones64, 1.0 / 64, 0)
        qk_norm_store(kb, R, 2, 3, R.KT, R.KT_b, qt, 5, ones64, 1.0 / 64, 1)
        v_store(kb, R, ht, htb, A_V, qt, 4 + (qt % 2))
    OB = 4
    for qt in range(NQ):
        rs = [r for r in range(8) if 4 * qt - 4 + r >= 0]

        def s_step(j):
            r = rs[j]
            kbk = 4 * qt - 4 + r
            sb = j % 2
            kb.op("pe", lambda: nc.tensor.matmul(R.ps[sb][:, :], lhsT=R.KT[0:64, kbk * 128:(kbk + 1) * 128],
                                                 rhs=R.QT[0:64, qt * QT_:(qt + 1) * QT_], start=True, stop=True),
                  reads=[R.KT_b, R.QT_b], writes=[R.ps_b[sb]])
            e, eb = R.e32[j % 2], R.e32_b[j % 2]
            p, pbuf = R.e16[j % 2], R.e16_b[j % 2]
            kb.op("act", lambda: nc.scalar.activation(out=e[:, 0, :], in_=R.ps[sb][:, :], func=AF.Exp),
                  reads=[R.ps_b[sb]], writes=[eb])
            kb.op("dve", lambda: nc.vector.tensor_tensor(out=p[:, 0, :], in0=e[:, 0, :], in1=R.EB[:, r, :], op=ALU.mult),
                  reads=[eb, R.EB_b], writes=[pbuf])

        def pv_step(j):
            kbk = 4 * qt - 4 + rs[j]
            p, pbuf = R.e16[j % 2], R.e16_b[j % 2]
            kb.op("pe", lambda: nc.tensor.matmul(R.ps[OB][:, :], lhsT=R.Va[:, kbk, :], rhs=p[:, 0, :],
                                                 start=(j == 0), stop=(j == len(rs) - 1)),
                  reads=[R.Va_b, pbuf], writes=[R.ps_b[OB]])

        s_step(0)
        for j in range(len(rs)):
            if j + 1 < len(rs):
                s_step(j + 1)
            pv_step(j)
        f0, b0 = R.fin[0], R.fin_b[0]
        kb.op("dve", lambda: nc.vector.reciprocal(out=f0[:, :], in_=R.ps[OB][64:128, :]), reads=[R.ps_b[OB]], writes=[b0])

        def fn(yo, yob):
            kb.op("dve", lambda: nc.vector.tensor_tensor(out=yo[:, :], in0=R.ps[OB][0:64, :], in1=f0[:, :], op=ALU.mult),
                  reads=[R.ps_b[OB], b0], writes=[yob])
        out_store(kb, R, yT_d, row0, qt, fn, None)


class MixResB:
    def __init__(self, kb, R):
        NB = R.NB
        self.Osig = R.EB[:, :, :].bitcast(BF16).rearrange("p a (b c) -> p (a b) c", c=64)[:, 0:NB, :]; self.Osig_b = R.EB_b
        self.G = kb.sbuf("Gates", [128, 8, NB], F32); self.G_b = kb.buf()
        self.trif = kb.sbuf("trif", [128, 2, 128], F32); self.trif_b = kb.buf()
        self.cw = kb.sbuf("convw", [128, 8], F32); self.cw_b = kb.buf()
        self.gob = kb.sbuf("gob", [128, 64], F32); self.gob_b = kb.buf()
        self.St = [kb.sbuf(f"St{i}", [64, 65], F32) for i in range(2)]; self.St_b = [kb.buf() for _ in range(2)]
        self.Sb = [kb.sbuf(f"Sb{i}", [64, 65], BF16) for i in range(3)]; self.Sb_b = [kb.buf() for _ in range(3)]
        self.tok = [kb.sbuf(f"tok{i}", [128, 3, 64], BF16) for i in range(2)]; self.tok_b = [kb.buf() for _ in range(2)]
        self.qkT = [kb.sbuf(f"qkT{i}", [64, 2, 128], BF16) for i in range(2)]; self.qkT_b = [kb.buf() for _ in range(2)]
        self.qkm = [kb.sbuf(f"qkm{i}", [128, 128], BF16) for i in range(2)]; self.qkm_b = [kb.buf() for _ in range(2)]
        self.cm = kb.sbuf("cmask", [128, 128], F32); self.cm_b = kb.buf()
        self.hn = [kb.sbuf(f"hn{i}", [128, 64], F32) for i in range(2)]; self.hn_b = [kb.buf() for _ in range(2)]
        self.hs = [kb.sbuf(f"hs{i}", [128, 8], F32) for i in range(2)]; self.hs_b = [kb.buf() for _ in range(2)]
        self.yb = [kb.sbuf(f"yb{i}", [128, 64], BF16) for i in range(2)]; self.yb_b = [kb.buf() for _ in range(2)]
        self.jk = kb.sbuf("jk", [128, 64], BF16); self.jk_b = kb.buf()


def mixer_b(kb, R, RB_, hT, yT_d, row0, bpar_d, trif_d, cmask_d, gob_d):
    nc = kb.nc
    S, NB, NQ = R.S, R.NB, R.NQ
    B = RB_
    kb.dma("sp", B.cw[:, :], bpar_d, writes=[B.cw_b])
    kb.dma("sp", B.trif[:, :, :], trif_d, writes=[B.trif_b])
    kb.dma("sp", B.cm[:, :], cmask_d, writes=[B.cm_b])
    kb.dma("sp", B.gob[:, :], gob_d, writes=[B.gob_b])
    kb.op("pool", lambda: nc.gpsimd.memset(R.Va[:, :, 64:65], 1.0), writes=[R.Va_b])
    kb.op("dve", lambda: nc.vector.tensor_scalar(out=B.cw[:, 7:8], in0=B.cw[:, 6:7], scalar1=-1.0, scalar2=None, op0=ALU.mult),
          reads=[B.cw_b], writes=[B.cw_b])
    cv = [R.e32[i][:, :, :].rearrange("p a b -> p (a b)") for i in range(2)]
    cvb = R.e32_b
    accA = R.e16[0][:, :, :].rearrange("p a b -> p (a b)").bitcast(F32); accB_ = R.e16[1][:, :, :].rearrange("p a b -> p (a b)").bitcast(F32)
    accA_b = R.e16_b[0]; accB_b = R.e16_b[1]
    for qt in range(NQ):
        ht, htb = load_ht(kb, R, hT, qt)
        ci = qt % 2
        proj_fm(kb, R, ht, htb, B_Q, 128, 0)
        if qt == 0:
            kb.op("dve", lambda: nc.vector.memset(cv[ci][:, 0:3], 0.0), writes=[cvb[ci]])
        else:
            kb.op("dve", lambda: nc.vector.tensor_copy(out=cv[ci][:, 0:3], in_=cv[1 - ci][:, 512:515]),
                  reads=[cvb[1 - ci]], writes=[cvb[ci]])
        kb.op("act", lambda: nc.scalar.copy(out=cv[ci][:, 3:515], in_=R.ps[0][:, :]), reads=[R.ps_b[0]], writes=[cvb[ci]])
        kb.op("dve", lambda: nc.vector.tensor_scalar(out=accA, in0=cv[ci][:, 3:515], scalar1=B.cw[:, 3:4], scalar2=B.cw[:, 4:5],
                                                     op0=ALU.mult, op1=ALU.add),
              reads=[cvb[ci], B.cw_b], writes=[accA_b])
        for j in (2, 1, 0):
            kb.op("dve", lambda j=j: nc.vector.scalar_tensor_tensor(out=accA, in0=cv[ci][:, j:j + 512], scalar=B.cw[:, j:j + 1],
                                                                    in1=accA, op0=ALU.mult, op1=ALU.add),
                  reads=[cvb[ci], B.cw_b, accA_b], writes=[accA_b])
        kb.op("act", lambda: nc.scalar.activation(out=accB_, in_=accA, func=AF.Sigmoid), reads=[accA_b], writes=[accB_b])
        kb.op("dve", lambda: nc.vector.tensor_tensor(out=accB_, in0=accA, in1=accB_, op=ALU.mult), reads=[accA_b, accB_b], writes=[accB_b])
        kb.op("act", lambda: nc.scalar.copy(out=R.QT[0:64, qt * QT_:(qt + 1) * QT_], in_=accB_[0:64, :]), reads=[accB_b], writes=[R.QT_b])
        kb.op("act", lambda: nc.scalar.copy(out=R.KT[0:64, qt * QT_:(qt + 1) * QT_], in_=accB_[64:128, :]), reads=[accB_b], writes=[R.KT_b])
        pb = 4 + (qt % 2)
        for s in range(4):
            nonlocal_pb = 3 + ((qt * 4 + s) % 4)
            for kc in range(8):
                kb.op("pe", lambda kc=kc: nc.tensor.matmul(R.ps[nonlocal_pb][:, 0:130], lhsT=ht[:, kc, s * 128:(s + 1) * 128],
                                                           rhs=R.Wm[:, kc, B_V:B_V + 130], start=(kc == 0), stop=(kc == 7)),
                      reads=[R.Wm_b, htb], writes=[R.ps_b[nonlocal_pb]], inc=(kc == 7))
            blk = qt * 4 + s
            kb.op("dve", lambda: nc.vector.tensor_copy(out=R.Va[:, blk, 0:64], in_=R.ps[nonlocal_pb][:, 0:64]),
                  reads=[R.ps_b[nonlocal_pb]], writes=[R.Va_b])
            kb.op("act", lambda: nc.scalar.activation(out=B.Osig[:, blk, :], in_=R.ps[nonlocal_pb][:, 64:128], func=AF.Sigmoid),
                  reads=[R.ps_b[nonlocal_pb]], writes=[B.Osig_b])
            kb.op("dve", lambda: nc.vector.tensor_copy(out=B.G[:, 0:2, blk], in_=R.ps[nonlocal_pb][:, 128:130]),
                  reads=[R.ps_b[nonlocal_pb]], writes=[B.G_b])
    G = B.G
    kb.op("act", lambda: nc.scalar.activation(out=G[:, 2, :], in_=G[:, 1, :], func=AF.Exp, scale=-1.0, bias=B.cw[:, 7:8]),
          reads=[B.G_b, B.cw_b], writes=[B.G_b])
    kb.op("act", lambda: nc.scalar.activation(out=G[:, 2, :], in_=G[:, 2, :], func=AF.Ln, bias=1.0), reads=[B.G_b], writes=[B.G_b])
    kb.op("dve", lambda: nc.vector.tensor_scalar(out=G[:, 2, :], in0=G[:, 2, :], scalar1=-1.0, scalar2=None, op0=ALU.mult),
          reads=[B.G_b], writes=[B.G_b])
    kb.op("pe", lambda: nc.tensor.matmul(R.ps[0][:, 0:NB], lhsT=B.trif[:, 0, :], rhs=G[:, 2, :], start=True, stop=True),
          reads=[B.trif_b, B.G_b], writes=[R.ps_b[0]])
    kb.op("pe", lambda: nc.tensor.matmul(R.ps[1][:, 0:NB], lhsT=B.trif[:, 1, :], rhs=G[:, 2, :], start=True, stop=True),
          reads=[B.trif_b, B.G_b], writes=[R.ps_b[1]])
    kb.op("dve", lambda: nc.vector.tensor_copy(out=G[:, 3, :], in_=R.ps[0][:, 0:NB]), reads=[R.ps_b[0]], writes=[B.G_b])
    kb.op("act", lambda: nc.scalar.activation(out=G[:, 4, :], in_=G[:, 3, :], func=AF.Exp), reads=[B.G_b], writes=[B.G_b])
    kb.op("dve", lambda: nc.vector.tensor_tensor(out=G[:, 5, :], in0=G[:, 0, :], in1=G[:, 3, :], op=ALU.subtract),
          reads=[B.G_b], writes=[B.G_b])
    kb.op("dve", lambda: nc.vector.tensor_tensor(out=G[:, 6, :], in0=G[:, 5, :], in1=R.ps[1][:, 0:NB], op=ALU.add),
          reads=[B.G_b, R.ps_b[1]], writes=[B.G_b])
    kb.op("act", lambda: nc.scalar.activation(out=G[:, 5, :], in_=G[:, 5, :], func=AF.Exp, bias=B.cw[:, 5:6]),
          reads=[B.G_b, B.cw_b], writes=[B.G_b])
    kb.op("act", lambda: nc.scalar.activation(out=G[:, 6, :], in_=G[:, 6, :], func=AF.Exp, bias=B.cw[:, 5:6]),
          reads=[B.G_b, B.cw_b], writes=[B.G_b])
    kb.op("dve", lambda: nc.vector.tensor_scalar(out=G[:, 5:7, :], in0=G[:, 5:7, :], scalar1=0.125, scalar2=None, op0=ALU.mult),
          reads=[B.G_b], writes=[B.G_b])
    kb.op("act", lambda: nc.scalar.activation(out=G[:, 7, :], in_=R.ps[1][:, 0:NB], func=AF.Exp), reads=[R.ps_b[1]], writes=[B.G_b])
    kb.op("dve", lambda: nc.vector.memset(B.St[0][:, :], 0.0), writes=[B.St_b[0]])
    kb.op("dve", lambda: nc.vector.memset(B.Sb[0][:, :], 0.0), writes=[B.Sb_b[0]])
    PT, PT2, PS_, PO, PU = 0, 1, 2, 3, 6
    def front(b):
        i2 = b % 2
        tok, tokb = B.tok[i2], B.tok_b[i2]
        qkT, qkTb = B.qkT[i2], B.qkT_b[i2]
        tpA = R.ps[0][:, :].bitcast(BF16); tpq = tpA[:, 0:128]; tpk = tpA[:, 128:256]
        kb.op("pe", lambda: nc.tensor.transpose(out=tpq[:, 0:64], in_=R.QT[0:64, b * 128:(b + 1) * 128], identity=R.ident[0:64, 0:64]),
              reads=[R.QT_b, R.ident_b], writes=[R.ps_b[0]])
        kb.op("pe", lambda: nc.tensor.transpose(out=tpk[:, 0:64], in_=R.KT[0:64, b * 128:(b + 1) * 128], identity=R.ident[0:64, 0:64]),
              reads=[R.KT_b, R.ident_b], writes=[R.ps_b[0]])
        kb.op("dve", lambda: nc.vector.tensor_scalar(out=tok[:, 0, :], in0=tpq[:, 0:64], scalar1=G[:, 4, b:b + 1], scalar2=None, op0=ALU.mult),
              reads=[R.ps_b[0], B.G_b], writes=[tokb])
        kb.op("dve", lambda: nc.vector.tensor_scalar(out=tok[:, 1, :], in0=tpk[:, 0:64], scalar1=G[:, 5, b:b + 1], scalar2=None, op0=ALU.mult),
              reads=[R.ps_b[0], B.G_b], writes=[tokb])
        kb.op("dve", lambda: nc.vector.tensor_scalar(out=tok[:, 2, :], in0=tpk[:, 0:64], scalar1=G[:, 6, b:b + 1], scalar2=None, op0=ALU.mult),
              reads=[R.ps_b[0], B.G_b], writes=[tokb])
        tpB = R.ps[1][:, :].bitcast(BF16); tq2 = tpB[:, 0:128]; tk2 = tpB[:, 128:256]
        kb.op("pe", lambda: nc.tensor.transpose(out=tq2[0:64, :], in_=tok[:, 0, :], identity=R.ident[:, :]),
              reads=[tokb, R.ident_b], writes=[R.ps_b[1]])
        kb.op("pe", lambda: nc.tensor.transpose(out=tk2[0:64, :], in_=tok[:, 1, :], identity=R.ident[:, :]),
              reads=[tokb, R.ident_b], writes=[R.ps_b[1]])
        kb.op("act", lambda: nc.scalar.copy(out=qkT[:, 0, :], in_=tq2[0:64, :]), reads=[R.ps_b[1]], writes=[qkTb])
        kb.op("act", lambda: nc.scalar.copy(out=qkT[:, 1, :], in_=tk2[0:64, :]), reads=[R.ps_b[1]], writes=[qkTb])
        kb.op("pe", lambda: nc.tensor.matmul(R.ps[PS_][:, 0:128], lhsT=qkT[:, 1, :], rhs=qkT[:, 0, :], start=True, stop=True),
              reads=[qkTb], writes=[R.ps_b[PS_]])
        qkm, qkmb = B.qkm[i2], B.qkm_b[i2]
        kb.op("dve", lambda: nc.vector.tensor_tensor(out=qkm[:, :], in0=R.ps[PS_][:, 0:128], in1=B.cm[:, :], op=ALU.mult),
              reads=[R.ps_b[PS_], B.cm_b], writes=[qkmb])
        kb.op("pe", lambda: nc.tensor.matmul(R.ps[PU][0:64, 0:65], lhsT=tok[:, 2, :], rhs=R.Va[:, b, 0:65], start=True, stop=True),
              reads=[tokb, R.Va_b], writes=[R.ps_b[PU]])
        Sn, Snb = B.St[1 - i2], B.St_b[1 - i2]
        So, Sob = B.St[i2], B.St_b[i2]
        kb.op("dve", lambda: nc.vector.scalar_tensor_tensor(out=Sn[:, :], in0=So[:, :], scalar=G[0:64, 7, b:b + 1], in1=R.ps[PU][0:64, 0:65],
                                                            op0=ALU.mult, op1=ALU.add),
              reads=[Sob, B.G_b, R.ps_b[PU]], writes=[Snb])
        kb.op("act", lambda: nc.scalar.copy(out=B.Sb[(b + 1) % 3][:, :], in_=Sn[:, :]), reads=[Snb], writes=[B.Sb_b[(b + 1) % 3]])

    def back(b):
        i2 = b % 2
        tok, tokb = B.tok[i2], B.tok_b[i2]
        qkT, qkTb = B.qkT[i2], B.qkT_b[i2]
        qkm, qkmb = B.qkm[i2], B.qkm_b[i2]
        po = PO + (b % 2)
        Sp, Spb = B.Sb[b % 3], B.Sb_b[b % 3]
        po = PO + (b % 2)
        kb.op("pe", lambda: nc.tensor.matmul(R.ps[po][:, 0:65], lhsT=qkm[:, :], rhs=R.Va[:, b, 0:65], start=True, stop=False),
              reads=[qkmb, R.Va_b], writes=[R.ps_b[po]], inc=False)
        kb.op("pe", lambda: nc.tensor.matmul(R.ps[po][:, 0:65], lhsT=qkT[:, 0, :], rhs=Sp[:, :], start=False, stop=True),
              reads=[qkTb, Spb], writes=[R.ps_b[po]])
        hs, hsb = B.hs[i2], B.hs_b[i2]
        hn, hnb = B.hn[i2], B.hn_b[i2]
        kb.op("act", lambda: nc.scalar.activation(out=hs[:, 5:6], in_=R.ps[po][:, 64:65], func=AF.Abs),
              reads=[R.ps_b[po]], writes=[hsb])
        kb.op("dve", lambda: nc.vector.tensor_scalar(out=hs[:, 0:1], in0=hs[:, 5:6], scalar1=1.0, scalar2=None, op0=ALU.max),
              reads=[hsb], writes=[hsb])
        kb.op("dve", lambda: nc.vector.reciprocal(out=hs[:, 1:2], in_=hs[:, 0:1]), reads=[hsb], writes=[hsb])
        kb.op("dve", lambda: nc.vector.tensor_scalar(out=hn[:, :], in0=R.ps[po][:, 0:64], scalar1=hs[:, 1:2], scalar2=None, op0=ALU.mult),
              reads=[R.ps_b[po], hsb], writes=[hnb])
        kb.op("act", lambda: nc.scalar.activation(out=B.jk[:, :], in_=hn[:, :], func=AF.Square, accum_out=hs[:, 2:3]),
              reads=[hnb], writes=[B.jk_b, hsb])
        kb.op("act", lambda: nc.scalar.activation(out=hs[:, 3:4], in_=hs[:, 2:3], func=AF.Sqrt, bias=EPS, scale=1.0 / 64),
              reads=[hsb], writes=[hsb])
        kb.op("dve", lambda: nc.vector.reciprocal(out=hs[:, 4:5], in_=hs[:, 3:4]), reads=[hsb], writes=[hsb])
        kb.op("dve", lambda: nc.vector.scalar_tensor_tensor(out=hn[:, :], in0=hn[:, :], scalar=hs[:, 4:5], in1=B.gob[:, :],
                                                            op0=ALU.mult, op1=ALU.mult),
              reads=[hnb, hsb, B.gob_b], writes=[hnb])
        yb, ybb = B.yb[i2], B.yb_b[i2]
        kb.op("dve", lambda: nc.vector.tensor_tensor(out=yb[:, :], in0=hn[:, :], in1=B.Osig[:, b, :], op=ALU.mult),
              reads=[hnb, B.Osig_b], writes=[ybb])
        ty = R.ps[5][:, :].bitcast(BF16)[:, 0:128]
        kb.op("pe", lambda: nc.tensor.transpose(out=ty[0:64, :], in_=yb[:, :], identity=R.ident[:, :]),
              reads=[ybb, R.ident_b], writes=[R.ps_b[5]])
        qt = b // 4
        if b % 4 == 0:
            R.cur_yo = R.nyo % 2; R.nyo += 1
        yo, yob = R.yo[R.cur_yo], R.yo_b[R.cur_yo]
        kb.op("act", lambda: nc.scalar.copy(out=yo[:, (b % 4) * 128:(b % 4 + 1) * 128], in_=ty[0:64, :]),
              reads=[R.ps_b[5]], writes=[yob])
        if b % 4 == 3:
            kb.dma("pool", ydst(yT_d, row0, qt), yo[:, :], reads=[yob])

    front(0)
    for b in range(NB):
        if b + 1 < NB:
            front(b + 1)
        back(b)


def mix_params(kb, R, praw_d, clam_d):
    nc = kb.nc
    pr = R.par
    kb.dma("sp", pr[0:64, 16:24], praw_d, writes=[R.par_b])
    cl = R.rr[0][:, 0:128].rearrange("p (a b) -> p a b", a=4)
    kb.dma("sp", cl, clam_d, writes=[R.rr_b[0]])
    V = nc.vector
    kb.op("dve", lambda: V.memset(pr[0:64, 13:14], EPS), writes=[R.par_b])
    kb.op("dve", lambda: V.tensor_scalar(out=pr[0:64, 0:1], in0=pr[0:64, 16:17], scalar1=32 ** -0.5, scalar2=None, op0=ALU.mult), reads=[R.par_b], writes=[R.par_b])
    kb.op("dve", lambda: V.tensor_copy(out=pr[0:64, 1:2], in_=pr[0:64, 17:18]), reads=[R.par_b], writes=[R.par_b])
    kb.op("dve", lambda: V.tensor_tensor(out=pr[0:64, 3:4], in0=pr[0:64, 18:19], in1=pr[0:64, 22:23], op=ALU.mult), reads=[R.par_b], writes=[R.par_b])
    kb.op("dve", lambda: V.tensor_scalar(out=pr[0:64, 4:5], in0=pr[0:64, 19:20], scalar1=0.125, scalar2=None, op0=ALU.mult), reads=[R.par_b], writes=[R.par_b])
    kb.op("dve", lambda: V.tensor_copy(out=pr[0:64, 5:6], in_=pr[0:64, 20:21]), reads=[R.par_b], writes=[R.par_b])
    pp = R.rr[1][:, 0:64].rearrange("p (a b) -> p a b", a=2)
    kb.op("dve", lambda: V.tensor_tensor(out=pp[:, 0, :], in0=cl[:, 0, :], in1=cl[:, 1, :], op=ALU.mult), reads=[R.rr_b[0]], writes=[R.rr_b[1]])
    kb.op("dve", lambda: V.tensor_tensor(out=pp[:, 1, :], in0=cl[:, 2, :], in1=cl[:, 3, :], op=ALU.mult), reads=[R.rr_b[0]], writes=[R.rr_b[1]])
    kb.op("dve", lambda: V.reduce_sum(out=pr[0:64, 8:10], in_=pp, axis=AX.X), reads=[R.rr_b[1]], writes=[R.par_b])
    kb.op("act", lambda: nc.scalar.activation(out=pr[0:64, 10:12], in_=pr[0:64, 8:10], func=AF.Exp), reads=[R.par_b], writes=[R.par_b])
    kb.op("dve", lambda: V.tensor_tensor(out=pr[0:64, 12:13], in0=pr[0:64, 11:12], in1=pr[0:64, 10:11], op=ALU.subtract), reads=[R.par_b], writes=[R.par_b])
    kb.op("dve", lambda: V.tensor_tensor(out=pr[0:64, 2:3], in0=pr[0:64, 12:13], in1=pr[0:64, 21:22], op=ALU.subtract), reads=[R.par_b], writes=[R.par_b])

import ml_dtypes
bf16 = ml_dtypes.bfloat16
GW = 256
OFF = dict(aq=0, ak=256, av=512, bqk=768, bv=1280, bo=1536, bi=1792, bf=1796, cq=1800, ck=2056, cv=2312, dq=2568, dk=2824, dv=3080)

def sel_cols(j):
    c = []
    r = lambda o: list(range(o + j * 64, o + j * 64 + 64))
    c += r(OFF['aq']) + r(OFF['ak']) + r(OFF['av'])
    c += r(OFF['bqk']) + r(OFF['bqk'] + 256) + r(OFF['bv']) + r(OFF['bo']) + [OFF['bi'] + j, OFF['bf'] + j]
    c += r(OFF['cq']) + r(OFF['ck']) + r(OFF['cv'])
    c += r(OFF['dq']) + r(OFF['dk']) + r(OFF['dv'])
    return np.array(c)

def const_inputs():
    d = {}
    d['ident'] = np.eye(128, dtype=np.float32).astype(bf16)
    cst = np.zeros((128, 256), np.float32)
    cst[0:64, 0:64] = 1.0
    cst[0:32, 64:96] = 1.0; cst[32:64, 96:128] = 1.0
    d['cst'] = cst.astype(bf16)
    s = np.arange(128)[:, None, None, None]; r = np.arange(4)[None, :, None, None]; t = np.arange(512)[None, None, None, :]
    mc = ((2 * r + (s >= 64)) <= (t // 64)).astype(np.float32)
    d['masks_c'] = np.broadcast_to(mc, (128, 4, 2, 512)).astype(bf16).copy()
    s = np.arange(128)[:, None, None]; r = np.arange(4)[None, :, None]; t = np.arange(512)[None, None, :]
    d['masks_d'] = ((128 * r + s) < t).astype(np.float32).astype(bf16)
    j = np.arange(128)[:, None]; s2 = np.arange(128)[None, :]
    tri = np.zeros((128, 3, 128), np.float32)
    tri[:, 0, :] = (j >= s2); tri[:, 1, :] = (j < s2); tri[:, 2, :] = 1.0
    d['tri'] = tri.astype(bf16)
    trif = np.zeros((128, 2, 128), np.float32)
    trif[:, 0, :] = (j <= s2); trif[:, 1, :] = 1.0
    d['trif'] = trif
    d['cmask'] = (j <= s2).astype(np.float32)
    return d

def bias_index():
    s = np.arange(128)[:, None, None]; r = np.arange(8)[None, :, None]; t = np.arange(512)[None, None, :]
    rel = t - s + 512 - 128 * r
    idx = np.clip(rel, -128, 128) + 128
    dd = t // 64 + 8 - 2 * r - s // 64
    vis = (dd >= 0) & (dd <= 8)
    return idx, vis

_IDX, _VIS = bias_index()

def layer_core_inputs(P, l, j, lam_init=None):
    d = {}
    d['wsel'] = np.ascontiguousarray(P['w_in'][l][:, sel_cols(j)])
    d['gmix'] = np.ascontiguousarray(P['mix_norm'][l].reshape(8, 128).T)
    praw = np.zeros((64, 8), np.float32)
    praw[:, 0] = np.tile(P['c_q_norm'][l], 2); praw[:, 1] = np.tile(P['c_k_norm'][l], 2)
    praw[:, 2] = P['c_out_norm'][l]; praw[:, 3] = P['a_q_norm'][l]; praw[:, 4] = P['a_k_norm'][l]
    if lam_init is None:
        lam_init = 0.8 - 0.6 * np.exp(-0.3 * l)
    praw[:, 5] = lam_init; praw[:, 6] = 1.0 - lam_init
    d['praw'] = praw
    d['clam'] = np.ascontiguousarray(np.broadcast_to(P['c_lambda'][l][None], (64, 4, 32))).astype(np.float32)
    rb = P['a_rel_bias'][l][j]
    d['biasT'] = np.where(_VIS, rb[_IDX], np.float32(-1e30)).astype(np.float32)
    bpar = np.zeros((128, 8), np.float32)
    ch = np.concatenate([np.arange(j * 64, j * 64 + 64), 256 + np.arange(j * 64, j * 64 + 64)])
    bpar[:, 0:4] = P['b_conv_w'][l][:, ch].T
    bpar[:, 4] = P['b_conv_b'][l][ch]
    bpar[:, 5] = P['b_gate_bias'][l][0, j]
    bpar[:, 6] = P['b_gate_bias'][l][1, j]
    d['bpar'] = bpar
    d['gob'] = np.ascontiguousarray(np.broadcast_to(P['b_out_norm'][l][j][None], (128, 64))).astype(np.float32)
    return d


from concourse.bass_utils import run_bass_kernel_spmd

SEQ = 16384
NCORE = 8
TPC = 4096
DEPTH = 2
GROUPS = [[0, 1, 2, 3], [4, 5, 6, 7]]


def _din(nc, name, shape, dt):
    return nc.dram_tensor(name, list(shape), dt, kind="ExternalInput").ap()


def _dout(nc, name, shape, dt):
    return nc.dram_tensor(name, list(shape), dt, kind="ExternalOutput").ap()


def _dint(nc, name, shape, dt):
    return nc.dram_tensor(name, list(shape), dt, kind="Internal").ap()


MIX_IN = dict(wsel=([D, NW], F32), gmix=([128, 8], F32), praw=([64, 8], F32), clam=([64, 4, 32], F32),
              biasT=([128, 8, 512], F32), bpar=([128, 8], F32), gob=([128, 64], F32))
CONST_IN = dict(ident=([128, 128], BF16), cst=([128, 256], BF16), masks_c=([128, 4, 2, 512], BF16),
                masks_d=([128, 4, 512], BF16), tri=([128, 3, 128], BF16), trif=([128, 2, 128], F32), cmask=([128, 128], F32))


def build_fused(S=SEQ, T=TPC):
    nc = bass.Bass("TRN2", target_bir_lowering=False)
    NQr = T // QT_
    x_in = _din(nc, "x_in", [T, D], F32)
    x_out = _dout(nc, "x_out", [T, D], F32)
    Cn = {k: _din(nc, k, sh, dt) for k, (sh, dt) in CONST_IN.items()}
    ffn = {}
    for l in range(DEPTH):
        for f in ("ffn1", "ffn2"):
            ffn[(f, l)] = dict(g=_din(nc, f"{f}_g{l}", [128, 8], F32), wg=_din(nc, f"{f}_wg{l}", [D, DFF], F32),
                               wu=_din(nc, f"{f}_wu{l}", [D, DFF], F32), wd=_din(nc, f"{f}_wd{l}", [DFF, D], F32))
    wo = [_din(nc, f"wo{l}", [D, D], F32) for l in range(DEPTH)]
    mx = [{k: _din(nc, f"{k}{l}", sh, dt) for k, (sh, dt) in MIX_IN.items()} for l in range(DEPTH)]
    xa = _dint(nc, "xa", [T, D], F32); xb = _dint(nc, "xb", [T, D], F32); xc = _dint(nc, "xc", [T, D], F32)
    hT_loc = _dint(nc, "hT_loc", [D, T], BF16)
    hT_all = _dint(nc, "hT_all", [4 * D, T], BF16)
    yT_loc = _dint(nc, "yT_loc", [D, T], BF16)
    yT_all = _dint(nc, "yT_all", [4 * D, T], BF16)
    yT_mine = _dint(nc, "yT_mine", [D, T], BF16)

    hv = hT_all.rearrange("(k r p) t -> k r p t", k=8, r=4)

    def hT_src(qt):
        r, o = qt // NQr, (qt % NQr) * QT_
        return hv[:, r, :, o:o + QT_]

    def y_dst(row0, qt):
        q, o = qt // NQr, (qt % NQr) * QT_
        return yT_loc[q * 256 + row0:q * 256 + row0 + 64, o:o + QT_]

    with ExitStack() as st:
        kb = KB(nc, st)
        pid = nc.sync.partition_id()
        qv = pid % 4

        ymine_b = kb.buf()

        def fetch_mine():
            yv2 = yT_all.rearrange("(q h j p) t -> q h j p t", q=4, h=2, j=4)
            for j in range(4):
                for h in range(2):
                    kb.dma("sp", yT_mine[j * 256 + h * 128:j * 256 + (h + 1) * 128, :],
                           yv2[bass.ds(qv, 1), h, j, :, :].rearrange("o p t -> (o p) t"), writes=[ymine_b])

        def tok_phase(passes):
            with ExitStack() as mem:
                kb.mem = mem
                R = TokRes(kb, any(p.get("wo") is not None for p in passes))
                load_consts(kb, R, Cn["ident"])
                prev_bufs = None
                for k, p in enumerate(passes):
                    w = p["ffn"]
                    load_ffn_weights(kb, R, w["g"], w["wg"], w["wu"], w["wd"], p.get("wo"))
                    ob = [kb.buf() for _ in range(T // TT)] if k + 1 < len(passes) else None
                    has_pre = p.get("wo") is not None
                    token_pass(kb, R, T, p["xi"], p["xo"], pre=(yT_mine if has_pre else None),
                               post=p.get("post"), in_bufs=prev_bufs, out_bufs=ob,
                               pre_bufs=([ymine_b] * (T // TT) if has_pre else None))
                    prev_bufs = ob
                kb.barrier()
            kb.mem = st

        def mix_phase(l):
            with ExitStack() as mem:
                kb.mem = mem
                R = MixRes(kb, S)
                RB = MixResB(kb, R)
                m = mx[l]
                mix_load_common(kb, R, m["wsel"], m["gmix"], Cn["ident"], Cn["cst"])
                mix_params(kb, R, m["praw"], m["clam"])
                mixer_a(kb, R, hT_src, y_dst, 0, m["biasT"])
                kb.barrier()
                mixer_b(kb, R, RB, hT_src, y_dst, 64, m["bpar"], Cn["trif"], Cn["cmask"], m["gob"])
                kb.barrier()
                mixer_c(kb, R, hT_src, y_dst, 128, Cn["masks_c"])
                kb.barrier()
                mixer_d(kb, R, hT_src, y_dst, 192, Cn["masks_d"], Cn["tri"])
                kb.barrier()
            kb.mem = st

        tok_phase([dict(ffn=ffn[("ffn1", 0)], xi=x_in, xo=xa, post=hT_loc)])
        kb.allgather(hT_loc, hT_all, GROUPS)
        mix_phase(0)
        kb.allgather(yT_loc, yT_all, GROUPS)
        fetch_mine()
        tok_phase([dict(ffn=ffn[("ffn2", 0)], wo=wo[0], xi=xa, xo=xb),
                   dict(ffn=ffn[("ffn1", 1)], xi=xb, xo=xc, post=hT_loc)])
        kb.allgather(hT_loc, hT_all, GROUPS)
        mix_phase(1)
        kb.allgather(yT_loc, yT_all, GROUPS)
        fetch_mine()
        tok_phase([dict(ffn=ffn[("ffn2", 1)], wo=wo[1], xi=xc, xo=x_out)])
        kb.finish()
    return nc


def build_mixer_prog(S=SEQ):
    nc = bass.Bass("TRN2", target_bir_lowering=False)
    hT = _din(nc, "hT", [D, S], BF16)
    Cn = {k: _din(nc, k, sh, dt) for k, (sh, dt) in CONST_IN.items()}
    m = {k: _din(nc, k, sh, dt) for k, (sh, dt) in MIX_IN.items()}
    yT = _dout(nc, "yT", [256, S], BF16)
    with ExitStack() as st:
        kb = KB(nc, st)
        R = MixRes(kb, S)
        RB = MixResB(kb, R)
        mix_load_common(kb, R, m["wsel"], m["gmix"], Cn["ident"], Cn["cst"])
        mix_params(kb, R, m["praw"], m["clam"])
        mixer_a(kb, R, hT, yT, 0, m["biasT"])
        kb.barrier()
        mixer_b(kb, R, RB, hT, yT, 64, m["bpar"], Cn["trif"], Cn["cmask"], m["gob"])
        kb.barrier()
        mixer_c(kb, R, hT, yT, 128, Cn["masks_c"])
        kb.barrier()
        mixer_d(kb, R, hT, yT, 192, Cn["masks_d"], Cn["tri"])
        kb.finish()
    return nc


def _lay(g):
    return np.ascontiguousarray(np.asarray(g, np.float32).reshape(8, 128).T)


def _wo_perm(w_out):
    idx = np.arange(1024).reshape(4, 4, 64)
    perm = idx.transpose(1, 0, 2).reshape(-1)
    return np.ascontiguousarray(w_out[perm, :])


def make_in_maps(P, TPC=TPC):
    x = np.ascontiguousarray(P["x"], dtype=np.float32).reshape(-1, D)
    C = const_inputs()
    shared = dict(C)
    for l in range(DEPTH):
        for f in ("ffn1", "ffn2"):
            shared[f"{f}_g{l}"] = _lay(P[f + "_norm"][l])
            shared[f"{f}_wg{l}"] = np.ascontiguousarray(P[f + "_wg"][l], dtype=np.float32)
            shared[f"{f}_wu{l}"] = np.ascontiguousarray(P[f + "_wu"][l], dtype=np.float32)
            shared[f"{f}_wd{l}"] = np.ascontiguousarray(P[f + "_wd"][l], dtype=np.float32)
        shared[f"wo{l}"] = _wo_perm(np.asarray(P["w_out"][l], np.float32))
    ims = []
    for c in range(NCORE):
        d = dict(shared)
        d["x_in"] = x[c * TPC:(c + 1) * TPC]
        j = c % 4
        for l in range(DEPTH):
            for k, v in layer_core_inputs(P, l, j).items():
                d[f"{k}{l}"] = v
        ims.append(d)
    return ims


def kernel(**inputs):
    P = {k: np.asarray(v) for k, v in inputs.items()}
    nc = build_fused()
    ims = make_in_maps(P)
    res = run_bass_kernel_spmd(nc, ims, core_ids=list(range(NCORE)))
    out = np.concatenate([r["x_out"] for r in res.results], axis=0).reshape(2, SEQ, D).astype(np.float32)
    return out
```

```python
import numpy as np
from contextlib import ExitStack
import concourse.bass as bass
import concourse.mybir as mybir

F32 = mybir.dt.float32
BF16 = mybir.dt.bfloat16
AF = mybir.ActivationFunctionType
ALU = mybir.AluOpType
AX = mybir.AxisListType

EPOCH = 4096


class Buf:
    __slots__ = ("w", "r", "name")

    def __init__(self, name=""):
        self.w = None
        self.r = {}
        self.name = name


class KB:
    def __init__(self, nc, stack):
        self.nc = nc
        self.st = stack
        self.E = {"pe": nc.tensor, "act": nc.scalar, "dve": nc.vector, "pool": nc.gpsimd, "sp": nc.sync}
        self.cnt = {e: 0 for e in self.E}
        self.sems = {e: [] for e in self.E}
        self.waited = {e: {} for e in self.E}
        self.ndma = 12
        self.dma_sems = {}
        self.dma_cnt = {}
        self.dma_rr = {}
        self.nsem = 0
        self.uid = 0
        self.mem = stack

    def sem(self, name):
        self.nsem += 1
        return self.st.enter_context(self.nc.semaphore(name))

    def sbuf(self, name, shape, dt):
        self.uid += 1
        return self.mem.enter_context(self.nc.sbuf_tensor(f"sb{self.uid}_" + name, list(shape), dt))

    def psum(self, name, shape, dt):
        self.uid += 1
        return self.mem.enter_context(self.nc.psum_tensor(f"ps{self.uid}_" + name, list(shape), dt))

    def buf(self, name=""):
        return Buf(name)

    def _esem(self, e, n):
        ep = (n - 1) // EPOCH
        while len(self.sems[e]) <= ep:
            self.sems[e].append(self.sem(f"c_{e}_{len(self.sems[e])}"))
        return self.sems[e][ep], (n - 1) % EPOCH + 1

    def _wait(self, e, ev):
        if ev[0] == "e":
            _, src, n = ev
            if src == e and e == "pe":
                return
            key = ("e", src)
            if self.waited[e].get(key, 0) >= n:
                return
            if src == e and n > self.cnt[e]:
                raise RuntimeError("self-wait on future event")
            s, v = self._esem(src, n)
            self.E[e].wait_ge(s, v)
            self.waited[e][key] = n
        else:
            _, q, i, k = ev
            key = ("d", q, i)
            if self.waited[e].get(key, 0) >= k:
                return
            self.E[e].wait_ge(self.dma_sems[q][i], 16 * k)
            self.waited[e][key] = k

    @staticmethod
    def _evkey(ev):
        return (ev[0], ev[1]) if ev[0] == "e" else (ev[0], ev[1], ev[2])

    def _collect(self, reads, writes):
        deps = []
        for b in reads:
            if b.w is not None:
                deps.append(b.w)
        for b in writes:
            if b.w is not None:
                deps.append(b.w)
            deps.extend(b.r.values())
        return deps

    def _record(self, ev, reads, writes):
        k = self._evkey(ev)
        for b in reads:
            b.r[k] = ev
        for b in writes:
            b.w = ev
            b.r = {}

    def op(self, e, fn, reads=(), writes=(), inc=True):
        for ev in self._collect(reads, writes):
            self._wait(e, ev)
        ins = fn()
        if inc:
            self.cnt[e] += 1
            s, v = self._esem(e, self.cnt[e])
            ins.then_inc(s, 1)
            ev = ("e", e, self.cnt[e])
        else:
            ev = ("e", e, self.cnt[e] + 1)
        self._record(ev, reads, writes)
        return ins

    def dma(self, q, out, in_, reads=(), writes=(), **kw):
        for ev in self._collect(reads, writes):
            self._wait(q, ev)
        if q not in self.dma_sems:
            self.dma_sems[q] = [self.sem(f"d_{q}_{i}") for i in range(self.ndma)]
            self.dma_cnt[q] = [0] * self.ndma
            self.dma_rr[q] = 0
        i = self.dma_rr[q]
        self.dma_rr[q] = (i + 1) % self.ndma
        if self.dma_cnt[q][i] > 0:
            self._wait(q, ("d", q, i, self.dma_cnt[q][i]))
        self.dma_cnt[q][i] += 1
        ins = self.E[q].dma_start(out=out, in_=in_, **kw)
        ins.then_inc(self.dma_sems[q][i], 16)
        ev = ("d", q, i, self.dma_cnt[q][i])
        self._record(ev, reads, writes)
        return ins

    def barrier(self, extra_sems=()):
        for e in self.E:
            for q in self.dma_sems:
                for i in range(self.ndma):
                    if self.dma_cnt[q][i] > 0:
                        self._wait(e, ("d", q, i, self.dma_cnt[q][i]))
            for src in ("pe", "act", "dve", "pool"):
                if self.cnt[src] > 0 and not (src == e and e == "pe"):
                    self._wait(e, ("e", src, self.cnt[src]))
            for (sm, v) in extra_sems:
                self.E[e].wait_ge(sm, v)
            self.wait_cc(e)

    def allgather(self, src2d, dst2d, groups, chunk_rows=128):
        self.barrier()
        R_ = src2d.shape[0]
        nk = R_ // chunk_rows
        ng = len(groups[0])
        if not hasattr(self, "cc_sem"):
            self.cc_sem = self.sem("ccsem")
            self.cc_cnt = 0
        for k in range(nk):
            self.nc.gpsimd.collective_compute("AllGather", ALU.bypass, replica_groups=groups,
                                              ins=[src2d[k * chunk_rows:(k + 1) * chunk_rows, :]],
                                              outs=[dst2d[k * ng * chunk_rows:(k + 1) * ng * chunk_rows, :]]).then_inc(self.cc_sem, 1)
            self.cc_cnt += 1

    def wait_cc(self, e):
        if hasattr(self, "cc_sem") and self.cc_cnt > 0:
            self.E[e].wait_ge(self.cc_sem, self.cc_cnt)

    def finish(self):
        for q in self.dma_sems:
            for i in range(self.ndma):
                if self.dma_cnt[q][i] > 0:
                    self._wait("sp", ("d", q, i, self.dma_cnt[q][i]))
        for e in ("pe", "act", "dve", "pool"):
            if self.cnt[e] > 0:
                self._wait("sp", ("e", e, self.cnt[e]))


D = 1024
DFF = 2816
NFC = DFF // 128
TT = 256
SUB = TT // 128
EPS = 1e-6


class TokRes:
    def __init__(self, kb, with_pre):
        nc = kb.nc
        self.kb = kb
        self.Wg = kb.sbuf("Wg", [128, 8, DFF], BF16); self.Wg_b = kb.buf()
        self.Wu = kb.sbuf("Wu", [128, 8, DFF], BF16); self.Wu_b = kb.buf()
        self.Wd = kb.sbuf("Wd", [128, NFC, D], BF16); self.Wd_b = kb.buf()
        self.stage = [kb.sbuf(f"stage{i}", [128, 1024], F32) for i in range(2)]
        self.stage_b = [kb.buf() for _ in range(2)]
        self.gt = kb.sbuf("gt", [128, 8], F32); self.gt_b = kb.buf()
        self.ident = kb.sbuf("ident", [128, 128], BF16); self.ident_b = kb.buf()
        self.xt = [kb.sbuf(f"xt{i}", [128, SUB, D], F32) for i in range(2)]
        self.xt_b = [kb.buf() for _ in range(2)]
        self.xn = kb.sbuf("xn", [128, D], BF16); self.xn_b = kb.buf()
        self.st = kb.sbuf("stat", [128, 8], F32); self.st_b = kb.buf()
        self.xnT = [kb.sbuf(f"xnT{i}", [128, 8, TT], BF16) for i in range(2)]
        self.xnT_b = [kb.buf() for _ in range(2)]
        self.hid = kb.sbuf("hid", [128, NFC, TT], BF16)
        self.hid_b = [kb.buf() for _ in range(NFC)]
        self.sg = [kb.sbuf(f"sg{i}", [128, TT], F32) for i in range(2)]
        self.sg_b = [kb.buf() for _ in range(2)]
        self.hTo = kb.sbuf("hTo", [128, 8, TT], BF16); self.hTo_b = kb.buf()
        self.with_pre = with_pre
        if with_pre:
            self.Wo = kb.sbuf("Wo", [128, 8, D], BF16); self.Wo_b = kb.buf()
            self.yt = [kb.sbuf(f"yt{i}", [128, 8, TT], BF16) for i in range(2)]
            self.yt_b = [kb.buf() for _ in range(2)]
        self.psg = [kb.psum(f"psg{i}", [128, 512], F32) for i in range(2)]
        self.psg_b = [kb.buf() for _ in range(2)]
        self.psd = [kb.psum(f"psd{i}", [128, 512], F32) for i in range(2)]
        self.psd_b = [kb.buf() for _ in range(2)]
        self.tp = [kb.psum(f"tp{i}", [128, 8, 128], BF16) for i in range(2)]
        self.tp_b = [kb.buf() for _ in range(2)]
        self.ntp = 0
        self.npsd = 0
        self.ncast = 0


def load_consts(kb, R, ident_d):
    kb.dma("sp", R.ident[:], ident_d, writes=[R.ident_b])


def load_ffn_weights(kb, R, g_lay, wg, wu, wd, w_out=None):
    nc = kb.nc
    kb.dma("sp", R.gt[:], g_lay, writes=[R.gt_b])

    def cast(dst_ap, dst_b, src_ap, src_b, scal):
        e = ("dve", "act", "dve", "act", "pool")[R.ncast % 5]
        R.ncast += 1
        E = kb.E[e]
        if e == "act":
            if scal is None:
                kb.op(e, lambda: E.copy(out=dst_ap, in_=src_ap), reads=[src_b], writes=[dst_b])
            else:
                kb.op(e, lambda: E.activation(out=dst_ap, in_=src_ap, func=AF.Copy, scale=scal),
                      reads=[src_b, R.gt_b], writes=[dst_b])
        elif scal is None:
            kb.op(e, lambda: E.tensor_copy(out=dst_ap, in_=src_ap), reads=[src_b], writes=[dst_b])
        else:
            kb.op(e, lambda: E.tensor_scalar(out=dst_ap, in0=src_ap, scalar1=scal, scalar2=None, op0=ALU.mult),
                  reads=[src_b, R.gt_b], writes=[dst_b])

    k = 0
    for (W, Wb, src) in ((R.Wg, R.Wg_b, wg), (R.Wu, R.Wu_b, wu)):
        for kc in range(8):
            for (c0, c1) in ((0, 1024), (1024, 2048), (2048, DFF)):
                sb = k % 2; k += 1
                kb.dma("sp", R.stage[sb][:, 0:c1 - c0], src[kc * 128:(kc + 1) * 128, c0:c1],
                       writes=[R.stage_b[sb]])
                cast(W[:, kc, c0:c1], Wb, R.stage[sb][:, 0:c1 - c0], R.stage_b[sb], R.gt[:, kc:kc + 1])
    for fc in range(NFC):
        sb = k % 2; k += 1
        kb.dma("sp", R.stage[sb][:, 0:D], wd[fc * 128:(fc + 1) * 128, :], writes=[R.stage_b[sb]])
        cast(R.Wd[:, fc, :], R.Wd_b, R.stage[sb][:, 0:D], R.stage_b[sb], None)
    if w_out is not None:
        for kc in range(8):
            sb = k % 2; k += 1
            kb.dma("sp", R.stage[sb][:, 0:D], w_out[kc * 128:(kc + 1) * 128, :], writes=[R.stage_b[sb]])
            cast(R.Wo[:, kc, :], R.Wo_b, R.stage[sb][:, 0:D], R.stage_b[sb], None)


def norm_transpose(kb, R, x_ap, x_b, dstT, dstT_b, s):
    nc = kb.nc
    ss = R.st[:, 0:1]; rs = R.st[:, 1:2]; rstd = R.st[:, 2:3]
    kb.op("act", lambda: nc.scalar.activation(out=R.xn[:], in_=x_ap, func=AF.Square, accum_out=ss),
          reads=[x_b], writes=[R.xn_b, R.st_b])
    kb.op("act", lambda: nc.scalar.activation(out=rs, in_=ss, func=AF.Sqrt, bias=EPS, scale=1.0 / D),
          reads=[R.st_b], writes=[R.st_b])
    kb.op("dve", lambda: nc.vector.reciprocal(out=rstd, in_=rs), reads=[R.st_b], writes=[R.st_b])
    kb.op("dve", lambda: nc.vector.tensor_scalar(out=R.xn[:], in0=x_ap, scalar1=rstd, scalar2=None, op0=ALU.mult),
          reads=[x_b, R.st_b], writes=[R.xn_b])
    ti = R.ntp % 2; R.ntp += 1
    tp = R.tp[ti]; tpb = R.tp_b[ti]
    for kc in range(8):
        kb.op("pe", lambda kc=kc: nc.tensor.transpose(out=tp[:, kc, :], in_=R.xn[:, kc * 128:(kc + 1) * 128],
                                                      identity=R.ident[:]),
              reads=[R.xn_b, R.ident_b], writes=[tpb], inc=(kc == 7))
    kb.op("act", lambda: nc.scalar.copy(out=dstT[:, :, s * 128:(s + 1) * 128], in_=tp[:, :, :]),
          reads=[tpb], writes=[dstT_b])


def token_pass(kb, R, T, x_in, x_out, pre=None, post=None, in_bufs=None, out_bufs=None, pre_bufs=None, post_bufs=None):
    nc = kb.nc
    NT = T // TT

    def stage_load(i):
        bi = i % 2
        kb.dma("sp", R.xt[bi][:, :, :], x_in[i * TT:(i + 1) * TT, :].rearrange("(s p) d -> p s d", p=128),
               reads=([in_bufs[i]] if in_bufs else []), writes=[R.xt_b[bi]])
        if pre is not None:
            kb.dma("sp", R.yt[bi][:, :, :], pre[:, i * TT:(i + 1) * TT].rearrange("(c p) t -> p c t", p=128),
                   reads=([pre_bufs[i]] if pre_bufs else []), writes=[R.yt_b[bi]])

    def stage_pre(i):
        bi = i % 2
        if pre is None:
            return
        for s in range(SUB):
            for h in range(2):
                pi = R.npsd % 2; R.npsd += 1
                for kc in range(8):
                    kb.op("pe", lambda kc=kc: nc.tensor.matmul(R.psd[pi][:, :], lhsT=R.yt[bi][:, kc, s * 128:(s + 1) * 128],
                                                               rhs=R.Wo[:, kc, h * 512:(h + 1) * 512],
                                                               start=(kc == 0), stop=(kc == 7)),
                          reads=[R.yt_b[bi], R.Wo_b], writes=[R.psd_b[pi]], inc=(kc == 7))
                xs = R.xt[bi][:, s, h * 512:(h + 1) * 512]
                kb.op("dve", lambda: nc.vector.tensor_tensor(out=xs, in0=R.psd[pi][:, :], in1=xs, op=ALU.add),
                      reads=[R.psd_b[pi], R.xt_b[bi]], writes=[R.xt_b[bi]])

    def stage_a(i):
        bi = i % 2
        for s in range(SUB):
            norm_transpose(kb, R, R.xt[bi][:, s, :], R.xt_b[bi], R.xnT[bi], R.xnT_b[bi], s)

    def stage_b(i):
        bi = i % 2
        for fc in range(NFC):
            gi = fc % 2
            for (W, Wb, off) in ((R.Wg, R.Wg_b, 0), (R.Wu, R.Wu_b, 256)):
                for kc in range(8):
                    kb.op("pe", lambda kc=kc, W=W, off=off: nc.tensor.matmul(
                        R.psg[gi][:, off:off + TT], lhsT=W[:, kc, fc * 128:(fc + 1) * 128], rhs=R.xnT[bi][:, kc, :],
                        start=(kc == 0), stop=(kc == 7)),
                          reads=[Wb, R.xnT_b[bi]], writes=[R.psg_b[gi]], inc=(kc == 7))
            kb.op("act", lambda: nc.scalar.activation(out=R.sg[gi][:, :], in_=R.psg[gi][:, 0:TT], func=AF.Silu),
                  reads=[R.psg_b[gi]], writes=[R.sg_b[gi]])
            kb.op("dve", lambda: nc.vector.tensor_tensor(out=R.hid[:, fc, :], in0=R.sg[gi][:, :],
                                                         in1=R.psg[gi][:, 256:256 + TT], op=ALU.mult),
                  reads=[R.sg_b[gi], R.psg_b[gi]], writes=[R.hid_b[fc]])

    def stage_c(i):
        bi = i % 2
        for s in range(SUB):
            for h in range(2):
                pi = R.npsd % 2; R.npsd += 1
                for fc in range(NFC):
                    kb.op("pe", lambda fc=fc: nc.tensor.matmul(R.psd[pi][:, :], lhsT=R.hid[:, fc, s * 128:(s + 1) * 128],
                                                               rhs=R.Wd[:, fc, h * 512:(h + 1) * 512],
                                                               start=(fc == 0), stop=(fc == NFC - 1)),
                          reads=[R.hid_b[fc], R.Wd_b], writes=[R.psd_b[pi]], inc=(fc == NFC - 1))
                xs = R.xt[bi][:, s, h * 512:(h + 1) * 512]
                kb.op("dve", lambda: nc.vector.scalar_tensor_tensor(out=xs, in0=R.psd[pi][:, :], scalar=0.5, in1=xs,
                                                                    op0=ALU.mult, op1=ALU.add),
                      reads=[R.psd_b[pi], R.xt_b[bi]], writes=[R.xt_b[bi]])
            if post is not None:
                norm_transpose(kb, R, R.xt[bi][:, s, :], R.xt_b[bi], R.hTo, R.hTo_b, s)
        kb.dma("pool", x_out[i * TT:(i + 1) * TT, :].rearrange("(s p) d -> p s d", p=128), R.xt[bi][:, :, :],
               reads=[R.xt_b[bi]], writes=([out_bufs[i]] if out_bufs else []))
        if post is not None:
            kb.dma("pool", post[:, i * TT:(i + 1) * TT].rearrange("(c p) t -> p c t", p=128), R.hTo[:, :, :],
                   reads=[R.hTo_b], writes=([post_bufs[i]] if post_bufs else []))

    stage_load(0)
    stage_pre(0)
    stage_a(0)
    for i in range(NT):
        if i + 1 < NT:
            stage_load(i + 1)
        stage_b(i)
        if i + 1 < NT:
            stage_pre(i + 1)
            stage_a(i + 1)
        stage_c(i)


QT_ = 512
NW = 834
A_Q, A_K, A_V = 0, 64, 128
B_Q, B_K, B_V, B_O, B_I, B_F = 192, 256, 320, 384, 448, 449
C_Q, C_K, C_V = 450, 514, 578
D_Q, D_K, D_V = 642, 706, 770


class MixRes:
    def __init__(self, kb, S):
        self.S = S
        self.NB = S // 128
        self.NQ = S // QT_
        NB = self.NB
        self.Wm = kb.sbuf("Wm", [128, 8, NW], BF16); self.Wm_b = kb.buf()
        self.gm = kb.sbuf("gm", [128, 8], F32); self.gm_b = kb.buf()
        self.ident = kb.sbuf("identm", [128, 128], BF16); self.ident_b = kb.buf()
        self.ht = [kb.sbuf(f"ht{i}", [128, 8, QT_], BF16) for i in range(2)]; self.ht_b = [kb.buf() for _ in range(2)]
        self.QT = kb.sbuf("QT", [128, S], BF16); self.QT_b = kb.buf()
        self.KT = kb.sbuf("KT", [128, S], BF16); self.KT_b = kb.buf()
        self.Va = kb.sbuf("Va", [128, NB, 128], BF16); self.Va_b = kb.buf()
        self.par = kb.sbuf("par", [128, 32], F32); self.par_b = kb.buf()
        self.cst = kb.sbuf("cst", [128, 256], BF16); self.cst_b = kb.buf()
        self.sq = [kb.sbuf(f"sq{i}", [64, QT_], BF16) for i in range(2)]; self.sq_b = [kb.buf() for _ in range(2)]
        self.rr = [kb.sbuf(f"rr{i}", [64, QT_], F32) for i in range(2)]; self.rr_b = [kb.buf() for _ in range(2)]
        self.e32 = [kb.sbuf(f"e32_{i}", [128, 2, QT_], F32) for i in range(2)]; self.e32_b = [kb.buf() for _ in range(2)]
        self.wst = [self.e32[i][:, :, :].rearrange("p a b -> p (a b)")[:, 0:NW] for i in range(2)]; self.wst_b = self.e32_b
        self.e32c = kb.sbuf("e32_c", [128, 2, QT_], F32)
        self.e16 = [kb.sbuf(f"e16_{i}", [128, 2, QT_], BF16) for i in range(4)]; self.e16_b = [kb.buf() for _ in range(4)]
        self.spp = [kb.sbuf(f"spp_{i}", [128, 2, QT_], BF16) for i in range(2)]; self.spp_b = [kb.buf() for _ in range(2)]
        self.fin = [kb.sbuf(f"fin{i}", [64, QT_], F32) for i in range(3)]; self.fin_b = [kb.buf() for _ in range(3)]
        self.yo = [kb.sbuf(f"yo{i}", [64, QT_], BF16) for i in range(2)]; self.yo_b = [kb.buf() for _ in range(2)]
        self.mask = kb.sbuf("mask", [128, 4, QT_], BF16); self.mask_b = kb.buf()
        self.EB = kb.sbuf("EB", [128, 8, QT_], F32); self.EB_b = kb.buf()
        self.tri = kb.sbuf("tri", [128, 3, 128], BF16); self.tri_b = kb.buf()
        self.pp = [kb.psum(f"pp{i}", [128, 2, 512], F32) for i in range(4)]
        self.ps = [self.pp[i // 2][:, i % 2, :] for i in range(8)]
        self.ps_b = [kb.buf() for _ in range(8)]
        self.nyo = 0
        self.ne16 = 0


def mix_load_common(kb, R, wsel, gmix_lay, ident_d, cst_d):
    nc = kb.nc
    kb.dma("sp", R.gm[:], gmix_lay, writes=[R.gm_b])
    kb.dma("sp", R.ident[:], ident_d, writes=[R.ident_b])
    kb.dma("sp", R.cst[:], cst_d, writes=[R.cst_b])
    for kc in range(8):
        sb = kc % 2
        kb.dma("sp", R.wst[sb][:, :], wsel[kc * 128:(kc + 1) * 128, :], writes=[R.wst_b[sb]])
        kb.op("dve", lambda: nc.vector.tensor_scalar(out=R.Wm[:, kc, :], in0=R.wst[sb][:, :], scalar1=R.gm[:, kc:kc + 1],
                                                     scalar2=None, op0=ALU.mult),
              reads=[R.wst_b[sb], R.gm_b], writes=[R.Wm_b])


def load_ht(kb, R, hT, qt):
    bi = qt % 2
    if callable(hT):
        src = hT(qt).rearrange("c p t -> p c t")
    else:
        src = hT[:, qt * QT_:(qt + 1) * QT_].rearrange("(c p) t -> p c t", p=128)
    kb.dma("sp", R.ht[bi][:, :, :], src, writes=[R.ht_b[bi]])
    return R.ht[bi], R.ht_b[bi]


def proj_fm(kb, R, ht, ht_b, c0, ncol, pb):
    nc = kb.nc
    for kc in range(8):
        kb.op("pe", lambda kc=kc: nc.tensor.matmul(R.ps[pb][0:ncol, :], lhsT=R.Wm[:, kc, c0:c0 + ncol], rhs=ht[:, kc, :],
                                                   start=(kc == 0), stop=(kc == 7)),
              reads=[R.Wm_b, ht_b], writes=[R.ps_b[pb]], inc=(kc == 7))


def proj_tm(kb, R, ht, ht_b, c0, ncol, pb, s):
    nc = kb.nc
    for kc in range(8):
        kb.op("pe", lambda kc=kc: nc.tensor.matmul(R.ps[pb][:, s * 128:s * 128 + ncol], lhsT=ht[:, kc, s * 128:(s + 1) * 128],
                                                   rhs=R.Wm[:, kc, c0:c0 + ncol], start=(kc == 0), stop=(kc == 7)),
              reads=[R.Wm_b, ht_b], writes=[R.ps_b[pb]], inc=(kc == 7))


def qk_norm_store(kb, R, pb, pb2, dst, dst_b, qt, gcol, cmat, inv_n, i2):
    nc = kb.nc
    sq, sqb = R.sq[i2], R.sq_b[i2]
    rr, rrb = R.rr[i2], R.rr_b[i2]
    kb.op("act", lambda: nc.scalar.activation(out=sq[:, :], in_=R.ps[pb][0:64, :], func=AF.Square),
          reads=[R.ps_b[pb]], writes=[sqb])
    kb.op("pe", lambda: nc.tensor.matmul(R.ps[pb2][0:64, :], lhsT=cmat, rhs=sq[:, :], start=True, stop=True),
          reads=[sqb, R.cst_b], writes=[R.ps_b[pb2]])
    kb.op("act", lambda: nc.scalar.activation(out=rr[:, :], in_=R.ps[pb2][0:64, :], func=AF.Ln, bias=R.par[0:64, 13:14], scale=inv_n),
          reads=[R.ps_b[pb2], R.par_b], writes=[rrb])
    kb.op("act", lambda: nc.scalar.activation(out=rr[:, :], in_=rr[:, :], func=AF.Exp, scale=-0.5), reads=[rrb], writes=[rrb])
    kb.op("dve", lambda: nc.vector.scalar_tensor_tensor(out=dst[0:64, qt * QT_:(qt + 1) * QT_], in0=R.ps[pb][0:64, :],
                                                        scalar=R.par[0:64, gcol:gcol + 1], in1=rr[:, :],
                                                        op0=ALU.mult, op1=ALU.mult),
          reads=[R.ps_b[pb], rrb, R.par_b], writes=[dst_b])


def v_store(kb, R, ht, ht_b, c0, qt, pb):
    nc = kb.nc
    for s in range(4):
        proj_tm(kb, R, ht, ht_b, c0, 64, pb, s)
    src = R.ps[pb][:, :].rearrange("p (s c) -> p s c", c=128)[:, :, 0:64]
    kb.op("act", lambda: nc.scalar.copy(out=R.Va[:, qt * 4:(qt + 1) * 4, 0:64], in_=src),
          reads=[R.ps_b[pb]], writes=[R.Va_b])


def ydst(yT_d, row0, qt):
    if callable(yT_d):
        return yT_d(row0, qt)
    return yT_d[row0:row0 + 64, qt * QT_:(qt + 1) * QT_]


def out_store(kb, R, yT_d, row0, qt, src_fn, reads):
    i = R.nyo % 2; R.nyo += 1
    src_fn(R.yo[i], R.yo_b[i])
    kb.dma("pool", ydst(yT_d, row0, qt), R.yo[i][:, :], reads=[R.yo_b[i]])


def mixer_c(kb, R, hT, yT_d, row0, masks_c):
    nc = kb.nc
    S, NB, NQ = R.S, R.NB, R.NQ
    kb.dma("sp", R.mask[:, :, :], masks_c[:, :, 0, :], writes=[R.mask_b])
    kb.op("pool", lambda: nc.gpsimd.memset(R.Va[:, :, 64:128], 1.0), writes=[R.Va_b])
    bd32 = R.cst[0:64, 64:128]
    ones64 = R.cst[0:64, 0:64]
    for qt in range(NQ):
        ht, htb = load_ht(kb, R, hT, qt)
        proj_fm(kb, R, ht, htb, C_Q, 64, 0)
        proj_fm(kb, R, ht, htb, C_K, 64, 2)
        qk_norm_store(kb, R, 0, 1, R.QT, R.QT_b, qt, 0, bd32, 1.0 / 32, 0)
        qk_norm_store(kb, R, 2, 3, R.KT, R.KT_b, qt, 1, bd32, 1.0 / 32, 1)
        v_store(kb, R, ht, htb, C_V, qt, 4 + (qt % 2))
    for qt in range(NQ):
        nkb = 4 * qt + 4
        O0, O1 = 6, 7
        estate = {}

        def s_step(kbk):
            pj = kbk % 3
            sb = 2 * pj
            for m in range(2):
                kb.op("pe", lambda m=m: nc.tensor.matmul(R.ps[sb + m],
                                                         lhsT=R.KT[m * 32:(m + 1) * 32, kbk * 128:(kbk + 1) * 128],
                                                         rhs=R.QT[m * 32:(m + 1) * 32, qt * QT_:(qt + 1) * QT_],
                                                         start=True, stop=True),
                      reads=[R.KT_b, R.QT_b], writes=[R.ps_b[sb + m]])
            ei = R.ne16 % 3; R.ne16 += 1
            e, eb = R.e16[ei], R.e16_b[ei]
            estate[kbk] = (e, eb)
            kb.op("act", lambda: nc.scalar.activation(out=e[:, :, :], in_=R.pp[pj][:, :, :], func=AF.Exp),
                  reads=[R.ps_b[sb], R.ps_b[sb + 1]], writes=[eb])
            r = kbk - 4 * qt
            if r >= 0:
                for m in range(2):
                    kb.op("dve", lambda m=m: nc.vector.tensor_tensor(out=e[:, m, :], in0=e[:, m, :], in1=R.mask[:, r, :], op=ALU.mult),
                          reads=[eb, R.mask_b], writes=[eb])

        def pv_step(kbk):
            e, eb = estate.pop(kbk)
            for m in range(2):
                kb.op("pe", lambda m=m: nc.tensor.matmul(R.ps[O0 + m][:, :], lhsT=R.Va[:, kbk, :], rhs=e[:, m, :],
                                                         start=(kbk == 0), stop=(kbk == nkb - 1)),
                      reads=[R.Va_b, eb], writes=[R.ps_b[O0 + m]])

        s_step(0)
        if nkb > 1:
            s_step(1)
        for kbk in range(nkb):
            if kbk + 2 < nkb:
                s_step(kbk + 2)
            pv_step(kbk)
        f0, f1, f2 = R.fin
        b0, b1, b2 = R.fin_b
        kb.op("dve", lambda: nc.vector.reciprocal(out=f0[:, :], in_=R.ps[O0][64:128, :]), reads=[R.ps_b[O0]], writes=[b0])
        kb.op("dve", lambda: nc.vector.tensor_tensor(out=f0[:, :], in0=R.ps[O0][0:64, :], in1=f0[:, :], op=ALU.mult),
              reads=[R.ps_b[O0], b0], writes=[b0])
        kb.op("dve", lambda: nc.vector.reciprocal(out=f1[:, :], in_=R.ps[O1][64:128, :]), reads=[R.ps_b[O1]], writes=[b1])
        kb.op("dve", lambda: nc.vector.tensor_tensor(out=f1[:, :], in0=R.ps[O1][0:64, :], in1=f1[:, :], op=ALU.mult),
              reads=[R.ps_b[O1], b1], writes=[b1])
        kb.op("dve", lambda: nc.vector.scalar_tensor_tensor(out=f2[:, :], in0=f1[:, :], scalar=R.par[0:64, 2:3], in1=f0[:, :],
                                                            op0=ALU.mult, op1=ALU.add),
              reads=[b0, b1, R.par_b], writes=[b2])
        kb.op("act", lambda: nc.scalar.activation(out=R.sq[0][:, :], in_=f2[:, :], func=AF.Square), reads=[b2], writes=[R.sq_b[0]])
        kb.op("pe", lambda: nc.tensor.matmul(R.ps[0][0:64, :], lhsT=ones64, rhs=R.sq[0][:, :], start=True, stop=True),
              reads=[R.sq_b[0], R.cst_b], writes=[R.ps_b[0]])
        kb.op("act", lambda: nc.scalar.activation(out=R.rr[0][:, :], in_=R.ps[0][0:64, :], func=AF.Ln, bias=R.par[0:64, 13:14], scale=1.0 / 64),
              reads=[R.ps_b[0], R.par_b], writes=[R.rr_b[0]])
        kb.op("act", lambda: nc.scalar.activation(out=R.rr[0][:, :], in_=R.rr[0][:, :], func=AF.Exp, scale=-0.5), reads=[R.rr_b[0]], writes=[R.rr_b[0]])

        def fn(yo, yob):
            kb.op("dve", lambda: nc.vector.scalar_tensor_tensor(out=yo[:, :], in0=f2[:, :], scalar=R.par[0:64, 3:4], in1=R.rr[0][:, :],
                                                                op0=ALU.mult, op1=ALU.mult),
                  reads=[b2, R.rr_b[0], R.par_b], writes=[yob])
        out_store(kb, R, yT_d, row0, qt, fn, None)


def mixer_d(kb, R, hT, yT_d, row0, masks_d, tri_d):
    nc = kb.nc
    S, NB, NQ = R.S, R.NB, R.NQ
    kb.dma("sp", R.mask[:, :, :], masks_d, writes=[R.mask_b])
    kb.dma("sp", R.tri[:, :, :], tri_d, writes=[R.tri_b])
    for qt in range(NQ):
        ht, htb = load_ht(kb, R, hT, qt)
        proj_fm(kb, R, ht, htb, D_Q, 64, 0)
        proj_fm(kb, R, ht, htb, D_K, 64, 1)
        for half in range(2):
            rows = slice(half * 64, half * 64 + 64)
            kb.op("act", lambda rows=rows: nc.scalar.activation(out=R.QT[rows, qt * QT_:(qt + 1) * QT_], in_=R.ps[0][0:64, :], func=AF.Copy, scale=0.125),
                  reads=[R.ps_b[0]], writes=[R.QT_b])
            kb.op("dve", lambda rows=rows: nc.vector.tensor_copy(out=R.KT[rows, qt * QT_:(qt + 1) * QT_], in_=R.ps[1][0:64, :]),
                  reads=[R.ps_b[1]], writes=[R.KT_b])
        v_store(kb, R, ht, htb, D_V, qt, 4 + (qt % 2))
    RA, RB, OB = 4, 5, 6
    ed_b = [kb.buf() for _ in range(3)]
    ed = [R.e32[0], R.e32[1], R.e32c]
    for qt in range(NQ):
        kbs = list(range(4 * qt + 3, -1, -1))
        npair = len(kbs) // 2
        qsl = slice(qt * QT_, (qt + 1) * QT_)

        def blocks(j):
            return kbs[2 * j], kbs[2 * j + 1]

        def z_mm(j):
            for h, kbk in enumerate(blocks(j)):
                zb = 2 * (j % 2) + h
                rows = slice(h * 64, h * 64 + 64)
                kb.op("pe", lambda kbk=kbk, zb=zb, rows=rows: nc.tensor.matmul(R.ps[zb], lhsT=R.KT[rows, kbk * 128:(kbk + 1) * 128],
                                                                               rhs=R.QT[rows, qsl], start=True, stop=True),
                      reads=[R.KT_b, R.QT_b], writes=[R.ps_b[zb]])

        def esp(j):
            zp = j % 2
            e, eb = ed[j % 3], ed_b[j % 3]
            sp, spb = R.spp[j % 2], R.spp_b[j % 2]
            kb.op("act", lambda: nc.scalar.activation(out=e[:, :, :], in_=R.pp[zp][:, :, :], func=AF.Exp),
                  reads=[R.ps_b[2 * zp], R.ps_b[2 * zp + 1]], writes=[eb])
            kb.op("act", lambda: nc.scalar.activation(out=sp[:, :, :], in_=e[:, :, :], func=AF.Ln, bias=1.0),
                  reads=[eb], writes=[spb])
            for h, kbk in enumerate(blocks(j)):
                r = kbk - 4 * qt
                if r >= 0:
                    kb.op("dve", lambda h=h, r=r: nc.vector.tensor_tensor(out=sp[:, h, :], in0=sp[:, h, :], in1=R.mask[:, r, :], op=ALU.mult),
                          reads=[spb, R.mask_b], writes=[spb])

        def mmR(bank, t_i, sp_ap, spb, start, stop=False):
            kb.op("pe", lambda: nc.tensor.matmul(R.ps[bank], lhsT=R.tri[:, t_i, :], rhs=sp_ap, start=start, stop=stop),
                  reads=[R.tri_b, spb], writes=[R.ps_b[bank]])

        def chain_a(j):
            sp, spb = R.spp[j % 2], R.spp_b[j % 2]
            mmR(RA, 0, sp[:, 0, :], spb, j == 0)
            mmR(RB, 2, sp[:, 0, :], spb, j == 0)
            mmR(RB, 0, sp[:, 1, :], spb, False)

        def chain_b(j):
            e, eb = ed[j % 3], ed_b[j % 3]
            sp, spb = R.spp[j % 2], R.spp_b[j % 2]
            tt, ttb = R.e16[j % 2], R.e16_b[j % 2]
            aa, aab = R.e16[2 + j % 2], R.e16_b[2 + j % 2]
            kb.op("act", lambda: nc.scalar.activation(out=tt[:, :, :], in_=R.pp[2][:, :, :], func=AF.Exp, scale=-1.0),
                  reads=[R.ps_b[RA], R.ps_b[RB]], writes=[ttb])
            mmR(RA, 1, sp[:, 0, :], spb, False)
            mmR(RA, 2, sp[:, 1, :], spb, False, j == npair - 1)
            mmR(RB, 1, sp[:, 1, :], spb, False, j == npair - 1)
            kb.op("dve", lambda: nc.vector.tensor_tensor(out=aa[:, :, :], in0=e[:, :, :], in1=tt[:, :, :], op=ALU.mult),
                  reads=[eb, ttb], writes=[aab])
            for h, kbk in enumerate(blocks(j)):
                r = kbk - 4 * qt
                if r >= 0:
                    kb.op("dve", lambda h=h, r=r: nc.vector.tensor_tensor(out=aa[:, h, :], in0=aa[:, h, :], in1=R.mask[:, r, :], op=ALU.mult),
                          reads=[aab, R.mask_b], writes=[aab])

        def pv(j):
            aa, aab = R.e16[2 + j % 2], R.e16_b[2 + j % 2]
            for h, kbk in enumerate(blocks(j)):
                kb.op("pe", lambda h=h, kbk=kbk: nc.tensor.matmul(R.ps[OB][0:64, :], lhsT=R.Va[:, kbk, 0:64], rhs=aa[:, h, :],
                                                                  start=(j == 0 and h == 0), stop=(j == npair - 1 and h == 1)),
                      reads=[R.Va_b, aab], writes=[R.ps_b[OB]])

        z_mm(0)
        if npair > 1:
            z_mm(1)
        esp(0)
        for j in range(npair):
            if j + 1 < npair:
                esp(j + 1)
            chain_a(j)
            if j + 2 < npair:
                z_mm(j + 2)
            if j >= 1:
                pv(j - 1)
            chain_b(j)
        pv(npair - 1)

        def fn(yo, yob):
            kb.op("dve", lambda: nc.vector.tensor_copy(out=yo[:, :], in_=R.ps[OB][0:64, :]), reads=[R.ps_b[OB]], writes=[yob])
        out_store(kb, R, yT_d, row0, qt, fn, None)


def mixer_a(kb, R, hT, yT_d, row0, biasT_d):
    nc = kb.nc
    S, NB, NQ = R.S, R.NB, R.NQ
    kb.dma("sp", R.EB[:, :, :], biasT_d, writes=[R.EB_b])
    for r in range(8):
        kb.op("act", lambda r=r: nc.scalar.activation(out=R.EB[:, r, :], in_=R.EB[:, r, :], func=AF.Exp),
              reads=[R.EB_b], writes=[R.EB_b])
    kb.op("pool", lambda: nc.gpsimd.memset(R.Va[:, :, 64:128], 1.0), writes=[R.Va_b])
    ones64 = R.cst[0:64, 0:64]
    for qt in range(NQ):
        ht, htb = load_ht(kb, R, hT, qt)
        proj_fm(kb, R, ht, htb, A_Q, 64, 0)
        proj_fm(kb, R, ht, htb, A_K, 64, 2)
        qk_norm_store(kb, R, 0, 1, R.QT, R.QT_b, qt, 4, ones64, 1.0 / 64, 0)
        qk_norm_store(kb, R, 2, 3, R.KT, R.KT_b, qt, 5, ones64, 1.0 / 64, 1)
        v_store(kb, R, ht, htb, A_V, qt, 4 + (qt % 2))
    OB = 4
    for qt in range(NQ):
        rs = [r for r in range(8) if 4 * qt - 4 + r >= 0]

        def s_step(j):
            r = rs[j]
            kbk = 4 * qt - 4 + r
            sb = j % 2
            kb.op("pe", lambda: nc.tensor.matmul(R.ps[sb][:, :], lhsT=R.KT[0:64, kbk * 128:(kbk + 1) * 128],
                                                 rhs=R.QT[0:64, qt * QT_:(qt + 1) * QT_], start=True, stop=True),
                  reads=[R.KT_b, R.QT_b], writes=[R.ps_b[sb]])
            e, eb = R.e32[j % 2], R.e32_b[j % 2]
            p, pbuf = R.e16[j % 2], R.e16_b[j % 2]
            kb.op("act", lambda: nc.scalar.activation(out=e[:, 0, :], in_=R.ps[sb][:, :], func=AF.Exp),
                  reads=[R.ps_b[sb]], writes=[eb])
            kb.op("dve", lambda: nc.vector.tensor_tensor(out=p[:, 0, :], in0=e[:, 0, :], in1=R.EB[:, r, :], op=ALU.mult),
                  reads=[eb, R.EB_b], writes=[pbuf])

        def pv_step(j):
            kbk = 4 * qt - 4 + rs[j]
            p, pbuf = R.e16[j % 2], R.e16_b[j % 2]
            kb.op("pe", lambda: nc.tensor.matmul(R.ps[OB][:, :], lhsT=R.Va[:, kbk, :], rhs=p[:, 0, :],
                                                 start=(j == 0), stop=(j == len(rs) - 1)),
                  reads=[R.Va_b, pbuf], writes=[R.ps_b[OB]])

        s_step(0)
        for j in range(len(rs)):
            if j + 1 < len(rs):
                s_step(j + 1)
            pv_step(j)
        f0, b0 = R.fin[0], R.fin_b[0]
        kb.op("dve", lambda: nc.vector.reciprocal(out=f0[:, :], in_=R.ps[OB][64:128, :]), reads=[R.ps_b[OB]], writes=[b0])

        def fn(yo, yob):
            kb.op("dve", lambda: nc.vector.tensor_tensor(out=yo[:, :], in0=R.ps[OB][0:64, :], in1=f0[:, :], op=ALU.mult),
                  reads=[R.ps_b[OB], b0], writes=[yob])
        out_store(kb, R, yT_d, row0, qt, fn, None)


class MixResB:
    def __init__(self, kb, R):
        NB = R.NB
        self.Osig = R.EB[:, :, :].bitcast(BF16).rearrange("p a (b c) -> p (a b) c", c=64)[:, 0:NB, :]; self.Osig_b = R.EB_b
        self.G = kb.sbuf("Gates", [128, 8, NB], F32); self.G_b = kb.buf()
        self.trif = kb.sbuf("trif", [128, 2, 128], F32); self.trif_b = kb.buf()
        self.cw = kb.sbuf("convw", [128, 8], F32); self.cw_b = kb.buf()
        self.gob = kb.sbuf("gob", [128, 64], F32); self.gob_b = kb.buf()
        self.St = [kb.sbuf(f"St{i}", [64, 65], F32) for i in range(2)]; self.St_b = [kb.buf() for _ in range(2)]
        self.Sb = [kb.sbuf(f"Sb{i}", [64, 65], BF16) for i in range(3)]; self.Sb_b = [kb.buf() for _ in range(3)]
        self.tok = [kb.sbuf(f"tok{i}", [128, 3, 64], BF16) for i in range(2)]; self.tok_b = [kb.buf() for _ in range(2)]
        self.qkT = [kb.sbuf(f"qkT{i}", [64, 2, 128], BF16) for i in range(2)]; self.qkT_b = [kb.buf() for _ in range(2)]
        self.qkm = [kb.sbuf(f"qkm{i}", [128, 128], BF16) for i in range(2)]; self.qkm_b = [kb.buf() for _ in range(2)]
        self.cm = kb.sbuf("cmask", [128, 128], F32); self.cm_b = kb.buf()
        self.hn = [kb.sbuf(f"hn{i}", [128, 64], F32) for i in range(2)]; self.hn_b = [kb.buf() for _ in range(2)]
        self.hs = [kb.sbuf(f"hs{i}", [128, 8], F32) for i in range(2)]; self.hs_b = [kb.buf() for _ in range(2)]
        self.yb = [kb.sbuf(f"yb{i}", [128, 64], BF16) for i in range(2)]; self.yb_b = [kb.buf() for _ in range(2)]
        self.jk = kb.sbuf("jk", [128, 64], BF16); self.jk_b = kb.buf()


def mixer_b(kb, R, RB_, hT, yT_d, row0, bpar_d, trif_d, cmask_d, gob_d):
    nc = kb.nc
    S, NB, NQ = R.S, R.NB, R.NQ
    B = RB_
    kb.dma("sp", B.cw[:, :], bpar_d, writes=[B.cw_b])
    kb.dma("sp", B.trif[:, :, :], trif_d, writes=[B.trif_b])
    kb.dma("sp", B.cm[:, :], cmask_d, writes=[B.cm_b])
    kb.dma("sp", B.gob[:, :], gob_d, writes=[B.gob_b])
    kb.op("pool", lambda: nc.gpsimd.memset(R.Va[:, :, 64:65], 1.0), writes=[R.Va_b])
    kb.op("dve", lambda: nc.vector.tensor_scalar(out=B.cw[:, 7:8], in0=B.cw[:, 6:7], scalar1=-1.0, scalar2=None, op0=ALU.mult),
          reads=[B.cw_b], writes=[B.cw_b])
    cv = [R.e32[i][:, :, :].rearrange("p a b -> p (a b)") for i in range(2)]
    cvb = R.e32_b
    accA = R.e16[0][:, :, :].rearrange("p a b -> p (a b)").bitcast(F32); accB_ = R.e16[1][:, :, :].rearrange("p a b -> p (a b)").bitcast(F32)
    accA_b = R.e16_b[0]; accB_b = R.e16_b[1]
    for qt in range(NQ):
        ht, htb = load_ht(kb, R, hT, qt)
        ci = qt % 2
        proj_fm(kb, R, ht, htb, B_Q, 128, 0)
        if qt == 0:
            kb.op("dve", lambda: nc.vector.memset(cv[ci][:, 0:3], 0.0), writes=[cvb[ci]])
        else:
            kb.op("dve", lambda: nc.vector.tensor_copy(out=cv[ci][:, 0:3], in_=cv[1 - ci][:, 512:515]),
                  reads=[cvb[1 - ci]], writes=[cvb[ci]])
        kb.op("act", lambda: nc.scalar.copy(out=cv[ci][:, 3:515], in_=R.ps[0][:, :]), reads=[R.ps_b[0]], writes=[cvb[ci]])
        kb.op("dve", lambda: nc.vector.tensor_scalar(out=accA, in0=cv[ci][:, 3:515], scalar1=B.cw[:, 3:4], scalar2=B.cw[:, 4:5],
                                                     op0=ALU.mult, op1=ALU.add),
              reads=[cvb[ci], B.cw_b], writes=[accA_b])
        for j in (2, 1, 0):
            kb.op("dve", lambda j=j: nc.vector.scalar_tensor_tensor(out=accA, in0=cv[ci][:, j:j + 512], scalar=B.cw[:, j:j + 1],
                                                                    in1=accA, op0=ALU.mult, op1=ALU.add),
                  reads=[cvb[ci], B.cw_b, accA_b], writes=[accA_b])
        kb.op("act", lambda: nc.scalar.activation(out=accB_, in_=accA, func=AF.Sigmoid), reads=[accA_b], writes=[accB_b])
        kb.op("dve", lambda: nc.vector.tensor_tensor(out=accB_, in0=accA, in1=accB_, op=ALU.mult), reads=[accA_b, accB_b], writes=[accB_b])
        kb.op("act", lambda: nc.scalar.copy(out=R.QT[0:64, qt * QT_:(qt + 1) * QT_], in_=accB_[0:64, :]), reads=[accB_b], writes=[R.QT_b])
        kb.op("act", lambda: nc.scalar.copy(out=R.KT[0:64, qt * QT_:(qt + 1) * QT_], in_=accB_[64:128, :]), reads=[accB_b], writes=[R.KT_b])
        pb = 4 + (qt % 2)
        for s in range(4):
            nonlocal_pb = 3 + ((qt * 4 + s) % 4)
            for kc in range(8):
                kb.op("pe", lambda kc=kc: nc.tensor.matmul(R.ps[nonlocal_pb][:, 0:130], lhsT=ht[:, kc, s * 128:(s + 1) * 128],
                                                           rhs=R.Wm[:, kc, B_V:B_V + 130], start=(kc == 0), stop=(kc == 7)),
                      reads=[R.Wm_b, htb], writes=[R.ps_b[nonlocal_pb]], inc=(kc == 7))
            blk = qt * 4 + s
            kb.op("dve", lambda: nc.vector.tensor_copy(out=R.Va[:, blk, 0:64], in_=R.ps[nonlocal_pb][:, 0:64]),
                  reads=[R.ps_b[nonlocal_pb]], writes=[R.Va_b])
            kb.op("act", lambda: nc.scalar.activation(out=B.Osig[:, blk, :], in_=R.ps[nonlocal_pb][:, 64:128], func=AF.Sigmoid),
                  reads=[R.ps_b[nonlocal_pb]], writes=[B.Osig_b])
            kb.op("dve", lambda: nc.vector.tensor_copy(out=B.G[:, 0:2, blk], in_=R.ps[nonlocal_pb][:, 128:130]),
                  reads=[R.ps_b[nonlocal_pb]], writes=[B.G_b])
    G = B.G
    kb.op("act", lambda: nc.scalar.activation(out=G[:, 2, :], in_=G[:, 1, :], func=AF.Exp, scale=-1.0, bias=B.cw[:, 7:8]),
          reads=[B.G_b, B.cw_b], writes=[B.G_b])
    kb.op("act", lambda: nc.scalar.activation(out=G[:, 2, :], in_=G[:, 2, :], func=AF.Ln, bias=1.0), reads=[B.G_b], writes=[B.G_b])
    kb.op("dve", lambda: nc.vector.tensor_scalar(out=G[:, 2, :], in0=G[:, 2, :], scalar1=-1.0, scalar2=None, op0=ALU.mult),
          reads=[B.G_b], writes=[B.G_b])
    kb.op("pe", lambda: nc.tensor.matmul(R.ps[0][:, 0:NB], lhsT=B.trif[:, 0, :], rhs=G[:, 2, :], start=True, stop=True),
          reads=[B.trif_b, B.G_b], writes=[R.ps_b[0]])
    kb.op("pe", lambda: nc.tensor.matmul(R.ps[1][:, 0:NB], lhsT=B.trif[:, 1, :], rhs=G[:, 2, :], start=True, stop=True),
          reads=[B.trif_b, B.G_b], writes=[R.ps_b[1]])
    kb.op("dve", lambda: nc.vector.tensor_copy(out=G[:, 3, :], in_=R.ps[0][:, 0:NB]), reads=[R.ps_b[0]], writes=[B.G_b])
    kb.op("act", lambda: nc.scalar.activation(out=G[:, 4, :], in_=G[:, 3, :], func=AF.Exp), reads=[B.G_b], writes=[B.G_b])
    kb.op("dve", lambda: nc.vector.tensor_tensor(out=G[:, 5, :], in0=G[:, 0, :], in1=G[:, 3, :], op=ALU.subtract),
          reads=[B.G_b], writes=[B.G_b])
    kb.op("dve", lambda: nc.vector.tensor_tensor(out=G[:, 6, :], in0=G[:, 5, :], in1=R.ps[1][:, 0:NB], op=ALU.add),
          reads=[B.G_b, R.ps_b[1]], writes=[B.G_b])
    kb.op("act", lambda: nc.scalar.activation(out=G[:, 5, :], in_=G[:, 5, :], func=AF.Exp, bias=B.cw[:, 5:6]),
          reads=[B.G_b, B.cw_b], writes=[B.G_b])
    kb.op("act", lambda: nc.scalar.activation(out=G[:, 6, :], in_=G[:, 6, :], func=AF.Exp, bias=B.cw[:, 5:6]),
          reads=[B.G_b, B.cw_b], writes=[B.G_b])
    kb.op("dve", lambda: nc.vector.tensor_scalar(out=G[:, 5:7, :], in0=G[:, 5:7, :], scalar1=0.125, scalar2=None, op0=ALU.mult),
          reads=[B.G_b], writes=[B.G_b])
    kb.op("act", lambda: nc.scalar.activation(out=G[:, 7, :], in_=R.ps[1][:, 0:NB], func=AF.Exp), reads=[R.ps_b[1]], writes=[B.G_b])
    kb.op("dve", lambda: nc.vector.memset(B.St[0][:, :], 0.0), writes=[B.St_b[0]])
    kb.op("dve", lambda: nc.vector.memset(B.Sb[0][:, :], 0.0), writes=[B.Sb_b[0]])
    PT, PT2, PS_, PO, PU = 0, 1, 2, 3, 6
    def front(b):
        i2 = b % 2
        tok, tokb = B.tok[i2], B.tok_b[i2]
        qkT, qkTb = B.qkT[i2], B.qkT_b[i2]
        tpA = R.ps[0][:, :].bitcast(BF16); tpq = tpA[:, 0:128]; tpk = tpA[:, 128:256]
        kb.op("pe", lambda: nc.tensor.transpose(out=tpq[:, 0:64], in_=R.QT[0:64, b * 128:(b + 1) * 128], identity=R.ident[0:64, 0:64]),
              reads=[R.QT_b, R.ident_b], writes=[R.ps_b[0]])
        kb.op("pe", lambda: nc.tensor.transpose(out=tpk[:, 0:64], in_=R.KT[0:64, b * 128:(b + 1) * 128], identity=R.ident[0:64, 0:64]),
              reads=[R.KT_b, R.ident_b], writes=[R.ps_b[0]])
        kb.op("dve", lambda: nc.vector.tensor_scalar(out=tok[:, 0, :], in0=tpq[:, 0:64], scalar1=G[:, 4, b:b + 1], scalar2=None, op0=ALU.mult),
              reads=[R.ps_b[0], B.G_b], writes=[tokb])
        kb.op("dve", lambda: nc.vector.tensor_scalar(out=tok[:, 1, :], in0=tpk[:, 0:64], scalar1=G[:, 5, b:b + 1], scalar2=None, op0=ALU.mult),
              reads=[R.ps_b[0], B.G_b], writes=[tokb])
        kb.op("dve", lambda: nc.vector.tensor_scalar(out=tok[:, 2, :], in0=tpk[:, 0:64], scalar1=G[:, 6, b:b + 1], scalar2=None, op0=ALU.mult),
              reads=[R.ps_b[0], B.G_b], writes=[tokb])
        tpB = R.ps[1][:, :].bitcast(BF16); tq2 = tpB[:, 0:128]; tk2 = tpB[:, 128:256]
        kb.op("pe", lambda: nc.tensor.transpose(out=tq2[0:64, :], in_=tok[:, 0, :], identity=R.ident[:, :]),
              reads=[tokb, R.ident_b], writes=[R.ps_b[1]])
        kb.op("pe", lambda: nc.tensor.transpose(out=tk2[0:64, :], in_=tok[:, 1, :], identity=R.ident[:, :]),
              reads=[tokb, R.ident_b], writes=[R.ps_b[1]])
        kb.op("act", lambda: nc.scalar.copy(out=qkT[:, 0, :], in_=tq2[0:64, :]), reads=[R.ps_b[1]], writes=[qkTb])
        kb.op("act", lambda: nc.scalar.copy(out=qkT[:, 1, :], in_=tk2[0:64, :]), reads=[R.ps_b[1]], writes=[qkTb])
        kb.op("pe", lambda: nc.tensor.matmul(R.ps[PS_][:, 0:128], lhsT=qkT[:, 1, :], rhs=qkT[:, 0, :], start=True, stop=True),
              reads=[qkTb], writes=[R.ps_b[PS_]])
        qkm, qkmb = B.qkm[i2], B.qkm_b[i2]
        kb.op("dve", lambda: nc.vector.tensor_tensor(out=qkm[:, :], in0=R.ps[PS_][:, 0:128], in1=B.cm[:, :], op=ALU.mult),
              reads=[R.ps_b[PS_], B.cm_b], writes=[qkmb])
        kb.op("pe", lambda: nc.tensor.matmul(R.ps[PU][0:64, 0:65], lhsT=tok[:, 2, :], rhs=R.Va[:, b, 0:65], start=True, stop=True),
              reads=[tokb, R.Va_b], writes=[R.ps_b[PU]])
        Sn, Snb = B.St[1 - i2], B.St_b[1 - i2]
        So, Sob = B.St[i2], B.St_b[i2]
        kb.op("dve", lambda: nc.vector.scalar_tensor_tensor(out=Sn[:, :], in0=So[:, :], scalar=G[0:64, 7, b:b + 1], in1=R.ps[PU][0:64, 0:65],
                                                            op0=ALU.mult, op1=ALU.add),
              reads=[Sob, B.G_b, R.ps_b[PU]], writes=[Snb])
        kb.op("act", lambda: nc.scalar.copy(out=B.Sb[(b + 1) % 3][:, :], in_=Sn[:, :]), reads=[Snb], writes=[B.Sb_b[(b + 1) % 3]])

    def back(b):
        i2 = b % 2
        tok, tokb = B.tok[i2], B.tok_b[i2]
        qkT, qkTb = B.qkT[i2], B.qkT_b[i2]
        qkm, qkmb = B.qkm[i2], B.qkm_b[i2]
        po = PO + (b % 2)
        Sp, Spb = B.Sb[b % 3], B.Sb_b[b % 3]
        po = PO + (b % 2)
        kb.op("pe", lambda: nc.tensor.matmul(R.ps[po][:, 0:65], lhsT=qkm[:, :], rhs=R.Va[:, b, 0:65], start=True, stop=False),
              reads=[qkmb, R.Va_b], writes=[R.ps_b[po]], inc=False)
        kb.op("pe", lambda: nc.tensor.matmul(R.ps[po][:, 0:65], lhsT=qkT[:, 0, :], rhs=Sp[:, :], start=False, stop=True),
              reads=[qkTb, Spb], writes=[R.ps_b[po]])
        hs, hsb = B.hs[i2], B.hs_b[i2]
        hn, hnb = B.hn[i2], B.hn_b[i2]
        kb.op("act", lambda: nc.scalar.activation(out=hs[:, 5:6], in_=R.ps[po][:, 64:65], func=AF.Abs),
              reads=[R.ps_b[po]], writes=[hsb])
        kb.op("dve", lambda: nc.vector.tensor_scalar(out=hs[:, 0:1], in0=hs[:, 5:6], scalar1=1.0, scalar2=None, op0=ALU.max),
              reads=[hsb], writes=[hsb])
        kb.op("dve", lambda: nc.vector.reciprocal(out=hs[:, 1:2], in_=hs[:, 0:1]), reads=[hsb], writes=[hsb])
        kb.op("dve", lambda: nc.vector.tensor_scalar(out=hn[:, :], in0=R.ps[po][:, 0:64], scalar1=hs[:, 1:2], scalar2=None, op0=ALU.mult),
              reads=[R.ps_b[po], hsb], writes=[hnb])
        kb.op("act", lambda: nc.scalar.activation(out=B.jk[:, :], in_=hn[:, :], func=AF.Square, accum_out=hs[:, 2:3]),
              reads=[hnb], writes=[B.jk_b, hsb])
        kb.op("act", lambda: nc.scalar.activation(out=hs[:, 3:4], in_=hs[:, 2:3], func=AF.Sqrt, bias=EPS, scale=1.0 / 64),
              reads=[hsb], writes=[hsb])
        kb.op("dve", lambda: nc.vector.reciprocal(out=hs[:, 4:5], in_=hs[:, 3:4]), reads=[hsb], writes=[hsb])
        kb.op("dve", lambda: nc.vector.scalar_tensor_tensor(out=hn[:, :], in0=hn[:, :], scalar=hs[:, 4:5], in1=B.gob[:, :],
                                                            op0=ALU.mult, op1=ALU.mult),
              reads=[hnb, hsb, B.gob_b], writes=[hnb])
        yb, ybb = B.yb[i2], B.yb_b[i2]
        kb.op("dve", lambda: nc.vector.tensor_tensor(out=yb[:, :], in0=hn[:, :], in1=B.Osig[:, b, :], op=ALU.mult),
              reads=[hnb, B.Osig_b], writes=[ybb])
        ty = R.ps[5][:, :].bitcast(BF16)[:, 0:128]
        kb.op("pe", lambda: nc.tensor.transpose(out=ty[0:64, :], in_=yb[:, :], identity=R.ident[:, :]),
              reads=[ybb, R.ident_b], writes=[R.ps_b[5]])
        qt = b // 4
        if b % 4 == 0:
            R.cur_yo = R.nyo % 2; R.nyo += 1
        yo, yob = R.yo[R.cur_yo], R.yo_b[R.cur_yo]
        kb.op("act", lambda: nc.scalar.copy(out=yo[:, (b % 4) * 128:(b % 4 + 1) * 128], in_=ty[0:64, :]),
              reads=[R.ps_b[5]], writes=[yob])
        if b % 4 == 3:
            kb.dma("pool", ydst(yT_d, row0, qt), yo[:, :], reads=[yob])

    front(0)
    for b in range(NB):
        if b + 1 < NB:
            front(b + 1)
        back(b)


def mix_params(kb, R, praw_d, clam_d):
    nc = kb.nc
    pr = R.par
    kb.dma("sp", pr[0:64, 16:24], praw_d, writes=[R.par_b])
    cl = R.rr[0][:, 0:128].rearrange("p (a b) -> p a b", a=4)
    kb.dma("sp", cl, clam_d, writes=[R.rr_b[0]])
    V = nc.vector
    kb.op("dve", lambda: V.memset(pr[0:64, 13:14], EPS), writes=[R.par_b])
    kb.op("dve", lambda: V.tensor_scalar(out=pr[0:64, 0:1], in0=pr[0:64, 16:17], scalar1=32 ** -0.5, scalar2=None, op0=ALU.mult), reads=[R.par_b], writes=[R.par_b])
    kb.op("dve", lambda: V.tensor_copy(out=pr[0:64, 1:2], in_=pr[0:64, 17:18]), reads=[R.par_b], writes=[R.par_b])
    kb.op("dve", lambda: V.tensor_tensor(out=pr[0:64, 3:4], in0=pr[0:64, 18:19], in1=pr[0:64, 22:23], op=ALU.mult), reads=[R.par_b], writes=[R.par_b])
    kb.op("dve", lambda: V.tensor_scalar(out=pr[0:64, 4:5], in0=pr[0:64, 19:20], scalar1=0.125, scalar2=None, op0=ALU.mult), reads=[R.par_b], writes=[R.par_b])
    kb.op("dve", lambda: V.tensor_copy(out=pr[0:64, 5:6], in_=pr[0:64, 20:21]), reads=[R.par_b], writes=[R.par_b])
    pp = R.rr[1][:, 0:64].rearrange("p (a b) -> p a b", a=2)
    kb.op("dve", lambda: V.tensor_tensor(out=pp[:, 0, :], in0=cl[:, 0, :], in1=cl[:, 1, :], op=ALU.mult), reads=[R.rr_b[0]], writes=[R.rr_b[1]])
    kb.op("dve", lambda: V.tensor_tensor(out=pp[:, 1, :], in0=cl[:, 2, :], in1=cl[:, 3, :], op=ALU.mult), reads=[R.rr_b[0]], writes=[R.rr_b[1]])
    kb.op("dve", lambda: V.reduce_sum(out=pr[0:64, 8:10], in_=pp, axis=AX.X), reads=[R.rr_b[1]], writes=[R.par_b])
    kb.op("act", lambda: nc.scalar.activation(out=pr[0:64, 10:12], in_=pr[0:64, 8:10], func=AF.Exp), reads=[R.par_b], writes=[R.par_b])
    kb.op("dve", lambda: V.tensor_tensor(out=pr[0:64, 12:13], in0=pr[0:64, 11:12], in1=pr[0:64, 10:11], op=ALU.subtract), reads=[R.par_b], writes=[R.par_b])
    kb.op("dve", lambda: V.tensor_tensor(out=pr[0:64, 2:3], in0=pr[0:64, 12:13], in1=pr[0:64, 21:22], op=ALU.subtract), reads=[R.par_b], writes=[R.par_b])

import ml_dtypes
bf16 = ml_dtypes.bfloat16
GW = 256
OFF = dict(aq=0, ak=256, av=512, bqk=768, bv=1280, bo=1536, bi=1792, bf=1796, cq=1800, ck=2056, cv=2312, dq=2568, dk=2824, dv=3080)

def sel_cols(j):
    c = []
    r = lambda o: list(range(o + j * 64, o + j * 64 + 64))
    c += r(OFF['aq']) + r(OFF['ak']) + r(OFF['av'])
    c += r(OFF['bqk']) + r(OFF['bqk'] + 256) + r(OFF['bv']) + r(OFF['bo']) + [OFF['bi'] + j, OFF['bf'] + j]
    c += r(OFF['cq']) + r(OFF['ck']) + r(OFF['cv'])
    c += r(OFF['dq']) + r(OFF['dk']) + r(OFF['dv'])
    return np.array(c)

def const_inputs():
    d = {}
    d['ident'] = np.eye(128, dtype=np.float32).astype(bf16)
    cst = np.zeros((128, 256), np.float32)
    cst[0:64, 0:64] = 1.0
    cst[0:32, 64:96] = 1.0; cst[32:64, 96:128] = 1.0
    d['cst'] = cst.astype(bf16)
    s = np.arange(128)[:, None, None, None]; r = np.arange(4)[None, :, None, None]; t = np.arange(512)[None, None, None, :]
    mc = ((2 * r + (s >= 64)) <= (t // 64)).astype(np.float32)
    d['masks_c'] = np.broadcast_to(mc, (128, 4, 2, 512)).astype(bf16).copy()
    s = np.arange(128)[:, None, None]; r = np.arange(4)[None, :, None]; t = np.arange(512)[None, None, :]
    d['masks_d'] = ((128 * r + s) < t).astype(np.float32).astype(bf16)
    j = np.arange(128)[:, None]; s2 = np.arange(128)[None, :]
    tri = np.zeros((128, 3, 128), np.float32)
    tri[:, 0, :] = (j >= s2); tri[:, 1, :] = (j < s2); tri[:, 2, :] = 1.0
    d['tri'] = tri.astype(bf16)
    trif = np.zeros((128, 2, 128), np.float32)
    trif[:, 0, :] = (j <= s2); trif[:, 1, :] = 1.0
    d['trif'] = trif
    d['cmask'] = (j <= s2).astype(np.float32)
    return d

def bias_index():
    s = np.arange(128)[:, None, None]; r = np.arange(8)[None, :, None]; t = np.arange(512)[None, None, :]
    rel = t - s + 512 - 128 * r
    idx = np.clip(rel, -128, 128) + 128
    dd = t // 64 + 8 - 2 * r - s // 64
    vis = (dd >= 0) & (dd <= 8)
    return idx, vis

_IDX, _VIS = bias_index()

def layer_core_inputs(P, l, j, lam_init=None):
    d = {}
    d['wsel'] = np.ascontiguousarray(P['w_in'][l][:, sel_cols(j)])
    d['gmix'] = np.ascontiguousarray(P['mix_norm'][l].reshape(8, 128).T)
    praw = np.zeros((64, 8), np.float32)
    praw[:, 0] = np.tile(P['c_q_norm'][l], 2); praw[:, 1] = np.tile(P['c_k_norm'][l], 2)
    praw[:, 2] = P['c_out_norm'][l]; praw[:, 3] = P['a_q_norm'][l]; praw[:, 4] = P['a_k_norm'][l]
    if lam_init is None:
        lam_init = 0.8 - 0.6 * np.exp(-0.3 * l)
    praw[:, 5] = lam_init; praw[:, 6] = 1.0 - lam_init
    d['praw'] = praw
    d['clam'] = np.ascontiguousarray(np.broadcast_to(P['c_lambda'][l][None], (64, 4, 32))).astype(np.float32)
    rb = P['a_rel_bias'][l][j]
    d['biasT'] = np.where(_VIS, rb[_IDX], np.float32(-1e30)).astype(np.float32)
    bpar = np.zeros((128, 8), np.float32)
    ch = np.concatenate([np.arange(j * 64, j * 64 + 64), 256 + np.arange(j * 64, j * 64 + 64)])
    bpar[:, 0:4] = P['b_conv_w'][l][:, ch].T
    bpar[:, 4] = P['b_conv_b'][l][ch]
    bpar[:, 5] = P['b_gate_bias'][l][0, j]
    bpar[:, 6] = P['b_gate_bias'][l][1, j]
    d['bpar'] = bpar
    d['gob'] = np.ascontiguousarray(np.broadcast_to(P['b_out_norm'][l][j][None], (128, 64))).astype(np.float32)
    return d


from concourse.bass_utils import run_bass_kernel_spmd

SEQ = 16384
NCORE = 8
TPC = 4096
DEPTH = 2
GROUPS = [[0, 1, 2, 3], [4, 5, 6, 7]]


def _din(nc, name, shape, dt):
    return nc.dram_tensor(name, list(shape), dt, kind="ExternalInput").ap()


def _dout(nc, name, shape, dt):
    return nc.dram_tensor(name, list(shape), dt, kind="ExternalOutput").ap()


def _dint(nc, name, shape, dt):
    return nc.dram_tensor(name, list(shape), dt, kind="Internal").ap()


MIX_IN = dict(wsel=([D, NW], F32), gmix=([128, 8], F32), praw=([64, 8], F32), clam=([64, 4, 32], F32),
              biasT=([128, 8, 512], F32), bpar=([128, 8], F32), gob=([128, 64], F32))
CONST_IN = dict(ident=([128, 128], BF16), cst=([128, 256], BF16), masks_c=([128, 4, 2, 512], BF16),
                masks_d=([128, 4, 512], BF16), tri=([128, 3, 128], BF16), trif=([128, 2, 128], F32), cmask=([128, 128], F32))


def build_fused(S=SEQ, T=TPC):
    nc = bass.Bass("TRN2", target_bir_lowering=False)
    NQr = T // QT_
    x_in = _din(nc, "x_in", [T, D], F32)
    x_out = _dout(nc, "x_out", [T, D], F32)
    Cn = {k: _din(nc, k, sh, dt) for k, (sh, dt) in CONST_IN.items()}
    ffn = {}
    for l in range(DEPTH):
        for f in ("ffn1", "ffn2"):
            ffn[(f, l)] = dict(g=_din(nc, f"{f}_g{l}", [128, 8], F32), wg=_din(nc, f"{f}_wg{l}", [D, DFF], F32),
                               wu=_din(nc, f"{f}_wu{l}", [D, DFF], F32), wd=_din(nc, f"{f}_wd{l}", [DFF, D], F32))
    wo = [_din(nc, f"wo{l}", [D, D], F32) for l in range(DEPTH)]
    mx = [{k: _din(nc, f"{k}{l}", sh, dt) for k, (sh, dt) in MIX_IN.items()} for l in range(DEPTH)]
    xa = _dint(nc, "xa", [T, D], F32); xb = _dint(nc, "xb", [T, D], F32); xc = _dint(nc, "xc", [T, D], F32)
    hT_loc = _dint(nc, "hT_loc", [D, T], BF16)
    hT_all = _dint(nc, "hT_all", [4 * D, T], BF16)
    yT_loc = _dint(nc, "yT_loc", [D, T], BF16)
    yT_all = _dint(nc, "yT_all", [4 * D, T], BF16)
    yT_mine = _dint(nc, "yT_mine", [D, T], BF16)

    hv = hT_all.rearrange("(k r p) t -> k r p t", k=8, r=4)

    def hT_src(qt):
        r, o = qt // NQr, (qt % NQr) * QT_
        return hv[:, r, :, o:o + QT_]

    def y_dst(row0, qt):
        q, o = qt // NQr, (qt % NQr) * QT_
        return yT_loc[q * 256 + row0:q * 256 + row0 + 64, o:o + QT_]

    with ExitStack() as st:
        kb = KB(nc, st)
        pid = nc.sync.partition_id()
        qv = pid % 4

        ymine_b = kb.buf()

        def fetch_mine():
            kb.wait_cc("sp")
            yv2 = yT_all.rearrange("(q h j p) t -> q h j p t", q=4, h=2, j=4)
            for j in range(4):
                for h in range(2):
                    kb.dma("sp", yT_mine[j * 256 + h * 128:j * 256 + (h + 1) * 128, :],
                           yv2[bass.ds(qv, 1), h, j, :, :].rearrange("o p t -> (o p) t"), writes=[ymine_b])

        def tok_phase(passes, after_first_load=None):
            with ExitStack() as mem:
                kb.mem = mem
                R = TokRes(kb, any(p.get("wo") is not None for p in passes))
                load_consts(kb, R, Cn["ident"])
                prev_bufs = None
                for k, p in enumerate(passes):
                    w = p["ffn"]
                    load_ffn_weights(kb, R, w["g"], w["wg"], w["wu"], w["wd"], p.get("wo"))
                    if k == 0 and after_first_load is not None:
                        after_first_load()
                    ob = [kb.buf() for _ in range(T // TT)] if k + 1 < len(passes) else None
                    has_pre = p.get("wo") is not None
                    token_pass(kb, R, T, p["xi"], p["xo"], pre=(yT_mine if has_pre else None),
                               post=p.get("post"), in_bufs=prev_bufs, out_bufs=ob,
                               pre_bufs=([ymine_b] * (T // TT) if has_pre else None))
                    prev_bufs = ob
                kb.barrier()
            kb.mem = st

        def mix_phase(l):
            with ExitStack() as mem:
                kb.mem = mem
                R = MixRes(kb, S)
                RB = MixResB(kb, R)
                m = mx[l]
                mix_load_common(kb, R, m["wsel"], m["gmix"], Cn["ident"], Cn["cst"])
                mix_params(kb, R, m["praw"], m["clam"])
                kb.wait_cc("sp")
                mixer_a(kb, R, hT_src, y_dst, 0, m["biasT"])
                kb.barrier()
                mixer_b(kb, R, RB, hT_src, y_dst, 64, m["bpar"], Cn["trif"], Cn["cmask"], m["gob"])
                kb.barrier()
                mixer_c(kb, R, hT_src, y_dst, 128, Cn["masks_c"])
                kb.barrier()
                mixer_d(kb, R, hT_src, y_dst, 192, Cn["masks_d"], Cn["tri"])
                kb.barrier()
            kb.mem = st

        tok_phase([dict(ffn=ffn[("ffn1", 0)], xi=x_in, xo=xa, post=hT_loc)])
        kb.allgather(hT_loc, hT_all, GROUPS)
        mix_phase(0)
        kb.allgather(yT_loc, yT_all, GROUPS)
        tok_phase([dict(ffn=ffn[("ffn2", 0)], wo=wo[0], xi=xa, xo=xb),
                   dict(ffn=ffn[("ffn1", 1)], xi=xb, xo=xc, post=hT_loc)], after_first_load=fetch_mine)
        kb.allgather(hT_loc, hT_all, GROUPS)
        mix_phase(1)
        kb.allgather(yT_loc, yT_all, GROUPS)
        tok_phase([dict(ffn=ffn[("ffn2", 1)], wo=wo[1], xi=xc, xo=x_out)], after_first_load=fetch_mine)
        kb.finish()
    return nc


def build_mixer_prog(S=SEQ):
    nc = bass.Bass("TRN2", target_bir_lowering=False)
    hT = _din(nc, "hT", [D, S], BF16)
    Cn = {k: _din(nc, k, sh, dt) for k, (sh, dt) in CONST_IN.items()}
    m = {k: _din(nc, k, sh, dt) for k, (sh, dt) in MIX_IN.items()}
    yT = _dout(nc, "yT", [256, S], BF16)
    with ExitStack() as st:
        kb = KB(nc, st)
        R = MixRes(kb, S)
        RB = MixResB(kb, R)
        mix_load_common(kb, R, m["wsel"], m["gmix"], Cn["ident"], Cn["cst"])
        mix_params(kb, R, m["praw"], m["clam"])
        mixer_a(kb, R, hT, yT, 0, m["biasT"])
        kb.barrier()
        mixer_b(kb, R, RB, hT, yT, 64, m["bpar"], Cn["trif"], Cn["cmask"], m["gob"])
        kb.barrier()
        mixer_c(kb, R, hT, yT, 128, Cn["masks_c"])
        kb.barrier()
        mixer_d(kb, R, hT, yT, 192, Cn["masks_d"], Cn["tri"])
        kb.finish()
    return nc


def _lay(g):
    return np.ascontiguousarray(np.asarray(g, np.float32).reshape(8, 128).T)


def _wo_perm(w_out):
    idx = np.arange(1024).reshape(4, 4, 64)
    perm = idx.transpose(1, 0, 2).reshape(-1)
    return np.ascontiguousarray(w_out[perm, :])


def make_in_maps(P, TPC=TPC):
    x = np.ascontiguousarray(P["x"], dtype=np.float32).reshape(-1, D)
    C = const_inputs()
    shared = dict(C)
    for l in range(DEPTH):
        for f in ("ffn1", "ffn2"):
            shared[f"{f}_g{l}"] = _lay(P[f + "_norm"][l])
            shared[f"{f}_wg{l}"] = np.ascontiguousarray(P[f + "_wg"][l], dtype=np.float32)
            shared[f"{f}_wu{l}"] = np.ascontiguousarray(P[f + "_wu"][l], dtype=np.float32)
            shared[f"{f}_wd{l}"] = np.ascontiguousarray(P[f + "_wd"][l], dtype=np.float32)
        shared[f"wo{l}"] = _wo_perm(np.asarray(P["w_out"][l], np.float32))
    ims = []
    for c in range(NCORE):
        d = dict(shared)
        d["x_in"] = x[c * TPC:(c + 1) * TPC]
        j = c % 4
        for l in range(DEPTH):
            for k, v in layer_core_inputs(P, l, j).items():
                d[f"{k}{l}"] = v
        ims.append(d)
    return ims


def kernel(**inputs):
    P = {k: np.asarray(v) for k, v in inputs.items()}
    nc = build_fused()
    ims = make_in_maps(P)
    res = run_bass_kernel_spmd(nc, ims, core_ids=list(range(NCORE)))
    out = np.concatenate([r["x_out"] for r in res.results], axis=0).reshape(2, SEQ, D).astype(np.float32)
    return out
```
